# Optimizing a Trainium2 kernel written in Bass

```python
import jax, jax.numpy as jnp
from jax import lax
import numpy as np

D_MODEL = 1024
BATCH = 4
SEQ = 4096
DEPTH = 2

HEAD_DIM = 64
ATT_HEADS = D_MODEL // 128
ATT_WIDTH = ATT_HEADS * HEAD_DIM
DILATED_CONFIGS = ((128, 1), (512, 4), (2048, 16))
BLOCK = 128
ROPE_THETA = 10000.0
RWKV_HEAD = 64
RWKV_HEADS = D_MODEL // 128
RWKV_WIDTH = RWKV_HEADS * RWKV_HEAD
DECAY_LORA = 64
AAA_LORA = 64
GATE_LORA = 128
RWKV_COLS = 3 * RWKV_WIDTH + DECAY_LORA + AAA_LORA + GATE_LORA
GMLP_CHUNK = 128
GMLP_GROUPS = 8
GMLP_WIDTH = D_MODEL // 2
GMLP_GROUP_DIM = GMLP_WIDTH // GMLP_GROUPS
N_BRANCH = 3
IN_COLS = 3 * ATT_WIDTH + RWKV_COLS + 2 * GMLP_WIDTH + N_BRANCH * D_MODEL
D_FF = 2816
CONV_WIDTH = 3
RMS_EPS = 1e-6
LN_EPS = 1e-5
GN_EPS = 64e-5

kernel_name = 'hybrid_gated_parallel_mixer_block'


def rms_norm(x, g):
    xf = x.astype(jnp.float32)
    y = xf * lax.rsqrt(jnp.mean(xf * xf, axis=-1, keepdims=True) + RMS_EPS)
    return (y * g).astype(x.dtype)


def layer_norm(x, g, b, eps):
    xf = x.astype(jnp.float32)
    mu = jnp.mean(xf, axis=-1, keepdims=True)
    var = jnp.mean(jnp.square(xf - mu), axis=-1, keepdims=True)
    return (xf - mu) * lax.rsqrt(var + eps) * g + b


def modulate(h, shift, scale):
    return h * (1.0 + scale[:, None, :]) + shift[:, None, :]


def rope_tables(seq_len):
    inv = 1.0 / (ROPE_THETA ** (jnp.arange(0, HEAD_DIM, 2, dtype=jnp.float32) / HEAD_DIM))
    ang = jnp.arange(seq_len, dtype=jnp.float32)[:, None] * inv[None, :]
    return jnp.cos(ang), jnp.sin(ang)


def apply_rope(t, cos, sin):
    t1, t2 = jnp.split(t.astype(jnp.float32), 2, axis=-1)
    c = cos[None, :, None, :]
    s = sin[None, :, None, :]
    return jnp.concatenate([t1 * c - t2 * s, t2 * c + t1 * s], axis=-1)


def dilated_window_attention(q, k, v, window, dilation):
    B, S, H, Dh = q.shape
    span = window // dilation
    L = -(-S // (dilation * BLOCK)) * BLOCK
    pad = L * dilation - S
    nb = L // BLOCK

    def to_blocks(t):
        t = jnp.pad(t, ((0, 0), (0, pad), (0, 0), (0, 0)))
        t = t.reshape(B, L, dilation, H, Dh).transpose(0, 2, 3, 1, 4)
        return t.reshape(B, dilation, H, nb, BLOCK, Dh)

    def with_prev(t):
        prev = jnp.pad(t, ((0, 0), (0, 0), (0, 0), (1, 0), (0, 0), (0, 0)))[:, :, :, :-1]
        return jnp.concatenate([prev, t], axis=4)

    qb = to_blocks(q)
    kc = with_prev(to_blocks(k))
    vc = with_prev(to_blocks(v))
    s = jnp.einsum('brhnqe,brhnke->brhnqk', qb, kc).astype(jnp.float32) * (Dh ** -0.5)
    qi = jnp.arange(BLOCK)[:, None] + BLOCK
    ki = jnp.arange(2 * BLOCK)[None, :]
    dist = qi - ki
    band = (dist >= 0) & (dist <= span)
    has_prev = (jnp.arange(nb) > 0)[:, None, None] | (ki >= BLOCK)[None]
    mask = band[None] & has_prev
    s = jnp.where(mask, s, -jnp.inf)
    m = jnp.max(s, axis=-1, keepdims=True)
    p = jnp.exp(s - m)
    l = jnp.sum(p, axis=-1, keepdims=True)
    o = jnp.einsum('brhnqk,brhnke->brhnqe', p, vc.astype(jnp.float32)) / l
    lse = (m + jnp.log(l))[..., 0]

    def from_blocks(t):
        tail = t.shape[5:]
        t = t.reshape((B, dilation, H, L) + tail)
        t = jnp.moveaxis(t, 3, 1)
        return t.reshape((B, L * dilation, H) + tail)[:, :S]

    return from_blocks(o), from_blocks(lse)


def dilated_mixture_attention(q, k, v):
    outs, lses = [], []
    for window, dilation in DILATED_CONFIGS:
        o, lse = dilated_window_attention(q, k, v, window, dilation)
        outs.append(o)
        lses.append(lse)
    wts = jax.nn.softmax(jnp.stack(lses, axis=0), axis=0)
    return jnp.sum(wts[..., None] * jnp.stack(outs, axis=0), axis=0)


def token_shift(t):
    return jnp.pad(t, ((0, 0), (1, 0), (0, 0)))[:, :-1]


def wkv7_scan(r, decay, k, v, a, b):
    B, S, H, N = r.shape
    seq = tuple(jnp.moveaxis(t, 1, 0) for t in (r, decay, k, v, a, b))

    def step(state, inp):
        r_t, w_t, k_t, v_t, a_t, b_t = inp
        sa = jnp.einsum('bhij,bhj->bhi', state, a_t)
        state = (state * w_t[:, :, None, :] + sa[..., None] * b_t[:, :, None, :]
                 + v_t[..., None] * k_t[:, :, None, :])
        return state, jnp.einsum('bhij,bhj->bhi', state, r_t)

    _, y = lax.scan(step, jnp.zeros((B, H, N, N), jnp.float32), seq)
    return jnp.moveaxis(y, 0, 1)


def rwkv7_time_mix(p_rwkv, mu, w0, w_up, a0, a_up, g_up, k_k, k_a, r_k, gn_w, gn_b):
    B, S, _ = p_rwkv.shape
    z = p_rwkv.astype(jnp.float32)
    z = z + (token_shift(z) - z) * mu
    o1 = RWKV_WIDTH
    o2 = 2 * RWKV_WIDTH
    o3 = 3 * RWKV_WIDTH
    o4 = o3 + DECAY_LORA
    o5 = o4 + AAA_LORA
    r, k, v = z[..., :o1], z[..., o1:o2], z[..., o2:o3]
    xw, xa, xg = z[..., o3:o4], z[..., o4:o5], z[..., o5:]
    w = -jax.nn.softplus(-(w0 + jnp.tanh(xw) @ w_up)) - 0.5
    a = jax.nn.sigmoid(a0 + xa @ a_up)
    g = jax.nn.sigmoid(xg) @ g_up
    hs = lambda t: t.reshape(B, S, RWKV_HEADS, RWKV_HEAD)
    kk = hs(k * k_k)
    kk = kk / jnp.maximum(jnp.sqrt(jnp.sum(kk * kk, axis=-1, keepdims=True)), 1e-12)
    k = k * (1.0 + (a - 1.0) * k_a)
    r4, k4, v4, a4 = hs(r), hs(k), hs(v), hs(a)
    decay = jnp.exp(-jnp.exp(hs(w)))
    y = wkv7_scan(r4, decay, k4, v4, -kk, kk * a4)
    mu_y = jnp.mean(y, axis=-1, keepdims=True)
    var_y = jnp.mean(jnp.square(y - mu_y), axis=-1, keepdims=True)
    y = (y - mu_y) * lax.rsqrt(var_y + GN_EPS) * gn_w.reshape(RWKV_HEADS, RWKV_HEAD) \
        + gn_b.reshape(RWKV_HEADS, RWKV_HEAD)
    y = y + jnp.sum(r4 * k4 * r_k, axis=-1, keepdims=True) * v4
    return y.reshape(B, S, RWKV_WIDTH) * g


def chunked_spatial_gating(p_uv, ln_g, ln_b, w_s, b_s):
    B, S, _ = p_uv.shape
    z = jax.nn.gelu(p_uv.astype(jnp.float32))
    u, v = z[..., :GMLP_WIDTH], z[..., GMLP_WIDTH:]
    v = layer_norm(v, ln_g, ln_b, LN_EPS)
    nc = S // GMLP_CHUNK
    vg = v.reshape(B, nc, GMLP_CHUNK, GMLP_GROUPS, GMLP_GROUP_DIM)
    causal = jnp.tril(jnp.ones((GMLP_CHUNK, GMLP_CHUNK), dtype=bool))
    ws = jnp.where(causal[None], w_s, 0.0)
    s = jnp.einsum('gij,bcjgd->bcigd', ws, vg) + b_s.T[None, None, :, :, None]
    return u * s.reshape(B, S, GMLP_WIDTH)


def conv_gated_ffn(h, w_up, conv_w, conv_b, w_down):
    S = h.shape[1]
    up = h @ w_up
    a, g = up[..., :D_FF], up[..., D_FF:]
    ap = jnp.pad(a, ((0, 0), (CONV_WIDTH - 1, 0), (0, 0)))
    a = conv_b + sum(conv_w[j] * ap[:, j:j + S] for j in range(CONV_WIDTH))
    return (jax.nn.gelu(a) * g) @ w_down


def setup_inputs(seed: int = 0) -> dict:
    key = jax.random.key(seed)
    ks = jax.random.split(key, 32)
    nrm = lambda k, shape, scale: jax.random.normal(k, shape, jnp.float32) * scale
    L = DEPTH
    return {
        'x': nrm(ks[0], (BATCH, SEQ, D_MODEL), 1.0),
        'c': nrm(ks[1], (BATCH, D_MODEL), 1.0),
        'w_ada': nrm(ks[2], (L, D_MODEL, 6 * D_MODEL), 0.5 * D_MODEL ** -0.5),
        'b_ada': nrm(ks[3], (L, 6 * D_MODEL), 0.02),
        'norm_mix': 1.0 + nrm(ks[4], (L, D_MODEL), 0.05),
        'w_in': nrm(ks[5], (L, D_MODEL, IN_COLS), D_MODEL ** -0.5),
        'rwkv_mu': jax.random.uniform(ks[6], (L, RWKV_COLS), jnp.float32),
        'rwkv_w0': jax.random.uniform(ks[7], (L, RWKV_WIDTH), jnp.float32, minval=-5.0, maxval=-1.0),
        'rwkv_w_up': nrm(ks[8], (L, DECAY_LORA, RWKV_WIDTH), 0.5 * DECAY_LORA ** -0.5),
        'rwkv_a0': nrm(ks[9], (L, RWKV_WIDTH), 0.1),
        'rwkv_a_up': nrm(ks[10], (L, AAA_LORA, RWKV_WIDTH), AAA_LORA ** -0.5),
        'rwkv_g_up': nrm(ks[11], (L, GATE_LORA, RWKV_WIDTH), GATE_LORA ** -0.5),
        'rwkv_k_k': 0.85 + nrm(ks[12], (L, RWKV_WIDTH), 0.05),
        'rwkv_k_a': 1.0 + nrm(ks[13], (L, RWKV_WIDTH), 0.05),
        'rwkv_r_k': nrm(ks[14], (L, RWKV_HEADS, RWKV_HEAD), 0.1),
        'rwkv_gn_w': 1.0 + nrm(ks[15], (L, RWKV_WIDTH), 0.05),
        'rwkv_gn_b': nrm(ks[16], (L, RWKV_WIDTH), 0.02),
        'gmlp_ln_g': 1.0 + nrm(ks[17], (L, GMLP_WIDTH), 0.05),
        'gmlp_ln_b': nrm(ks[18], (L, GMLP_WIDTH), 0.02),
        'gmlp_w_s': nrm(ks[19], (L, GMLP_GROUPS, GMLP_CHUNK, GMLP_CHUNK), 0.5 * GMLP_CHUNK ** -0.5),
        'gmlp_b_s': 1.0 + nrm(ks[20], (L, GMLP_GROUPS, GMLP_CHUNK), 0.05),
        'w_branch': nrm(ks[21], (L, N_BRANCH, ATT_WIDTH, D_MODEL), ATT_WIDTH ** -0.5),
        'w_out': nrm(ks[22], (L, D_MODEL, D_MODEL), D_MODEL ** -0.5),
        'norm_ffn': 1.0 + nrm(ks[23], (L, D_MODEL), 0.05),
        'ffn_w_up': nrm(ks[24], (L, D_MODEL, 2 * D_FF), D_MODEL ** -0.5),
        'ffn_conv_w': nrm(ks[25], (L, CONV_WIDTH, D_FF), CONV_WIDTH ** -0.5),
        'ffn_conv_b': nrm(ks[26], (L, D_FF), 0.02),
        'ffn_w_down': nrm(ks[27], (L, D_FF, D_MODEL), D_FF ** -0.5),
        'norm_final': 1.0 + nrm(ks[28], (D_MODEL,), 0.05),
    }


def reference(x, c, w_ada, b_ada, norm_mix, w_in, rwkv_mu, rwkv_w0, rwkv_w_up, rwkv_a0,
              rwkv_a_up, rwkv_g_up, rwkv_k_k, rwkv_k_a, rwkv_r_k, rwkv_gn_w, rwkv_gn_b,
              gmlp_ln_g, gmlp_ln_b, gmlp_w_s, gmlp_b_s, w_branch, w_out, norm_ffn,
              ffn_w_up, ffn_conv_w, ffn_conv_b, ffn_w_down, norm_final):
    B, S, D = x.shape
    cos, sin = rope_tables(S)
    c_act = jax.nn.silu(c)
    i0 = 3 * ATT_WIDTH
    i1 = i0 + RWKV_COLS
    i2 = i1 + 2 * GMLP_WIDTH
    for l in range(DEPTH):
        mod = c_act @ w_ada[l] + b_ada[l]
        sh1, sc1, gt1, sh2, sc2, gt2 = jnp.split(mod, 6, axis=-1)

        h = modulate(rms_norm(x, norm_mix[l]), sh1, sc1)
        p = h @ w_in[l]
        qkv = p[..., :i0].reshape(B, S, 3, ATT_HEADS, HEAD_DIM)
        q = apply_rope(qkv[:, :, 0], cos, sin)
        k = apply_rope(qkv[:, :, 1], cos, sin)
        y_att = dilated_mixture_attention(q, k, qkv[:, :, 2]).reshape(B, S, ATT_WIDTH)
        y_rwkv = rwkv7_time_mix(p[..., i0:i1], rwkv_mu[l], rwkv_w0[l], rwkv_w_up[l], rwkv_a0[l],
                                rwkv_a_up[l], rwkv_g_up[l], rwkv_k_k[l], rwkv_k_a[l], rwkv_r_k[l],
                                rwkv_gn_w[l], rwkv_gn_b[l])
        y_gmlp = chunked_spatial_gating(p[..., i1:i2], gmlp_ln_g[l], gmlp_ln_b[l], gmlp_w_s[l], gmlp_b_s[l])
        ys = jnp.stack([y_att, y_rwkv, y_gmlp], axis=2).astype(x.dtype)
        gates = jax.nn.sigmoid(p[..., i2:].reshape(B, S, N_BRANCH, D))
        branch = jnp.einsum('bsnw,nwd->bsnd', ys, w_branch[l])
        merged = jnp.sum(gates * branch, axis=2)
        x = x + gt1[:, None, :] * (merged @ w_out[l])

        h = modulate(rms_norm(x, norm_ffn[l]), sh2, sc2)
        x = x + gt2[:, None, :] * conv_gated_ffn(h, ffn_w_up[l], ffn_conv_w[l], ffn_conv_b[l], ffn_w_down[l])
    return rms_norm(x, norm_final)
```

```python
import numpy as np
import ml_dtypes
import concourse.bass as bass
import concourse.mybir as mybir
from concourse.bass_utils import run_bass_kernel_spmd

F32 = mybir.dt.float32
BF16 = mybir.dt.bfloat16
AF = mybir.ActivationFunctionType
ALU = mybir.AluOpType
AX = mybir.AxisListType

D = 1024
import os as _os
NL = int(_os.environ.get("KNL", "2"))
TT = 512
IN_COLS = 7424
DFF = 2816
NF = 22


class Buf:
    __slots__ = ("name", "lw", "rd", "excl")

    def __init__(self, name="", excl=False):
        self.name = name
        self.lw = None
        self.rd = []
        self.excl = excl


class Tl:
    __slots__ = ("t", "b")

    def __init__(self, t, b):
        self.t = t
        self.b = b

    def __getitem__(self, k):
        return self.t[k]


def _b(x):
    return x.b if isinstance(x, Tl) else x


def View(ap, buf):
    return Tl(ap, _b(buf))


class Prog:
    ENGS = ("pe", "dve", "act", "pool", "sp")
    NDMA = 8

    def __init__(self):
        self.nc = bass.Bass("TRN2", target_bir_lowering=False)
        self.ops = {e: [] for e in self.ENGS}
        self.cnt = {e: 0 for e in self.ENGS}
        self.seen = {e: {} for e in self.ENGS}
        self.sems = {}
        self._stack = []
        nc = self.nc
        for e in ("pe", "dve", "act", "pool"):
            self.sems[e] = self._enter(nc.semaphore("s_" + e))
        self.dslots = {}
        for q in ("sp", "act", "pool"):
            sl = []
            for i in range(self.NDMA):
                key = "d_%s%d" % (q, i)
                self.sems[key] = self._enter(nc.semaphore(key))
                sl.append([key, 0])
            self.dslots[q] = [sl, 0]
        self.n_sb = 0

    def _enter(self, cm):
        v = cm.__enter__()
        self._stack.append(cm)
        return v

    def sb(self, shape, dtype, name=None):
        self.n_sb += 1
        t = self._enter(self.nc.sbuf_tensor("s_%s_%d" % (name or "t", self.n_sb), list(shape), dtype))
        return Tl(t, Buf(name or ""))

    def ps(self, shape, dtype=F32, name=None):
        self.n_sb += 1
        t = self._enter(self.nc.psum_tensor(name or ("ps%d" % self.n_sb), list(shape), dtype))
        return Tl(t, Buf(name or "", excl=True))

    def _deps(self, eng, reads, writes):
        toks = []
        for b in reads:
            b = _b(b)
            if b.lw is not None:
                toks.append(b.lw)
        for b in writes:
            b = _b(b)
            if b.lw is not None:
                toks.append(b.lw)
            toks.extend(b.rd)
        need = {}
        for (k, v) in toks:
            if k == "pe" and eng == "pe":
                continue
            if self.seen[eng].get(k, 0) >= v:
                continue
            if need.get(k, 0) < v:
                need[k] = v
        for k, v in need.items():
            self.seen[eng][k] = v
        return list(need.items())

    def _commit(self, tok, reads, writes):
        for b in reads:
            b = _b(b)
            b.rd.append(tok)
            if len(b.rd) > 64:
                mx = {}
                for (k, v) in b.rd:
                    if mx.get(k, 0) < v:
                        mx[k] = v
                b.rd = list(mx.items())
        for b in writes:
            b = _b(b)
            b.lw = tok
            b.rd = []

    def op(self, eng, fn, reads=(), writes=()):
        if eng != "pe":
            ex = [b for b in reads if _b(b).excl]
            if ex:
                reads = [b for b in reads if not _b(b).excl]
                writes = list(writes) + ex
        waits = self._deps(eng, reads, writes)
        self.cnt[eng] += 1
        tok = (eng, self.cnt[eng])
        self.ops[eng].append((waits, fn, (eng, 1)))
        self._commit(tok, reads, writes)
        return tok

    def dma(self, q, out, in_, reads=(), writes=()):
        sl, idx = self.dslots[q]
        slot = sl[idx % self.NDMA]
        self.dslots[q][1] = idx + 1
        key = slot[0]
        waits = self._deps(q, reads, writes)
        if slot[1] > 0 and self.seen[q].get(key, 0) < slot[1]:
            waits.append((key, slot[1]))
            self.seen[q][key] = slot[1]
        slot[1] += 16
        tok = (key, slot[1])

        def fn(e, out=out, in_=in_):
            return e.dma_start(out=out, in_=in_)
        self.ops[q].append((waits, fn, (key, 16)))
        self._commit(tok, reads, writes)
        return tok

    def fence(self, src, dst):
        toks = []
        for s in src:
            s = _b(s)
            if s.lw is not None:
                toks.append(s.lw)
            toks.extend(s.rd)
        for d in dst:
            _b(d).rd.extend(toks)

    def finish(self, final_bufs):
        waits = self._deps("sp", final_bufs, ())
        self.ops["sp"].append((waits, None, None))
        nc = self.nc
        sems = self.sems
        ops = self.ops
        with nc.Block() as block:
            def mk(e):
                def body(eng):
                    for (waits, fn, inc) in ops[e]:
                        for (k, v) in waits:
                            eng.wait_ge(sems[k], v)
                        if fn is not None:
                            ins = fn(eng)
                            ins.then_inc(sems[inc[0]], inc[1])
                return body
            block.tensor(mk("pe"))
            block.vector(mk("dve"))
            block.scalar(mk("act"))
            block.gpsimd(mk("pool"))
            block.sync(mk("sp"))
        for cm in reversed(self._stack):
            cm.__exit__(None, None, None)
        self._stack = []
        return nc


class _Stop(Exception):
    pass


def build(T, dbg=False):
    import os
    STOP = os.environ.get("KSTOP")
    P = Prog()

    def chk(name):
        if STOP == name:
            raise _Stop()
    nc = P.nc
    NTI = T // TT
    NB = T // 128

    def din(name, shape, dt=F32):
        return nc.dram_tensor(name, list(shape), dt, kind="ExternalInput").ap()

    xT_in = din("xT", [D, T])
    c_in = din("c128", [128, 8])
    w_ada = din("w_ada", [NL, D, 6 * D])
    b_ada = din("b_ada", [NL, 128, 48])
    g_mix = din("g_mix", [NL, 128, 8])
    g_ffn = din("g_ffn", [NL, 128, 8])
    g_fin = din("g_fin", [128, 8])
    w_in = din("w_in", [NL, D, IN_COLS])
    mu_d = din("mu", [NL, 128, 14])
    pr_d = din("rwp", [NL, 128, 5, 4])
    wup_d = din("r_wup", [NL, 64, 512])
    aup_d = din("r_aup", [NL, 64, 512])
    gup_d = din("r_gup", [NL, 128, 512])
    gnw_d = din("gnw", [NL, 128, 512])
    gnb_d = din("gnb", [NL, 128, 512])
    lng_d = din("lng", [NL, 128, 512])
    lnb_d = din("lnb", [NL, 128, 512])
    wsT_d = din("wsT", [NL, 128, 8, 128])
    bs_d = din("bsT", [NL, 128, 4, 128])
    wbr_d = din("w_branch", [NL, 3, 512, D])
    wout_d = din("w_out", [NL, D, D])
    wfu_d = din("ffn_up", [NL, D, 2 * DFF])
    cw_d = din("convw", [NL, 128, NF, 3])
    cb_d = din("convb", [NL, 128, NF])
    wfd_d = din("ffn_down", [NL, DFF, D])
    cos_d = din("cosT", [128, T])
    sin_d = din("sinT", [128, T])
    cst_d = din("cst_f32", [128, 5, 128])
    cstb_d = din("cst_bf", [128, 8, 128], BF16)
    m12_d = din("m12", [128, 2, 256], BF16)
    m3_d = din("m3", [128, 4, 4, 64], BF16)
    mrw_d = din("mrw", [128, 4, 2, 128], BF16)
    mlo_d = din("mlo", [128, 4, 128], BF16)
    sel_d = din("sel", [128, 4, 8], BF16)
    yT_out = nc.dram_tensor("yT", [D, T], F32, kind="ExternalOutput").ap()
    x1T = nc.dram_tensor("x1T", [D, T], F32, kind="Internal").ap()
    Vd = [nc.dram_tensor("Vd%d" % l, [T, 512], BF16, kind="Internal").ap() for l in range(NL)]
    b_x1 = Buf("x1T")
    b_vd = [Buf("vd0"), Buf("vd1")]
    b_out = Buf("out")

    def MM(out, lhsT, rhs, start, stop, r, w):
        P.op("pe", lambda e: e.matmul(out, lhsT, rhs, start=start, stop=stop), reads=r, writes=w)

    def TR(out, in_, ident, r, w):
        P.op("pe", lambda e: e.transpose(out, in_, ident), reads=r, writes=w)

    def ACT(out, in_, func, r, w, bias=0.0, scale=1.0):
        P.op("act", lambda e: e.activation(out=out, in_=in_, func=func, bias=bias, scale=scale), reads=r, writes=w)

    def TTe(eng, out, in0, in1, op, r, w):
        P.op(eng, lambda e: e.tensor_tensor(out=out, in0=in0, in1=in1, op=op), reads=r, writes=w)

    def TS(eng, out, in0, s1, s2, op0, op1, r, w):
        if s2 is None:
            P.op(eng, lambda e: e.tensor_scalar(out=out, in0=in0, scalar1=s1, scalar2=None, op0=op0), reads=r, writes=w)
        else:
            P.op(eng, lambda e: e.tensor_scalar(out=out, in0=in0, scalar1=s1, scalar2=s2, op0=op0, op1=op1), reads=r, writes=w)

    def STT(eng, out, in0, sc, in1, op0, op1, r, w):
        P.op(eng, lambda e: e.scalar_tensor_tensor(out=out, in0=in0, scalar=sc, in1=in1, op0=op0, op1=op1), reads=r, writes=w)

    def CP(eng, out, in_, r, w):
        if eng == "act":
            P.op("act", lambda e: e.copy(out=out, in_=in_), reads=r, writes=w)
        else:
            P.op(eng, lambda e: e.tensor_copy(out=out, in_=in_), reads=r, writes=w)

    def MSET(eng, ap, val, w):
        P.op(eng, lambda e: e.memset(ap, val), writes=w)

    psf = [P.ps([128, 512], F32) for _ in range(6)]
    psb = [P.ps([128, 1024], BF16) for _ in range(2)]
    psi = [0, 0]

    def PSF():
        psi[0] += 1
        return psf[psi[0] % 6]

    def PSB():
        psi[1] += 1
        return psb[psi[1] % 2]

    cst = P.sb([128, 5, 128], F32)
    cstb = P.sb([128, 8, 128], BF16)
    m12 = P.sb([128, 2, 256], BF16)
    m3 = P.sb([128, 4, 4, 64], BF16)
    mrw = P.sb([128, 4, 2, 128], BF16)
    mlo = P.sb([128, 4, 128], BF16)
    sel = P.sb([128, 4, 8], BF16)
    for (t_, d_) in ((cst, cst_d), (cstb, cstb_d), (m12, m12_d), (m3, m3_d), (mrw, mrw_d), (mlo, mlo_d), (sel, sel_d)):
        P.dma("sp", t_[:], d_, writes=[t_])
    identf = cst[:, 0, :]
    onesf = cst[:, 1, :]
    blkones = cst[:, 2, :]
    identb = cstb[:, 0, :]
    permb = cstb[:, 1, :]
    onesb = cstb[:, 2, :]

    xt = P.sb([128, 8, TT], F32, "xt")
    scr = P.sb([128, 8, TT], F32, "scr")
    hT = P.sb([128, 8, TT], BF16, "hT")
    rstd = P.sb([128, TT], F32, "rstd")
    NWB = 2
    wb = [P.sb([128, 8, 512], BF16, "wb%d" % i) for i in range(NWB)]
    wbi = [0]
    QT = P.sb([128, 4, TT], BF16, "QT")
    KT = P.sb([128, 4, T], BF16, "KT")
    V1r = P.sb([128, 8, 512], BF16, "V1r")
    V2 = P.sb([128, 2, 4, 512], BF16, "V2")
    V3g = [P.sb([128, 2, 4, 512], BF16, "V3g0")] * 2
    acc = P.sb([128, 4, 2, TT], F32, "acc")
    big = P.sb([128, NF, TT], BF16, "big")
    YB = [P.sb([128, 4, TT], BF16, "YB%d" % i) for i in range(3)]
    cosb = P.sb([128, TT], F32, "cos")
    sinb = P.sb([128, TT], F32, "sin")
    zcar = P.sb([128, 14], F32, "zcar")
    H0 = P.sb([128, 4, 64], F32, "H0")
    H0b = P.sb([128, 4, 64], BF16, "H0b")
    ccar = P.sb([128, NF, 2], F32, "ccar")
    MSET("pool", KT[:], 0.0, [KT])

    modp = P.sb([128, NL, 48], F32, "mod")
    A1 = P.sb([128, NL, 8], F32, "A1")
    A2 = P.sb([128, NL, 8], F32, "A2")
    gfin = P.sb([128, 8], F32, "gfin")
    cact = P.sb([128, 8], F32, "cact")
    wtmp = scr
    bad = P.sb([128, NL, 48], F32, "bad")
    gm = P.sb([128, NL, 8], F32, "gm")
    gf = P.sb([128, NL, 8], F32, "gf")
    P.dma("sp", cact[:], c_in, writes=[cact])
    P.dma("sp", gfin[:], g_fin, writes=[gfin])
    for l in range(NL):
        P.dma("sp", bad[:, l, :], b_ada[l], writes=[bad])
        P.dma("sp", gm[:, l, :], g_mix[l], writes=[gm])
        P.dma("sp", gf[:, l, :], g_ffn[l], writes=[gf])
    ACT(cact[:], cact[:], AF.Silu, [cact], [cact])
    for l in range(NL):
        for pc in range(12):
            P.dma("sp", wtmp[:], w_ada[l][:, pc * 512:(pc + 1) * 512].rearrange("(k p) n -> p k n", p=128), writes=[wtmp])
            ps = PSF()
            for mc in range(4):
                for k in range(8):
                    MM(ps[:, mc:mc + 1], wtmp[:, k, mc * 128:(mc + 1) * 128], cact[:, k:k + 1], k == 0, k == 7, [wtmp, cact], [ps])
            TTe("dve", modp[:, l, pc * 4:(pc + 1) * 4], ps[:, 0:4], bad[:, l, pc * 4:(pc + 1) * 4], ALU.add, [ps, bad], [modp])
        STT("dve", A1[:, l, :], modp[:, l, 8:16], 1.0, gm[:, l, :], ALU.add, ALU.mult, [modp, gm], [A1])
        STT("dve", A2[:, l, :], modp[:, l, 32:40], 1.0, gf[:, l, :], ALU.add, ALU.mult, [modp, gf], [A2])

    mu = P.sb([128, 14], F32, "mu")
    omu = P.sb([128, 14], F32, "omu")
    rwp = P.sb([128, 5, 4], F32, "rwp")
    omka = P.sb([128, 4], F32, "omka")
    nw0 = P.sb([128, 4], F32, "nw0")
    wupb = P.sb([128, 512], BF16, "wupb")
    aupb = P.sb([128, 512], BF16, "aupb")
    gupb = P.sb([128, 512], BF16, "gupb")
    gnw = P.sb([128, 512], F32, "gnw")
    gnb = P.sb([128, 512], F32, "gnb")
    lng = P.sb([128, 512], F32, "lng")
    lnb = P.sb([128, 512], F32, "lnb")
    wsT = P.sb([128, 8, 128], BF16, "wsT")
    wsTf = View(scr[:, 0:2, :].rearrange("p a (g i) -> p (a g) i", g=4), scr)
    bsT = P.sb([128, 4, 128], F32, "bsT")
    cw = P.sb([128, NF, 3], F32, "cw")
    cb = P.sb([128, NF], F32, "cb")

    def load_w(src_ap, kc, ncols):
        wbi[0] += 1
        w = wb[wbi[0] % NWB]
        P.dma("pool", w[:, 0:kc, 0:ncols], src_ap.rearrange("(k p) n -> p k n", p=128), writes=[w])
        return w

    def rmsnorm_mod(Acol, shcol, l):
        ACT(scr[:], xt[:], AF.Square, [xt], [scr])
        ps = PSF()
        for k in range(8):
            MM(ps[:], onesf, scr[:, k, :], k == 0, k == 7, [cst, scr], [ps])
        ACT(rstd[:], ps[:], AF.Sqrt, [ps], [rstd], bias=1e-6, scale=1.0 / D)
        P.op("dve", lambda e: e.reciprocal(out=rstd[:], in_=rstd[:]), reads=[rstd], writes=[rstd])
        for k in range(8):
            STT("dve", scr[:, k, :], xt[:, k, :], Acol[:, k:k + 1], rstd[:], ALU.mult, ALU.mult, [xt, rstd, A1, A2, gfin], [scr])
            if shcol is not None:
                TS("pool", hT[:, k, :], scr[:, k, :], shcol[:, k:k + 1], None, ALU.add, None, [scr, modp], [hT])

    GC = 0.7978845608028654

    def gelu_from(out, src, rd, wr, tmpa, tmpb):
        CP("act", tmpa, src, rd, wr)
        TTe("dve", tmpb, tmpa, tmpa, ALU.mult, wr, wr)
        TS("dve", tmpb, tmpb, 0.044715, 1.0, ALU.mult, ALU.add, wr, wr)
        TTe("dve", tmpb, tmpb, tmpa, ALU.mult, wr, wr)
        ACT(tmpb, tmpb, AF.Sigmoid, wr, wr, scale=2.0 * GC)
        TTe("dve", out, tmpb, tmpa, ALU.mult, wr, wr)

    try:
        for l in range(NL):
            x_src = xT_in if l == 0 else x1T
            P.dma("sp", mu[:], mu_d[l], writes=[mu])
            TS("dve", omu[:], mu[:], -1.0, 1.0, ALU.mult, ALU.add, [mu], [omu])
            P.dma("sp", rwp[:], pr_d[l], writes=[rwp])
            TS("dve", omka[:], rwp[:, 3, :], -1.0, 1.0, ALU.mult, ALU.add, [rwp], [omka])
            TS("dve", nw0[:], rwp[:, 0, :], -1.0, None, ALU.mult, None, [rwp], [nw0])
            P.dma("pool", wupb[0:64, :], wup_d[l], writes=[wupb])
            P.dma("pool", aupb[64:128, :], aup_d[l], writes=[aupb])
            P.dma("pool", gupb[:], gup_d[l], writes=[gupb])
            for (t_, d_) in ((gnw, gnw_d), (gnb, gnb_d), (lng, lng_d), (lnb, lnb_d)):
                P.dma("sp", t_[:], d_[l], writes=[t_])
            P.dma("sp", wsTf[:], wsT_d[l], writes=[wsTf])
            for g in range(8):
                TTe("dve", wsT[:, g, :], wsTf[:, g, :], mrw[:, 0, 1, :], ALU.mult, [wsTf, mrw], [wsT])
            P.dma("sp", bsT[:], bs_d[l], writes=[bsT])
            P.dma("sp", cw[:], cw_d[l], writes=[cw])
            P.dma("sp", cb[:], cb_d[l], writes=[cb])
            MSET("pool", zcar[:], 0.0, [zcar])
            MSET("pool", H0[:], 0.0, [H0])
            MSET("pool", H0b[:], 0.0, [H0b])
            MSET("pool", ccar[:], 0.0, [ccar])
            MSET("pool", big[:, 0:4, :], 0.0, [big])
            for q in range(T // 512):
                P.dma("sp", Vd[l][q * 512:(q + 1) * 512, :].rearrange("(p a) c -> p (a c)", p=128), big[:, 0:4, :].rearrange("p a b -> p (a b)"), reads=[big], writes=[b_vd[l]])

            for it in range(NTI):
                t0 = it * TT
                P.dma("sp", xt[:], x_src[:, t0:t0 + TT].rearrange("(k p) t -> p k t", p=128), reads=[b_x1] if l else [], writes=[xt])
                P.dma("sp", cosb[:], cos_d[:, t0:t0 + TT], writes=[cosb])
                P.dma("sp", sinb[:], sin_d[:, t0:t0 + TT], writes=[sinb])
                chk('pro')
                rmsnorm_mod(A1[:, l, :], modp[:, l, 0:8], l)
                chk('norm')

                for pc in range(2):
                    w = load_w(w_in[l][:, pc * 512:(pc + 1) * 512], 8, 512)
                    for mc in range(4):
                        ps = PSF()
                        for k in range(8):
                            MM(ps[:], w[:, k, mc * 128:(mc + 1) * 128], hT[:, k, :], k == 0, k == 7, [w, hT], [ps])
                        chk('qk1')
                        qs = scr[:, mc, :]
                        qbf = big[:, 14 + mc, :]
                        CP("act", qbf, ps[:], [ps], [big])
                        ps2 = PSF()
                        MM(ps2[:], permb, qbf, True, True, [cstb, big], [ps2])
                        chk('qk2')
                        TTe("dve", qs, ps[:], cosb[:], ALU.mult, [ps, cosb, big, ps2], [scr])
                        TTe("dve", scr[:, 4 + mc, :], ps2[:], sinb[:], ALU.mult, [ps2, sinb], [scr])
                        chk('qk3')
                        dst = QT[:, mc, :] if pc == 0 else KT[:, mc, t0:t0 + TT]
                        TTe("pool", dst, qs, scr[:, 4 + mc, :], ALU.add, [scr], [QT if pc == 0 else KT])
                chk('qk')
                w = load_w(w_in[l][:, 1024:1536], 8, 512)
                for blk in range(4):
                    ps = PSF()
                    for k in range(8):
                        MM(ps[:], hT[:, k, blk * 128:(blk + 1) * 128], w[:, k, :], k == 0, k == 7, [w, hT], [ps])
                    gb = 4 * it + blk
                    CP("act", V1r[:, gb % 8, :], ps[:], [ps], [V1r])
                    P.dma("sp", Vd[l][t0 + blk * 128:t0 + (blk + 1) * 128, :], V1r[:, gb % 8, :], reads=[V1r], writes=[b_vd[l]])
                P.dma("sp", V2[:, it % 2, :, :], Vd[l][t0:t0 + TT, :].rearrange("(i r) c -> i r c", r=4), reads=[b_vd[l]], writes=[V2])

                chk('v')
                for cfg in range(3):
                    n_sp = t0 // min(2048, T)
                    j3 = (t0 % min(2048, T)) // TT
                    if cfg < 2:
                        items = [(None, c, hh) for c in range(4) for hh in range(2)]
                    else:
                        items = [(rg_, c, hh) for rg_ in range(4) for c in range(4) for hh in range(2)]
                    for (rg_, c, hh) in items:
                        if True:
                            rows = slice(hh * 64, hh * 64 + 64)
                            if cfg < 2:
                                for half in range(2):
                                    psS = PSF()
                                    hp = []
                                    for u in range(2):
                                        qi = half * 2 + u
                                        if cfg == 0:
                                            gb = 4 * it + qi
                                            q_ap = QT[rows, c, qi * 128:(qi + 1) * 128]
                                            kc_ap = KT[rows, c, gb * 128:(gb + 1) * 128]
                                            kp_ap = KT[rows, c, (gb - 1) * 128:gb * 128] if gb > 0 else None
                                            vc_ap = V1r[:, gb % 8, c * 128:(c + 1) * 128]
                                            vp_ap = V1r[:, (gb - 1) % 8, c * 128:(c + 1) * 128]
                                            vb = V1r
                                        else:
                                            q_ap = QT[rows, c, qi:TT:4]
                                            kc_ap = KT[rows, c, t0 + qi:t0 + TT:4]
                                            kp_ap = KT[rows, c, t0 - TT + qi:t0:4] if it > 0 else None
                                            vc_ap = V2[:, it % 2, qi, c * 128:(c + 1) * 128]
                                            vp_ap = V2[:, (it - 1) % 2, qi, c * 128:(c + 1) * 128]
                                            vb = V2
                                        hp.append(kp_ap is not None)
                                        if kp_ap is not None:
                                            MM(psS[:, u * 256:u * 256 + 128], kp_ap, q_ap, True, True, [KT, QT], [psS])
                                        MM(psS[:, u * 256 + 128:u * 256 + 256], kc_ap, q_ap, True, True, [KT, QT], [psS])
                                        hp.append((vp_ap, vc_ap, vb))
                                    Eb = big[:, 18, :]
                                    for u in range(2):
                                        if hp[2 * u]:
                                            ACT(big[:, 18, u * 256:(u + 1) * 256], psS[:, u * 256:(u + 1) * 256], AF.Exp, [psS], [big], scale=0.125)
                                        else:
                                            MSET("pool", big[:, 18, u * 256:u * 256 + 128], 0.0, [big])
                                            ACT(big[:, 18, u * 256 + 128:u * 256 + 256], psS[:, u * 256 + 128:u * 256 + 256], AF.Exp, [psS], [big], scale=0.125)
                                    Pm = big[:, 19, :]
                                    TTe("dve", Pm, Eb, m12[:].rearrange("p a b -> p (a b)"), ALU.mult, [big, m12], [big])
                                    psN = PSF()
                                    for u in range(2):
                                        vp_ap, vc_ap, vb = hp[2 * u + 1]
                                        has_prev = hp[2 * u]
                                        for z in range(2):
                                            o = psN[:, z * 256 + u * 128:z * 256 + u * 128 + 128]
                                            if has_prev:
                                                MM(o, vp_ap if z == 0 else onesb, Pm[:, u * 256:u * 256 + 128], True, False, [vb, big, cstb], [psN])
                                            MM(o, vc_ap if z == 0 else onesb, Pm[:, u * 256 + 128:u * 256 + 256], not has_prev, True, [vb, big, cstb], [psN])
                                    src = psN[rows, :].rearrange("p (z u q) -> p z u q", z=2, u=2)
                                    if cfg == 0:
                                        dsta = acc[rows, c, :, half * 256:(half + 1) * 256].rearrange("p z (u q) -> p z u q", u=2)
                                        CP("act", dsta, src, [psN], [acc])
                                    else:
                                        dsta = acc[rows, c, :, :].rearrange("p z (q r) -> p z r q", r=4)[:, :, half * 2:half * 2 + 2, :]
                                        TTe("dve", dsta, src, dsta, ALU.add, [psN, acc], [acc])
                            else:
                                SPAN = min(2048, T)
                                nk3 = SPAN // 16
                                kp_ = slice(0, nk3)
                                for rg in (rg_,):
                                    if c == 0 and hh == 0:
                                        v3 = V3g[rg % 2]
                                        for sp_ in range(2):
                                            nn = n_sp - 1 + sp_
                                            if nn < 0:
                                                continue
                                            P.dma("sp", v3[kp_, sp_, :, :],
                                                  Vd[l][nn * SPAN:(nn + 1) * SPAN, :].rearrange("(i r) c -> i r c", r=16)[:, rg * 4:rg * 4 + 4, :],
                                                  reads=[b_vd[l]], writes=[v3])
                                    v3 = V3g[rg % 2]
                                    psS = PSF()
                                    hasp = n_sp > 0
                                    for rr in range(4):
                                        r = rg * 4 + rr
                                        q_ap = QT[rows, c, r:TT:16]
                                        kc_ap = KT[rows, c, n_sp * SPAN + r:(n_sp + 1) * SPAN:16]
                                        if hasp:
                                            kp_ap = KT[rows, c, (n_sp - 1) * SPAN + r:n_sp * SPAN:16]
                                            MM(psS[kp_, rr * 64:rr * 64 + 32], kp_ap, q_ap, True, True, [KT, QT], [psS])
                                        MM(psS[kp_, rr * 64 + 32:rr * 64 + 64], kc_ap, q_ap, True, True, [KT, QT], [psS])
                                    if not hasp:
                                        MSET("pool", big[kp_, 18, 0:256], 0.0, [big])
                                        for rr in range(4):
                                            ACT(big[kp_, 18, rr * 64 + 32:rr * 64 + 64], psS[kp_, rr * 64 + 32:rr * 64 + 64], AF.Exp, [psS], [big], scale=0.125)
                                    else:
                                        ACT(big[kp_, 18, 0:256], psS[kp_, 0:256], AF.Exp, [psS], [big], scale=0.125)
                                    Pm = big[kp_, 19, 0:256]
                                    TTe("dve", Pm, big[kp_, 18, 0:256], m3[kp_, j3, :, :].rearrange("p a b -> p (a b)"), ALU.mult, [big, m3], [big])
                                    psN = PSF()
                                    for rr in range(4):
                                        for z in range(2):
                                            o = psN[:, z * 128 + rr * 32:z * 128 + rr * 32 + 32]
                                            if hasp:
                                                MM(o, v3[kp_, 0, rr, c * 128:(c + 1) * 128] if z == 0 else onesb[kp_, :], Pm[:, rr * 64:rr * 64 + 32], True, False, [v3, big, cstb], [psN])
                                            MM(o, v3[kp_, 1, rr, c * 128:(c + 1) * 128] if z == 0 else onesb[kp_, :], Pm[:, rr * 64 + 32:rr * 64 + 64], not hasp, True, [v3, big, cstb], [psN])
                                    src = psN[rows, 0:256].rearrange("p (z r q) -> p z r q", z=2, r=4)
                                    dsta = acc[rows, c, :, :].rearrange("p z (q r) -> p z r q", r=16)[:, :, rg * 4:rg * 4 + 4, :]
                                    TTe("dve", dsta, src, dsta, ALU.add, [psN, acc], [acc])
                for c in range(4):
                    P.op("dve", lambda e, c=c: e.reciprocal(out=acc[:, c, 1, :], in_=acc[:, c, 1, :]), reads=[acc], writes=[acc])
                    TTe("dve", YB[0][:, c, :], acc[:, c, 0, :], acc[:, c, 1, :], ALU.mult, [acc], [YB[0]])
                chk('att')

                for pc in range(4):
                    ncol = 512 if pc < 3 else 256
                    w = load_w(w_in[l][:, 1536 + pc * 512:1536 + pc * 512 + ncol], 8, ncol)
                    for mc in range(ncol // 128):
                        ch = pc * 4 + mc
                        ps = PSF()
                        for k in range(8):
                            MM(ps[:], w[:, k, mc * 128:(mc + 1) * 128], hT[:, k, :], k == 0, k == 7, [w, hT], [ps])
                        zb = scr[:, ch % 4, :]
                        zp = scr[:, 4 + ch % 4, :]
                        CP("act", zb, ps[:], [ps], [scr])
                        CP("pool", zp[:, 1:TT], zb[:, 0:TT - 1], [scr], [scr])
                        CP("pool", zp[:, 0:1], zcar[:, ch:ch + 1], [zcar, scr], [scr])
                        CP("pool", zcar[:, ch:ch + 1], zb[:, TT - 1:TT], [scr], [zcar])
                        TS("dve", zb, zb, omu[:, ch:ch + 1], None, ALU.mult, None, [scr, omu], [scr])
                        STT("dve", big[:, ch, :], zp, mu[:, ch:ch + 1], zb, ALU.mult, ALU.add, [scr, mu], [big])
                chk('rwin'); rwkv_tile(P, locals()); chk('rw')

                gmlp_tile(P, locals()); chk('gmlp')

                wbrs = []
                for br in range(3):
                    for half in range(2):
                        wg = load_w(w_in[l][:, 4352 + br * 1024 + half * 512:4352 + br * 1024 + (half + 1) * 512], 8, 512)
                        wbt = load_w(wbr_d[l, br][:, half * 512:(half + 1) * 512], 4, 512)
                        for mc in range(4):
                            m = half * 4 + mc
                            psg = PSF()
                            for k in range(8):
                                MM(psg[:], wg[:, k, mc * 128:(mc + 1) * 128], hT[:, k, :], k == 0, k == 7, [wg, hT], [psg])
                            psbr = PSF()
                            for k in range(4):
                                MM(psbr[:], wbt[:, k, mc * 128:(mc + 1) * 128], YB[br][:, k, :], k == 0, k == 3, [wbt, YB[br]], [psbr])
                            sg = acc[:, 0, 0, :]
                            ACT(sg, psg[:], AF.Sigmoid, [psg], [acc])
                            if br == 0:
                                TTe("dve", scr[:, m, :], sg, psbr[:], ALU.mult, [acc, psbr], [scr])
                            else:
                                TTe("dve", sg, sg, psbr[:], ALU.mult, [acc, psbr], [acc])
                                TTe("pool", scr[:, m, :], scr[:, m, :], sg, ALU.add, [acc, scr], [scr])
                chk('br')
                for m in range(8):
                    CP("act", hT[:, m, :], scr[:, m, :], [scr], [hT])
                for half in range(2):
                    w = load_w(wout_d[l][:, half * 512:(half + 1) * 512], 8, 512)
                    for mc in range(4):
                        m = half * 4 + mc
                        ps = PSF()
                        for k in range(8):
                            MM(ps[:], w[:, k, mc * 128:(mc + 1) * 128], hT[:, k, :], k == 0, k == 7, [w, hT], [ps])
                        STT("dve", xt[:, m, :], ps[:], modp[:, l, 16 + m:17 + m], xt[:, m, :], ALU.mult, ALU.add, [ps, modp, xt], [xt])

                chk('out')
                rmsnorm_mod(A2[:, l, :], modp[:, l, 24:32], l)
                for fg in range(6):
                    nf = 4 if fg < 5 else 2
                    wa = load_w(wfu_d[l][:, fg * 512:fg * 512 + nf * 128], 8, nf * 128)
                    wg = load_w(wfu_d[l][:, DFF + fg * 512:DFF + fg * 512 + nf * 128], 8, nf * 128)
                    for mc in range(nf):
                        f = fg * 4 + mc
                        psa = PSF()
                        for k in range(8):
                            MM(psa[:], wa[:, k, mc * 128:(mc + 1) * 128], hT[:, k, :], k == 0, k == 7, [wa, hT], [psa])
                        psg = PSF()
                        for k in range(8):
                            MM(psg[:], wg[:, k, mc * 128:(mc + 1) * 128], hT[:, k, :], k == 0, k == 7, [wg, hT], [psg])
                        ab = acc[:, 0, :, :].rearrange("p a b -> p (a b)")
                        CP("act", ab[:, 2:2 + TT], psa[:], [psa], [acc])
                        CP("pool", ab[:, 0:2], ccar[:, f, :], [ccar, acc], [acc])
                        CP("pool", ccar[:, f, :], ab[:, TT:TT + 2], [acc], [ccar])
                        c1 = acc[:, 1, 0, :]
                        TS("dve", c1, ab[:, 2:2 + TT], cw[:, f, 2:3], cb[:, f:f + 1], ALU.mult, ALU.add, [acc, cw, cb], [acc])
                        STT("dve", c1, ab[:, 1:1 + TT], cw[:, f, 1:2], c1, ALU.mult, ALU.add, [acc, cw], [acc])
                        STT("dve", c1, ab[:, 0:TT], cw[:, f, 0:1], c1, ALU.mult, ALU.add, [acc, cw], [acc])
                        ge = acc[:, 2, 0, :]
                        gelu_from(ge, c1, [acc], [acc], acc[:, 2, 1, :], acc[:, 3, 0, :])
                        TTe("dve", big[:, f, :], ge, psg[:], ALU.mult, [acc, psg], [big])
                for m in range(8):
                    wbi[0] += 1
                    wd = wb[wbi[0] % NWB]
                    wdv = wd[:].rearrange("p a b -> p (a b)")[:, 0:NF * 128].rearrange("p (f n) -> p f n", f=NF)
                    P.dma("pool", wdv, wfd_d[l][:, m * 128:(m + 1) * 128].rearrange("(f p) n -> p f n", p=128), writes=[wd])
                    ps = PSF()
                    for f in range(NF):
                        MM(ps[:], wdv[:, f, :], big[:, f, :], f == 0, f == NF - 1, [wd, big], [ps])
                    STT("dve", xt[:, m, :], ps[:], modp[:, l, 40 + m:41 + m], xt[:, m, :], ALU.mult, ALU.add, [ps, modp, xt], [xt])

                chk('ffn')
                if l < NL - 1:
                    P.dma("sp", x1T[:, t0:t0 + TT].rearrange("(k p) t -> p k t", p=128), xt[:], reads=[xt], writes=[b_x1])
                else:
                    rmsnorm_mod(gfin[:], None, l)
                    P.dma("sp", yT_out[:, t0:t0 + TT].rearrange("(k p) t -> p k t", p=128), scr[:], reads=[scr], writes=[b_out])
    except _Stop:
        src_t = xt
        ybi = {"att": 0, "rw": 1, "gmlp": 2}.get(STOP)
        if ybi is not None:
            MSET("pool", scr[:], 0.0, [scr])
            for c_ in range(4):
                CP("dve", scr[:, c_, :], YB[ybi][:, c_, :], [YB[ybi]], [scr])
            src_t = scr
        elif STOP == "br":
            src_t = scr
        elif STOP == "rw7":
            MSET("pool", scr[:], 0.0, [scr])
            CP("dve", scr[:, 0, :], acc[:, 0, 0, :], [acc], [scr])
            src_t = scr
        P.dma("sp", yT_out[:, 0:TT].rearrange("(k p) t -> p k t", p=128), src_t[:], reads=[src_t], writes=[b_out])
    return P.finish([b_out])


def gmlp_tile(P, L):
    g = L
    PSF, MM, ACT, TTe, TS, STT, CP = g["PSF"], g["MM"], g["ACT"], g["TTe"], g["TS"], g["STT"], g["CP"]
    hT, big, acc, scr, YB, wsT, bsT, lng, lnb = g["hT"], g["big"], g["acc"], g["scr"], g["YB"], g["wsT"], g["bsT"], g["lng"], g["lnb"]
    load_w, w_in, l, gelu_from = g["load_w"], g["w_in"], g["l"], g["gelu_from"]
    w = load_w(w_in[l][:, 3328:3840], 8, 512)
    for mc in range(4):
        ps = PSF()
        for k in range(8):
            MM(ps[:], w[:, k, mc * 128:(mc + 1) * 128], hT[:, k, :], k == 0, k == 7, [w, hT], [ps])
        gelu_from(scr[:, mc, :], ps[:], [ps], [scr], scr[:, 4, :], scr[:, 5, :])
    w = load_w(w_in[l][:, 3840:4352], 8, 512)
    stats = acc[:, 3, 1, 0:8]
    for blk in range(4):
        ps = PSF()
        for k in range(8):
            MM(ps[:], hT[:, k, blk * 128:(blk + 1) * 128], w[:, k, :], k == 0, k == 7, [w, hT], [ps])
        v = acc[:, 0, 0, :]
        gelu_from(v, ps[:], [ps], [acc], acc[:, 0, 1, :], acc[:, 1, 0, :])
        P.op("dve", lambda e: e.bn_stats(out=acc[:, 3, 1, 0:6], in_=v), reads=[acc], writes=[acc])
        P.op("dve", lambda e: e.bn_aggr(out=acc[:, 3, 1, 6:8], in_=acc[:, 3, 1, 0:6]), reads=[acc], writes=[acc])
        ACT(acc[:, 3, 1, 7:8], acc[:, 3, 1, 7:8], AF.Sqrt, [acc], [acc], bias=1e-5)
        P.op("dve", lambda e: e.reciprocal(out=acc[:, 3, 1, 7:8], in_=acc[:, 3, 1, 7:8]), reads=[acc], writes=[acc])
        TS("dve", v, v, acc[:, 3, 1, 6:7], acc[:, 3, 1, 7:8], ALU.subtract, ALU.mult, [acc], [acc])
        TTe("dve", v, v, lng[:], ALU.mult, [acc, lng], [acc])
        vb = big[:, 18, :]
        TTe("dve", vb, v, lnb[:], ALU.add, [acc, lnb], [big])
        psA = PSF()
        psB = PSF()
        for gq in range(8):
            pp = psA if gq < 4 else psB
            c = gq // 2
            MM(pp[:, (gq % 4) * 128:(gq % 4) * 128 + 128], vb[:, c * 128:(c + 1) * 128], wsT[:, gq, :], True, True, [big, wsT], [pp])
        for gq in range(8):
            pp = psA if gq < 4 else psB
            c = gq // 2
            rows = slice((gq % 2) * 64, (gq % 2) * 64 + 64)
            tmp = acc[rows, 1, 1, 0:128]
            TTe("dve", tmp, pp[rows, (gq % 4) * 128:(gq % 4) * 128 + 128], bsT[rows, c, :], ALU.add, [pp, bsT], [acc])
            TTe("dve", YB[2][rows, c, blk * 128:(blk + 1) * 128], tmp, scr[rows, c, blk * 128:(blk + 1) * 128], ALU.mult, [acc, scr], [YB[2]])


def rwkv_tile(P, L):
    g = L
    PSF, PSB, MM, TR, ACT, TTe, TS, STT, CP, MSET = (g[k] for k in ("PSF", "PSB", "MM", "TR", "ACT", "TTe", "TS", "STT", "CP", "MSET"))
    big, scr, acc, YB, H0, H0b = g["big"], g["scr"], g["acc"], g["YB"], g["H0"], g["H0b"]
    rwp, omka, wupb, aupb, gupb, gnw, gnb = g["rwp"], g["omka"], g["wupb"], g["aupb"], g["gupb"], g["gnw"], g["gnb"]
    cst, cstb, mrw, mlo, sel = g["cst"], g["cstb"], g["mrw"], g["mlo"], g["sel"]
    chk = g["chk"]
    identb, identf, blkones = g["identb"], g["identf"], g["blkones"]
    rw = getattr(P, "_rw", None)
    if rw is None:
        sb = P.sb
        rw = dict(
            th=sb([128, 128], BF16), sgx=sb([128, 128], BF16),
            **{nm: View(scr[:, i_, :].rearrange("p (a b) -> p a b", a=4), scr) for i_, nm in enumerate(("lw", "cum", "cum2", "av", "kk", "km", "t1", "t2"))},
            AR=sb([128, 4, 2, 128], BF16), BK=sb([128, 4, 2, 128], BF16),
            wc=sb([128, 4], F32),
            tok=View(g["QT"][:, 0:3, :], g["QT"]),
            Xs=View(g["QT"][:, 3, :], g["QT"]),
            **{nm: View(g["V3g"][0][:, i_ // 2, (i_ % 2) * 2:(i_ % 2) * 2 + 2, :].rearrange("p a (h t) -> p (a h) t", h=4), g["V3g"][0])
               for i_, nm in enumerate(("N1", "L1", "N2", "L2"))},
            **{nm: View(YB[2][:, 2 * i_:2 * i_ + 2, :].rearrange("p a (h t) -> p (a h) t", h=4), YB[2]) for i_, nm in enumerate(("IL", "Mm"))},
            **{nm: View(big[:, 14 + 2 * i_:16 + 2 * i_, :].rearrange("p a (h t) -> p (a h) t", h=4), big)
               for i_, nm in enumerate(("Mm2", "Arb", "Aak", "Ark"))},
            Y=View(acc[:, 0, 0, :], acc), Y2=View(acc[:, 0, 1, :], acc), st=sb([128, 4, 8], F32), gt=View(acc[:, 1, 0, :], acc),
            Ht=sb([128, 4, 64], F32),
            Noff=View(g["rstd"][:].bitcast(BF16).rearrange("p (h t) -> p h t", h=8), g["rstd"]),
        )
        P._rw = rw
    rw = P._rw
    th, sgx, lw, cum, cum2, av, kk, km, t1, t2 = (rw[k] for k in ("th", "sgx", "lw", "cum", "cum2", "av", "kk", "km", "t1", "t2"))
    AR, BK, wc, tok = rw["AR"], rw["BK"], rw["wc"], rw["tok"]
    N1, L1, N2, L2, IL, Mm, Mm2, Arb, Aak, Ark = (rw[k] for k in ("N1", "L1", "N2", "L2", "IL", "Mm", "Mm2", "Arb", "Aak", "Ark"))
    V1r_ = g["V1r"]
    fs_ = (4, 5, 6) if g["it"] % 2 == 0 else (0, 1, 2)
    Us = View(V1r_[:, fs_[0], :], V1r_)
    yb = View(V1r_[:, fs_[1], :], V1r_)
    prod = View(V1r_[:, fs_[2], :].rearrange("p (a b) -> p a b", a=4), V1r_)
    Xs, Y, Y2, st, gt, Ht = (rw[k] for k in ("Xs", "Y", "Y2", "st", "gt", "Ht"))
    W0, A0, KK_, KA, RK = (rwp[:, i, :] for i in range(5))
    EH = 0.6065306597126334

    for blk in range(4):
        ts = slice(blk * 128, (blk + 1) * 128)
        zr = lambda c: big[:, 0 + c, ts]
        zk = lambda c: big[:, 4 + c, ts]
        zv = lambda c: big[:, 8 + c, ts]
        zwa = big[:, 12, ts]
        zg = big[:, 13, ts]
        ACT(th[0:64, :], big[0:64, 12, ts], AF.Tanh, [big], [th])
        ACT(sgx[:], zg, AF.Sigmoid, [big], [sgx])
        ps = PSF()
        psa = PSF()
        for c in range(4):
            MM(ps[:, c * 128:(c + 1) * 128], wupb[0:64, c * 128:(c + 1) * 128], th[0:64, :], True, True, [wupb, th], [ps])
            MM(psa[:, c * 128:(c + 1) * 128], aupb[64:128, c * 128:(c + 1) * 128], big[64:128, 12, ts], True, True, [aupb, big], [psa])
        for c in range(4):
            ACT(lw[:, c, :], ps[:, c * 128:(c + 1) * 128], AF.Exp, [ps, g["nw0"]], [lw], bias=g["nw0"][:, c:c + 1], scale=-1.0)
            ACT(av[:, c, :], psa[:, c * 128:(c + 1) * 128], AF.Sigmoid, [psa, rwp], [av], bias=A0[:, c:c + 1])
        TS("dve", lw[:], lw[:], 1.0, None, ALU.add, None, [lw], [lw])
        P.op("dve", lambda e: e.reciprocal(out=lw[:], in_=lw[:]), reads=[lw], writes=[lw])
        TS("dve", lw[:], lw[:], -EH, None, ALU.mult, None, [lw], [lw])
        psg = PSF()
        MM(psg[:], sgx[:], gupb[:], True, True, [sgx, gupb], [psg])
        CP("act", gt[:], psg[:], [psg], [gt])
        chk('rw1')
        for c in range(4):
            TS("dve", kk[:, c, :], zk(c), KK_[:, c:c + 1], None, ALU.mult, None, [big, rwp], [kk])
        ACT(t1[:], kk[:], AF.Square, [kk], [t1])
        ps = PSF()
        for c in range(4):
            MM(ps[:, c * 128:(c + 1) * 128], blkones, t1[:, c, :], True, True, [cst, t1], [ps])
        ACT(t1[:].rearrange("p a b -> p (a b)"), ps[:], AF.Sqrt, [ps], [t1])
        TS("dve", t1[:], t1[:], 1e-12, None, ALU.max, None, [t1], [t1])
        P.op("dve", lambda e: e.reciprocal(out=t1[:], in_=t1[:]), reads=[t1], writes=[t1])
        TTe("dve", kk[:], kk[:], t1[:], ALU.mult, [kk, t1], [kk])
        for c in range(4):
            TS("dve", t2[:, c, :], av[:, c, :], KA[:, c:c + 1], omka[:, c:c + 1], ALU.mult, ALU.add, [av, rwp, omka], [t2])
            TTe("dve", km[:, c, :], t2[:, c, :], zk(c), ALU.mult, [t2, big], [km])
        for c in range(4):
            STT("dve", prod[:, c, :], zr(c), RK[:, c:c + 1], km[:, c, :], ALU.mult, ALU.mult, [big, rwp, km], [prod])
        psr = PSF()
        for c in range(4):
            MM(psr[:, 0:8], prod[:, c, :], sel[:, c, :], c == 0, c == 3, [prod, sel], [psr])
        CP("act", st[:, 0, :], psr[:, 0:8], [psr], [st])
        chk('rw2')
        src, dst = lw, cum
        CP("pool", cum[:], lw[:], [lw], [cum])
        a_, b_ = cum, cum2
        for s in (1, 2, 4, 8, 16, 32, 64):
            CP("pool", b_[:, :, 0:s], a_[:, :, 0:s], [a_], [b_])
            TTe("dve", b_[:, :, s:128], a_[:, :, s:128], a_[:, :, 0:128 - s], ALU.add, [a_], [b_])
            a_, b_ = b_, a_
        cm = a_
        ot = b_
        ACT(t1[:], cm[:], AF.Exp, [cm], [t1])
        for c in range(4):
            TTe("dve", AR[:, c, 1, :], t1[:, c, :], zr(c), ALU.mult, [t1, big], [AR])
        ACT(wc[:], cm[:, :, 127], AF.Exp, [cm], [wc])
        TTe("dve", ot[:], cm[:], lw[:], ALU.subtract, [cm, lw], [ot])
        ACT(t1[:], ot[:], AF.Exp, [ot], [t1])
        STT("dve", AR[:, :, 0, :], kk[:], -1.0, t1[:], ALU.mult, ALU.mult, [kk, t1], [AR])
        ACT(t1[:], cm[:], AF.Exp, [cm], [t1], scale=-1.0)
        TTe("dve", t2[:], kk[:], av[:], ALU.mult, [kk, av], [t2])
        TTe("dve", BK[:, :, 0, :], t2[:], t1[:], ALU.mult, [t2, t1], [BK])
        TTe("dve", BK[:, :, 1, :], km[:], t1[:], ALU.mult, [km, t1], [BK])
        chk('rw3')
        pb = PSB()
        for c in range(4):
            TR(pb[:, c * 128:(c + 1) * 128], zv(c), identb, [big, cstb], [pb])
        CP("act", tok[:, 0, :], pb[:, 0:512], [pb], [tok])
        for q in range(2):
            pb = PSB()
            for c in range(4):
                TR(pb[:, c * 128:(c + 1) * 128], BK[:, c, q, :], identb, [BK, cstb], [pb])
            CP("act", tok[:, 1 + q, :], pb[:, 0:512], [pb], [tok])
        chk('rw4')
        for hg in range(2):
            p1 = PSF(); p2 = PSF(); p3 = PSF(); p4 = PSF()
            for hq in range(4):
                c, hh = hq, hg
                rows = slice(hh * 64, hh * 64 + 64)
                half = hq % 2
                pa = p1 if hq < 2 else p2
                MM(pa[:, half * 256:half * 256 + 256], BK[rows, c, 0, :], AR[rows, c, :, :].rearrange("p a b -> p (a b)"), True, True, [BK, AR], [pa])
                pk = p3 if hq < 2 else p4
                MM(pk[:, half * 256:half * 256 + 256], BK[rows, c, 1, :], AR[rows, c, :, :].rearrange("p a b -> p (a b)"), True, True, [BK, AR], [pk])
            chk('rw4a')
            for i2, (pa, pk) in enumerate(((p1, p3), (p2, p4))):
                h0 = hg * 4 + i2 * 2
                sa = pa[:].rearrange("p (h z t) -> p h z t", h=2, z=2)
                sk = pk[:].rearrange("p (h z t) -> p h z t", h=2, z=2)
                TTe("dve", N1[:, h0:h0 + 2, :], sa[:, :, 0, :], mrw[:, 0:2, 0, :], ALU.mult, [pa, mrw], [N1])
                TTe("dve", Arb[:, h0:h0 + 2, :], sa[:, :, 1, :], mrw[:, 0:2, 1, :], ALU.mult, [pa, mrw], [Arb])
                TTe("dve", Aak[:, h0:h0 + 2, :], sk[:, :, 0, :], mrw[:, 0:2, 0, :], ALU.mult, [pk, mrw], [Aak])
                TTe("dve", Ark[:, h0:h0 + 2, :], sk[:, :, 1, :], mrw[:, 0:2, 1, :], ALU.mult, [pk, mrw], [Ark])
            chk('rw4b')
            pl = PSF()
            for hq in range(4):
                c, hh = hq, hg
                rows = slice(hh * 64, hh * 64 + 64)
                MM(pl[:, hq * 128:(hq + 1) * 128], AR[rows, c, 0, :], BK[rows, c, 0, :], True, True, [AR, BK], [pl])
            TTe("dve", L1[:, hg * 4:hg * 4 + 4, :], pl[:].rearrange("p (h t) -> p h t", h=4), mlo[:], ALU.mult, [pl, mlo], [L1])
        chk('rw5')
        Noff = rw["Noff"]
        MSET("pool", Noff[:], 0.0, [Noff])
        CP("pool", Noff[0:64, :, 64:128], N1[0:64, :, 64:128], [N1], [Noff])
        MSET("pool", N1[0:64, :, 64:128], 0.0, [N1])
        MSET("pool", L1[64:128, :, 0:64], 0.0, [L1])
        for hg in range(2):
            hs = slice(hg * 4, hg * 4 + 4)
            TTe("pool", Mm[:, hs, :], N1[:, hs, :], cstb[:, 4:8, :], ALU.add, [N1, cstb], [Mm])
        Nk, Lk, Nn, Ln = N1, L1, N2, L2
        Mc, Mn = Mm, Mm2
        for lev in range(6):
            last = lev == 5
            for hg in range(2):
                hs = slice(hg * 4, hg * 4 + 4)
                pn = PSF()
                for hq in range(4):
                    h = hg * 4 + hq
                    MM(pn[:, hq * 128:(hq + 1) * 128], Lk[:, h, :], Nk[:, h, :], True, True, [Lk, Nk], [pn])
                if not last:
                    CP("act", Nn[:, hs, :], pn[:].rearrange("p (h t) -> p h t", h=4), [pn], [Nn])
                pL = PSF()
                for hq in range(4):
                    h = hg * 4 + hq
                    MM(pL[:, hq * 128:(hq + 1) * 128], Nk[:, h, :], Lk[:, h, :], True, True, [Lk, Nk], [pL])
                if not last:
                    CP("act", Ln[:, hs, :], pL[:].rearrange("p (h t) -> p h t", h=4), [pL], [Ln])
                TTe("dve", IL[:, hs, :], pL[:].rearrange("p (h t) -> p h t", h=4), cstb[:, 4:8, :], ALU.add, [pL, cstb], [IL])
                pm = PSF()
                for hq in range(4):
                    h = hg * 4 + hq
                    MM(pm[:, hq * 128:(hq + 1) * 128], IL[:, h, :], Mc[:, h, :], True, True, [IL, Mc], [pm])
                CP("act", Mn[:, hs, :], pm[:].rearrange("p (h t) -> p h t", h=4), [pm], [Mn])
            Nk, Nn = Nn, Nk
            Lk, Ln = Ln, Lk
            Mc, Mn = Mn, Mc
        Mf = Mc
        chk('rw6')
        px = PSF()
        for h in range(8):
            c, hh = h % 4, h // 4
            hn = 2 * c + hh
            rows = slice(hh * 64, hh * 64 + 64)
            o = px[:, hn * 64:(hn + 1) * 64]
            MM(o, AR[rows, c, 0, :], H0b[rows, c, :], True, False, [AR, H0b], [px])
            MM(o, Aak[:, h, :], tok[:, 0, hn * 64:(hn + 1) * 64], False, True, [Aak, tok], [px])
        CP("act", Xs[:], px[:], [px], [Xs])
        Noff = rw["Noff"]
        for rnd in range(2):
            pu = PSF()
            for h in range(8):
                c, hh = h % 4, h // 4
                hn = 2 * c + hh
                MM(pu[:, hn * 64:(hn + 1) * 64], Mf[:, h, :], Xs[:, hn * 64:(hn + 1) * 64], True, True, [Mf, Xs], [pu])
            CP("act", Us[:], pu[:], [pu], [Us])
            if rnd == 0:
                pw = PSF()
                for h in range(8):
                    c, hh = h % 4, h // 4
                    hn = 2 * c + hh
                    MM(pw[:, hn * 64:(hn + 1) * 64], Noff[:, h, :], Us[:, hn * 64:(hn + 1) * 64], True, True, [Noff, Us], [pw])
                TTe("dve", Xs[:], pw[:], Xs[:], ALU.add, [pw, Xs], [Xs])
        py = PSF()
        for h in range(8):
            c, hh = h % 4, h // 4
            hn = 2 * c + hh
            rows = slice(hh * 64, hh * 64 + 64)
            o = py[:, hn * 64:(hn + 1) * 64]
            MM(o, AR[rows, c, 1, :], H0b[rows, c, :], True, False, [AR, H0b], [py])
            MM(o, Arb[:, h, :], Us[:, hn * 64:(hn + 1) * 64], False, False, [Arb, Us], [py])
            MM(o, Ark[:, h, :], tok[:, 0, hn * 64:(hn + 1) * 64], False, True, [Ark, tok], [py])
        CP("act", Y[:], py[:], [py], [Y])
        ph = PSF()
        for c in range(4):
            o = ph[:, c * 128:(c + 1) * 128]
            MM(o, tok[:, 1, c * 128:(c + 1) * 128], Us[:, c * 128:(c + 1) * 128], True, False, [tok, Us], [ph])
            MM(o, tok[:, 2, c * 128:(c + 1) * 128], tok[:, 0, c * 128:(c + 1) * 128], False, True, [tok], [ph])
        for hh in range(2):
            rows = slice(hh * 64, hh * 64 + 64)
            src = ph[rows, :].rearrange("p (c x i) -> p c x i", c=4, x=2)[:, :, hh, :]
            TTe("dve", Ht[rows, :, :], src, H0[rows, :, :], ALU.add, [ph, H0], [Ht])
        for c in range(4):
            TS("dve", H0[:, c, :], Ht[:, c, :], wc[:, c:c + 1], None, ALU.mult, None, [Ht, wc], [H0])
        CP("act", H0b[:], H0[:], [H0], [H0b])
        chk('rw7')
        Y3 = Y[:].rearrange("p (h i) -> p h i", h=8)
        P.op("dve", lambda e: e.tensor_reduce(out=st[:, 1, :], in_=Y3, axis=AX.X, op=ALU.add), reads=[Y], writes=[st])
        ACT(Y2[:], Y[:], AF.Square, [Y], [Y2])
        P.op("dve", lambda e: e.tensor_reduce(out=st[:, 2, :], in_=Y2[:].rearrange("p (h i) -> p h i", h=8), axis=AX.X, op=ALU.add), reads=[Y2], writes=[st])
        TS("dve", st[:, 1, :], st[:, 1, :], 1.0 / 64, None, ALU.mult, None, [st], [st])
        TTe("dve", st[:, 3, :], st[:, 1, :], st[:, 1, :], ALU.mult, [st], [st])
        STT("dve", st[:, 2, :], st[:, 2, :], 1.0 / 64, st[:, 3, :], ALU.mult, ALU.subtract, [st], [st])
        ACT(st[:, 2, :], st[:, 2, :], AF.Sqrt, [st], [st], bias=64e-5)
        P.op("dve", lambda e: e.reciprocal(out=st[:, 2, :], in_=st[:, 2, :]), reads=[st], writes=[st])
        for h in range(8):
            TS("dve", Y2[:, h * 64:(h + 1) * 64], Y[:, h * 64:(h + 1) * 64], st[:, 1, h:h + 1], st[:, 2, h:h + 1], ALU.subtract, ALU.mult, [Y, st], [Y2])
        TTe("dve", Y2[:], Y2[:], gnw[:], ALU.mult, [Y2, gnw], [Y2])
        TTe("dve", Y2[:], Y2[:], gnb[:], ALU.add, [Y2, gnb], [Y2])
        for h in range(8):
            STT("dve", Y2[:, h * 64:(h + 1) * 64], tok[:, 0, h * 64:(h + 1) * 64], st[:, 0, h:h + 1], Y2[:, h * 64:(h + 1) * 64], ALU.mult, ALU.add, [tok, st, Y2], [Y2])
        TTe("dve", yb[:], Y2[:], gt[:], ALU.mult, [Y2, gt], [yb])
        pb = PSB()
        for c in range(4):
            TR(pb[:, c * 128:(c + 1) * 128], yb[:, c * 128:(c + 1) * 128], identb, [yb, cstb], [pb])
        CP("act", YB[1][:, :, ts], pb[:, 0:512].rearrange("p (c t) -> p c t", c=4), [pb], [YB[1]])


def _consts(T):
    bf = ml_dtypes.bfloat16
    cst = np.zeros((128, 5, 128), np.float32)
    cst[:, 0, :] = np.eye(128)
    cst[:, 1, :] = 1.0
    cst[0:64, 2, 0:64] = 1.0
    cst[64:128, 2, 64:128] = 1.0
    cstb = np.zeros((128, 8, 128), np.float32)
    cstb[:, 0, :] = np.eye(128)
    for m in range(128):
        cstb[(m // 64) * 64 + ((m % 64) + 32) % 64, 1, m] = 1.0
    cstb[:, 2, :] = 1.0
    for q_ in range(4):
        cstb[:, 4 + q_, :] = np.eye(128)
    ki = np.arange(128)[:, None]
    qi = np.arange(128)[None, :]
    m12 = np.zeros((128, 2, 256), np.float32)
    for u in range(2):
        m12[:, u, 0:128] = (ki >= qi)
        m12[:, u, 128:256] = (ki <= qi)
    m3 = np.zeros((128, 4, 4, 64), np.float32)
    for j in range(4):
        q = 32 * j + np.arange(32)[None, :]
        for rr in range(4):
            m3[:, j, rr, 0:32] = (ki >= q)
            m3[:, j, rr, 32:64] = (ki <= q)
    mrw = np.zeros((128, 4, 2, 128), np.float32)
    mrw[:, :, 0, :] = (ki < qi)[:, None, :]
    mrw[:, :, 1, :] = (ki <= qi)[:, None, :]
    mlo = np.zeros((128, 4, 128), np.float32)
    mlo[:, :, :] = (qi < ki)[:, None, :]
    sel = np.zeros((128, 4, 8), np.float32)
    for p in range(128):
        for c in range(4):
            sel[p, c, 2 * c + p // 64] = 1.0
    inv = (1.0 / (np.float32(10000.0) ** (np.arange(0, 64, 2, dtype=np.float32) / np.float32(64)))).astype(np.float32)
    ang = (np.arange(T, dtype=np.float32)[:, None] * inv[None, :]).astype(np.float32)
    cosv, sinv = np.cos(ang).astype(np.float32), np.sin(ang).astype(np.float32)
    cosT = np.zeros((128, T), np.float32)
    sinT = np.zeros((128, T), np.float32)
    for p in range(128):
        d = p % 64
        cosT[p] = cosv[:, d % 32]
        sinT[p] = sinv[:, d % 32] * (-1.0 if d < 32 else 1.0)
    return dict(cst_f32=cst, cst_bf=cstb.astype(bf), m12=m12.astype(bf), m3=m3.astype(bf), mrw=mrw.astype(bf),
                mlo=mlo.astype(bf), sel=sel.astype(bf), cosT=cosT, sinT=sinT)


def _col(v, n):
    return np.ascontiguousarray(np.asarray(v, np.float32).reshape(n, 128).T)


def _shared_inputs(inp, T):
    f = lambda a: np.ascontiguousarray(np.asarray(a, np.float32)[:NL]) if np.asarray(a).shape[0] == 2 and np.asarray(a).ndim >= 2 else np.ascontiguousarray(np.asarray(a, np.float32))
    d = dict(_consts(T))
    d["w_ada"] = f(inp["w_ada"])
    d["b_ada"] = np.stack([_col(inp["b_ada"][l], 48) for l in range(NL)])
    d["g_mix"] = np.stack([_col(inp["norm_mix"][l], 8) for l in range(NL)])
    d["g_ffn"] = np.stack([_col(inp["norm_ffn"][l], 8) for l in range(NL)])
    d["g_fin"] = _col(inp["norm_final"], 8)
    d["w_in"] = f(inp["w_in"])
    d["mu"] = np.stack([_col(inp["rwkv_mu"][l], 14) for l in range(NL)])
    d["rwp"] = np.stack([np.stack([_col(np.asarray(inp[k][l]).reshape(-1), 4) for k in
                                   ("rwkv_w0", "rwkv_a0", "rwkv_k_k", "rwkv_k_a", "rwkv_r_k")], axis=1) for l in range(NL)])
    d["r_wup"] = f(inp["rwkv_w_up"])
    d["r_aup"] = f(inp["rwkv_a_up"])
    d["r_gup"] = f(inp["rwkv_g_up"])
    bc = lambda a: np.ascontiguousarray(np.broadcast_to(f(a).reshape(NL, 1, 512), (NL, 128, 512)))
    d["gnw"] = bc(inp["rwkv_gn_w"])
    d["gnb"] = bc(inp["rwkv_gn_b"])
    d["lng"] = bc(inp["gmlp_ln_g"])
    d["lnb"] = bc(inp["gmlp_ln_b"])
    d["wsT"] = np.ascontiguousarray(np.transpose(f(inp["gmlp_w_s"]), (0, 3, 1, 2)))
    bsv = f(inp["gmlp_b_s"])
    bsT = np.zeros((NL, 128, 4, 128), np.float32)
    for g_ in range(8):
        bsT[:, (g_ % 2) * 64:(g_ % 2) * 64 + 64, g_ // 2, :] = bsv[:, g_, None, :]
    d["bsT"] = bsT
    d["w_branch"] = f(inp["w_branch"])
    d["w_out"] = f(inp["w_out"])
    d["ffn_up"] = f(inp["ffn_w_up"])
    d["convw"] = np.stack([np.ascontiguousarray(np.transpose(np.asarray(inp["ffn_conv_w"][l], np.float32).reshape(3, NF, 128), (2, 1, 0))) for l in range(NL)])
    d["convb"] = np.stack([_col(inp["ffn_conv_b"][l], NF) for l in range(NL)])
    d["ffn_down"] = f(inp["ffn_w_down"])
    return d


_NC_CACHE = {}


def run(inp, T=None):
    x = np.asarray(inp["x"], np.float32)
    B, S, _ = x.shape
    T = S
    if T not in _NC_CACHE:
        _NC_CACHE[T] = build(T)
    nc = _NC_CACHE[T]
    shared = _shared_inputs(inp, T)
    c = np.asarray(inp["c"], np.float32)
    in_maps = []
    for core in range(8):
        b = core % B
        m = dict(shared)
        m["xT"] = np.ascontiguousarray(x[b].T)
        m["c128"] = _col(c[b], 8)
        in_maps.append(m)
    res = run_bass_kernel_spmd(nc, in_maps, core_ids=list(range(8)))
    out = np.stack([np.ascontiguousarray(res.results[b]["yT"].T) for b in range(B)])
    return out.astype(np.float32)


def kernel(**inputs):
    return run(inputs)
```

```python
import numpy as np
import ml_dtypes
import concourse.bass as bass
import concourse.mybir as mybir
from concourse.bass_utils import run_bass_kernel_spmd

F32 = mybir.dt.float32
BF16 = mybir.dt.bfloat16
AF = mybir.ActivationFunctionType
ALU = mybir.AluOpType
AX = mybir.AxisListType

D = 1024
import os as _os
NL = int(_os.environ.get("KNL", "2"))
TT = 512
IN_COLS = 7424
DFF = 2816
NF = 22


class Buf:
    __slots__ = ("name", "lw", "rd", "excl")

    def __init__(self, name="", excl=False):
        self.name = name
        self.lw = None
        self.rd = []
        self.excl = excl


class Tl:
    __slots__ = ("t", "b")

    def __init__(self, t, b):
        self.t = t
        self.b = b

    def __getitem__(self, k):
        return self.t[k]


def _b(x):
    return x.b if isinstance(x, Tl) else x


def View(ap, buf):
    return Tl(ap, _b(buf))


class Prog:
    ENGS = ("pe", "dve", "act", "pool", "sp")
    NDMA = 8

    def __init__(self):
        self.nc = bass.Bass("TRN2", target_bir_lowering=False)
        self.ops = {e: [] for e in self.ENGS}
        self.cnt = {e: 0 for e in self.ENGS}
        self.seen = {e: {} for e in self.ENGS}
        self.sems = {}
        self._stack = []
        nc = self.nc
        for e in ("pe", "dve", "act", "pool"):
            self.sems[e] = self._enter(nc.semaphore("s_" + e))
        self.dslots = {}
        for q in ("sp", "act", "pool"):
            sl = []
            for i in range(self.NDMA):
                key = "d_%s%d" % (q, i)
                self.sems[key] = self._enter(nc.semaphore(key))
                sl.append([key, 0])
            self.dslots[q] = [sl, 0]
        self.n_sb = 0

    def _enter(self, cm):
        v = cm.__enter__()
        self._stack.append(cm)
        return v

    def sb(self, shape, dtype, name=None):
        self.n_sb += 1
        t = self._enter(self.nc.sbuf_tensor("s_%s_%d" % (name or "t", self.n_sb), list(shape), dtype))
        return Tl(t, Buf(name or ""))

    def ps(self, shape, dtype=F32, name=None):
        self.n_sb += 1
        t = self._enter(self.nc.psum_tensor(name or ("ps%d" % self.n_sb), list(shape), dtype))
        return Tl(t, Buf(name or "", excl=True))

    def _deps(self, eng, reads, writes):
        toks = []
        for b in reads:
            b = _b(b)
            if b.lw is not None:
                toks.append(b.lw)
        for b in writes:
            b = _b(b)
            if b.lw is not None:
                toks.append(b.lw)
            toks.extend(b.rd)
        need = {}
        for (k, v) in toks:
            if k == "pe" and eng == "pe":
                continue
            if self.seen[eng].get(k, 0) >= v:
                continue
            if need.get(k, 0) < v:
                need[k] = v
        for k, v in need.items():
            self.seen[eng][k] = v
        return list(need.items())

    def _commit(self, tok, reads, writes):
        for b in reads:
            b = _b(b)
            b.rd.append(tok)
            if len(b.rd) > 64:
                mx = {}
                for (k, v) in b.rd:
                    if mx.get(k, 0) < v:
                        mx[k] = v
                b.rd = list(mx.items())
        for b in writes:
            b = _b(b)
            b.lw = tok
            b.rd = []

    def op(self, eng, fn, reads=(), writes=()):
        if eng == "pool":
            eng = "dve"
        if eng != "pe":
            ex = [b for b in reads if _b(b).excl]
            if ex:
                reads = [b for b in reads if not _b(b).excl]
                writes = list(writes) + ex
        waits = self._deps(eng, reads, writes)
        self.cnt[eng] += 1
        tok = (eng, self.cnt[eng])
        self.ops[eng].append((waits, fn, (eng, 1)))
        self._commit(tok, reads, writes)
        return tok

    def dma(self, q, out, in_, reads=(), writes=()):
        sl, idx = self.dslots[q]
        slot = sl[idx % self.NDMA]
        self.dslots[q][1] = idx + 1
        key = slot[0]
        waits = self._deps(q, reads, writes)
        if slot[1] > 0 and self.seen[q].get(key, 0) < slot[1]:
            waits.append((key, slot[1]))
            self.seen[q][key] = slot[1]
        slot[1] += 16
        tok = (key, slot[1])

        def fn(e, out=out, in_=in_):
            return e.dma_start(out=out, in_=in_)
        self.ops[q].append((waits, fn, (key, 16)))
        self._commit(tok, reads, writes)
        return tok

    def fence(self, src, dst):
        toks = []
        for s in src:
            s = _b(s)
            if s.lw is not None:
                toks.append(s.lw)
            toks.extend(s.rd)
        for d in dst:
            _b(d).rd.extend(toks)

    def finish(self, final_bufs):
        waits = self._deps("sp", final_bufs, ())
        self.ops["sp"].append((waits, None, None))
        nc = self.nc
        sems = self.sems
        ops = self.ops
        with nc.Block() as block:
            def mk(e):
                def body(eng):
                    for (waits, fn, inc) in ops[e]:
                        for (k, v) in waits:
                            eng.wait_ge(sems[k], v)
                        if fn is not None:
                            ins = fn(eng)
                            ins.then_inc(sems[inc[0]], inc[1])
                return body
            block.tensor(mk("pe"))
            block.vector(mk("dve"))
            block.scalar(mk("act"))
            block.gpsimd(mk("pool"))
            block.sync(mk("sp"))
        for cm in reversed(self._stack):
            cm.__exit__(None, None, None)
        self._stack = []
        return nc


class _Stop(Exception):
    pass


def build(T, dbg=False):
    import os
    STOP = os.environ.get("KSTOP")
    P = Prog()

    def chk(name):
        if STOP == name:
            raise _Stop()
    nc = P.nc
    NTI = T // TT
    NB = T // 128

    def din(name, shape, dt=F32):
        return nc.dram_tensor(name, list(shape), dt, kind="ExternalInput").ap()

    xT_in = din("xT", [D, T])
    c_in = din("c128", [128, 8])
    w_ada = din("w_ada", [NL, D, 6 * D])
    b_ada = din("b_ada", [NL, 128, 48])
    g_mix = din("g_mix", [NL, 128, 8])
    g_ffn = din("g_ffn", [NL, 128, 8])
    g_fin = din("g_fin", [128, 8])
    w_in = din("w_in", [NL, D, IN_COLS])
    mu_d = din("mu", [NL, 128, 14])
    pr_d = din("rwp", [NL, 128, 5, 4])
    wup_d = din("r_wup", [NL, 64, 512])
    aup_d = din("r_aup", [NL, 64, 512])
    gup_d = din("r_gup", [NL, 128, 512])
    gnw_d = din("gnw", [NL, 128, 512])
    gnb_d = din("gnb", [NL, 128, 512])
    lng_d = din("lng", [NL, 128, 512])
    lnb_d = din("lnb", [NL, 128, 512])
    wsT_d = din("wsT", [NL, 128, 8, 128])
    bs_d = din("bsT", [NL, 128, 4, 128])
    wbr_d = din("w_branch", [NL, 3, 512, D])
    wout_d = din("w_out", [NL, D, D])
    wfu_d = din("ffn_up", [NL, D, 2 * DFF])
    cw_d = din("convw", [NL, 128, NF, 3])
    cb_d = din("convb", [NL, 128, NF])
    wfd_d = din("ffn_down", [NL, DFF, D])
    cos_d = din("cosT", [128, T])
    sin_d = din("sinT", [128, T])
    cst_d = din("cst_f32", [128, 5, 128])
    cstb_d = din("cst_bf", [128, 8, 128], BF16)
    m12_d = din("m12", [128, 2, 256], BF16)
    m3_d = din("m3", [128, 4, 4, 64], BF16)
    mrw_d = din("mrw", [128, 4, 2, 128], BF16)
    mlo_d = din("mlo", [128, 4, 128], BF16)
    sel_d = din("sel", [128, 4, 8], BF16)
    yT_out = nc.dram_tensor("yT", [D, T], F32, kind="ExternalOutput").ap()
    x1T = nc.dram_tensor("x1T", [D, T], F32, kind="Internal").ap()
    Vd = [nc.dram_tensor("Vd%d" % l, [T, 512], BF16, kind="Internal").ap() for l in range(NL)]
    b_x1 = Buf("x1T")
    b_vd = [Buf("vd0"), Buf("vd1")]
    b_out = Buf("out")

    def MM(out, lhsT, rhs, start, stop, r, w):
        P.op("pe", lambda e: e.matmul(out, lhsT, rhs, start=start, stop=stop), reads=r, writes=w)

    def TR(out, in_, ident, r, w):
        P.op("pe", lambda e: e.transpose(out, in_, ident), reads=r, writes=w)

    def ACT(out, in_, func, r, w, bias=0.0, scale=1.0):
        P.op("act", lambda e: e.activation(out=out, in_=in_, func=func, bias=bias, scale=scale), reads=r, writes=w)

    def TTe(eng, out, in0, in1, op, r, w):
        P.op(eng, lambda e: e.tensor_tensor(out=out, in0=in0, in1=in1, op=op), reads=r, writes=w)

    def TS(eng, out, in0, s1, s2, op0, op1, r, w):
        if s2 is None:
            P.op(eng, lambda e: e.tensor_scalar(out=out, in0=in0, scalar1=s1, scalar2=None, op0=op0), reads=r, writes=w)
        else:
            P.op(eng, lambda e: e.tensor_scalar(out=out, in0=in0, scalar1=s1, scalar2=s2, op0=op0, op1=op1), reads=r, writes=w)

    def STT(eng, out, in0, sc, in1, op0, op1, r, w):
        P.op(eng, lambda e: e.scalar_tensor_tensor(out=out, in0=in0, scalar=sc, in1=in1, op0=op0, op1=op1), reads=r, writes=w)

    def CP(eng, out, in_, r, w):
        if eng == "act":
            P.op("act", lambda e: e.copy(out=out, in_=in_), reads=r, writes=w)
        else:
            P.op(eng, lambda e: e.tensor_copy(out=out, in_=in_), reads=r, writes=w)

    def MSET(eng, ap, val, w):
        P.op(eng, lambda e: e.memset(ap, val), writes=w)

    psf = [P.ps([128, 512], F32) for _ in range(6)]
    psb = [P.ps([128, 1024], BF16) for _ in range(2)]
    psi = [0, 0]

    def PSF():
        psi[0] += 1
        return psf[psi[0] % 6]

    def PSB():
        psi[1] += 1
        return psb[psi[1] % 2]

    cst = P.sb([128, 5, 128], F32)
    cstb = P.sb([128, 8, 128], BF16)
    m12 = P.sb([128, 2, 256], BF16)
    m3 = P.sb([128, 4, 4, 64], BF16)
    mrw = P.sb([128, 4, 2, 128], BF16)
    mlo = P.sb([128, 4, 128], BF16)
    sel = P.sb([128, 4, 8], BF16)
    for (t_, d_) in ((cst, cst_d), (cstb, cstb_d), (m12, m12_d), (m3, m3_d), (mrw, mrw_d), (mlo, mlo_d), (sel, sel_d)):
        P.dma("sp", t_[:], d_, writes=[t_])
    identf = cst[:, 0, :]
    onesf = cst[:, 1, :]
    blkones = cst[:, 2, :]
    identb = cstb[:, 0, :]
    permb = cstb[:, 1, :]
    onesb = cstb[:, 2, :]

    xt = P.sb([128, 8, TT], F32, "xt")
    scr = P.sb([128, 8, TT], F32, "scr")
    hT = P.sb([128, 8, TT], BF16, "hT")
    rstd = P.sb([128, TT], F32, "rstd")
    NWB = 2
    wb = [P.sb([128, 8, 512], BF16, "wb%d" % i) for i in range(NWB)]
    wbi = [0]
    QT = P.sb([128, 4, TT], BF16, "QT")
    KT = P.sb([128, 4, T], BF16, "KT")
    V1r = P.sb([128, 8, 512], BF16, "V1r")
    V2 = P.sb([128, 2, 4, 512], BF16, "V2")
    V3g = [P.sb([128, 2, 4, 512], BF16, "V3g0")] * 2
    acc = P.sb([128, 4, 2, TT], F32, "acc")
    big = P.sb([128, NF, TT], BF16, "big")
    YB = [P.sb([128, 4, TT], BF16, "YB%d" % i) for i in range(3)]
    cosb = P.sb([128, TT], F32, "cos")
    sinb = P.sb([128, TT], F32, "sin")
    zcar = P.sb([128, 14], F32, "zcar")
    H0 = P.sb([128, 4, 64], F32, "H0")
    H0b = P.sb([128, 4, 64], BF16, "H0b")
    ccar = P.sb([128, NF, 2], F32, "ccar")
    MSET("pool", KT[:], 0.0, [KT])

    modp = P.sb([128, NL, 48], F32, "mod")
    A1 = P.sb([128, NL, 8], F32, "A1")
    A2 = P.sb([128, NL, 8], F32, "A2")
    gfin = P.sb([128, 8], F32, "gfin")
    cact = P.sb([128, 8], F32, "cact")
    wtmp = scr
    bad = P.sb([128, NL, 48], F32, "bad")
    gm = P.sb([128, NL, 8], F32, "gm")
    gf = P.sb([128, NL, 8], F32, "gf")
    P.dma("sp", cact[:], c_in, writes=[cact])
    P.dma("sp", gfin[:], g_fin, writes=[gfin])
    for l in range(NL):
        P.dma("sp", bad[:, l, :], b_ada[l], writes=[bad])
        P.dma("sp", gm[:, l, :], g_mix[l], writes=[gm])
        P.dma("sp", gf[:, l, :], g_ffn[l], writes=[gf])
    ACT(cact[:], cact[:], AF.Silu, [cact], [cact])
    for l in range(NL):
        for pc in range(12):
            P.dma("sp", wtmp[:], w_ada[l][:, pc * 512:(pc + 1) * 512].rearrange("(k p) n -> p k n", p=128), writes=[wtmp])
            ps = PSF()
            for mc in range(4):
                for k in range(8):
                    MM(ps[:, mc:mc + 1], wtmp[:, k, mc * 128:(mc + 1) * 128], cact[:, k:k + 1], k == 0, k == 7, [wtmp, cact], [ps])
            TTe("dve", modp[:, l, pc * 4:(pc + 1) * 4], ps[:, 0:4], bad[:, l, pc * 4:(pc + 1) * 4], ALU.add, [ps, bad], [modp])
        STT("dve", A1[:, l, :], modp[:, l, 8:16], 1.0, gm[:, l, :], ALU.add, ALU.mult, [modp, gm], [A1])
        STT("dve", A2[:, l, :], modp[:, l, 32:40], 1.0, gf[:, l, :], ALU.add, ALU.mult, [modp, gf], [A2])

    mu = P.sb([128, 14], F32, "mu")
    omu = P.sb([128, 14], F32, "omu")
    rwp = P.sb([128, 5, 4], F32, "rwp")
    omka = P.sb([128, 4], F32, "omka")
    nw0 = P.sb([128, 4], F32, "nw0")
    wupb = P.sb([128, 512], BF16, "wupb")
    aupb = P.sb([128, 512], BF16, "aupb")
    gupb = P.sb([128, 512], BF16, "gupb")
    gnw = P.sb([128, 512], F32, "gnw")
    gnb = P.sb([128, 512], F32, "gnb")
    lng = P.sb([128, 512], F32, "lng")
    lnb = P.sb([128, 512], F32, "lnb")
    wsT = P.sb([128, 8, 128], BF16, "wsT")
    wsTf = View(scr[:, 0:2, :].rearrange("p a (g i) -> p (a g) i", g=4), scr)
    bsT = P.sb([128, 4, 128], F32, "bsT")
    cw = P.sb([128, NF, 3], F32, "cw")
    cb = P.sb([128, NF], F32, "cb")

    def load_w(src_ap, kc, ncols):
        wbi[0] += 1
        w = wb[wbi[0] % NWB]
        P.dma("pool", w[:, 0:kc, 0:ncols], src_ap.rearrange("(k p) n -> p k n", p=128), writes=[w])
        return w

    def rmsnorm_mod(Acol, shcol, l):
        ACT(scr[:], xt[:], AF.Square, [xt], [scr])
        ps = PSF()
        for k in range(8):
            MM(ps[:], onesf, scr[:, k, :], k == 0, k == 7, [cst, scr], [ps])
        ACT(rstd[:], ps[:], AF.Sqrt, [ps], [rstd], bias=1e-6, scale=1.0 / D)
        P.op("dve", lambda e: e.reciprocal(out=rstd[:], in_=rstd[:]), reads=[rstd], writes=[rstd])
        for k in range(8):
            STT("dve", scr[:, k, :], xt[:, k, :], Acol[:, k:k + 1], rstd[:], ALU.mult, ALU.mult, [xt, rstd, A1, A2, gfin], [scr])
            if shcol is not None:
                TS("pool", hT[:, k, :], scr[:, k, :], shcol[:, k:k + 1], None, ALU.add, None, [scr, modp], [hT])

    GC = 0.7978845608028654

    def gelu_from(out, src, rd, wr, tmpa, tmpb):
        CP("act", tmpa, src, rd, wr)
        TTe("dve", tmpb, tmpa, tmpa, ALU.mult, wr, wr)
        TS("dve", tmpb, tmpb, 0.044715, 1.0, ALU.mult, ALU.add, wr, wr)
        TTe("dve", tmpb, tmpb, tmpa, ALU.mult, wr, wr)
        ACT(tmpb, tmpb, AF.Sigmoid, wr, wr, scale=2.0 * GC)
        TTe("dve", out, tmpb, tmpa, ALU.mult, wr, wr)

    try:
        for l in range(NL):
            x_src = xT_in if l == 0 else x1T
            P.dma("sp", mu[:], mu_d[l], writes=[mu])
            TS("dve", omu[:], mu[:], -1.0, 1.0, ALU.mult, ALU.add, [mu], [omu])
            P.dma("sp", rwp[:], pr_d[l], writes=[rwp])
            TS("dve", omka[:], rwp[:, 3, :], -1.0, 1.0, ALU.mult, ALU.add, [rwp], [omka])
            TS("dve", nw0[:], rwp[:, 0, :], -1.0, None, ALU.mult, None, [rwp], [nw0])
            P.dma("pool", wupb[0:64, :], wup_d[l], writes=[wupb])
            P.dma("pool", aupb[64:128, :], aup_d[l], writes=[aupb])
            P.dma("pool", gupb[:], gup_d[l], writes=[gupb])
            for (t_, d_) in ((gnw, gnw_d), (gnb, gnb_d), (lng, lng_d), (lnb, lnb_d)):
                P.dma("sp", t_[:], d_[l], writes=[t_])
            P.dma("sp", wsTf[:], wsT_d[l], writes=[wsTf])
            for g in range(8):
                TTe("dve", wsT[:, g, :], wsTf[:, g, :], mrw[:, 0, 1, :], ALU.mult, [wsTf, mrw], [wsT])
            P.dma("sp", bsT[:], bs_d[l], writes=[bsT])
            P.dma("sp", cw[:], cw_d[l], writes=[cw])
            P.dma("sp", cb[:], cb_d[l], writes=[cb])
            MSET("pool", zcar[:], 0.0, [zcar])
            MSET("pool", H0[:], 0.0, [H0])
            MSET("pool", H0b[:], 0.0, [H0b])
            MSET("pool", ccar[:], 0.0, [ccar])
            MSET("pool", big[:, 0:4, :], 0.0, [big])
            for q in range(T // 512):
                P.dma("sp", Vd[l][q * 512:(q + 1) * 512, :].rearrange("(p a) c -> p (a c)", p=128), big[:, 0:4, :].rearrange("p a b -> p (a b)"), reads=[big], writes=[b_vd[l]])

            for it in range(NTI):
                t0 = it * TT
                P.dma("sp", xt[:], x_src[:, t0:t0 + TT].rearrange("(k p) t -> p k t", p=128), reads=[b_x1] if l else [], writes=[xt])
                P.dma("sp", cosb[:], cos_d[:, t0:t0 + TT], writes=[cosb])
                P.dma("sp", sinb[:], sin_d[:, t0:t0 + TT], writes=[sinb])
                chk('pro')
                rmsnorm_mod(A1[:, l, :], modp[:, l, 0:8], l)
                chk('norm')

                for pc in range(2):
                    w = load_w(w_in[l][:, pc * 512:(pc + 1) * 512], 8, 512)
                    for mc in range(4):
                        ps = PSF()
                        for k in range(8):
                            MM(ps[:], w[:, k, mc * 128:(mc + 1) * 128], hT[:, k, :], k == 0, k == 7, [w, hT], [ps])
                        chk('qk1')
                        qs = scr[:, mc, :]
                        qbf = big[:, 14 + mc, :]
                        CP("act", qbf, ps[:], [ps], [big])
                        ps2 = PSF()
                        MM(ps2[:], permb, qbf, True, True, [cstb, big], [ps2])
                        chk('qk2')
                        TTe("dve", qs, ps[:], cosb[:], ALU.mult, [ps, cosb, big, ps2], [scr])
                        TTe("dve", scr[:, 4 + mc, :], ps2[:], sinb[:], ALU.mult, [ps2, sinb], [scr])
                        chk('qk3')
                        dst = QT[:, mc, :] if pc == 0 else KT[:, mc, t0:t0 + TT]
                        TTe("pool", dst, qs, scr[:, 4 + mc, :], ALU.add, [scr], [QT if pc == 0 else KT])
                chk('qk')
                w = load_w(w_in[l][:, 1024:1536], 8, 512)
                for blk in range(4):
                    ps = PSF()
                    for k in range(8):
                        MM(ps[:], hT[:, k, blk * 128:(blk + 1) * 128], w[:, k, :], k == 0, k == 7, [w, hT], [ps])
                    gb = 4 * it + blk
                    CP("act", V1r[:, gb % 8, :], ps[:], [ps], [V1r])
                    P.dma("sp", Vd[l][t0 + blk * 128:t0 + (blk + 1) * 128, :], V1r[:, gb % 8, :], reads=[V1r], writes=[b_vd[l]])
                P.dma("sp", V2[:, it % 2, :, :], Vd[l][t0:t0 + TT, :].rearrange("(i r) c -> i r c", r=4), reads=[b_vd[l]], writes=[V2])

                chk('v')
                if "EP" not in P.__dict__:
                    P.EP = [(View(big[:, 16 + 2 * i_, :], Buf("E%d" % i_)), View(big[:, 17 + 2 * i_, :], Buf("P%d" % i_))) for i_ in range(3)]
                    P.epi = 0
                EPb = [x for pr in P.EP for x in pr]
                P.fence([big], EPb)
                for cfg in range(3):
                    n_sp = t0 // min(2048, T)
                    j3 = (t0 % min(2048, T)) // TT
                    if cfg < 2:
                        items = [(None, c, hh) for c in range(4) for hh in range(2)]
                    else:
                        items = [(rg_, c, hh) for rg_ in range(4) for c in range(4) for hh in range(2)]
                    for (rg_, c, hh) in items:
                        if True:
                            rows = slice(hh * 64, hh * 64 + 64)
                            if cfg < 2:
                                for half in range(2):
                                    psS = PSF()
                                    hp = []
                                    for u in range(2):
                                        qi = half * 2 + u
                                        if cfg == 0:
                                            gb = 4 * it + qi
                                            q_ap = QT[rows, c, qi * 128:(qi + 1) * 128]
                                            kc_ap = KT[rows, c, gb * 128:(gb + 1) * 128]
                                            kp_ap = KT[rows, c, (gb - 1) * 128:gb * 128] if gb > 0 else None
                                            vc_ap = V1r[:, gb % 8, c * 128:(c + 1) * 128]
                                            vp_ap = V1r[:, (gb - 1) % 8, c * 128:(c + 1) * 128]
                                            vb = V1r
                                        else:
                                            q_ap = QT[rows, c, qi:TT:4]
                                            kc_ap = KT[rows, c, t0 + qi:t0 + TT:4]
                                            kp_ap = KT[rows, c, t0 - TT + qi:t0:4] if it > 0 else None
                                            vc_ap = V2[:, it % 2, qi, c * 128:(c + 1) * 128]
                                            vp_ap = V2[:, (it - 1) % 2, qi, c * 128:(c + 1) * 128]
                                            vb = V2
                                        hp.append(kp_ap is not None)
                                        if kp_ap is not None:
                                            MM(psS[:, u * 256:u * 256 + 128], kp_ap, q_ap, True, True, [KT, QT], [psS])
                                        MM(psS[:, u * 256 + 128:u * 256 + 256], kc_ap, q_ap, True, True, [KT, QT], [psS])
                                        hp.append((vp_ap, vc_ap, vb))
                                    P.epi += 1
                                    Et, Pt = P.EP[P.epi % 3]
                                    for u in range(2):
                                        if hp[2 * u]:
                                            ACT(Et[:, u * 256:(u + 1) * 256], psS[:, u * 256:(u + 1) * 256], AF.Exp, [psS], [Et], scale=0.125)
                                        else:
                                            MSET("pool", Et[:, u * 256:u * 256 + 128], 0.0, [Et])
                                            ACT(Et[:, u * 256 + 128:u * 256 + 256], psS[:, u * 256 + 128:u * 256 + 256], AF.Exp, [psS], [Et], scale=0.125)
                                    Pm = Pt[:]
                                    TTe("dve", Pm, Et[:], m12[:].rearrange("p a b -> p (a b)"), ALU.mult, [Et, m12], [Pt])
                                    psN = PSF()
                                    for u in range(2):
                                        vp_ap, vc_ap, vb = hp[2 * u + 1]
                                        has_prev = hp[2 * u]
                                        for z in range(2):
                                            o = psN[:, z * 256 + u * 128:z * 256 + u * 128 + 128]
                                            if has_prev:
                                                MM(o, vp_ap if z == 0 else onesb, Pm[:, u * 256:u * 256 + 128], True, False, [vb, Pt, cstb], [psN])
                                            MM(o, vc_ap if z == 0 else onesb, Pm[:, u * 256 + 128:u * 256 + 256], not has_prev, True, [vb, Pt, cstb], [psN])
                                    src = psN[rows, :].rearrange("p (z u q) -> p z u q", z=2, u=2)
                                    if cfg == 0:
                                        dsta = acc[rows, c, :, half * 256:(half + 1) * 256].rearrange("p z (u q) -> p z u q", u=2)
                                        CP("act", dsta, src, [psN], [acc])
                                    else:
                                        dsta = acc[rows, c, :, :].rearrange("p z (q r) -> p z r q", r=4)[:, :, half * 2:half * 2 + 2, :]
                                        TTe("dve", dsta, src, dsta, ALU.add, [psN, acc], [acc])
                            else:
                                SPAN = min(2048, T)
                                nk3 = SPAN // 16
                                kp_ = slice(0, nk3)
                                for rg in (rg_,):
                                    if c == 0 and hh == 0:
                                        v3 = V3g[rg % 2]
                                        for sp_ in range(2):
                                            nn = n_sp - 1 + sp_
                                            if nn < 0:
                                                continue
                                            P.dma("sp", v3[kp_, sp_, :, :],
                                                  Vd[l][nn * SPAN:(nn + 1) * SPAN, :].rearrange("(i r) c -> i r c", r=16)[:, rg * 4:rg * 4 + 4, :],
                                                  reads=[b_vd[l]], writes=[v3])
                                    v3 = V3g[rg % 2]
                                    psS = PSF()
                                    hasp = n_sp > 0
                                    for rr in range(4):
                                        r = rg * 4 + rr
                                        q_ap = QT[rows, c, r:TT:16]
                                        kc_ap = KT[rows, c, n_sp * SPAN + r:(n_sp + 1) * SPAN:16]
                                        if hasp:
                                            kp_ap = KT[rows, c, (n_sp - 1) * SPAN + r:n_sp * SPAN:16]
                                            MM(psS[kp_, rr * 64:rr * 64 + 32], kp_ap, q_ap, True, True, [KT, QT], [psS])
                                        MM(psS[kp_, rr * 64 + 32:rr * 64 + 64], kc_ap, q_ap, True, True, [KT, QT], [psS])
                                    P.epi += 1
                                    Et, Pt = P.EP[P.epi % 3]
                                    if not hasp:
                                        MSET("pool", Et[kp_, 0:256], 0.0, [Et])
                                        for rr in range(4):
                                            ACT(Et[kp_, rr * 64 + 32:rr * 64 + 64], psS[kp_, rr * 64 + 32:rr * 64 + 64], AF.Exp, [psS], [Et], scale=0.125)
                                    else:
                                        ACT(Et[kp_, 0:256], psS[kp_, 0:256], AF.Exp, [psS], [Et], scale=0.125)
                                    Pm = Pt[kp_, 0:256]
                                    TTe("dve", Pm, Et[kp_, 0:256], m3[kp_, j3, :, :].rearrange("p a b -> p (a b)"), ALU.mult, [Et, m3], [Pt])
                                    psN = PSF()
                                    for rr in range(4):
                                        for z in range(2):
                                            o = psN[:, z * 128 + rr * 32:z * 128 + rr * 32 + 32]
                                            if hasp:
                                                MM(o, v3[kp_, 0, rr, c * 128:(c + 1) * 128] if z == 0 else onesb[kp_, :], Pm[:, rr * 64:rr * 64 + 32], True, False, [v3, Pt, cstb], [psN])
                                            MM(o, v3[kp_, 1, rr, c * 128:(c + 1) * 128] if z == 0 else onesb[kp_, :], Pm[:, rr * 64 + 32:rr * 64 + 64], not hasp, True, [v3, Pt, cstb], [psN])
                                    src = psN[rows, 0:256].rearrange("p (z r q) -> p z r q", z=2, r=4)
                                    dsta = acc[rows, c, :, :].rearrange("p z (q r) -> p z r q", r=16)[:, :, rg * 4:rg * 4 + 4, :]
                                    TTe("dve", dsta, src, dsta, ALU.add, [psN, acc], [acc])
                P.fence(EPb, [big])
                for c in range(4):
                    P.op("dve", lambda e, c=c: e.reciprocal(out=acc[:, c, 1, :], in_=acc[:, c, 1, :]), reads=[acc], writes=[acc])
                    TTe("dve", YB[0][:, c, :], acc[:, c, 0, :], acc[:, c, 1, :], ALU.mult, [acc], [YB[0]])
                chk('att')

                for pc in range(4):
                    ncol = 512 if pc < 3 else 256
                    w = load_w(w_in[l][:, 1536 + pc * 512:1536 + pc * 512 + ncol], 8, ncol)
                    for mc in range(ncol // 128):
                        ch = pc * 4 + mc
                        ps = PSF()
                        for k in range(8):
                            MM(ps[:], w[:, k, mc * 128:(mc + 1) * 128], hT[:, k, :], k == 0, k == 7, [w, hT], [ps])
                        zb = scr[:, ch % 4, :]
                        zp = scr[:, 4 + ch % 4, :]
                        CP("act", zb, ps[:], [ps], [scr])
                        CP("pool", zp[:, 1:TT], zb[:, 0:TT - 1], [scr], [scr])
                        CP("pool", zp[:, 0:1], zcar[:, ch:ch + 1], [zcar, scr], [scr])
                        CP("pool", zcar[:, ch:ch + 1], zb[:, TT - 1:TT], [scr], [zcar])
                        TS("dve", zb, zb, omu[:, ch:ch + 1], None, ALU.mult, None, [scr, omu], [scr])
                        STT("dve", big[:, ch, :], zp, mu[:, ch:ch + 1], zb, ALU.mult, ALU.add, [scr, mu], [big])
                chk('rwin'); rwkv_tile(P, locals()); chk('rw')

                gmlp_tile(P, locals()); chk('gmlp')

                wbrs = []
                for br in range(3):
                    for half in range(2):
                        wg = load_w(w_in[l][:, 4352 + br * 1024 + half * 512:4352 + br * 1024 + (half + 1) * 512], 8, 512)
                        wbt = load_w(wbr_d[l, br][:, half * 512:(half + 1) * 512], 4, 512)
                        for mc in range(4):
                            m = half * 4 + mc
                            psg = PSF()
                            for k in range(8):
                                MM(psg[:], wg[:, k, mc * 128:(mc + 1) * 128], hT[:, k, :], k == 0, k == 7, [wg, hT], [psg])
                            psbr = PSF()
                            for k in range(4):
                                MM(psbr[:], wbt[:, k, mc * 128:(mc + 1) * 128], YB[br][:, k, :], k == 0, k == 3, [wbt, YB[br]], [psbr])
                            sg = acc[:, 0, 0, :]
                            ACT(sg, psg[:], AF.Sigmoid, [psg], [acc])
                            if br == 0:
                                TTe("dve", scr[:, m, :], sg, psbr[:], ALU.mult, [acc, psbr], [scr])
                            else:
                                TTe("dve", sg, sg, psbr[:], ALU.mult, [acc, psbr], [acc])
                                TTe("pool", scr[:, m, :], scr[:, m, :], sg, ALU.add, [acc, scr], [scr])
                chk('br')
                for m in range(8):
                    CP("act", hT[:, m, :], scr[:, m, :], [scr], [hT])
                for half in range(2):
                    w = load_w(wout_d[l][:, half * 512:(half + 1) * 512], 8, 512)
                    for mc in range(4):
                        m = half * 4 + mc
                        ps = PSF()
                        for k in range(8):
                            MM(ps[:], w[:, k, mc * 128:(mc + 1) * 128], hT[:, k, :], k == 0, k == 7, [w, hT], [ps])
                        STT("dve", xt[:, m, :], ps[:], modp[:, l, 16 + m:17 + m], xt[:, m, :], ALU.mult, ALU.add, [ps, modp, xt], [xt])

                chk('out')
                rmsnorm_mod(A2[:, l, :], modp[:, l, 24:32], l)
                for fg in range(6):
                    nf = 4 if fg < 5 else 2
                    wa = load_w(wfu_d[l][:, fg * 512:fg * 512 + nf * 128], 8, nf * 128)
                    wg = load_w(wfu_d[l][:, DFF + fg * 512:DFF + fg * 512 + nf * 128], 8, nf * 128)
                    for mc in range(nf):
                        f = fg * 4 + mc
                        psa = PSF()
                        for k in range(8):
                            MM(psa[:], wa[:, k, mc * 128:(mc + 1) * 128], hT[:, k, :], k == 0, k == 7, [wa, hT], [psa])
                        psg = PSF()
                        for k in range(8):
                            MM(psg[:], wg[:, k, mc * 128:(mc + 1) * 128], hT[:, k, :], k == 0, k == 7, [wg, hT], [psg])
                        ab = acc[:, 0, :, :].rearrange("p a b -> p (a b)")
                        CP("act", ab[:, 2:2 + TT], psa[:], [psa], [acc])
                        CP("pool", ab[:, 0:2], ccar[:, f, :], [ccar, acc], [acc])
                        CP("pool", ccar[:, f, :], ab[:, TT:TT + 2], [acc], [ccar])
                        c1 = acc[:, 1, 0, :]
                        TS("dve", c1, ab[:, 2:2 + TT], cw[:, f, 2:3], cb[:, f:f + 1], ALU.mult, ALU.add, [acc, cw, cb], [acc])
                        STT("dve", c1, ab[:, 1:1 + TT], cw[:, f, 1:2], c1, ALU.mult, ALU.add, [acc, cw], [acc])
                        STT("dve", c1, ab[:, 0:TT], cw[:, f, 0:1], c1, ALU.mult, ALU.add, [acc, cw], [acc])
                        ge = acc[:, 2, 0, :]
                        gelu_from(ge, c1, [acc], [acc], acc[:, 2, 1, :], acc[:, 3, 0, :])
                        TTe("dve", big[:, f, :], ge, psg[:], ALU.mult, [acc, psg], [big])
                for m in range(8):
                    wbi[0] += 1
                    wd = wb[wbi[0] % NWB]
                    wdv = wd[:].rearrange("p a b -> p (a b)")[:, 0:NF * 128].rearrange("p (f n) -> p f n", f=NF)
                    P.dma("pool", wdv, wfd_d[l][:, m * 128:(m + 1) * 128].rearrange("(f p) n -> p f n", p=128), writes=[wd])
                    ps = PSF()
                    for f in range(NF):
                        MM(ps[:], wdv[:, f, :], big[:, f, :], f == 0, f == NF - 1, [wd, big], [ps])
                    STT("dve", xt[:, m, :], ps[:], modp[:, l, 40 + m:41 + m], xt[:, m, :], ALU.mult, ALU.add, [ps, modp, xt], [xt])

                chk('ffn')
                if l < NL - 1:
                    P.dma("sp", x1T[:, t0:t0 + TT].rearrange("(k p) t -> p k t", p=128), xt[:], reads=[xt], writes=[b_x1])
                else:
                    rmsnorm_mod(gfin[:], None, l)
                    P.dma("sp", yT_out[:, t0:t0 + TT].rearrange("(k p) t -> p k t", p=128), scr[:], reads=[scr], writes=[b_out])
    except _Stop:
        src_t = xt
        ybi = {"att": 0, "rw": 1, "gmlp": 2}.get(STOP)
        if ybi is not None:
            MSET("pool", scr[:], 0.0, [scr])
            for c_ in range(4):
                CP("dve", scr[:, c_, :], YB[ybi][:, c_, :], [YB[ybi]], [scr])
            src_t = scr
        elif STOP == "br":
            src_t = scr
        elif STOP == "rw7":
            MSET("pool", scr[:], 0.0, [scr])
            CP("dve", scr[:, 0, :], acc[:, 0, 0, :], [acc], [scr])
            src_t = scr
        P.dma("sp", yT_out[:, 0:TT].rearrange("(k p) t -> p k t", p=128), src_t[:], reads=[src_t], writes=[b_out])
    return P.finish([b_out])


def gmlp_tile(P, L):
    g = L
    PSF, MM, ACT, TTe, TS, STT, CP = g["PSF"], g["MM"], g["ACT"], g["TTe"], g["TS"], g["STT"], g["CP"]
    hT, big, acc, scr, YB, wsT, bsT, lng, lnb = g["hT"], g["big"], g["acc"], g["scr"], g["YB"], g["wsT"], g["bsT"], g["lng"], g["lnb"]
    load_w, w_in, l, gelu_from = g["load_w"], g["w_in"], g["l"], g["gelu_from"]
    w = load_w(w_in[l][:, 3328:3840], 8, 512)
    for mc in range(4):
        ps = PSF()
        for k in range(8):
            MM(ps[:], w[:, k, mc * 128:(mc + 1) * 128], hT[:, k, :], k == 0, k == 7, [w, hT], [ps])
        gelu_from(scr[:, mc, :], ps[:], [ps], [scr], scr[:, 4, :], scr[:, 5, :])
    w = load_w(w_in[l][:, 3840:4352], 8, 512)
    stats = acc[:, 3, 1, 0:8]
    for blk in range(4):
        ps = PSF()
        for k in range(8):
            MM(ps[:], hT[:, k, blk * 128:(blk + 1) * 128], w[:, k, :], k == 0, k == 7, [w, hT], [ps])
        v = acc[:, 0, 0, :]
        gelu_from(v, ps[:], [ps], [acc], acc[:, 0, 1, :], acc[:, 1, 0, :])
        P.op("dve", lambda e: e.bn_stats(out=acc[:, 3, 1, 0:6], in_=v), reads=[acc], writes=[acc])
        P.op("dve", lambda e: e.bn_aggr(out=acc[:, 3, 1, 6:8], in_=acc[:, 3, 1, 0:6]), reads=[acc], writes=[acc])
        ACT(acc[:, 3, 1, 7:8], acc[:, 3, 1, 7:8], AF.Sqrt, [acc], [acc], bias=1e-5)
        P.op("dve", lambda e: e.reciprocal(out=acc[:, 3, 1, 7:8], in_=acc[:, 3, 1, 7:8]), reads=[acc], writes=[acc])
        TS("dve", v, v, acc[:, 3, 1, 6:7], acc[:, 3, 1, 7:8], ALU.subtract, ALU.mult, [acc], [acc])
        TTe("dve", v, v, lng[:], ALU.mult, [acc, lng], [acc])
        vb = big[:, 18, :]
        TTe("dve", vb, v, lnb[:], ALU.add, [acc, lnb], [big])
        psA = PSF()
        psB = PSF()
        for gq in range(8):
            pp = psA if gq < 4 else psB
            c = gq // 2
            MM(pp[:, (gq % 4) * 128:(gq % 4) * 128 + 128], vb[:, c * 128:(c + 1) * 128], wsT[:, gq, :], True, True, [big, wsT], [pp])
        for gq in range(8):
            pp = psA if gq < 4 else psB
            c = gq // 2
            rows = slice((gq % 2) * 64, (gq % 2) * 64 + 64)
            tmp = acc[rows, 1, 1, 0:128]
            TTe("dve", tmp, pp[rows, (gq % 4) * 128:(gq % 4) * 128 + 128], bsT[rows, c, :], ALU.add, [pp, bsT], [acc])
            TTe("dve", YB[2][rows, c, blk * 128:(blk + 1) * 128], tmp, scr[rows, c, blk * 128:(blk + 1) * 128], ALU.mult, [acc, scr], [YB[2]])


def rwkv_tile(P, L):
    g = L
    PSF, PSB, MM, TR, ACT, TTe, TS, STT, CP, MSET = (g[k] for k in ("PSF", "PSB", "MM", "TR", "ACT", "TTe", "TS", "STT", "CP", "MSET"))
    big, scr, acc, YB, H0, H0b = g["big"], g["scr"], g["acc"], g["YB"], g["H0"], g["H0b"]
    rwp, omka, wupb, aupb, gupb, gnw, gnb = g["rwp"], g["omka"], g["wupb"], g["aupb"], g["gupb"], g["gnw"], g["gnb"]
    cst, cstb, mrw, mlo, sel = g["cst"], g["cstb"], g["mrw"], g["mlo"], g["sel"]
    chk = g["chk"]
    identb, identf, blkones = g["identb"], g["identf"], g["blkones"]
    rw = getattr(P, "_rw", None)
    if rw is None:
        sb = P.sb
        rw = dict(
            th=sb([128, 128], BF16), sgx=sb([128, 128], BF16),
            **{nm: View(scr[:, i_, :].rearrange("p (a b) -> p a b", a=4), scr) for i_, nm in enumerate(("lw", "cum", "cum2", "av", "kk", "km", "t1", "t2"))},
            AR=sb([128, 4, 2, 128], BF16), BK=sb([128, 4, 2, 128], BF16),
            wc=sb([128, 4], F32),
            tok=View(g["QT"][:, 0:3, :], g["QT"]),
            Xs=View(g["QT"][:, 3, :], g["QT"]),
            **{nm: View(g["V3g"][0][:, i_ // 2, (i_ % 2) * 2:(i_ % 2) * 2 + 2, :].rearrange("p a (h t) -> p (a h) t", h=4), g["V3g"][0])
               for i_, nm in enumerate(("N1", "L1", "N2", "L2"))},
            **{nm: View(YB[2][:, 2 * i_:2 * i_ + 2, :].rearrange("p a (h t) -> p (a h) t", h=4), YB[2]) for i_, nm in enumerate(("IL", "Mm"))},
            **{nm: View(big[:, 14 + 2 * i_:16 + 2 * i_, :].rearrange("p a (h t) -> p (a h) t", h=4), big)
               for i_, nm in enumerate(("Mm2", "Arb", "Aak", "Ark"))},
            Y=View(acc[:, 0, 0, :], acc), Y2=View(acc[:, 0, 1, :], acc), st=sb([128, 4, 8], F32), gt=View(acc[:, 1, 0, :], acc),
            Ht=sb([128, 4, 64], F32),
            Noff=View(g["rstd"][:].bitcast(BF16).rearrange("p (h t) -> p h t", h=8), g["rstd"]),
        )
        P._rw = rw
    rw = P._rw
    th, sgx, lw, cum, cum2, av, kk, km, t1, t2 = (rw[k] for k in ("th", "sgx", "lw", "cum", "cum2", "av", "kk", "km", "t1", "t2"))
    AR, BK, wc, tok = rw["AR"], rw["BK"], rw["wc"], rw["tok"]
    N1, L1, N2, L2, IL, Mm, Mm2, Arb, Aak, Ark = (rw[k] for k in ("N1", "L1", "N2", "L2", "IL", "Mm", "Mm2", "Arb", "Aak", "Ark"))
    V1r_ = g["V1r"]
    fs_ = (4, 5, 6) if g["it"] % 2 == 0 else (0, 1, 2)
    Us = View(V1r_[:, fs_[0], :], V1r_)
    yb = View(V1r_[:, fs_[1], :], V1r_)
    prod = View(V1r_[:, fs_[2], :].rearrange("p (a b) -> p a b", a=4), V1r_)
    Xs, Y, Y2, st, gt, Ht = (rw[k] for k in ("Xs", "Y", "Y2", "st", "gt", "Ht"))
    W0, A0, KK_, KA, RK = (rwp[:, i, :] for i in range(5))
    EH = 0.6065306597126334

    for blk in range(4):
        ts = slice(blk * 128, (blk + 1) * 128)
        zr = lambda c: big[:, 0 + c, ts]
        zk = lambda c: big[:, 4 + c, ts]
        zv = lambda c: big[:, 8 + c, ts]
        zwa = big[:, 12, ts]
        zg = big[:, 13, ts]
        ACT(th[0:64, :], big[0:64, 12, ts], AF.Tanh, [big], [th])
        ACT(sgx[:], zg, AF.Sigmoid, [big], [sgx])
        ps = PSF()
        psa = PSF()
        for c in range(4):
            MM(ps[:, c * 128:(c + 1) * 128], wupb[0:64, c * 128:(c + 1) * 128], th[0:64, :], True, True, [wupb, th], [ps])
            MM(psa[:, c * 128:(c + 1) * 128], aupb[64:128, c * 128:(c + 1) * 128], big[64:128, 12, ts], True, True, [aupb, big], [psa])
        for c in range(4):
            ACT(lw[:, c, :], ps[:, c * 128:(c + 1) * 128], AF.Exp, [ps, g["nw0"]], [lw], bias=g["nw0"][:, c:c + 1], scale=-1.0)
            ACT(av[:, c, :], psa[:, c * 128:(c + 1) * 128], AF.Sigmoid, [psa, rwp], [av], bias=A0[:, c:c + 1])
        TS("dve", lw[:], lw[:], 1.0, None, ALU.add, None, [lw], [lw])
        P.op("dve", lambda e: e.reciprocal(out=lw[:], in_=lw[:]), reads=[lw], writes=[lw])
        TS("dve", lw[:], lw[:], -EH, None, ALU.mult, None, [lw], [lw])
        psg = PSF()
        MM(psg[:], sgx[:], gupb[:], True, True, [sgx, gupb], [psg])
        CP("act", gt[:], psg[:], [psg], [gt])
        chk('rw1')
        for c in range(4):
            TS("dve", kk[:, c, :], zk(c), KK_[:, c:c + 1], None, ALU.mult, None, [big, rwp], [kk])
        ACT(t1[:], kk[:], AF.Square, [kk], [t1])
        ps = PSF()
        for c in range(4):
            MM(ps[:, c * 128:(c + 1) * 128], blkones, t1[:, c, :], True, True, [cst, t1], [ps])
        ACT(t1[:].rearrange("p a b -> p (a b)"), ps[:], AF.Sqrt, [ps], [t1])
        TS("dve", t1[:], t1[:], 1e-12, None, ALU.max, None, [t1], [t1])
        P.op("dve", lambda e: e.reciprocal(out=t1[:], in_=t1[:]), reads=[t1], writes=[t1])
        TTe("dve", kk[:], kk[:], t1[:], ALU.mult, [kk, t1], [kk])
        for c in range(4):
            TS("dve", t2[:, c, :], av[:, c, :], KA[:, c:c + 1], omka[:, c:c + 1], ALU.mult, ALU.add, [av, rwp, omka], [t2])
            TTe("dve", km[:, c, :], t2[:, c, :], zk(c), ALU.mult, [t2, big], [km])
        for c in range(4):
            STT("dve", prod[:, c, :], zr(c), RK[:, c:c + 1], km[:, c, :], ALU.mult, ALU.mult, [big, rwp, km], [prod])
        psr = PSF()
        for c in range(4):
            MM(psr[:, 0:8], prod[:, c, :], sel[:, c, :], c == 0, c == 3, [prod, sel], [psr])
        CP("act", st[:, 0, :], psr[:, 0:8], [psr], [st])
        chk('rw2')
        src, dst = lw, cum
        CP("pool", cum[:], lw[:], [lw], [cum])
        a_, b_ = cum, cum2
        for s in (1, 2, 4, 8, 16, 32, 64):
            CP("pool", b_[:, :, 0:s], a_[:, :, 0:s], [a_], [b_])
            TTe("dve", b_[:, :, s:128], a_[:, :, s:128], a_[:, :, 0:128 - s], ALU.add, [a_], [b_])
            a_, b_ = b_, a_
        cm = a_
        ot = b_
        ACT(t1[:], cm[:], AF.Exp, [cm], [t1])
        for c in range(4):
            TTe("dve", AR[:, c, 1, :], t1[:, c, :], zr(c), ALU.mult, [t1, big], [AR])
        ACT(wc[:], cm[:, :, 127], AF.Exp, [cm], [wc])
        TTe("dve", ot[:], cm[:], lw[:], ALU.subtract, [cm, lw], [ot])
        ACT(t1[:], ot[:], AF.Exp, [ot], [t1])
        STT("dve", AR[:, :, 0, :], kk[:], -1.0, t1[:], ALU.mult, ALU.mult, [kk, t1], [AR])
        ACT(t1[:], cm[:], AF.Exp, [cm], [t1], scale=-1.0)
        TTe("dve", t2[:], kk[:], av[:], ALU.mult, [kk, av], [t2])
        TTe("dve", BK[:, :, 0, :], t2[:], t1[:], ALU.mult, [t2, t1], [BK])
        TTe("dve", BK[:, :, 1, :], km[:], t1[:], ALU.mult, [km, t1], [BK])
        chk('rw3')
        pb = PSB()
        for c in range(4):
            TR(pb[:, c * 128:(c + 1) * 128], zv(c), identb, [big, cstb], [pb])
        CP("act", tok[:, 0, :], pb[:, 0:512], [pb], [tok])
        for q in range(2):
            pb = PSB()
            for c in range(4):
                TR(pb[:, c * 128:(c + 1) * 128], BK[:, c, q, :], identb, [BK, cstb], [pb])
            CP("act", tok[:, 1 + q, :], pb[:, 0:512], [pb], [tok])
        chk('rw4')
        for hg in range(2):
            p1 = PSF(); p2 = PSF(); p3 = PSF(); p4 = PSF()
            for hq in range(4):
                c, hh = hq, hg
                rows = slice(hh * 64, hh * 64 + 64)
                half = hq % 2
                pa = p1 if hq < 2 else p2
                MM(pa[:, half * 256:half * 256 + 256], BK[rows, c, 0, :], AR[rows, c, :, :].rearrange("p a b -> p (a b)"), True, True, [BK, AR], [pa])
                pk = p3 if hq < 2 else p4
                MM(pk[:, half * 256:half * 256 + 256], BK[rows, c, 1, :], AR[rows, c, :, :].rearrange("p a b -> p (a b)"), True, True, [BK, AR], [pk])
            chk('rw4a')
            for i2, (pa, pk) in enumerate(((p1, p3), (p2, p4))):
                h0 = hg * 4 + i2 * 2
                sa = pa[:].rearrange("p (h z t) -> p h z t", h=2, z=2)
                sk = pk[:].rearrange("p (h z t) -> p h z t", h=2, z=2)
                TTe("dve", N1[:, h0:h0 + 2, :], sa[:, :, 0, :], mrw[:, 0:2, 0, :], ALU.mult, [pa, mrw], [N1])
                TTe("dve", Arb[:, h0:h0 + 2, :], sa[:, :, 1, :], mrw[:, 0:2, 1, :], ALU.mult, [pa, mrw], [Arb])
                TTe("dve", Aak[:, h0:h0 + 2, :], sk[:, :, 0, :], mrw[:, 0:2, 0, :], ALU.mult, [pk, mrw], [Aak])
                TTe("dve", Ark[:, h0:h0 + 2, :], sk[:, :, 1, :], mrw[:, 0:2, 1, :], ALU.mult, [pk, mrw], [Ark])
            chk('rw4b')
            pl = PSF()
            for hq in range(4):
                c, hh = hq, hg
                rows = slice(hh * 64, hh * 64 + 64)
                MM(pl[:, hq * 128:(hq + 1) * 128], AR[rows, c, 0, :], BK[rows, c, 0, :], True, True, [AR, BK], [pl])
            TTe("dve", L1[:, hg * 4:hg * 4 + 4, :], pl[:].rearrange("p (h t) -> p h t", h=4), mlo[:], ALU.mult, [pl, mlo], [L1])
        chk('rw5')
        Noff = rw["Noff"]
        MSET("pool", Noff[:], 0.0, [Noff])
        CP("pool", Noff[0:64, :, 64:128], N1[0:64, :, 64:128], [N1], [Noff])
        MSET("pool", N1[0:64, :, 64:128], 0.0, [N1])
        MSET("pool", L1[64:128, :, 0:64], 0.0, [L1])
        for hg in range(2):
            hs = slice(hg * 4, hg * 4 + 4)
            TTe("pool", Mm[:, hs, :], N1[:, hs, :], cstb[:, 4:8, :], ALU.add, [N1, cstb], [Mm])
        Nk, Lk, Nn, Ln = N1, L1, N2, L2
        Mc, Mn = Mm, Mm2
        for lev in range(5):
            last = lev == 4
            for hg in range(2):
                hs = slice(hg * 4, hg * 4 + 4)
                pn = PSF()
                for hq in range(4):
                    h = hg * 4 + hq
                    MM(pn[:, hq * 128:(hq + 1) * 128], Lk[:, h, :], Nk[:, h, :], True, True, [Lk, Nk], [pn])
                if not last:
                    CP("act", Nn[:, hs, :], pn[:].rearrange("p (h t) -> p h t", h=4), [pn], [Nn])
                pL = PSF()
                for hq in range(4):
                    h = hg * 4 + hq
                    MM(pL[:, hq * 128:(hq + 1) * 128], Nk[:, h, :], Lk[:, h, :], True, True, [Lk, Nk], [pL])
                if not last:
                    CP("act", Ln[:, hs, :], pL[:].rearrange("p (h t) -> p h t", h=4), [pL], [Ln])
                TTe("dve", IL[:, hs, :], pL[:].rearrange("p (h t) -> p h t", h=4), cstb[:, 4:8, :], ALU.add, [pL, cstb], [IL])
                pm = PSF()
                for hq in range(4):
                    h = hg * 4 + hq
                    MM(pm[:, hq * 128:(hq + 1) * 128], IL[:, h, :], Mc[:, h, :], True, True, [IL, Mc], [pm])
                CP("act", Mn[:, hs, :], pm[:].rearrange("p (h t) -> p h t", h=4), [pm], [Mn])
            Nk, Nn = Nn, Nk
            Lk, Ln = Ln, Lk
            Mc, Mn = Mn, Mc
        Mf = Mc
        chk('rw6')
        px = PSF()
        for h in range(8):
            c, hh = h % 4, h // 4
            hn = 2 * c + hh
            rows = slice(hh * 64, hh * 64 + 64)
            o = px[:, hn * 64:(hn + 1) * 64]
            MM(o, AR[rows, c, 0, :], H0b[rows, c, :], True, False, [AR, H0b], [px])
            MM(o, Aak[:, h, :], tok[:, 0, hn * 64:(hn + 1) * 64], False, True, [Aak, tok], [px])
        CP("act", Xs[:], px[:], [px], [Xs])
        Noff = rw["Noff"]
        for rnd in range(2):
            pu = PSF()
            for h in range(8):
                c, hh = h % 4, h // 4
                hn = 2 * c + hh
                MM(pu[:, hn * 64:(hn + 1) * 64], Mf[:, h, :], Xs[:, hn * 64:(hn + 1) * 64], True, True, [Mf, Xs], [pu])
            CP("act", Us[:], pu[:], [pu], [Us])
            if rnd == 0:
                pw = PSF()
                for h in range(8):
                    c, hh = h % 4, h // 4
                    hn = 2 * c + hh
                    MM(pw[:, hn * 64:(hn + 1) * 64], Noff[:, h, :], Us[:, hn * 64:(hn + 1) * 64], True, True, [Noff, Us], [pw])
                TTe("dve", Xs[:], pw[:], Xs[:], ALU.add, [pw, Xs], [Xs])
        py = PSF()
        for h in range(8):
            c, hh = h % 4, h // 4
            hn = 2 * c + hh
            rows = slice(hh * 64, hh * 64 + 64)
            o = py[:, hn * 64:(hn + 1) * 64]
            MM(o, AR[rows, c, 1, :], H0b[rows, c, :], True, False, [AR, H0b], [py])
            MM(o, Arb[:, h, :], Us[:, hn * 64:(hn + 1) * 64], False, False, [Arb, Us], [py])
            MM(o, Ark[:, h, :], tok[:, 0, hn * 64:(hn + 1) * 64], False, True, [Ark, tok], [py])
        CP("act", Y[:], py[:], [py], [Y])
        ph = PSF()
        for c in range(4):
            o = ph[:, c * 128:(c + 1) * 128]
            MM(o, tok[:, 1, c * 128:(c + 1) * 128], Us[:, c * 128:(c + 1) * 128], True, False, [tok, Us], [ph])
            MM(o, tok[:, 2, c * 128:(c + 1) * 128], tok[:, 0, c * 128:(c + 1) * 128], False, True, [tok], [ph])
        for hh in range(2):
            rows = slice(hh * 64, hh * 64 + 64)
            src = ph[rows, :].rearrange("p (c x i) -> p c x i", c=4, x=2)[:, :, hh, :]
            TTe("dve", Ht[rows, :, :], src, H0[rows, :, :], ALU.add, [ph, H0], [Ht])
        for c in range(4):
            TS("dve", H0[:, c, :], Ht[:, c, :], wc[:, c:c + 1], None, ALU.mult, None, [Ht, wc], [H0])
        CP("act", H0b[:], H0[:], [H0], [H0b])
        chk('rw7')
        Y3 = Y[:].rearrange("p (h i) -> p h i", h=8)
        P.op("dve", lambda e: e.tensor_reduce(out=st[:, 1, :], in_=Y3, axis=AX.X, op=ALU.add), reads=[Y], writes=[st])
        ACT(Y2[:], Y[:], AF.Square, [Y], [Y2])
        P.op("dve", lambda e: e.tensor_reduce(out=st[:, 2, :], in_=Y2[:].rearrange("p (h i) -> p h i", h=8), axis=AX.X, op=ALU.add), reads=[Y2], writes=[st])
        TS("dve", st[:, 1, :], st[:, 1, :], 1.0 / 64, None, ALU.mult, None, [st], [st])
        TTe("dve", st[:, 3, :], st[:, 1, :], st[:, 1, :], ALU.mult, [st], [st])
        STT("dve", st[:, 2, :], st[:, 2, :], 1.0 / 64, st[:, 3, :], ALU.mult, ALU.subtract, [st], [st])
        ACT(st[:, 2, :], st[:, 2, :], AF.Sqrt, [st], [st], bias=64e-5)
        P.op("dve", lambda e: e.reciprocal(out=st[:, 2, :], in_=st[:, 2, :]), reads=[st], writes=[st])
        for h in range(8):
            TS("dve", Y2[:, h * 64:(h + 1) * 64], Y[:, h * 64:(h + 1) * 64], st[:, 1, h:h + 1], st[:, 2, h:h + 1], ALU.subtract, ALU.mult, [Y, st], [Y2])
        TTe("dve", Y2[:], Y2[:], gnw[:], ALU.mult, [Y2, gnw], [Y2])
        TTe("dve", Y2[:], Y2[:], gnb[:], ALU.add, [Y2, gnb], [Y2])
        for h in range(8):
            STT("dve", Y2[:, h * 64:(h + 1) * 64], tok[:, 0, h * 64:(h + 1) * 64], st[:, 0, h:h + 1], Y2[:, h * 64:(h + 1) * 64], ALU.mult, ALU.add, [tok, st, Y2], [Y2])
        TTe("dve", yb[:], Y2[:], gt[:], ALU.mult, [Y2, gt], [yb])
        pb = PSB()
        for c in range(4):
            TR(pb[:, c * 128:(c + 1) * 128], yb[:, c * 128:(c + 1) * 128], identb, [yb, cstb], [pb])
        CP("act", YB[1][:, :, ts], pb[:, 0:512].rearrange("p (c t) -> p c t", c=4), [pb], [YB[1]])


def _consts(T):
    bf = ml_dtypes.bfloat16
    cst = np.zeros((128, 5, 128), np.float32)
    cst[:, 0, :] = np.eye(128)
    cst[:, 1, :] = 1.0
    cst[0:64, 2, 0:64] = 1.0
    cst[64:128, 2, 64:128] = 1.0
    cstb = np.zeros((128, 8, 128), np.float32)
    cstb[:, 0, :] = np.eye(128)
    for m in range(128):
        cstb[(m // 64) * 64 + ((m % 64) + 32) % 64, 1, m] = 1.0
    cstb[:, 2, :] = 1.0
    for q_ in range(4):
        cstb[:, 4 + q_, :] = np.eye(128)
    ki = np.arange(128)[:, None]
    qi = np.arange(128)[None, :]
    m12 = np.zeros((128, 2, 256), np.float32)
    for u in range(2):
        m12[:, u, 0:128] = (ki >= qi)
        m12[:, u, 128:256] = (ki <= qi)
    m3 = np.zeros((128, 4, 4, 64), np.float32)
    for j in range(4):
        q = 32 * j + np.arange(32)[None, :]
        for rr in range(4):
            m3[:, j, rr, 0:32] = (ki >= q)
            m3[:, j, rr, 32:64] = (ki <= q)
    mrw = np.zeros((128, 4, 2, 128), np.float32)
    mrw[:, :, 0, :] = (ki < qi)[:, None, :]
    mrw[:, :, 1, :] = (ki <= qi)[:, None, :]
    mlo = np.zeros((128, 4, 128), np.float32)
    mlo[:, :, :] = (qi < ki)[:, None, :]
    sel = np.zeros((128, 4, 8), np.float32)
    for p in range(128):
        for c in range(4):
            sel[p, c, 2 * c + p // 64] = 1.0
    inv = (1.0 / (np.float32(10000.0) ** (np.arange(0, 64, 2, dtype=np.float32) / np.float32(64)))).astype(np.float32)
    ang = (np.arange(T, dtype=np.float32)[:, None] * inv[None, :]).astype(np.float32)
    cosv, sinv = np.cos(ang).astype(np.float32), np.sin(ang).astype(np.float32)
    cosT = np.zeros((128, T), np.float32)
    sinT = np.zeros((128, T), np.float32)
    for p in range(128):
        d = p % 64
        cosT[p] = cosv[:, d % 32]
        sinT[p] = sinv[:, d % 32] * (-1.0 if d < 32 else 1.0)
    return dict(cst_f32=cst, cst_bf=cstb.astype(bf), m12=m12.astype(bf), m3=m3.astype(bf), mrw=mrw.astype(bf),
                mlo=mlo.astype(bf), sel=sel.astype(bf), cosT=cosT, sinT=sinT)


def _col(v, n):
    return np.ascontiguousarray(np.asarray(v, np.float32).reshape(n, 128).T)


def _shared_inputs(inp, T):
    f = lambda a: np.ascontiguousarray(np.asarray(a, np.float32)[:NL]) if np.asarray(a).shape[0] == 2 and np.asarray(a).ndim >= 2 else np.ascontiguousarray(np.asarray(a, np.float32))
    d = dict(_consts(T))
    d["w_ada"] = f(inp["w_ada"])
    d["b_ada"] = np.stack([_col(inp["b_ada"][l], 48) for l in range(NL)])
    d["g_mix"] = np.stack([_col(inp["norm_mix"][l], 8) for l in range(NL)])
    d["g_ffn"] = np.stack([_col(inp["norm_ffn"][l], 8) for l in range(NL)])
    d["g_fin"] = _col(inp["norm_final"], 8)
    d["w_in"] = f(inp["w_in"])
    d["mu"] = np.stack([_col(inp["rwkv_mu"][l], 14) for l in range(NL)])
    d["rwp"] = np.stack([np.stack([_col(np.asarray(inp[k][l]).reshape(-1), 4) for k in
                                   ("rwkv_w0", "rwkv_a0", "rwkv_k_k", "rwkv_k_a", "rwkv_r_k")], axis=1) for l in range(NL)])
    d["r_wup"] = f(inp["rwkv_w_up"])
    d["r_aup"] = f(inp["rwkv_a_up"])
    d["r_gup"] = f(inp["rwkv_g_up"])
    bc = lambda a: np.ascontiguousarray(np.broadcast_to(f(a).reshape(NL, 1, 512), (NL, 128, 512)))
    d["gnw"] = bc(inp["rwkv_gn_w"])
    d["gnb"] = bc(inp["rwkv_gn_b"])
    d["lng"] = bc(inp["gmlp_ln_g"])
    d["lnb"] = bc(inp["gmlp_ln_b"])
    d["wsT"] = np.ascontiguousarray(np.transpose(f(inp["gmlp_w_s"]), (0, 3, 1, 2)))
    bsv = f(inp["gmlp_b_s"])
    bsT = np.zeros((NL, 128, 4, 128), np.float32)
    for g_ in range(8):
        bsT[:, (g_ % 2) * 64:(g_ % 2) * 64 + 64, g_ // 2, :] = bsv[:, g_, None, :]
    d["bsT"] = bsT
    d["w_branch"] = f(inp["w_branch"])
    d["w_out"] = f(inp["w_out"])
    d["ffn_up"] = f(inp["ffn_w_up"])
    d["convw"] = np.stack([np.ascontiguousarray(np.transpose(np.asarray(inp["ffn_conv_w"][l], np.float32).reshape(3, NF, 128), (2, 1, 0))) for l in range(NL)])
    d["convb"] = np.stack([_col(inp["ffn_conv_b"][l], NF) for l in range(NL)])
    d["ffn_down"] = f(inp["ffn_w_down"])
    return d


_NC_CACHE = {}


def run(inp, T=None):
    x = np.asarray(inp["x"], np.float32)
    B, S, _ = x.shape
    T = S
    if T not in _NC_CACHE:
        _NC_CACHE[T] = build(T)
    nc = _NC_CACHE[T]
    shared = _shared_inputs(inp, T)
    c = np.asarray(inp["c"], np.float32)
    in_maps = []
    for core in range(8):
        b = core % B
        m = dict(shared)
        m["xT"] = np.ascontiguousarray(x[b].T)
        m["c128"] = _col(c[b], 8)
        in_maps.append(m)
    res = run_bass_kernel_spmd(nc, in_maps, core_ids=list(range(8)))
    out = np.stack([np.ascontiguousarray(res.results[b]["yT"].T) for b in range(B)])
    return out.astype(np.float32)


def kernel(**inputs):
    return run(inputs)
```

```python
import numpy as np
import ml_dtypes
import concourse.bass as bass
import concourse.mybir as mybir
from concourse.bass_utils import run_bass_kernel_spmd

F32 = mybir.dt.float32
BF16 = mybir.dt.bfloat16
AF = mybir.ActivationFunctionType
ALU = mybir.AluOpType
AX = mybir.AxisListType

D = 1024
import os as _os
NL = int(_os.environ.get("KNL", "2"))
TT = 512
IN_COLS = 7424
DFF = 2816
NF = 22


class Buf:
    __slots__ = ("name", "lw", "rd", "excl")

    def __init__(self, name="", excl=False):
        self.name = name
        self.lw = None
        self.rd = []
        self.excl = excl


class Tl:
    __slots__ = ("t", "b")

    def __init__(self, t, b):
        self.t = t
        self.b = b

    def __getitem__(self, k):
        return self.t[k]


def _b(x):
    return x.b if isinstance(x, Tl) else x


def View(ap, buf):
    return Tl(ap, _b(buf))


class Prog:
    ENGS = ("pe", "dve", "act", "pool", "sp")
    NDMA = 8

    def __init__(self):
        self.nc = bass.Bass("TRN2", target_bir_lowering=False)
        self.ops = {e: [] for e in self.ENGS}
        self.cnt = {e: 0 for e in self.ENGS}
        self.seen = {e: {} for e in self.ENGS}
        self.sems = {}
        self._stack = []
        nc = self.nc
        for e in ("pe", "dve", "act", "pool"):
            self.sems[e] = self._enter(nc.semaphore("s_" + e))
        self.dslots = {}
        for q in ("sp", "act", "pool"):
            sl = []
            for i in range(self.NDMA):
                key = "d_%s%d" % (q, i)
                self.sems[key] = self._enter(nc.semaphore(key))
                sl.append([key, 0])
            self.dslots[q] = [sl, 0]
        self.n_sb = 0
        self.reg = {}

    def _enter(self, cm):
        v = cm.__enter__()
        self._stack.append(cm)
        return v

    def sb(self, shape, dtype, name=None):
        self.n_sb += 1
        t = self._enter(self.nc.sbuf_tensor("s_%s_%d" % (name or "t", self.n_sb), list(shape), dtype))
        return Tl(t, Buf(name or ""))

    def ps(self, shape, dtype=F32, name=None):
        self.n_sb += 1
        t = self._enter(self.nc.psum_tensor(name or ("ps%d" % self.n_sb), list(shape), dtype))
        return Tl(t, Buf(name or "", excl=True))

    def _deps(self, eng, reads, writes):
        toks = []
        for b in reads:
            b = _b(b)
            if b.lw is not None:
                toks.append(b.lw)
        for b in writes:
            b = _b(b)
            if b.lw is not None:
                toks.append(b.lw)
            toks.extend(b.rd)
        need = {}
        for (k, v) in toks:
            if k == "pe" and eng == "pe":
                continue
            if self.seen[eng].get(k, 0) >= v:
                continue
            if need.get(k, 0) < v:
                need[k] = v
        for k, v in need.items():
            self.seen[eng][k] = v
        return list(need.items())

    def _commit(self, tok, reads, writes):
        for b in reads:
            b = _b(b)
            b.rd.append(tok)
            if len(b.rd) > 64:
                mx = {}
                for (k, v) in b.rd:
                    if mx.get(k, 0) < v:
                        mx[k] = v
                b.rd = list(mx.items())
        for b in writes:
            b = _b(b)
            b.lw = tok
            b.rd = []

    @staticmethod
    def _onchip(ap):
        n = ap.name
        return n.startswith("s_") or n.startswith("ps")

    def _reg(self, ap):
        pstep, pcnt = ap.ap[0]
        off = ap.offset
        if ap.name.startswith("ps"):
            return (ap.name, 0, 128, 0, 1 << 30, True)
        p0 = off // pstep
        f0 = off % pstep
        span = 0
        for st, cnt in ap.ap[1:]:
            span += (cnt - 1) * abs(st)
        esz = 2 if ap.dtype == BF16 else 4
        return (ap.name, p0, p0 + pcnt, f0 * esz, (f0 + span + 1) * esz, False)

    def _rdeps(self, eng, ars, aws):
        toks = []
        new = []
        for (aps, isw) in ((ars, False), (aws, True)):
            for ap in aps:
                name, p0, p1, b0, b1, isps = self._reg(ap)
                w = isw or (isps and eng != "pe")
                recs = self.reg.setdefault(name, [])
                for r in recs:
                    if r[0] < p1 and p0 < r[1] and r[2] < b1 and b0 < r[3] and (w or r[5]):
                        toks.append(r[4])
                new.append((name, [p0, p1, b0, b1, None, w]))
        return toks, new

    def _rcommit(self, tok, new):
        for name, rec in new:
            rec[4] = tok
            recs = self.reg[name]
            if rec[5]:
                recs[:] = [r for r in recs if not (rec[0] <= r[0] and r[1] <= rec[1] and rec[2] <= r[2] and r[3] <= rec[3])]
            else:
                recs[:] = [r for r in recs if not ((not r[5]) and r[4][0] == tok[0] and rec[0] <= r[0] and r[1] <= rec[1]
                                                   and rec[2] <= r[2] and r[3] <= rec[3])]
            recs.append(rec)

    def _filter(self, eng, toks):
        need = {}
        for (k, v) in toks:
            if k == "pe" and eng == "pe":
                continue
            if self.seen[eng].get(k, 0) >= v:
                continue
            if need.get(k, 0) < v:
                need[k] = v
        for k, v in need.items():
            self.seen[eng][k] = v
        return list(need.items())

    def _buftoks(self, reads, writes):
        toks = []
        for b in reads:
            if b.lw is not None:
                toks.append(b.lw)
        for b in writes:
            if b.lw is not None:
                toks.append(b.lw)
            toks.extend(b.rd)
        return toks

    def op(self, eng, fn, reads=(), writes=(), ar=None, aw=None):
        if eng == "pool":
            eng = "dve"
        assert ar is not None and aw is not None
        reads = [b for b in reads if isinstance(b, Buf)]
        writes = [b for b in writes if isinstance(b, Buf)]
        rt, new = self._rdeps(eng, ar, aw)
        waits = self._filter(eng, self._buftoks(reads, writes) + rt)
        self.cnt[eng] += 1
        tok = (eng, self.cnt[eng])
        self.ops[eng].append((waits, fn, (eng, 1)))
        self._commit(tok, reads, writes)
        self._rcommit(tok, new)
        return tok

    def dma(self, q, out, in_, reads=(), writes=()):
        sl, idx = self.dslots[q]
        slot = sl[idx % self.NDMA]
        self.dslots[q][1] = idx + 1
        key = slot[0]
        reads = [b for b in reads if isinstance(b, Buf)]
        writes = [b for b in writes if isinstance(b, Buf)]
        ar = [in_] if self._onchip(in_) else []
        aw = [out] if self._onchip(out) else []
        rt, new = self._rdeps(q, ar, aw)
        waits = self._filter(q, self._buftoks(reads, writes) + rt)
        if slot[1] > 0 and self.seen[q].get(key, 0) < slot[1]:
            waits.append((key, slot[1]))
            self.seen[q][key] = slot[1]
        slot[1] += 16
        tok = (key, slot[1])

        def fn(e, out=out, in_=in_):
            return e.dma_start(out=out, in_=in_)
        self.ops[q].append((waits, fn, (key, 16)))
        self._commit(tok, reads, writes)
        self._rcommit(tok, new)
        return tok

    def fence(self, src, dst):
        return
        toks = []
        for s in src:
            s = _b(s)
            if s.lw is not None:
                toks.append(s.lw)
            toks.extend(s.rd)
        for d in dst:
            _b(d).rd.extend(toks)

    def finish(self, final_bufs):
        waits = self._filter("sp", self._buftoks([_b(b) for b in final_bufs], []))
        self.ops["sp"].append((waits, None, None))
        nc = self.nc
        sems = self.sems
        ops = self.ops
        with nc.Block() as block:
            def mk(e):
                def body(eng):
                    for (waits, fn, inc) in ops[e]:
                        for (k, v) in waits:
                            eng.wait_ge(sems[k], v)
                        if fn is not None:
                            ins = fn(eng)
                            ins.then_inc(sems[inc[0]], inc[1])
                return body
            block.tensor(mk("pe"))
            block.vector(mk("dve"))
            block.scalar(mk("act"))
            block.gpsimd(mk("pool"))
            block.sync(mk("sp"))
        for cm in reversed(self._stack):
            cm.__exit__(None, None, None)
        self._stack = []
        return nc


class _Stop(Exception):
    pass


def build(T, dbg=False):
    import os
    STOP = os.environ.get("KSTOP")
    P = Prog()

    def chk(name):
        if STOP == name:
            raise _Stop()
    nc = P.nc
    NTI = T // TT
    NB = T // 128

    def din(name, shape, dt=F32):
        return nc.dram_tensor(name, list(shape), dt, kind="ExternalInput").ap()

    xT_in = din("xT", [D, T])
    c_in = din("c128", [128, 8])
    w_ada = din("w_ada", [NL, D, 6 * D])
    b_ada = din("b_ada", [NL, 128, 48])
    g_mix = din("g_mix", [NL, 128, 8])
    g_ffn = din("g_ffn", [NL, 128, 8])
    g_fin = din("g_fin", [128, 8])
    w_in = din("w_in", [NL, D, IN_COLS])
    mu_d = din("mu", [NL, 128, 14])
    pr_d = din("rwp", [NL, 128, 5, 4])
    wup_d = din("r_wup", [NL, 64, 512])
    aup_d = din("r_aup", [NL, 64, 512])
    gup_d = din("r_gup", [NL, 128, 512])
    gnw_d = din("gnw", [NL, 128, 512])
    gnb_d = din("gnb", [NL, 128, 512])
    lng_d = din("lng", [NL, 128, 512])
    lnb_d = din("lnb", [NL, 128, 512])
    wsT_d = din("wsT", [NL, 128, 8, 128])
    bs_d = din("bsT", [NL, 128, 4, 128])
    wbr_d = din("w_branch", [NL, 3, 512, D])
    wout_d = din("w_out", [NL, D, D])
    wfu_d = din("ffn_up", [NL, D, 2 * DFF])
    cw_d = din("convw", [NL, 128, NF, 3])
    cb_d = din("convb", [NL, 128, NF])
    wfd_d = din("ffn_down", [NL, DFF, D])
    cos_d = din("cosT", [128, T])
    sin_d = din("sinT", [128, T])
    cst_d = din("cst_f32", [128, 5, 128])
    cstb_d = din("cst_bf", [128, 8, 128], BF16)
    m12_d = din("m12", [128, 2, 256], BF16)
    m3_d = din("m3", [128, 4, 4, 64], BF16)
    mrw_d = din("mrw", [128, 4, 2, 128], BF16)
    mlo_d = din("mlo", [128, 4, 128], BF16)
    sel_d = din("sel", [128, 4, 8], BF16)
    yT_out = nc.dram_tensor("yT", [D, T], F32, kind="ExternalOutput").ap()
    x1T = nc.dram_tensor("x1T", [D, T], F32, kind="Internal").ap()
    Vd = [nc.dram_tensor("Vd%d" % l, [T, 512], BF16, kind="Internal").ap() for l in range(NL)]
    b_x1 = Buf("x1T")
    b_vd = [Buf("vd0"), Buf("vd1")]
    b_out = Buf("out")

    def MM(out, lhsT, rhs, start, stop, r, w):
        P.op("pe", lambda e: e.matmul(out, lhsT, rhs, start=start, stop=stop), reads=r, writes=w, ar=[lhsT, rhs], aw=[out])

    def TR(out, in_, ident, r, w):
        P.op("pe", lambda e: e.transpose(out, in_, ident), reads=r, writes=w, ar=[in_, ident], aw=[out])

    def ACT(out, in_, func, r, w, bias=0.0, scale=1.0):
        P.op("act", lambda e: e.activation(out=out, in_=in_, func=func, bias=bias, scale=scale), reads=r, writes=w,
             ar=[in_] + [x for x in (bias, scale) if hasattr(x, "offset")], aw=[out])

    def TTe(eng, out, in0, in1, op, r, w):
        P.op(eng, lambda e: e.tensor_tensor(out=out, in0=in0, in1=in1, op=op), reads=r, writes=w, ar=[in0, in1], aw=[out])

    def TS(eng, out, in0, s1, s2, op0, op1, r, w):
        if s2 is None:
            P.op(eng, lambda e: e.tensor_scalar(out=out, in0=in0, scalar1=s1, scalar2=None, op0=op0), reads=r, writes=w,
                 ar=[in0] + [x for x in (s1,) if hasattr(x, "offset")], aw=[out])
        else:
            P.op(eng, lambda e: e.tensor_scalar(out=out, in0=in0, scalar1=s1, scalar2=s2, op0=op0, op1=op1), reads=r, writes=w,
                 ar=[in0] + [x for x in (s1, s2) if hasattr(x, "offset")], aw=[out])

    def STT(eng, out, in0, sc, in1, op0, op1, r, w):
        P.op(eng, lambda e: e.scalar_tensor_tensor(out=out, in0=in0, scalar=sc, in1=in1, op0=op0, op1=op1), reads=r, writes=w,
             ar=[in0, in1] + [x for x in (sc,) if hasattr(x, "offset")], aw=[out])

    def CP(eng, out, in_, r, w):
        if eng == "act":
            P.op("act", lambda e: e.copy(out=out, in_=in_), reads=r, writes=w, ar=[in_], aw=[out])
        else:
            P.op(eng, lambda e: e.tensor_copy(out=out, in_=in_), reads=r, writes=w, ar=[in_], aw=[out])

    def MSET(eng, ap, val, w):
        P.op(eng, lambda e: e.memset(ap, val), writes=w, ar=[], aw=[ap])

    psf = [P.ps([128, 512], F32) for _ in range(6)]
    psb = [P.ps([128, 1024], BF16) for _ in range(2)]
    psi = [0, 0]

    def PSF():
        psi[0] += 1
        return psf[psi[0] % 6]

    def PSB():
        psi[1] += 1
        return psb[psi[1] % 2]

    cst = P.sb([128, 5, 128], F32)
    cstb = P.sb([128, 8, 128], BF16)
    m12 = P.sb([128, 2, 256], BF16)
    m3 = P.sb([128, 4, 4, 64], BF16)
    mrw = P.sb([128, 4, 2, 128], BF16)
    mlo = P.sb([128, 4, 128], BF16)
    sel = P.sb([128, 4, 8], BF16)
    for (t_, d_) in ((cst, cst_d), (cstb, cstb_d), (m12, m12_d), (m3, m3_d), (mrw, mrw_d), (mlo, mlo_d), (sel, sel_d)):
        P.dma("sp", t_[:], d_, writes=[t_])
    identf = cst[:, 0, :]
    onesf = cst[:, 1, :]
    blkones = cst[:, 2, :]
    identb = cstb[:, 0, :]
    permb = cstb[:, 1, :]
    onesb = cstb[:, 2, :]

    xt = P.sb([128, 8, TT], F32, "xt")
    scr = P.sb([128, 8, TT], F32, "scr")
    hT = P.sb([128, 8, TT], BF16, "hT")
    rstd = P.sb([128, TT], F32, "rstd")
    NWB = 2
    wb = [P.sb([128, 8, 512], BF16, "wb%d" % i) for i in range(NWB)]
    wbi = [0]
    QT = P.sb([128, 4, TT], BF16, "QT")
    KT = P.sb([128, 4, T], BF16, "KT")
    V1r = P.sb([128, 8, 512], BF16, "V1r")
    V2 = P.sb([128, 2, 4, 512], BF16, "V2")
    V3g = [P.sb([128, 2, 4, 512], BF16, "V3g0")] * 2
    acc = P.sb([128, 4, 2, TT], F32, "acc")
    big = P.sb([128, NF, TT], BF16, "big")
    YB = [P.sb([128, 4, TT], BF16, "YB%d" % i) for i in range(3)]
    cosb = P.sb([128, TT], F32, "cos")
    sinb = P.sb([128, TT], F32, "sin")
    zcar = P.sb([128, 14], F32, "zcar")
    H0 = P.sb([128, 4, 64], F32, "H0")
    H0b = P.sb([128, 4, 64], BF16, "H0b")
    ccar = P.sb([128, NF, 2], F32, "ccar")
    MSET("pool", KT[:], 0.0, [KT])

    modp = P.sb([128, NL, 48], F32, "mod")
    A1 = P.sb([128, NL, 8], F32, "A1")
    A2 = P.sb([128, NL, 8], F32, "A2")
    gfin = P.sb([128, 8], F32, "gfin")
    cact = P.sb([128, 8], F32, "cact")
    wtmp = scr
    bad = P.sb([128, NL, 48], F32, "bad")
    gm = P.sb([128, NL, 8], F32, "gm")
    gf = P.sb([128, NL, 8], F32, "gf")
    P.dma("sp", cact[:], c_in, writes=[cact])
    P.dma("sp", gfin[:], g_fin, writes=[gfin])
    for l in range(NL):
        P.dma("sp", bad[:, l, :], b_ada[l], writes=[bad])
        P.dma("sp", gm[:, l, :], g_mix[l], writes=[gm])
        P.dma("sp", gf[:, l, :], g_ffn[l], writes=[gf])
    ACT(cact[:], cact[:], AF.Silu, [cact], [cact])
    for l in range(NL):
        for pc in range(12):
            P.dma("sp", wtmp[:], w_ada[l][:, pc * 512:(pc + 1) * 512].rearrange("(k p) n -> p k n", p=128), writes=[wtmp])
            ps = PSF()
            for mc in range(4):
                for k in range(8):
                    MM(ps[:, mc:mc + 1], wtmp[:, k, mc * 128:(mc + 1) * 128], cact[:, k:k + 1], k == 0, k == 7, [wtmp, cact], [ps])
            TTe("dve", modp[:, l, pc * 4:(pc + 1) * 4], ps[:, 0:4], bad[:, l, pc * 4:(pc + 1) * 4], ALU.add, [ps, bad], [modp])
        STT("dve", A1[:, l, :], modp[:, l, 8:16], 1.0, gm[:, l, :], ALU.add, ALU.mult, [modp, gm], [A1])
        STT("dve", A2[:, l, :], modp[:, l, 32:40], 1.0, gf[:, l, :], ALU.add, ALU.mult, [modp, gf], [A2])

    mu = P.sb([128, 14], F32, "mu")
    omu = P.sb([128, 14], F32, "omu")
    rwp = P.sb([128, 5, 4], F32, "rwp")
    omka = P.sb([128, 4], F32, "omka")
    nw0 = P.sb([128, 4], F32, "nw0")
    wupb = P.sb([128, 512], BF16, "wupb")
    aupb = P.sb([128, 512], BF16, "aupb")
    gupb = P.sb([128, 512], BF16, "gupb")
    gnw = P.sb([128, 512], F32, "gnw")
    gnb = P.sb([128, 512], F32, "gnb")
    lng = P.sb([128, 512], F32, "lng")
    lnb = P.sb([128, 512], F32, "lnb")
    wsT = P.sb([128, 8, 128], BF16, "wsT")
    wsTf = View(scr[:, 0:2, :].rearrange("p a (g i) -> p (a g) i", g=4), scr)
    bsT = P.sb([128, 4, 128], F32, "bsT")
    cw = P.sb([128, NF, 3], F32, "cw")
    cb = P.sb([128, NF], F32, "cb")

    def load_w(src_ap, kc, ncols):
        wbi[0] += 1
        w = wb[wbi[0] % NWB]
        P.dma("pool", w[:, 0:kc, 0:ncols], src_ap.rearrange("(k p) n -> p k n", p=128), writes=[w])
        return w

    def rmsnorm_mod(Acol, shcol, l):
        ACT(scr[:], xt[:], AF.Square, [xt], [scr])
        ps = PSF()
        for k in range(8):
            MM(ps[:], onesf, scr[:, k, :], k == 0, k == 7, [cst, scr], [ps])
        ACT(rstd[:], ps[:], AF.Sqrt, [ps], [rstd], bias=1e-6, scale=1.0 / D)
        P.op("dve", lambda e: e.reciprocal(out=rstd[:], in_=rstd[:]), ar=[rstd[:]], aw=[rstd[:]])
        for k in range(8):
            STT("dve", scr[:, k, :], xt[:, k, :], Acol[:, k:k + 1], rstd[:], ALU.mult, ALU.mult, [xt, rstd, A1, A2, gfin], [scr])
            if shcol is not None:
                TS("pool", hT[:, k, :], scr[:, k, :], shcol[:, k:k + 1], None, ALU.add, None, [scr, modp], [hT])

    GC = 0.7978845608028654

    def gelu_from(out, src, rd, wr, tmpa, tmpb):
        CP("act", tmpa, src, rd, wr)
        TTe("dve", tmpb, tmpa, tmpa, ALU.mult, wr, wr)
        TS("dve", tmpb, tmpb, 0.044715, 1.0, ALU.mult, ALU.add, wr, wr)
        TTe("dve", tmpb, tmpb, tmpa, ALU.mult, wr, wr)
        ACT(tmpb, tmpb, AF.Sigmoid, wr, wr, scale=2.0 * GC)
        TTe("dve", out, tmpb, tmpa, ALU.mult, wr, wr)

    try:
        for l in range(NL):
            x_src = xT_in if l == 0 else x1T
            P.dma("sp", mu[:], mu_d[l], writes=[mu])
            TS("dve", omu[:], mu[:], -1.0, 1.0, ALU.mult, ALU.add, [mu], [omu])
            P.dma("sp", rwp[:], pr_d[l], writes=[rwp])
            TS("dve", omka[:], rwp[:, 3, :], -1.0, 1.0, ALU.mult, ALU.add, [rwp], [omka])
            TS("dve", nw0[:], rwp[:, 0, :], -1.0, None, ALU.mult, None, [rwp], [nw0])
            P.dma("pool", wupb[0:64, :], wup_d[l], writes=[wupb])
            P.dma("pool", aupb[64:128, :], aup_d[l], writes=[aupb])
            P.dma("pool", gupb[:], gup_d[l], writes=[gupb])
            for (t_, d_) in ((gnw, gnw_d), (gnb, gnb_d), (lng, lng_d), (lnb, lnb_d)):
                P.dma("sp", t_[:], d_[l], writes=[t_])
            P.dma("sp", wsTf[:], wsT_d[l], writes=[wsTf])
            for g in range(8):
                TTe("dve", wsT[:, g, :], wsTf[:, g, :], mrw[:, 0, 1, :], ALU.mult, [wsTf, mrw], [wsT])
            P.dma("sp", bsT[:], bs_d[l], writes=[bsT])
            P.dma("sp", cw[:], cw_d[l], writes=[cw])
            P.dma("sp", cb[:], cb_d[l], writes=[cb])
            MSET("pool", zcar[:], 0.0, [zcar])
            MSET("pool", H0[:], 0.0, [H0])
            MSET("pool", H0b[:], 0.0, [H0b])
            MSET("pool", ccar[:], 0.0, [ccar])
            MSET("pool", big[:, 0:4, :], 0.0, [big])
            for q in range(T // 512):
                P.dma("sp", Vd[l][q * 512:(q + 1) * 512, :].rearrange("(p a) c -> p (a c)", p=128), big[:, 0:4, :].rearrange("p a b -> p (a b)"), reads=[big], writes=[b_vd[l]])

            for it in range(NTI):
                t0 = it * TT
                P.dma("sp", xt[:], x_src[:, t0:t0 + TT].rearrange("(k p) t -> p k t", p=128), reads=[b_x1] if l else [], writes=[xt])
                P.dma("sp", cosb[:], cos_d[:, t0:t0 + TT], writes=[cosb])
                P.dma("sp", sinb[:], sin_d[:, t0:t0 + TT], writes=[sinb])
                chk('pro')
                rmsnorm_mod(A1[:, l, :], modp[:, l, 0:8], l)
                chk('norm')

                for pc in range(2):
                    w = load_w(w_in[l][:, pc * 512:(pc + 1) * 512], 8, 512)
                    for mc in range(4):
                        ps = PSF()
                        for k in range(8):
                            MM(ps[:], w[:, k, mc * 128:(mc + 1) * 128], hT[:, k, :], k == 0, k == 7, [w, hT], [ps])
                        chk('qk1')
                        qs = scr[:, mc, :]
                        qbf = big[:, 14 + mc, :]
                        CP("act", qbf, ps[:], [ps], [big])
                        ps2 = PSF()
                        MM(ps2[:], permb, qbf, True, True, [cstb, big], [ps2])
                        chk('qk2')
                        TTe("dve", qs, ps[:], cosb[:], ALU.mult, [ps, cosb, big, ps2], [scr])
                        TTe("dve", scr[:, 4 + mc, :], ps2[:], sinb[:], ALU.mult, [ps2, sinb], [scr])
                        chk('qk3')
                        dst = QT[:, mc, :] if pc == 0 else KT[:, mc, t0:t0 + TT]
                        TTe("pool", dst, qs, scr[:, 4 + mc, :], ALU.add, [scr], [QT if pc == 0 else KT])
                chk('qk')
                w = load_w(w_in[l][:, 1024:1536], 8, 512)
                for blk in range(4):
                    ps = PSF()
                    for k in range(8):
                        MM(ps[:], hT[:, k, blk * 128:(blk + 1) * 128], w[:, k, :], k == 0, k == 7, [w, hT], [ps])
                    gb = 4 * it + blk
                    CP("act", V1r[:, gb % 8, :], ps[:], [ps], [V1r])
                    P.dma("sp", Vd[l][t0 + blk * 128:t0 + (blk + 1) * 128, :], V1r[:, gb % 8, :], reads=[V1r], writes=[b_vd[l]])
                P.dma("sp", V2[:, it % 2, :, :], Vd[l][t0:t0 + TT, :].rearrange("(i r) c -> i r c", r=4), reads=[b_vd[l]], writes=[V2])

                chk('v')
                if "EP" not in P.__dict__:
                    P.EP = [(View(big[:, 16 + 2 * i_, :], Buf("E%d" % i_)), View(big[:, 17 + 2 * i_, :], Buf("P%d" % i_))) for i_ in range(3)]
                    P.epi = 0
                EPb = [x for pr in P.EP for x in pr]
                P.fence([big], EPb)
                for cfg in range(3):
                    n_sp = t0 // min(2048, T)
                    j3 = (t0 % min(2048, T)) // TT
                    if cfg < 2:
                        items = [(None, c, hh) for c in range(4) for hh in range(2)]
                    else:
                        items = [(rg_, c, hh) for rg_ in range(4) for c in range(4) for hh in range(2)]
                    for (rg_, c, hh) in items:
                        if True:
                            rows = slice(hh * 64, hh * 64 + 64)
                            if cfg < 2:
                                for half in range(2):
                                    psS = PSF()
                                    hp = []
                                    for u in range(2):
                                        qi = half * 2 + u
                                        if cfg == 0:
                                            gb = 4 * it + qi
                                            q_ap = QT[rows, c, qi * 128:(qi + 1) * 128]
                                            kc_ap = KT[rows, c, gb * 128:(gb + 1) * 128]
                                            kp_ap = KT[rows, c, (gb - 1) * 128:gb * 128] if gb > 0 else None
                                            vc_ap = V1r[:, gb % 8, c * 128:(c + 1) * 128]
                                            vp_ap = V1r[:, (gb - 1) % 8, c * 128:(c + 1) * 128]
                                            vb = V1r
                                        else:
                                            q_ap = QT[rows, c, qi:TT:4]
                                            kc_ap = KT[rows, c, t0 + qi:t0 + TT:4]
                                            kp_ap = KT[rows, c, t0 - TT + qi:t0:4] if it > 0 else None
                                            vc_ap = V2[:, it % 2, qi, c * 128:(c + 1) * 128]
                                            vp_ap = V2[:, (it - 1) % 2, qi, c * 128:(c + 1) * 128]
                                            vb = V2
                                        hp.append(kp_ap is not None)
                                        if kp_ap is not None:
                                            MM(psS[:, u * 256:u * 256 + 128], kp_ap, q_ap, True, True, [KT, QT], [psS])
                                        MM(psS[:, u * 256 + 128:u * 256 + 256], kc_ap, q_ap, True, True, [KT, QT], [psS])
                                        hp.append((vp_ap, vc_ap, vb))
                                    P.epi += 1
                                    Et, Pt = P.EP[P.epi % 3]
                                    for u in range(2):
                                        if hp[2 * u]:
                                            ACT(Et[:, u * 256:(u + 1) * 256], psS[:, u * 256:(u + 1) * 256], AF.Exp, [psS], [Et], scale=0.125)
                                        else:
                                            MSET("pool", Et[:, u * 256:u * 256 + 128], 0.0, [Et])
                                            ACT(Et[:, u * 256 + 128:u * 256 + 256], psS[:, u * 256 + 128:u * 256 + 256], AF.Exp, [psS], [Et], scale=0.125)
                                    Pm = Pt[:]
                                    TTe("dve", Pm, Et[:], m12[:].rearrange("p a b -> p (a b)"), ALU.mult, [Et, m12], [Pt])
                                    psN = PSF()
                                    for u in range(2):
                                        vp_ap, vc_ap, vb = hp[2 * u + 1]
                                        has_prev = hp[2 * u]
                                        for z in range(2):
                                            o = psN[:, z * 256 + u * 128:z * 256 + u * 128 + 128]
                                            if has_prev:
                                                MM(o, vp_ap if z == 0 else onesb, Pm[:, u * 256:u * 256 + 128], True, False, [vb, Pt, cstb], [psN])
                                            MM(o, vc_ap if z == 0 else onesb, Pm[:, u * 256 + 128:u * 256 + 256], not has_prev, True, [vb, Pt, cstb], [psN])
                                    src = psN[rows, :].rearrange("p (z u q) -> p z u q", z=2, u=2)
                                    if cfg == 0:
                                        dsta = acc[rows, c, :, half * 256:(half + 1) * 256].rearrange("p z (u q) -> p z u q", u=2)
                                        CP("act", dsta, src, [psN], [acc])
                                    else:
                                        dsta = acc[rows, c, :, :].rearrange("p z (q r) -> p z r q", r=4)[:, :, half * 2:half * 2 + 2, :]
                                        TTe("dve", dsta, src, dsta, ALU.add, [psN, acc], [acc])
                            else:
                                SPAN = min(2048, T)
                                nk3 = SPAN // 16
                                kp_ = slice(0, nk3)
                                for rg in (rg_,):
                                    if c == 0 and hh == 0:
                                        v3 = V3g[rg % 2]
                                        for sp_ in range(2):
                                            nn = n_sp - 1 + sp_
                                            if nn < 0:
                                                continue
                                            P.dma("sp", v3[kp_, sp_, :, :],
                                                  Vd[l][nn * SPAN:(nn + 1) * SPAN, :].rearrange("(i r) c -> i r c", r=16)[:, rg * 4:rg * 4 + 4, :],
                                                  reads=[b_vd[l]], writes=[v3])
                                    v3 = V3g[rg % 2]
                                    psS = PSF()
                                    hasp = n_sp > 0
                                    for rr in range(4):
                                        r = rg * 4 + rr
                                        q_ap = QT[rows, c, r:TT:16]
                                        kc_ap = KT[rows, c, n_sp * SPAN + r:(n_sp + 1) * SPAN:16]
                                        if hasp:
                                            kp_ap = KT[rows, c, (n_sp - 1) * SPAN + r:n_sp * SPAN:16]
                                            MM(psS[kp_, rr * 64:rr * 64 + 32], kp_ap, q_ap, True, True, [KT, QT], [psS])
                                        MM(psS[kp_, rr * 64 + 32:rr * 64 + 64], kc_ap, q_ap, True, True, [KT, QT], [psS])
                                    P.epi += 1
                                    Et, Pt = P.EP[P.epi % 3]
                                    if not hasp:
                                        MSET("pool", Et[kp_, 0:256], 0.0, [Et])
                                        for rr in range(4):
                                            ACT(Et[kp_, rr * 64 + 32:rr * 64 + 64], psS[kp_, rr * 64 + 32:rr * 64 + 64], AF.Exp, [psS], [Et], scale=0.125)
                                    else:
                                        ACT(Et[kp_, 0:256], psS[kp_, 0:256], AF.Exp, [psS], [Et], scale=0.125)
                                    Pm = Pt[kp_, 0:256]
                                    TTe("dve", Pm, Et[kp_, 0:256], m3[kp_, j3, :, :].rearrange("p a b -> p (a b)"), ALU.mult, [Et, m3], [Pt])
                                    psN = PSF()
                                    for rr in range(4):
                                        for z in range(2):
                                            o = psN[:, z * 128 + rr * 32:z * 128 + rr * 32 + 32]
                                            if hasp:
                                                MM(o, v3[kp_, 0, rr, c * 128:(c + 1) * 128] if z == 0 else onesb[kp_, :], Pm[:, rr * 64:rr * 64 + 32], True, False, [v3, Pt, cstb], [psN])
                                            MM(o, v3[kp_, 1, rr, c * 128:(c + 1) * 128] if z == 0 else onesb[kp_, :], Pm[:, rr * 64 + 32:rr * 64 + 64], not hasp, True, [v3, Pt, cstb], [psN])
                                    src = psN[rows, 0:256].rearrange("p (z r q) -> p z r q", z=2, r=4)
                                    dsta = acc[rows, c, :, :].rearrange("p z (q r) -> p z r q", r=16)[:, :, rg * 4:rg * 4 + 4, :]
                                    TTe("dve", dsta, src, dsta, ALU.add, [psN, acc], [acc])
                P.fence(EPb, [big])
                for c in range(4):
                    P.op("dve", lambda e, c=c: e.reciprocal(out=acc[:, c, 1, :], in_=acc[:, c, 1, :]), ar=[acc[:, c, 1, :]], aw=[acc[:, c, 1, :]])
                    TTe("dve", YB[0][:, c, :], acc[:, c, 0, :], acc[:, c, 1, :], ALU.mult, [acc], [YB[0]])
                chk('att')

                for pc in range(4):
                    ncol = 512 if pc < 3 else 256
                    w = load_w(w_in[l][:, 1536 + pc * 512:1536 + pc * 512 + ncol], 8, ncol)
                    for mc in range(ncol // 128):
                        ch = pc * 4 + mc
                        ps = PSF()
                        for k in range(8):
                            MM(ps[:], w[:, k, mc * 128:(mc + 1) * 128], hT[:, k, :], k == 0, k == 7, [w, hT], [ps])
                        zb = scr[:, ch % 4, :]
                        zp = scr[:, 4 + ch % 4, :]
                        CP("act", zb, ps[:], [ps], [scr])
                        CP("pool", zp[:, 1:TT], zb[:, 0:TT - 1], [scr], [scr])
                        CP("pool", zp[:, 0:1], zcar[:, ch:ch + 1], [zcar, scr], [scr])
                        CP("pool", zcar[:, ch:ch + 1], zb[:, TT - 1:TT], [scr], [zcar])
                        TS("dve", zb, zb, omu[:, ch:ch + 1], None, ALU.mult, None, [scr, omu], [scr])
                        STT("dve", big[:, ch, :], zp, mu[:, ch:ch + 1], zb, ALU.mult, ALU.add, [scr, mu], [big])
                chk('rwin'); rwkv_tile(P, locals()); chk('rw')

                gmlp_tile(P, locals()); chk('gmlp')

                wbrs = []
                for br in range(3):
                    for half in range(2):
                        wg = load_w(w_in[l][:, 4352 + br * 1024 + half * 512:4352 + br * 1024 + (half + 1) * 512], 8, 512)
                        wbt = load_w(wbr_d[l, br][:, half * 512:(half + 1) * 512], 4, 512)
                        for mc in range(4):
                            m = half * 4 + mc
                            psg = PSF()
                            for k in range(8):
                                MM(psg[:], wg[:, k, mc * 128:(mc + 1) * 128], hT[:, k, :], k == 0, k == 7, [wg, hT], [psg])
                            psbr = PSF()
                            for k in range(4):
                                MM(psbr[:], wbt[:, k, mc * 128:(mc + 1) * 128], YB[br][:, k, :], k == 0, k == 3, [wbt, YB[br]], [psbr])
                            sg = acc[:, 0, 0, :]
                            ACT(sg, psg[:], AF.Sigmoid, [psg], [acc])
                            if br == 0:
                                TTe("dve", scr[:, m, :], sg, psbr[:], ALU.mult, [acc, psbr], [scr])
                            else:
                                TTe("dve", sg, sg, psbr[:], ALU.mult, [acc, psbr], [acc])
                                TTe("pool", scr[:, m, :], scr[:, m, :], sg, ALU.add, [acc, scr], [scr])
                chk('br')
                for m in range(8):
                    CP("act", hT[:, m, :], scr[:, m, :], [scr], [hT])
                for half in range(2):
                    w = load_w(wout_d[l][:, half * 512:(half + 1) * 512], 8, 512)
                    for mc in range(4):
                        m = half * 4 + mc
                        ps = PSF()
                        for k in range(8):
                            MM(ps[:], w[:, k, mc * 128:(mc + 1) * 128], hT[:, k, :], k == 0, k == 7, [w, hT], [ps])
                        STT("dve", xt[:, m, :], ps[:], modp[:, l, 16 + m:17 + m], xt[:, m, :], ALU.mult, ALU.add, [ps, modp, xt], [xt])

                chk('out')
                rmsnorm_mod(A2[:, l, :], modp[:, l, 24:32], l)
                for fg in range(6):
                    nf = 4 if fg < 5 else 2
                    wa = load_w(wfu_d[l][:, fg * 512:fg * 512 + nf * 128], 8, nf * 128)
                    wg = load_w(wfu_d[l][:, DFF + fg * 512:DFF + fg * 512 + nf * 128], 8, nf * 128)
                    for mc in range(nf):
                        f = fg * 4 + mc
                        psa = PSF()
                        for k in range(8):
                            MM(psa[:], wa[:, k, mc * 128:(mc + 1) * 128], hT[:, k, :], k == 0, k == 7, [wa, hT], [psa])
                        psg = PSF()
                        for k in range(8):
                            MM(psg[:], wg[:, k, mc * 128:(mc + 1) * 128], hT[:, k, :], k == 0, k == 7, [wg, hT], [psg])
                        ab = acc[:, 0, :, :].rearrange("p a b -> p (a b)")
                        CP("act", ab[:, 2:2 + TT], psa[:], [psa], [acc])
                        CP("pool", ab[:, 0:2], ccar[:, f, :], [ccar, acc], [acc])
                        CP("pool", ccar[:, f, :], ab[:, TT:TT + 2], [acc], [ccar])
                        c1 = acc[:, 1, 0, :]
                        TS("dve", c1, ab[:, 2:2 + TT], cw[:, f, 2:3], cb[:, f:f + 1], ALU.mult, ALU.add, [acc, cw, cb], [acc])
                        STT("dve", c1, ab[:, 1:1 + TT], cw[:, f, 1:2], c1, ALU.mult, ALU.add, [acc, cw], [acc])
                        STT("dve", c1, ab[:, 0:TT], cw[:, f, 0:1], c1, ALU.mult, ALU.add, [acc, cw], [acc])
                        ge = acc[:, 2, 0, :]
                        gelu_from(ge, c1, [acc], [acc], acc[:, 2, 1, :], acc[:, 3, 0, :])
                        TTe("dve", big[:, f, :], ge, psg[:], ALU.mult, [acc, psg], [big])
                for m in range(8):
                    wbi[0] += 1
                    wd = wb[wbi[0] % NWB]
                    wdv = wd[:].rearrange("p a b -> p (a b)")[:, 0:NF * 128].rearrange("p (f n) -> p f n", f=NF)
                    P.dma("pool", wdv, wfd_d[l][:, m * 128:(m + 1) * 128].rearrange("(f p) n -> p f n", p=128), writes=[wd])
                    ps = PSF()
                    for f in range(NF):
                        MM(ps[:], wdv[:, f, :], big[:, f, :], f == 0, f == NF - 1, [wd, big], [ps])
                    STT("dve", xt[:, m, :], ps[:], modp[:, l, 40 + m:41 + m], xt[:, m, :], ALU.mult, ALU.add, [ps, modp, xt], [xt])

                chk('ffn')
                if l < NL - 1:
                    P.dma("sp", x1T[:, t0:t0 + TT].rearrange("(k p) t -> p k t", p=128), xt[:], reads=[xt], writes=[b_x1])
                else:
                    rmsnorm_mod(gfin[:], None, l)
                    P.dma("sp", yT_out[:, t0:t0 + TT].rearrange("(k p) t -> p k t", p=128), scr[:], reads=[scr], writes=[b_out])
    except _Stop:
        src_t = xt
        ybi = {"att": 0, "rw": 1, "gmlp": 2}.get(STOP)
        if ybi is not None:
            MSET("pool", scr[:], 0.0, [scr])
            for c_ in range(4):
                CP("dve", scr[:, c_, :], YB[ybi][:, c_, :], [YB[ybi]], [scr])
            src_t = scr
        elif STOP == "br":
            src_t = scr
        elif STOP == "rw7":
            MSET("pool", scr[:], 0.0, [scr])
            CP("dve", scr[:, 0, :], acc[:, 0, 0, :], [acc], [scr])
            src_t = scr
        P.dma("sp", yT_out[:, 0:TT].rearrange("(k p) t -> p k t", p=128), src_t[:], reads=[src_t], writes=[b_out])
    return P.finish([b_out])


def gmlp_tile(P, L):
    g = L
    PSF, MM, ACT, TTe, TS, STT, CP = g["PSF"], g["MM"], g["ACT"], g["TTe"], g["TS"], g["STT"], g["CP"]
    hT, big, acc, scr, YB, wsT, bsT, lng, lnb = g["hT"], g["big"], g["acc"], g["scr"], g["YB"], g["wsT"], g["bsT"], g["lng"], g["lnb"]
    load_w, w_in, l, gelu_from = g["load_w"], g["w_in"], g["l"], g["gelu_from"]
    w = load_w(w_in[l][:, 3328:3840], 8, 512)
    for mc in range(4):
        ps = PSF()
        for k in range(8):
            MM(ps[:], w[:, k, mc * 128:(mc + 1) * 128], hT[:, k, :], k == 0, k == 7, [w, hT], [ps])
        gelu_from(scr[:, mc, :], ps[:], [ps], [scr], scr[:, 4, :], scr[:, 5, :])
    w = load_w(w_in[l][:, 3840:4352], 8, 512)
    stats = acc[:, 3, 1, 0:8]
    for blk in range(4):
        ps = PSF()
        for k in range(8):
            MM(ps[:], hT[:, k, blk * 128:(blk + 1) * 128], w[:, k, :], k == 0, k == 7, [w, hT], [ps])
        v = acc[:, 0, 0, :]
        gelu_from(v, ps[:], [ps], [acc], acc[:, 0, 1, :], acc[:, 1, 0, :])
        P.op("dve", lambda e: e.bn_stats(out=acc[:, 3, 1, 0:6], in_=v), ar=[v], aw=[acc[:, 3, 1, 0:6]])
        P.op("dve", lambda e: e.bn_aggr(out=acc[:, 3, 1, 6:8], in_=acc[:, 3, 1, 0:6]), ar=[acc[:, 3, 1, 0:6]], aw=[acc[:, 3, 1, 6:8]])
        ACT(acc[:, 3, 1, 7:8], acc[:, 3, 1, 7:8], AF.Sqrt, [acc], [acc], bias=1e-5)
        P.op("dve", lambda e: e.reciprocal(out=acc[:, 3, 1, 7:8], in_=acc[:, 3, 1, 7:8]), ar=[acc[:, 3, 1, 7:8]], aw=[acc[:, 3, 1, 7:8]])
        TS("dve", v, v, acc[:, 3, 1, 6:7], acc[:, 3, 1, 7:8], ALU.subtract, ALU.mult, [acc], [acc])
        TTe("dve", v, v, lng[:], ALU.mult, [acc, lng], [acc])
        vb = big[:, 18, :]
        TTe("dve", vb, v, lnb[:], ALU.add, [acc, lnb], [big])
        psA = PSF()
        psB = PSF()
        for gq in range(8):
            pp = psA if gq < 4 else psB
            c = gq // 2
            MM(pp[:, (gq % 4) * 128:(gq % 4) * 128 + 128], vb[:, c * 128:(c + 1) * 128], wsT[:, gq, :], True, True, [big, wsT], [pp])
        for gq in range(8):
            pp = psA if gq < 4 else psB
            c = gq // 2
            rows = slice((gq % 2) * 64, (gq % 2) * 64 + 64)
            tmp = acc[rows, 1, 1, 0:128]
            TTe("dve", tmp, pp[rows, (gq % 4) * 128:(gq % 4) * 128 + 128], bsT[rows, c, :], ALU.add, [pp, bsT], [acc])
            TTe("dve", YB[2][rows, c, blk * 128:(blk + 1) * 128], tmp, scr[rows, c, blk * 128:(blk + 1) * 128], ALU.mult, [acc, scr], [YB[2]])


def rwkv_tile(P, L):
    g = L
    PSF, PSB, MM, TR, ACT, TTe, TS, STT, CP, MSET = (g[k] for k in ("PSF", "PSB", "MM", "TR", "ACT", "TTe", "TS", "STT", "CP", "MSET"))
    big, scr, acc, YB, H0, H0b = g["big"], g["scr"], g["acc"], g["YB"], g["H0"], g["H0b"]
    rwp, omka, wupb, aupb, gupb, gnw, gnb = g["rwp"], g["omka"], g["wupb"], g["aupb"], g["gupb"], g["gnw"], g["gnb"]
    cst, cstb, mrw, mlo, sel = g["cst"], g["cstb"], g["mrw"], g["mlo"], g["sel"]
    chk = g["chk"]
    identb, identf, blkones = g["identb"], g["identf"], g["blkones"]
    rw = getattr(P, "_rw", None)
    if rw is None:
        sb = P.sb
        rw = dict(
            th=sb([128, 128], BF16), sgx=sb([128, 128], BF16),
            **{nm: View(scr[:, i_, :].rearrange("p (a b) -> p a b", a=4), scr) for i_, nm in enumerate(("lw", "cum", "cum2", "av", "kk", "km", "t1", "t2"))},
            AR=sb([128, 4, 2, 128], BF16), BK=sb([128, 4, 2, 128], BF16),
            wc=sb([128, 4], F32),
            tok=View(g["QT"][:, 0:3, :], g["QT"]),
            Xs=View(g["QT"][:, 3, :], g["QT"]),
            **{nm: View(g["V3g"][0][:, i_ // 2, (i_ % 2) * 2:(i_ % 2) * 2 + 2, :].rearrange("p a (h t) -> p (a h) t", h=4), g["V3g"][0])
               for i_, nm in enumerate(("N1", "L1", "N2", "L2"))},
            **{nm: View(YB[2][:, 2 * i_:2 * i_ + 2, :].rearrange("p a (h t) -> p (a h) t", h=4), YB[2]) for i_, nm in enumerate(("IL", "Mm"))},
            **{nm: View(big[:, 14 + 2 * i_:16 + 2 * i_, :].rearrange("p a (h t) -> p (a h) t", h=4), big)
               for i_, nm in enumerate(("Mm2", "Arb", "Aak", "Ark"))},
            Y=View(acc[:, 0, 0, :], acc), Y2=View(acc[:, 0, 1, :], acc), st=sb([128, 4, 8], F32), gt=View(acc[:, 1, 0, :], acc),
            Ht=sb([128, 4, 64], F32),
            Noff=View(g["rstd"][:].bitcast(BF16).rearrange("p (h t) -> p h t", h=8), g["rstd"]),
        )
        P._rw = rw
    rw = P._rw
    th, sgx, lw, cum, cum2, av, kk, km, t1, t2 = (rw[k] for k in ("th", "sgx", "lw", "cum", "cum2", "av", "kk", "km", "t1", "t2"))
    AR, BK, wc, tok = rw["AR"], rw["BK"], rw["wc"], rw["tok"]
    N1, L1, N2, L2, IL, Mm, Mm2, Arb, Aak, Ark = (rw[k] for k in ("N1", "L1", "N2", "L2", "IL", "Mm", "Mm2", "Arb", "Aak", "Ark"))
    V1r_ = g["V1r"]
    fs_ = (4, 5, 6) if g["it"] % 2 == 0 else (0, 1, 2)
    Us = View(V1r_[:, fs_[0], :], V1r_)
    yb = View(V1r_[:, fs_[1], :], V1r_)
    prod = View(V1r_[:, fs_[2], :].rearrange("p (a b) -> p a b", a=4), V1r_)
    Xs, Y, Y2, st, gt, Ht = (rw[k] for k in ("Xs", "Y", "Y2", "st", "gt", "Ht"))
    W0, A0, KK_, KA, RK = (rwp[:, i, :] for i in range(5))
    EH = 0.6065306597126334

    for blk in range(4):
        ts = slice(blk * 128, (blk + 1) * 128)
        zr = lambda c: big[:, 0 + c, ts]
        zk = lambda c: big[:, 4 + c, ts]
        zv = lambda c: big[:, 8 + c, ts]
        zwa = big[:, 12, ts]
        zg = big[:, 13, ts]
        ACT(th[0:64, :], big[0:64, 12, ts], AF.Tanh, [big], [th])
        ACT(sgx[:], zg, AF.Sigmoid, [big], [sgx])
        ps = PSF()
        psa = PSF()
        for c in range(4):
            MM(ps[:, c * 128:(c + 1) * 128], wupb[0:64, c * 128:(c + 1) * 128], th[0:64, :], True, True, [wupb, th], [ps])
            MM(psa[:, c * 128:(c + 1) * 128], aupb[64:128, c * 128:(c + 1) * 128], big[64:128, 12, ts], True, True, [aupb, big], [psa])
        for c in range(4):
            ACT(lw[:, c, :], ps[:, c * 128:(c + 1) * 128], AF.Exp, [ps, g["nw0"]], [lw], bias=g["nw0"][:, c:c + 1], scale=-1.0)
            ACT(av[:, c, :], psa[:, c * 128:(c + 1) * 128], AF.Sigmoid, [psa, rwp], [av], bias=A0[:, c:c + 1])
        TS("dve", lw[:], lw[:], 1.0, None, ALU.add, None, [lw], [lw])
        P.op("dve", lambda e: e.reciprocal(out=lw[:], in_=lw[:]), ar=[lw[:]], aw=[lw[:]])
        TS("dve", lw[:], lw[:], -EH, None, ALU.mult, None, [lw], [lw])
        psg = PSF()
        MM(psg[:], sgx[:], gupb[:], True, True, [sgx, gupb], [psg])
        CP("act", gt[:], psg[:], [psg], [gt])
        chk('rw1')
        for c in range(4):
            TS("dve", kk[:, c, :], zk(c), KK_[:, c:c + 1], None, ALU.mult, None, [big, rwp], [kk])
        ACT(t1[:], kk[:], AF.Square, [kk], [t1])
        ps = PSF()
        for c in range(4):
            MM(ps[:, c * 128:(c + 1) * 128], blkones, t1[:, c, :], True, True, [cst, t1], [ps])
        ACT(t1[:].rearrange("p a b -> p (a b)"), ps[:], AF.Sqrt, [ps], [t1])
        TS("dve", t1[:], t1[:], 1e-12, None, ALU.max, None, [t1], [t1])
        P.op("dve", lambda e: e.reciprocal(out=t1[:], in_=t1[:]), ar=[t1[:]], aw=[t1[:]])
        TTe("dve", kk[:], kk[:], t1[:], ALU.mult, [kk, t1], [kk])
        for c in range(4):
            TS("dve", t2[:, c, :], av[:, c, :], KA[:, c:c + 1], omka[:, c:c + 1], ALU.mult, ALU.add, [av, rwp, omka], [t2])
            TTe("dve", km[:, c, :], t2[:, c, :], zk(c), ALU.mult, [t2, big], [km])
        for c in range(4):
            STT("dve", prod[:, c, :], zr(c), RK[:, c:c + 1], km[:, c, :], ALU.mult, ALU.mult, [big, rwp, km], [prod])
        psr = PSF()
        for c in range(4):
            MM(psr[:, 0:8], prod[:, c, :], sel[:, c, :], c == 0, c == 3, [prod, sel], [psr])
        CP("act", st[:, 0, :], psr[:, 0:8], [psr], [st])
        chk('rw2')
        src, dst = lw, cum
        CP("pool", cum[:], lw[:], [lw], [cum])
        a_, b_ = cum, cum2
        for s in (1, 2, 4, 8, 16, 32, 64):
            CP("pool", b_[:, :, 0:s], a_[:, :, 0:s], [a_], [b_])
            TTe("dve", b_[:, :, s:128], a_[:, :, s:128], a_[:, :, 0:128 - s], ALU.add, [a_], [b_])
            a_, b_ = b_, a_
        cm = a_
        ot = b_
        ACT(t1[:], cm[:], AF.Exp, [cm], [t1])
        for c in range(4):
            TTe("dve", AR[:, c, 1, :], t1[:, c, :], zr(c), ALU.mult, [t1, big], [AR])
        ACT(wc[:], cm[:, :, 127], AF.Exp, [cm], [wc])
        TTe("dve", ot[:], cm[:], lw[:], ALU.subtract, [cm, lw], [ot])
        ACT(t1[:], ot[:], AF.Exp, [ot], [t1])
        STT("dve", AR[:, :, 0, :], kk[:], -1.0, t1[:], ALU.mult, ALU.mult, [kk, t1], [AR])
        ACT(t1[:], cm[:], AF.Exp, [cm], [t1], scale=-1.0)
        TTe("dve", t2[:], kk[:], av[:], ALU.mult, [kk, av], [t2])
        TTe("dve", BK[:, :, 0, :], t2[:], t1[:], ALU.mult, [t2, t1], [BK])
        TTe("dve", BK[:, :, 1, :], km[:], t1[:], ALU.mult, [km, t1], [BK])
        chk('rw3')
        pb = PSB()
        for c in range(4):
            TR(pb[:, c * 128:(c + 1) * 128], zv(c), identb, [big, cstb], [pb])
        CP("act", tok[:, 0, :], pb[:, 0:512], [pb], [tok])
        for q in range(2):
            pb = PSB()
            for c in range(4):
                TR(pb[:, c * 128:(c + 1) * 128], BK[:, c, q, :], identb, [BK, cstb], [pb])
            CP("act", tok[:, 1 + q, :], pb[:, 0:512], [pb], [tok])
        chk('rw4')
        for hg in range(2):
            p1 = PSF(); p2 = PSF(); p3 = PSF(); p4 = PSF()
            for hq in range(4):
                c, hh = hq, hg
                rows = slice(hh * 64, hh * 64 + 64)
                half = hq % 2
                pa = p1 if hq < 2 else p2
                MM(pa[:, half * 256:half * 256 + 256], BK[rows, c, 0, :], AR[rows, c, :, :].rearrange("p a b -> p (a b)"), True, True, [BK, AR], [pa])
                pk = p3 if hq < 2 else p4
                MM(pk[:, half * 256:half * 256 + 256], BK[rows, c, 1, :], AR[rows, c, :, :].rearrange("p a b -> p (a b)"), True, True, [BK, AR], [pk])
            chk('rw4a')
            for i2, (pa, pk) in enumerate(((p1, p3), (p2, p4))):
                h0 = hg * 4 + i2 * 2
                sa = pa[:].rearrange("p (h z t) -> p h z t", h=2, z=2)
                sk = pk[:].rearrange("p (h z t) -> p h z t", h=2, z=2)
                TTe("dve", N1[:, h0:h0 + 2, :], sa[:, :, 0, :], mrw[:, 0:2, 0, :], ALU.mult, [pa, mrw], [N1])
                TTe("dve", Arb[:, h0:h0 + 2, :], sa[:, :, 1, :], mrw[:, 0:2, 1, :], ALU.mult, [pa, mrw], [Arb])
                TTe("dve", Aak[:, h0:h0 + 2, :], sk[:, :, 0, :], mrw[:, 0:2, 0, :], ALU.mult, [pk, mrw], [Aak])
                TTe("dve", Ark[:, h0:h0 + 2, :], sk[:, :, 1, :], mrw[:, 0:2, 1, :], ALU.mult, [pk, mrw], [Ark])
            chk('rw4b')
            pl = PSF()
            for hq in range(4):
                c, hh = hq, hg
                rows = slice(hh * 64, hh * 64 + 64)
                MM(pl[:, hq * 128:(hq + 1) * 128], AR[rows, c, 0, :], BK[rows, c, 0, :], True, True, [AR, BK], [pl])
            TTe("dve", L1[:, hg * 4:hg * 4 + 4, :], pl[:].rearrange("p (h t) -> p h t", h=4), mlo[:], ALU.mult, [pl, mlo], [L1])
        chk('rw5')
        Noff = rw["Noff"]
        MSET("pool", Noff[:], 0.0, [Noff])
        CP("pool", Noff[0:64, :, 64:128], N1[0:64, :, 64:128], [N1], [Noff])
        MSET("pool", N1[0:64, :, 64:128], 0.0, [N1])
        MSET("pool", L1[64:128, :, 0:64], 0.0, [L1])
        for hg in range(2):
            hs = slice(hg * 4, hg * 4 + 4)
            TTe("pool", Mm[:, hs, :], N1[:, hs, :], cstb[:, 4:8, :], ALU.add, [N1, cstb], [Mm])
        Nk, Lk, Nn, Ln = N1, L1, N2, L2
        Mc, Mn = Mm, Mm2
        for lev in range(5):
            last = lev == 4
            for hg in range(2):
                hs = slice(hg * 4, hg * 4 + 4)
                pn = PSF()
                for hq in range(4):
                    h = hg * 4 + hq
                    MM(pn[:, hq * 128:(hq + 1) * 128], Lk[:, h, :], Nk[:, h, :], True, True, [Lk, Nk], [pn])
                if not last:
                    CP("act", Nn[:, hs, :], pn[:].rearrange("p (h t) -> p h t", h=4), [pn], [Nn])
                pL = PSF()
                for hq in range(4):
                    h = hg * 4 + hq
                    MM(pL[:, hq * 128:(hq + 1) * 128], Nk[:, h, :], Lk[:, h, :], True, True, [Lk, Nk], [pL])
                if not last:
                    CP("act", Ln[:, hs, :], pL[:].rearrange("p (h t) -> p h t", h=4), [pL], [Ln])
                TTe("dve", IL[:, hs, :], pL[:].rearrange("p (h t) -> p h t", h=4), cstb[:, 4:8, :], ALU.add, [pL, cstb], [IL])
                pm = PSF()
                for hq in range(4):
                    h = hg * 4 + hq
                    MM(pm[:, hq * 128:(hq + 1) * 128], IL[:, h, :], Mc[:, h, :], True, True, [IL, Mc], [pm])
                CP("act", Mn[:, hs, :], pm[:].rearrange("p (h t) -> p h t", h=4), [pm], [Mn])
            Nk, Nn = Nn, Nk
            Lk, Ln = Ln, Lk
            Mc, Mn = Mn, Mc
        Mf = Mc
        chk('rw6')
        px = PSF()
        for h in range(8):
            c, hh = h % 4, h // 4
            hn = 2 * c + hh
            rows = slice(hh * 64, hh * 64 + 64)
            o = px[:, hn * 64:(hn + 1) * 64]
            MM(o, AR[rows, c, 0, :], H0b[rows, c, :], True, False, [AR, H0b], [px])
            MM(o, Aak[:, h, :], tok[:, 0, hn * 64:(hn + 1) * 64], False, True, [Aak, tok], [px])
        CP("act", Xs[:], px[:], [px], [Xs])
        Noff = rw["Noff"]
        for rnd in range(2):
            pu = PSF()
            for h in range(8):
                c, hh = h % 4, h // 4
                hn = 2 * c + hh
                MM(pu[:, hn * 64:(hn + 1) * 64], Mf[:, h, :], Xs[:, hn * 64:(hn + 1) * 64], True, True, [Mf, Xs], [pu])
            CP("act", Us[:], pu[:], [pu], [Us])
            if rnd == 0:
                pw = PSF()
                for h in range(8):
                    c, hh = h % 4, h // 4
                    hn = 2 * c + hh
                    MM(pw[:, hn * 64:(hn + 1) * 64], Noff[:, h, :], Us[:, hn * 64:(hn + 1) * 64], True, True, [Noff, Us], [pw])
                TTe("dve", Xs[:], pw[:], Xs[:], ALU.add, [pw, Xs], [Xs])
        py = PSF()
        for h in range(8):
            c, hh = h % 4, h // 4
            hn = 2 * c + hh
            rows = slice(hh * 64, hh * 64 + 64)
            o = py[:, hn * 64:(hn + 1) * 64]
            MM(o, AR[rows, c, 1, :], H0b[rows, c, :], True, False, [AR, H0b], [py])
            MM(o, Arb[:, h, :], Us[:, hn * 64:(hn + 1) * 64], False, False, [Arb, Us], [py])
            MM(o, Ark[:, h, :], tok[:, 0, hn * 64:(hn + 1) * 64], False, True, [Ark, tok], [py])
        CP("act", Y[:], py[:], [py], [Y])
        ph = PSF()
        for c in range(4):
            o = ph[:, c * 128:(c + 1) * 128]
            MM(o, tok[:, 1, c * 128:(c + 1) * 128], Us[:, c * 128:(c + 1) * 128], True, False, [tok, Us], [ph])
            MM(o, tok[:, 2, c * 128:(c + 1) * 128], tok[:, 0, c * 128:(c + 1) * 128], False, True, [tok], [ph])
        for hh in range(2):
            rows = slice(hh * 64, hh * 64 + 64)
            src = ph[rows, :].rearrange("p (c x i) -> p c x i", c=4, x=2)[:, :, hh, :]
            TTe("dve", Ht[rows, :, :], src, H0[rows, :, :], ALU.add, [ph, H0], [Ht])
        for c in range(4):
            TS("dve", H0[:, c, :], Ht[:, c, :], wc[:, c:c + 1], None, ALU.mult, None, [Ht, wc], [H0])
        CP("act", H0b[:], H0[:], [H0], [H0b])
        chk('rw7')
        Y3 = Y[:].rearrange("p (h i) -> p h i", h=8)
        P.op("dve", lambda e: e.tensor_reduce(out=st[:, 1, :], in_=Y3, axis=AX.X, op=ALU.add), ar=[Y3], aw=[st[:, 1, :]])
        ACT(Y2[:], Y[:], AF.Square, [Y], [Y2])
        P.op("dve", lambda e: e.tensor_reduce(out=st[:, 2, :], in_=Y2[:].rearrange("p (h i) -> p h i", h=8), axis=AX.X, op=ALU.add), ar=[Y2[:]], aw=[st[:, 2, :]])
        TS("dve", st[:, 1, :], st[:, 1, :], 1.0 / 64, None, ALU.mult, None, [st], [st])
        TTe("dve", st[:, 3, :], st[:, 1, :], st[:, 1, :], ALU.mult, [st], [st])
        STT("dve", st[:, 2, :], st[:, 2, :], 1.0 / 64, st[:, 3, :], ALU.mult, ALU.subtract, [st], [st])
        ACT(st[:, 2, :], st[:, 2, :], AF.Sqrt, [st], [st], bias=64e-5)
        P.op("dve", lambda e: e.reciprocal(out=st[:, 2, :], in_=st[:, 2, :]), ar=[st[:, 2, :]], aw=[st[:, 2, :]])
        for h in range(8):
            TS("dve", Y2[:, h * 64:(h + 1) * 64], Y[:, h * 64:(h + 1) * 64], st[:, 1, h:h + 1], st[:, 2, h:h + 1], ALU.subtract, ALU.mult, [Y, st], [Y2])
        TTe("dve", Y2[:], Y2[:], gnw[:], ALU.mult, [Y2, gnw], [Y2])
        TTe("dve", Y2[:], Y2[:], gnb[:], ALU.add, [Y2, gnb], [Y2])
        for h in range(8):
            STT("dve", Y2[:, h * 64:(h + 1) * 64], tok[:, 0, h * 64:(h + 1) * 64], st[:, 0, h:h + 1], Y2[:, h * 64:(h + 1) * 64], ALU.mult, ALU.add, [tok, st, Y2], [Y2])
        TTe("dve", yb[:], Y2[:], gt[:], ALU.mult, [Y2, gt], [yb])
        pb = PSB()
        for c in range(4):
            TR(pb[:, c * 128:(c + 1) * 128], yb[:, c * 128:(c + 1) * 128], identb, [yb, cstb], [pb])
        CP("act", YB[1][:, :, ts], pb[:, 0:512].rearrange("p (c t) -> p c t", c=4), [pb], [YB[1]])


def _consts(T):
    bf = ml_dtypes.bfloat16
    cst = np.zeros((128, 5, 128), np.float32)
    cst[:, 0, :] = np.eye(128)
    cst[:, 1, :] = 1.0
    cst[0:64, 2, 0:64] = 1.0
    cst[64:128, 2, 64:128] = 1.0
    cstb = np.zeros((128, 8, 128), np.float32)
    cstb[:, 0, :] = np.eye(128)
    for m in range(128):
        cstb[(m // 64) * 64 + ((m % 64) + 32) % 64, 1, m] = 1.0
    cstb[:, 2, :] = 1.0
    for q_ in range(4):
        cstb[:, 4 + q_, :] = np.eye(128)
    ki = np.arange(128)[:, None]
    qi = np.arange(128)[None, :]
    m12 = np.zeros((128, 2, 256), np.float32)
    for u in range(2):
        m12[:, u, 0:128] = (ki >= qi)
        m12[:, u, 128:256] = (ki <= qi)
    m3 = np.zeros((128, 4, 4, 64), np.float32)
    for j in range(4):
        q = 32 * j + np.arange(32)[None, :]
        for rr in range(4):
            m3[:, j, rr, 0:32] = (ki >= q)
            m3[:, j, rr, 32:64] = (ki <= q)
    mrw = np.zeros((128, 4, 2, 128), np.float32)
    mrw[:, :, 0, :] = (ki < qi)[:, None, :]
    mrw[:, :, 1, :] = (ki <= qi)[:, None, :]
    mlo = np.zeros((128, 4, 128), np.float32)
    mlo[:, :, :] = (qi < ki)[:, None, :]
    sel = np.zeros((128, 4, 8), np.float32)
    for p in range(128):
        for c in range(4):
            sel[p, c, 2 * c + p // 64] = 1.0
    inv = (1.0 / (np.float32(10000.0) ** (np.arange(0, 64, 2, dtype=np.float32) / np.float32(64)))).astype(np.float32)
    ang = (np.arange(T, dtype=np.float32)[:, None] * inv[None, :]).astype(np.float32)
    cosv, sinv = np.cos(ang).astype(np.float32), np.sin(ang).astype(np.float32)
    cosT = np.zeros((128, T), np.float32)
    sinT = np.zeros((128, T), np.float32)
    for p in range(128):
        d = p % 64
        cosT[p] = cosv[:, d % 32]
        sinT[p] = sinv[:, d % 32] * (-1.0 if d < 32 else 1.0)
    return dict(cst_f32=cst, cst_bf=cstb.astype(bf), m12=m12.astype(bf), m3=m3.astype(bf), mrw=mrw.astype(bf),
                mlo=mlo.astype(bf), sel=sel.astype(bf), cosT=cosT, sinT=sinT)


def _col(v, n):
    return np.ascontiguousarray(np.asarray(v, np.float32).reshape(n, 128).T)


def _shared_inputs(inp, T):
    f = lambda a: np.ascontiguousarray(np.asarray(a, np.float32)[:NL]) if np.asarray(a).shape[0] == 2 and np.asarray(a).ndim >= 2 else np.ascontiguousarray(np.asarray(a, np.float32))
    d = dict(_consts(T))
    d["w_ada"] = f(inp["w_ada"])
    d["b_ada"] = np.stack([_col(inp["b_ada"][l], 48) for l in range(NL)])
    d["g_mix"] = np.stack([_col(inp["norm_mix"][l], 8) for l in range(NL)])
    d["g_ffn"] = np.stack([_col(inp["norm_ffn"][l], 8) for l in range(NL)])
    d["g_fin"] = _col(inp["norm_final"], 8)
    d["w_in"] = f(inp["w_in"])
    d["mu"] = np.stack([_col(inp["rwkv_mu"][l], 14) for l in range(NL)])
    d["rwp"] = np.stack([np.stack([_col(np.asarray(inp[k][l]).reshape(-1), 4) for k in
                                   ("rwkv_w0", "rwkv_a0", "rwkv_k_k", "rwkv_k_a", "rwkv_r_k")], axis=1) for l in range(NL)])
    d["r_wup"] = f(inp["rwkv_w_up"])
    d["r_aup"] = f(inp["rwkv_a_up"])
    d["r_gup"] = f(inp["rwkv_g_up"])
    bc = lambda a: np.ascontiguousarray(np.broadcast_to(f(a).reshape(NL, 1, 512), (NL, 128, 512)))
    d["gnw"] = bc(inp["rwkv_gn_w"])
    d["gnb"] = bc(inp["rwkv_gn_b"])
    d["lng"] = bc(inp["gmlp_ln_g"])
    d["lnb"] = bc(inp["gmlp_ln_b"])
    d["wsT"] = np.ascontiguousarray(np.transpose(f(inp["gmlp_w_s"]), (0, 3, 1, 2)))
    bsv = f(inp["gmlp_b_s"])
    bsT = np.zeros((NL, 128, 4, 128), np.float32)
    for g_ in range(8):
        bsT[:, (g_ % 2) * 64:(g_ % 2) * 64 + 64, g_ // 2, :] = bsv[:, g_, None, :]
    d["bsT"] = bsT
    d["w_branch"] = f(inp["w_branch"])
    d["w_out"] = f(inp["w_out"])
    d["ffn_up"] = f(inp["ffn_w_up"])
    d["convw"] = np.stack([np.ascontiguousarray(np.transpose(np.asarray(inp["ffn_conv_w"][l], np.float32).reshape(3, NF, 128), (2, 1, 0))) for l in range(NL)])
    d["convb"] = np.stack([_col(inp["ffn_conv_b"][l], NF) for l in range(NL)])
    d["ffn_down"] = f(inp["ffn_w_down"])
    return d


_NC_CACHE = {}


def run(inp, T=None):
    x = np.asarray(inp["x"], np.float32)
    B, S, _ = x.shape
    T = S
    if T not in _NC_CACHE:
        _NC_CACHE[T] = build(T)
    nc = _NC_CACHE[T]
    shared = _shared_inputs(inp, T)
    c = np.asarray(inp["c"], np.float32)
    in_maps = []
    for core in range(8):
        b = core % B
        m = dict(shared)
        m["xT"] = np.ascontiguousarray(x[b].T)
        m["c128"] = _col(c[b], 8)
        in_maps.append(m)
    res = run_bass_kernel_spmd(nc, in_maps, core_ids=list(range(8)))
    out = np.stack([np.ascontiguousarray(res.results[b]["yT"].T) for b in range(B)])
    return out.astype(np.float32)


def kernel(**inputs):
    return run(inputs)
```

```python
import numpy as np
import ml_dtypes
import concourse.bass as bass
import concourse.mybir as mybir
from concourse.bass_utils import run_bass_kernel_spmd

F32 = mybir.dt.float32
BF16 = mybir.dt.bfloat16
AF = mybir.ActivationFunctionType
ALU = mybir.AluOpType
AX = mybir.AxisListType

D = 1024
import os as _os
NL = int(_os.environ.get("KNL", "2"))
TT = 512
IN_COLS = 7424
DFF = 2816
NF = 22


class Buf:
    __slots__ = ("name", "lw", "rd", "excl")

    def __init__(self, name="", excl=False):
        self.name = name
        self.lw = None
        self.rd = []
        self.excl = excl


class Tl:
    __slots__ = ("t", "b")

    def __init__(self, t, b):
        self.t = t
        self.b = b

    def __getitem__(self, k):
        return self.t[k]


def _b(x):
    return x.b if isinstance(x, Tl) else x


def View(ap, buf):
    return Tl(ap, _b(buf))


class Prog:
    ENGS = ("pe", "dve", "act", "pool", "sp")
    NDMA = 8

    def __init__(self):
        self.nc = bass.Bass("TRN2", target_bir_lowering=False)
        self.ops = {e: [] for e in self.ENGS}
        self.cnt = {e: 0 for e in self.ENGS}
        self.seen = {e: {} for e in self.ENGS}
        self.sems = {}
        self._stack = []
        nc = self.nc
        for e in ("pe", "dve", "act", "pool"):
            self.sems[e] = self._enter(nc.semaphore("s_" + e))
        self.dslots = {}
        for q in ("sp", "act", "pool"):
            sl = []
            for i in range(self.NDMA):
                key = "d_%s%d" % (q, i)
                self.sems[key] = self._enter(nc.semaphore(key))
                sl.append([key, 0])
            self.dslots[q] = [sl, 0]
        self.n_sb = 0
        self.reg = {}

    def _enter(self, cm):
        v = cm.__enter__()
        self._stack.append(cm)
        return v

    def sb(self, shape, dtype, name=None):
        self.n_sb += 1
        t = self._enter(self.nc.sbuf_tensor("s_%s_%d" % (name or "t", self.n_sb), list(shape), dtype))
        return Tl(t, Buf(name or ""))

    def ps(self, shape, dtype=F32, name=None):
        self.n_sb += 1
        t = self._enter(self.nc.psum_tensor(name or ("ps%d" % self.n_sb), list(shape), dtype))
        return Tl(t, Buf(name or "", excl=True))

    def _deps(self, eng, reads, writes):
        toks = []
        for b in reads:
            b = _b(b)
            if b.lw is not None:
                toks.append(b.lw)
        for b in writes:
            b = _b(b)
            if b.lw is not None:
                toks.append(b.lw)
            toks.extend(b.rd)
        need = {}
        for (k, v) in toks:
            if k == "pe" and eng == "pe":
                continue
            if self.seen[eng].get(k, 0) >= v:
                continue
            if need.get(k, 0) < v:
                need[k] = v
        for k, v in need.items():
            self.seen[eng][k] = v
        return list(need.items())

    def _commit(self, tok, reads, writes):
        for b in reads:
            b = _b(b)
            b.rd.append(tok)
            if len(b.rd) > 64:
                mx = {}
                for (k, v) in b.rd:
                    if mx.get(k, 0) < v:
                        mx[k] = v
                b.rd = list(mx.items())
        for b in writes:
            b = _b(b)
            b.lw = tok
            b.rd = []

    @staticmethod
    def _onchip(ap):
        n = ap.name
        return n.startswith("s_") or n.startswith("ps")

    def _reg(self, ap):
        pstep, pcnt = ap.ap[0]
        off = ap.offset
        if ap.name.startswith("ps"):
            return (ap.name, 0, 128, 0, 1 << 30, True)
        p0 = off // pstep
        f0 = off % pstep
        span = 0
        for st, cnt in ap.ap[1:]:
            span += (cnt - 1) * abs(st)
        esz = 2 if ap.dtype == BF16 else 4
        return (ap.name, p0, p0 + pcnt, f0 * esz, (f0 + span + 1) * esz, False)

    def _rdeps(self, eng, ars, aws):
        toks = []
        new = []
        for (aps, isw) in ((ars, False), (aws, True)):
            for ap in aps:
                name, p0, p1, b0, b1, isps = self._reg(ap)
                w = isw or (isps and eng != "pe")
                recs = self.reg.setdefault(name, [])
                for r in recs:
                    if r[0] < p1 and p0 < r[1] and r[2] < b1 and b0 < r[3] and (w or r[5]):
                        toks.append(r[4])
                new.append((name, [p0, p1, b0, b1, None, w]))
        return toks, new

    def _rcommit(self, tok, new):
        for name, rec in new:
            rec[4] = tok
            recs = self.reg[name]
            if rec[5]:
                recs[:] = [r for r in recs if not (rec[0] <= r[0] and r[1] <= rec[1] and rec[2] <= r[2] and r[3] <= rec[3])]
            else:
                recs[:] = [r for r in recs if not ((not r[5]) and r[4][0] == tok[0] and rec[0] <= r[0] and r[1] <= rec[1]
                                                   and rec[2] <= r[2] and r[3] <= rec[3])]
            recs.append(rec)

    def _filter(self, eng, toks):
        need = {}
        for (k, v) in toks:
            if k == "pe" and eng == "pe":
                continue
            if self.seen[eng].get(k, 0) >= v:
                continue
            if need.get(k, 0) < v:
                need[k] = v
        for k, v in need.items():
            self.seen[eng][k] = v
        return list(need.items())

    def _buftoks(self, reads, writes):
        toks = []
        for b in reads:
            if b.lw is not None:
                toks.append(b.lw)
        for b in writes:
            if b.lw is not None:
                toks.append(b.lw)
            toks.extend(b.rd)
        return toks

    def op(self, eng, fn, reads=(), writes=(), ar=None, aw=None):
        if eng == "pool":
            eng = "dve"
        assert ar is not None and aw is not None
        reads = [b for b in reads if isinstance(b, Buf)]
        writes = [b for b in writes if isinstance(b, Buf)]
        rt, new = self._rdeps(eng, ar, aw)
        waits = self._filter(eng, self._buftoks(reads, writes) + rt)
        self.cnt[eng] += 1
        tok = (eng, self.cnt[eng])
        self.ops[eng].append((waits, fn, (eng, 1)))
        self._commit(tok, reads, writes)
        self._rcommit(tok, new)
        return tok

    def dma(self, q, out, in_, reads=(), writes=()):
        sl, idx = self.dslots[q]
        slot = sl[idx % self.NDMA]
        self.dslots[q][1] = idx + 1
        key = slot[0]
        reads = [b for b in reads if isinstance(b, Buf)]
        writes = [b for b in writes if isinstance(b, Buf)]
        ar = [in_] if self._onchip(in_) else []
        aw = [out] if self._onchip(out) else []
        rt, new = self._rdeps(q, ar, aw)
        waits = self._filter(q, self._buftoks(reads, writes) + rt)
        if slot[1] > 0 and self.seen[q].get(key, 0) < slot[1]:
            waits.append((key, slot[1]))
            self.seen[q][key] = slot[1]
        slot[1] += 16
        tok = (key, slot[1])

        def fn(e, out=out, in_=in_):
            return e.dma_start(out=out, in_=in_)
        self.ops[q].append((waits, fn, (key, 16)))
        self._commit(tok, reads, writes)
        self._rcommit(tok, new)
        return tok

    def fence(self, src, dst):
        return
        toks = []
        for s in src:
            s = _b(s)
            if s.lw is not None:
                toks.append(s.lw)
            toks.extend(s.rd)
        for d in dst:
            _b(d).rd.extend(toks)

    def finish(self, final_bufs):
        waits = self._filter("sp", self._buftoks([_b(b) for b in final_bufs], []))
        self.ops["sp"].append((waits, None, None))
        nc = self.nc
        sems = self.sems
        ops = self.ops
        with nc.Block() as block:
            def mk(e):
                def body(eng):
                    for (waits, fn, inc) in ops[e]:
                        for (k, v) in waits:
                            eng.wait_ge(sems[k], v)
                        if fn is not None:
                            ins = fn(eng)
                            ins.then_inc(sems[inc[0]], inc[1])
                return body
            block.tensor(mk("pe"))
            block.vector(mk("dve"))
            block.scalar(mk("act"))
            block.gpsimd(mk("pool"))
            block.sync(mk("sp"))
        for cm in reversed(self._stack):
            cm.__exit__(None, None, None)
        self._stack = []
        return nc


class _Stop(Exception):
    pass


def build(T, dbg=False):
    import os
    STOP = os.environ.get("KSTOP")
    P = Prog()

    def chk(name):
        if STOP == name:
            raise _Stop()
    nc = P.nc
    NTI = T // TT
    NB = T // 128

    def din(name, shape, dt=F32):
        return nc.dram_tensor(name, list(shape), dt, kind="ExternalInput").ap()

    xT_in = din("xT", [D, T])
    c_in = din("c128", [128, 8])
    w_ada = din("w_ada", [NL, D, 6 * D])
    b_ada = din("b_ada", [NL, 128, 48])
    g_mix = din("g_mix", [NL, 128, 8])
    g_ffn = din("g_ffn", [NL, 128, 8])
    g_fin = din("g_fin", [128, 8])
    w_in = din("w_in", [NL, D, IN_COLS])
    mu_d = din("mu", [NL, 128, 14])
    pr_d = din("rwp", [NL, 128, 5, 4])
    wup_d = din("r_wup", [NL, 64, 512])
    aup_d = din("r_aup", [NL, 64, 512])
    gup_d = din("r_gup", [NL, 128, 512])
    gnw_d = din("gnw", [NL, 128, 512])
    gnb_d = din("gnb", [NL, 128, 512])
    lng_d = din("lng", [NL, 128, 512])
    lnb_d = din("lnb", [NL, 128, 512])
    wsT_d = din("wsT", [NL, 128, 8, 128])
    bs_d = din("bsT", [NL, 128, 4, 128])
    wbr_d = din("w_branch", [NL, 3, 512, D])
    wout_d = din("w_out", [NL, D, D])
    wfu_d = din("ffn_up", [NL, D, 2 * DFF])
    cw_d = din("convw", [NL, 128, NF, 3])
    cb_d = din("convb", [NL, 128, NF])
    wfd_d = din("ffn_down", [NL, DFF, D])
    cos_d = din("cosT", [128, T])
    sin_d = din("sinT", [128, T])
    cst_d = din("cst_f32", [128, 5, 128])
    cstb_d = din("cst_bf", [128, 8, 128], BF16)
    m12_d = din("m12", [128, 2, 256], BF16)
    m3_d = din("m3", [128, 4, 4, 64], BF16)
    mrw_d = din("mrw", [128, 4, 2, 128], BF16)
    mlo_d = din("mlo", [128, 4, 128], BF16)
    sel_d = din("sel", [128, 4, 8], BF16)
    yT_out = nc.dram_tensor("yT", [D, T], F32, kind="ExternalOutput").ap()
    x1T = nc.dram_tensor("x1T", [D, T], F32, kind="Internal").ap()
    Vd = [nc.dram_tensor("Vd%d" % l, [T, 512], BF16, kind="Internal").ap() for l in range(NL)]
    b_x1 = Buf("x1T")
    b_vd = [Buf("vd0"), Buf("vd1")]
    b_out = Buf("out")

    def MM(out, lhsT, rhs, start, stop, r, w):
        P.op("pe", lambda e: e.matmul(out, lhsT, rhs, start=start, stop=stop), reads=r, writes=w, ar=[lhsT, rhs], aw=[out])

    def TR(out, in_, ident, r, w):
        P.op("pe", lambda e: e.transpose(out, in_, ident), reads=r, writes=w, ar=[in_, ident], aw=[out])

    def ACT(out, in_, func, r, w, bias=0.0, scale=1.0):
        P.op("act", lambda e: e.activation(out=out, in_=in_, func=func, bias=bias, scale=scale), reads=r, writes=w,
             ar=[in_] + [x for x in (bias, scale) if hasattr(x, "offset")], aw=[out])

    def TTe(eng, out, in0, in1, op, r, w):
        P.op(eng, lambda e: e.tensor_tensor(out=out, in0=in0, in1=in1, op=op), reads=r, writes=w, ar=[in0, in1], aw=[out])

    def TS(eng, out, in0, s1, s2, op0, op1, r, w):
        if s2 is None:
            P.op(eng, lambda e: e.tensor_scalar(out=out, in0=in0, scalar1=s1, scalar2=None, op0=op0), reads=r, writes=w,
                 ar=[in0] + [x for x in (s1,) if hasattr(x, "offset")], aw=[out])
        else:
            P.op(eng, lambda e: e.tensor_scalar(out=out, in0=in0, scalar1=s1, scalar2=s2, op0=op0, op1=op1), reads=r, writes=w,
                 ar=[in0] + [x for x in (s1, s2) if hasattr(x, "offset")], aw=[out])

    def STT(eng, out, in0, sc, in1, op0, op1, r, w):
        P.op(eng, lambda e: e.scalar_tensor_tensor(out=out, in0=in0, scalar=sc, in1=in1, op0=op0, op1=op1), reads=r, writes=w,
             ar=[in0, in1] + [x for x in (sc,) if hasattr(x, "offset")], aw=[out])

    def CP(eng, out, in_, r, w):
        if eng == "act":
            P.op("act", lambda e: e.copy(out=out, in_=in_), reads=r, writes=w, ar=[in_], aw=[out])
        else:
            P.op(eng, lambda e: e.tensor_copy(out=out, in_=in_), reads=r, writes=w, ar=[in_], aw=[out])

    def MSET(eng, ap, val, w):
        P.op(eng, lambda e: e.memset(ap, val), writes=w, ar=[], aw=[ap])

    psf = [P.ps([128, 512], F32) for _ in range(6)]
    psb = [P.ps([128, 1024], BF16) for _ in range(2)]
    psi = [0, 0]

    def PSF():
        psi[0] += 1
        return psf[psi[0] % 6]

    def PSB():
        psi[1] += 1
        return psb[psi[1] % 2]

    cst = P.sb([128, 5, 128], F32)
    cstb = P.sb([128, 8, 128], BF16)
    m12 = P.sb([128, 2, 256], BF16)
    m3 = P.sb([128, 4, 4, 64], BF16)
    mrw = P.sb([128, 4, 2, 128], BF16)
    mlo = P.sb([128, 4, 128], BF16)
    sel = P.sb([128, 4, 8], BF16)
    for (t_, d_) in ((cst, cst_d), (cstb, cstb_d), (m12, m12_d), (m3, m3_d), (mrw, mrw_d), (mlo, mlo_d), (sel, sel_d)):
        P.dma("sp", t_[:], d_, writes=[t_])
    identf = cst[:, 0, :]
    onesf = cst[:, 1, :]
    blkones = cst[:, 2, :]
    identb = cstb[:, 0, :]
    permb = cstb[:, 1, :]
    onesb = cstb[:, 2, :]

    xt = P.sb([128, 8, TT], F32, "xt")
    scr = P.sb([128, 8, TT], F32, "scr")
    hT = P.sb([128, 8, TT], BF16, "hT")
    rstd = P.sb([128, TT], F32, "rstd")
    NWB = 2
    wb = [P.sb([128, 8, 512], BF16, "wb%d" % i) for i in range(NWB)]
    wbi = [0]
    QT = P.sb([128, 4, TT], BF16, "QT")
    KT = P.sb([128, 4, T], BF16, "KT")
    V1r = P.sb([128, 8, 512], BF16, "V1r")
    V2 = P.sb([128, 2, 4, 512], BF16, "V2")
    V3g = [P.sb([128, 2, 4, 512], BF16, "V3g0")] * 2
    acc = P.sb([128, 4, 2, TT], F32, "acc")
    big = P.sb([128, NF, TT], BF16, "big")
    YB = [P.sb([128, 4, TT], BF16, "YB%d" % i) for i in range(3)]
    cosb = P.sb([128, TT], F32, "cos")
    sinb = P.sb([128, TT], F32, "sin")
    zcar = P.sb([128, 14], F32, "zcar")
    H0 = P.sb([128, 4, 64], F32, "H0")
    H0b = P.sb([128, 4, 64], BF16, "H0b")
    ccar = P.sb([128, NF, 2], F32, "ccar")
    MSET("pool", KT[:], 0.0, [KT])

    modp = P.sb([128, NL, 48], F32, "mod")
    A1 = P.sb([128, NL, 8], F32, "A1")
    A2 = P.sb([128, NL, 8], F32, "A2")
    gfin = P.sb([128, 8], F32, "gfin")
    cact = P.sb([128, 8], F32, "cact")
    wtmp = scr
    bad = P.sb([128, NL, 48], F32, "bad")
    gm = P.sb([128, NL, 8], F32, "gm")
    gf = P.sb([128, NL, 8], F32, "gf")
    P.dma("sp", cact[:], c_in, writes=[cact])
    P.dma("sp", gfin[:], g_fin, writes=[gfin])
    for l in range(NL):
        P.dma("sp", bad[:, l, :], b_ada[l], writes=[bad])
        P.dma("sp", gm[:, l, :], g_mix[l], writes=[gm])
        P.dma("sp", gf[:, l, :], g_ffn[l], writes=[gf])
    ACT(cact[:], cact[:], AF.Silu, [cact], [cact])
    for l in range(NL):
        for pc in range(12):
            P.dma("sp", wtmp[:], w_ada[l][:, pc * 512:(pc + 1) * 512].rearrange("(k p) n -> p k n", p=128), writes=[wtmp])
            ps = PSF()
            for mc in range(4):
                for k in range(8):
                    MM(ps[:, mc:mc + 1], wtmp[:, k, mc * 128:(mc + 1) * 128], cact[:, k:k + 1], k == 0, k == 7, [wtmp, cact], [ps])
            TTe("dve", modp[:, l, pc * 4:(pc + 1) * 4], ps[:, 0:4], bad[:, l, pc * 4:(pc + 1) * 4], ALU.add, [ps, bad], [modp])
        STT("dve", A1[:, l, :], modp[:, l, 8:16], 1.0, gm[:, l, :], ALU.add, ALU.mult, [modp, gm], [A1])
        STT("dve", A2[:, l, :], modp[:, l, 32:40], 1.0, gf[:, l, :], ALU.add, ALU.mult, [modp, gf], [A2])

    mu = P.sb([128, 14], F32, "mu")
    omu = P.sb([128, 14], F32, "omu")
    rwp = P.sb([128, 5, 4], F32, "rwp")
    omka = P.sb([128, 4], F32, "omka")
    nw0 = P.sb([128, 4], F32, "nw0")
    wupb = P.sb([128, 512], BF16, "wupb")
    aupb = P.sb([128, 512], BF16, "aupb")
    gupb = P.sb([128, 512], BF16, "gupb")
    gnw = P.sb([128, 512], F32, "gnw")
    gnb = P.sb([128, 512], F32, "gnb")
    lng = P.sb([128, 512], F32, "lng")
    lnb = P.sb([128, 512], F32, "lnb")
    wsT = P.sb([128, 8, 128], BF16, "wsT")
    wsTf = View(scr[:, 0:2, :].rearrange("p a (g i) -> p (a g) i", g=4), scr)
    bsT = P.sb([128, 4, 128], F32, "bsT")
    cw = P.sb([128, NF, 3], F32, "cw")
    cb = P.sb([128, NF], F32, "cb")

    def load_w(src_ap, kc, ncols):
        wbi[0] += 1
        w = wb[wbi[0] % NWB]
        P.dma("pool", w[:, 0:kc, 0:ncols], src_ap.rearrange("(k p) n -> p k n", p=128), writes=[w])
        return w

    def rmsnorm_mod(Acol, shcol, l):
        ACT(scr[:], xt[:], AF.Square, [xt], [scr])
        ps = PSF()
        for k in range(8):
            MM(ps[:], onesf, scr[:, k, :], k == 0, k == 7, [cst, scr], [ps])
        ACT(rstd[:], ps[:], AF.Sqrt, [ps], [rstd], bias=1e-6, scale=1.0 / D)
        P.op("dve", lambda e: e.reciprocal(out=rstd[:], in_=rstd[:]), ar=[rstd[:]], aw=[rstd[:]])
        for k in range(8):
            STT("dve", scr[:, k, :], xt[:, k, :], Acol[:, k:k + 1], rstd[:], ALU.mult, ALU.mult, [xt, rstd, A1, A2, gfin], [scr])
            if shcol is not None:
                TS("pool", hT[:, k, :], scr[:, k, :], shcol[:, k:k + 1], None, ALU.add, None, [scr, modp], [hT])

    GC = 0.7978845608028654

    def gelu_from(out, src, rd, wr, tmpa, tmpb):
        ACT(out, src, AF.Gelu_apprx_tanh, rd, wr)

    try:
        for l in range(NL):
            x_src = xT_in if l == 0 else x1T
            P.dma("sp", mu[:], mu_d[l], writes=[mu])
            TS("dve", omu[:], mu[:], -1.0, 1.0, ALU.mult, ALU.add, [mu], [omu])
            P.dma("sp", rwp[:], pr_d[l], writes=[rwp])
            TS("dve", omka[:], rwp[:, 3, :], -1.0, 1.0, ALU.mult, ALU.add, [rwp], [omka])
            TS("dve", nw0[:], rwp[:, 0, :], -1.0, None, ALU.mult, None, [rwp], [nw0])
            P.dma("pool", wupb[0:64, :], wup_d[l], writes=[wupb])
            P.dma("pool", aupb[64:128, :], aup_d[l], writes=[aupb])
            P.dma("pool", gupb[:], gup_d[l], writes=[gupb])
            for (t_, d_) in ((gnw, gnw_d), (gnb, gnb_d), (lng, lng_d), (lnb, lnb_d)):
                P.dma("sp", t_[:], d_[l], writes=[t_])
            P.dma("sp", wsTf[:], wsT_d[l], writes=[wsTf])
            for g in range(8):
                TTe("dve", wsT[:, g, :], wsTf[:, g, :], mrw[:, 0, 1, :], ALU.mult, [wsTf, mrw], [wsT])
            P.dma("sp", bsT[:], bs_d[l], writes=[bsT])
            P.dma("sp", cw[:], cw_d[l], writes=[cw])
            P.dma("sp", cb[:], cb_d[l], writes=[cb])
            MSET("pool", zcar[:], 0.0, [zcar])
            MSET("pool", H0[:], 0.0, [H0])
            MSET("pool", H0b[:], 0.0, [H0b])
            MSET("pool", ccar[:], 0.0, [ccar])
            MSET("pool", big[:, 0:4, :], 0.0, [big])
            for q in range(T // 512):
                P.dma("sp", Vd[l][q * 512:(q + 1) * 512, :].rearrange("(p a) c -> p (a c)", p=128), big[:, 0:4, :].rearrange("p a b -> p (a b)"), reads=[big], writes=[b_vd[l]])

            for it in range(NTI):
                t0 = it * TT
                P.dma("sp", xt[:], x_src[:, t0:t0 + TT].rearrange("(k p) t -> p k t", p=128), reads=[b_x1] if l else [], writes=[xt])
                P.dma("sp", cosb[:], cos_d[:, t0:t0 + TT], writes=[cosb])
                P.dma("sp", sinb[:], sin_d[:, t0:t0 + TT], writes=[sinb])
                chk('pro')
                rmsnorm_mod(A1[:, l, :], modp[:, l, 0:8], l)
                chk('norm')

                for pc in range(2):
                    w = load_w(w_in[l][:, pc * 512:(pc + 1) * 512], 8, 512)
                    for mc in range(4):
                        ps = PSF()
                        for k in range(8):
                            MM(ps[:], w[:, k, mc * 128:(mc + 1) * 128], hT[:, k, :], k == 0, k == 7, [w, hT], [ps])
                        chk('qk1')
                        qs = scr[:, mc, :]
                        qbf = big[:, 14 + mc, :]
                        CP("act", qbf, ps[:], [ps], [big])
                        ps2 = PSF()
                        MM(ps2[:], permb, qbf, True, True, [cstb, big], [ps2])
                        chk('qk2')
                        TTe("dve", qs, ps[:], cosb[:], ALU.mult, [ps, cosb, big, ps2], [scr])
                        TTe("dve", scr[:, 4 + mc, :], ps2[:], sinb[:], ALU.mult, [ps2, sinb], [scr])
                        chk('qk3')
                        dst = QT[:, mc, :] if pc == 0 else KT[:, mc, t0:t0 + TT]
                        TTe("pool", dst, qs, scr[:, 4 + mc, :], ALU.add, [scr], [QT if pc == 0 else KT])
                chk('qk')
                w = load_w(w_in[l][:, 1024:1536], 8, 512)
                for blk in range(4):
                    ps = PSF()
                    for k in range(8):
                        MM(ps[:], hT[:, k, blk * 128:(blk + 1) * 128], w[:, k, :], k == 0, k == 7, [w, hT], [ps])
                    gb = 4 * it + blk
                    CP("act", V1r[:, gb % 8, :], ps[:], [ps], [V1r])
                    P.dma("sp", Vd[l][t0 + blk * 128:t0 + (blk + 1) * 128, :], V1r[:, gb % 8, :], reads=[V1r], writes=[b_vd[l]])
                P.dma("sp", V2[:, it % 2, :, :], Vd[l][t0:t0 + TT, :].rearrange("(i r) c -> i r c", r=4), reads=[b_vd[l]], writes=[V2])

                chk('v')
                if "EP" not in P.__dict__:
                    P.EP = [(View(big[:, 16 + 2 * i_, :], Buf("E%d" % i_)), View(big[:, 17 + 2 * i_, :], Buf("P%d" % i_))) for i_ in range(3)]
                    P.epi = 0
                EPb = [x for pr in P.EP for x in pr]
                P.fence([big], EPb)
                for cfg in range(3):
                    n_sp = t0 // min(2048, T)
                    j3 = (t0 % min(2048, T)) // TT
                    if cfg < 2:
                        items = [(None, c, hh) for c in range(4) for hh in range(2)]
                    else:
                        items = [(rg_, c, hh) for rg_ in range(4) for c in range(4) for hh in range(2)]
                    for (rg_, c, hh) in items:
                        if True:
                            rows = slice(hh * 64, hh * 64 + 64)
                            if cfg < 2:
                                for half in range(2):
                                    psS = PSF()
                                    hp = []
                                    for u in range(2):
                                        qi = half * 2 + u
                                        if cfg == 0:
                                            gb = 4 * it + qi
                                            q_ap = QT[rows, c, qi * 128:(qi + 1) * 128]
                                            kc_ap = KT[rows, c, gb * 128:(gb + 1) * 128]
                                            kp_ap = KT[rows, c, (gb - 1) * 128:gb * 128] if gb > 0 else None
                                            vc_ap = V1r[:, gb % 8, c * 128:(c + 1) * 128]
                                            vp_ap = V1r[:, (gb - 1) % 8, c * 128:(c + 1) * 128]
                                            vb = V1r
                                        else:
                                            q_ap = QT[rows, c, qi:TT:4]
                                            kc_ap = KT[rows, c, t0 + qi:t0 + TT:4]
                                            kp_ap = KT[rows, c, t0 - TT + qi:t0:4] if it > 0 else None
                                            vc_ap = V2[:, it % 2, qi, c * 128:(c + 1) * 128]
                                            vp_ap = V2[:, (it - 1) % 2, qi, c * 128:(c + 1) * 128]
                                            vb = V2
                                        hp.append(kp_ap is not None)
                                        if kp_ap is not None:
                                            MM(psS[:, u * 256:u * 256 + 128], kp_ap, q_ap, True, True, [KT, QT], [psS])
                                        MM(psS[:, u * 256 + 128:u * 256 + 256], kc_ap, q_ap, True, True, [KT, QT], [psS])
                                        hp.append((vp_ap, vc_ap, vb))
                                    P.epi += 1
                                    Et, Pt = P.EP[P.epi % 3]
                                    for u in range(2):
                                        if hp[2 * u]:
                                            ACT(Et[:, u * 256:(u + 1) * 256], psS[:, u * 256:(u + 1) * 256], AF.Exp, [psS], [Et], scale=0.125)
                                        else:
                                            MSET("pool", Et[:, u * 256:u * 256 + 128], 0.0, [Et])
                                            ACT(Et[:, u * 256 + 128:u * 256 + 256], psS[:, u * 256 + 128:u * 256 + 256], AF.Exp, [psS], [Et], scale=0.125)
                                    Pm = Pt[:]
                                    TTe("dve", Pm, Et[:], m12[:].rearrange("p a b -> p (a b)"), ALU.mult, [Et, m12], [Pt])
                                    psN = PSF()
                                    for u in range(2):
                                        vp_ap, vc_ap, vb = hp[2 * u + 1]
                                        has_prev = hp[2 * u]
                                        for z in range(2):
                                            o = psN[:, z * 256 + u * 128:z * 256 + u * 128 + 128]
                                            if has_prev:
                                                MM(o, vp_ap if z == 0 else onesb, Pm[:, u * 256:u * 256 + 128], True, False, [vb, Pt, cstb], [psN])
                                            MM(o, vc_ap if z == 0 else onesb, Pm[:, u * 256 + 128:u * 256 + 256], not has_prev, True, [vb, Pt, cstb], [psN])
                                    src = psN[rows, :].rearrange("p (z u q) -> p z u q", z=2, u=2)
                                    if cfg == 0:
                                        dsta = acc[rows, c, :, half * 256:(half + 1) * 256].rearrange("p z (u q) -> p z u q", u=2)
                                        CP("act", dsta, src, [psN], [acc])
                                    else:
                                        dsta = acc[rows, c, :, :].rearrange("p z (q r) -> p z r q", r=4)[:, :, half * 2:half * 2 + 2, :]
                                        TTe("dve", dsta, src, dsta, ALU.add, [psN, acc], [acc])
                            else:
                                SPAN = min(2048, T)
                                nk3 = SPAN // 16
                                kp_ = slice(0, nk3)
                                for rg in (rg_,):
                                    if c == 0 and hh == 0:
                                        v3 = V3g[rg % 2]
                                        for sp_ in range(2):
                                            nn = n_sp - 1 + sp_
                                            if nn < 0:
                                                continue
                                            P.dma("sp", v3[kp_, sp_, :, :],
                                                  Vd[l][nn * SPAN:(nn + 1) * SPAN, :].rearrange("(i r) c -> i r c", r=16)[:, rg * 4:rg * 4 + 4, :],
                                                  reads=[b_vd[l]], writes=[v3])
                                    v3 = V3g[rg % 2]
                                    psS = PSF()
                                    hasp = n_sp > 0
                                    for rr in range(4):
                                        r = rg * 4 + rr
                                        q_ap = QT[rows, c, r:TT:16]
                                        kc_ap = KT[rows, c, n_sp * SPAN + r:(n_sp + 1) * SPAN:16]
                                        if hasp:
                                            kp_ap = KT[rows, c, (n_sp - 1) * SPAN + r:n_sp * SPAN:16]
                                            MM(psS[kp_, rr * 64:rr * 64 + 32], kp_ap, q_ap, True, True, [KT, QT], [psS])
                                        MM(psS[kp_, rr * 64 + 32:rr * 64 + 64], kc_ap, q_ap, True, True, [KT, QT], [psS])
                                    P.epi += 1
                                    Et, Pt = P.EP[P.epi % 3]
                                    if not hasp:
                                        MSET("pool", Et[kp_, 0:256], 0.0, [Et])
                                        for rr in range(4):
                                            ACT(Et[kp_, rr * 64 + 32:rr * 64 + 64], psS[kp_, rr * 64 + 32:rr * 64 + 64], AF.Exp, [psS], [Et], scale=0.125)
                                    else:
                                        ACT(Et[kp_, 0:256], psS[kp_, 0:256], AF.Exp, [psS], [Et], scale=0.125)
                                    Pm = Pt[kp_, 0:256]
                                    TTe("dve", Pm, Et[kp_, 0:256], m3[kp_, j3, :, :].rearrange("p a b -> p (a b)"), ALU.mult, [Et, m3], [Pt])
                                    psN = PSF()
                                    for rr in range(4):
                                        for z in range(2):
                                            o = psN[:, z * 128 + rr * 32:z * 128 + rr * 32 + 32]
                                            if hasp:
                                                MM(o, v3[kp_, 0, rr, c * 128:(c + 1) * 128] if z == 0 else onesb[kp_, :], Pm[:, rr * 64:rr * 64 + 32], True, False, [v3, Pt, cstb], [psN])
                                            MM(o, v3[kp_, 1, rr, c * 128:(c + 1) * 128] if z == 0 else onesb[kp_, :], Pm[:, rr * 64 + 32:rr * 64 + 64], not hasp, True, [v3, Pt, cstb], [psN])
                                    src = psN[rows, 0:256].rearrange("p (z r q) -> p z r q", z=2, r=4)
                                    dsta = acc[rows, c, :, :].rearrange("p z (q r) -> p z r q", r=16)[:, :, rg * 4:rg * 4 + 4, :]
                                    TTe("dve", dsta, src, dsta, ALU.add, [psN, acc], [acc])
                P.fence(EPb, [big])
                for c in range(4):
                    P.op("dve", lambda e, c=c: e.reciprocal(out=acc[:, c, 1, :], in_=acc[:, c, 1, :]), ar=[acc[:, c, 1, :]], aw=[acc[:, c, 1, :]])
                    TTe("dve", YB[0][:, c, :], acc[:, c, 0, :], acc[:, c, 1, :], ALU.mult, [acc], [YB[0]])
                chk('att')

                for pc in range(4):
                    ncol = 512 if pc < 3 else 256
                    w = load_w(w_in[l][:, 1536 + pc * 512:1536 + pc * 512 + ncol], 8, ncol)
                    for mc in range(ncol // 128):
                        ch = pc * 4 + mc
                        ps = PSF()
                        for k in range(8):
                            MM(ps[:], w[:, k, mc * 128:(mc + 1) * 128], hT[:, k, :], k == 0, k == 7, [w, hT], [ps])
                        zb = scr[:, ch % 4, :]
                        zp = scr[:, 4 + ch % 4, :]
                        CP("act", zb, ps[:], [ps], [scr])
                        CP("pool", zp[:, 1:TT], zb[:, 0:TT - 1], [scr], [scr])
                        CP("pool", zp[:, 0:1], zcar[:, ch:ch + 1], [zcar, scr], [scr])
                        CP("pool", zcar[:, ch:ch + 1], zb[:, TT - 1:TT], [scr], [zcar])
                        TS("dve", zb, zb, omu[:, ch:ch + 1], None, ALU.mult, None, [scr, omu], [scr])
                        STT("dve", big[:, ch, :], zp, mu[:, ch:ch + 1], zb, ALU.mult, ALU.add, [scr, mu], [big])
                chk('rwin'); rwkv_tile(P, locals()); chk('rw')

                gmlp_tile(P, locals()); chk('gmlp')

                wbrs = []
                for br in range(3):
                    for half in range(2):
                        wg = load_w(w_in[l][:, 4352 + br * 1024 + half * 512:4352 + br * 1024 + (half + 1) * 512], 8, 512)
                        wbt = load_w(wbr_d[l, br][:, half * 512:(half + 1) * 512], 4, 512)
                        for mc in range(4):
                            m = half * 4 + mc
                            psg = PSF()
                            for k in range(8):
                                MM(psg[:], wg[:, k, mc * 128:(mc + 1) * 128], hT[:, k, :], k == 0, k == 7, [wg, hT], [psg])
                            psbr = PSF()
                            for k in range(4):
                                MM(psbr[:], wbt[:, k, mc * 128:(mc + 1) * 128], YB[br][:, k, :], k == 0, k == 3, [wbt, YB[br]], [psbr])
                            sg = acc[:, 0, 0, :]
                            ACT(sg, psg[:], AF.Sigmoid, [psg], [acc])
                            if br == 0:
                                TTe("dve", scr[:, m, :], sg, psbr[:], ALU.mult, [acc, psbr], [scr])
                            else:
                                TTe("dve", sg, sg, psbr[:], ALU.mult, [acc, psbr], [acc])
                                TTe("pool", scr[:, m, :], scr[:, m, :], sg, ALU.add, [acc, scr], [scr])
                chk('br')
                for m in range(8):
                    CP("act", hT[:, m, :], scr[:, m, :], [scr], [hT])
                for half in range(2):
                    w = load_w(wout_d[l][:, half * 512:(half + 1) * 512], 8, 512)
                    for mc in range(4):
                        m = half * 4 + mc
                        ps = PSF()
                        for k in range(8):
                            MM(ps[:], w[:, k, mc * 128:(mc + 1) * 128], hT[:, k, :], k == 0, k == 7, [w, hT], [ps])
                        STT("dve", xt[:, m, :], ps[:], modp[:, l, 16 + m:17 + m], xt[:, m, :], ALU.mult, ALU.add, [ps, modp, xt], [xt])

                chk('out')
                rmsnorm_mod(A2[:, l, :], modp[:, l, 24:32], l)
                for fg in range(6):
                    nf = 4 if fg < 5 else 2
                    wa = load_w(wfu_d[l][:, fg * 512:fg * 512 + nf * 128], 8, nf * 128)
                    wg = load_w(wfu_d[l][:, DFF + fg * 512:DFF + fg * 512 + nf * 128], 8, nf * 128)
                    for mc in range(nf):
                        f = fg * 4 + mc
                        psa = PSF()
                        for k in range(8):
                            MM(psa[:], wa[:, k, mc * 128:(mc + 1) * 128], hT[:, k, :], k == 0, k == 7, [wa, hT], [psa])
                        psg = PSF()
                        for k in range(8):
                            MM(psg[:], wg[:, k, mc * 128:(mc + 1) * 128], hT[:, k, :], k == 0, k == 7, [wg, hT], [psg])
                        ab = acc[:, 0, :, :].rearrange("p a b -> p (a b)")
                        CP("act", ab[:, 2:2 + TT], psa[:], [psa], [acc])
                        CP("pool", ab[:, 0:2], ccar[:, f, :], [ccar, acc], [acc])
                        CP("pool", ccar[:, f, :], ab[:, TT:TT + 2], [acc], [ccar])
                        c1 = acc[:, 1, 0, :]
                        TS("dve", c1, ab[:, 2:2 + TT], cw[:, f, 2:3], cb[:, f:f + 1], ALU.mult, ALU.add, [acc, cw, cb], [acc])
                        STT("dve", c1, ab[:, 1:1 + TT], cw[:, f, 1:2], c1, ALU.mult, ALU.add, [acc, cw], [acc])
                        STT("dve", c1, ab[:, 0:TT], cw[:, f, 0:1], c1, ALU.mult, ALU.add, [acc, cw], [acc])
                        ge = acc[:, 2, 0, :]
                        gelu_from(ge, c1, [acc], [acc], acc[:, 2, 1, :], acc[:, 3, 0, :])
                        TTe("dve", big[:, f, :], ge, psg[:], ALU.mult, [acc, psg], [big])
                for m in range(8):
                    wbi[0] += 1
                    wd = wb[wbi[0] % NWB]
                    wdv = wd[:].rearrange("p a b -> p (a b)")[:, 0:NF * 128].rearrange("p (f n) -> p f n", f=NF)
                    P.dma("pool", wdv, wfd_d[l][:, m * 128:(m + 1) * 128].rearrange("(f p) n -> p f n", p=128), writes=[wd])
                    ps = PSF()
                    for f in range(NF):
                        MM(ps[:], wdv[:, f, :], big[:, f, :], f == 0, f == NF - 1, [wd, big], [ps])
                    STT("dve", xt[:, m, :], ps[:], modp[:, l, 40 + m:41 + m], xt[:, m, :], ALU.mult, ALU.add, [ps, modp, xt], [xt])

                chk('ffn')
                if l < NL - 1:
                    P.dma("sp", x1T[:, t0:t0 + TT].rearrange("(k p) t -> p k t", p=128), xt[:], reads=[xt], writes=[b_x1])
                else:
                    rmsnorm_mod(gfin[:], None, l)
                    P.dma("sp", yT_out[:, t0:t0 + TT].rearrange("(k p) t -> p k t", p=128), scr[:], reads=[scr], writes=[b_out])
    except _Stop:
        src_t = xt
        ybi = {"att": 0, "rw": 1, "gmlp": 2}.get(STOP)
        if ybi is not None:
            MSET("pool", scr[:], 0.0, [scr])
            for c_ in range(4):
                CP("dve", scr[:, c_, :], YB[ybi][:, c_, :], [YB[ybi]], [scr])
            src_t = scr
        elif STOP == "br":
            src_t = scr
        elif STOP == "rw7":
            MSET("pool", scr[:], 0.0, [scr])
            CP("dve", scr[:, 0, :], acc[:, 0, 0, :], [acc], [scr])
            src_t = scr
        P.dma("sp", yT_out[:, 0:TT].rearrange("(k p) t -> p k t", p=128), src_t[:], reads=[src_t], writes=[b_out])
    return P.finish([b_out])


def gmlp_tile(P, L):
    g = L
    PSF, MM, ACT, TTe, TS, STT, CP = g["PSF"], g["MM"], g["ACT"], g["TTe"], g["TS"], g["STT"], g["CP"]
    hT, big, acc, scr, YB, wsT, bsT, lng, lnb = g["hT"], g["big"], g["acc"], g["scr"], g["YB"], g["wsT"], g["bsT"], g["lng"], g["lnb"]
    load_w, w_in, l, gelu_from = g["load_w"], g["w_in"], g["l"], g["gelu_from"]
    w = load_w(w_in[l][:, 3328:3840], 8, 512)
    for mc in range(4):
        ps = PSF()
        for k in range(8):
            MM(ps[:], w[:, k, mc * 128:(mc + 1) * 128], hT[:, k, :], k == 0, k == 7, [w, hT], [ps])
        gelu_from(scr[:, mc, :], ps[:], [ps], [scr], scr[:, 4, :], scr[:, 5, :])
    w = load_w(w_in[l][:, 3840:4352], 8, 512)
    stats = acc[:, 3, 1, 0:8]
    for blk in range(4):
        ps = PSF()
        for k in range(8):
            MM(ps[:], hT[:, k, blk * 128:(blk + 1) * 128], w[:, k, :], k == 0, k == 7, [w, hT], [ps])
        v = acc[:, 0, 0, :]
        gelu_from(v, ps[:], [ps], [acc], acc[:, 0, 1, :], acc[:, 1, 0, :])
        P.op("dve", lambda e: e.bn_stats(out=acc[:, 3, 1, 0:6], in_=v), ar=[v], aw=[acc[:, 3, 1, 0:6]])
        P.op("dve", lambda e: e.bn_aggr(out=acc[:, 3, 1, 6:8], in_=acc[:, 3, 1, 0:6]), ar=[acc[:, 3, 1, 0:6]], aw=[acc[:, 3, 1, 6:8]])
        ACT(acc[:, 3, 1, 7:8], acc[:, 3, 1, 7:8], AF.Sqrt, [acc], [acc], bias=1e-5)
        P.op("dve", lambda e: e.reciprocal(out=acc[:, 3, 1, 7:8], in_=acc[:, 3, 1, 7:8]), ar=[acc[:, 3, 1, 7:8]], aw=[acc[:, 3, 1, 7:8]])
        TS("dve", v, v, acc[:, 3, 1, 6:7], acc[:, 3, 1, 7:8], ALU.subtract, ALU.mult, [acc], [acc])
        TTe("dve", v, v, lng[:], ALU.mult, [acc, lng], [acc])
        vb = big[:, 18, :]
        TTe("dve", vb, v, lnb[:], ALU.add, [acc, lnb], [big])
        psA = PSF()
        psB = PSF()
        for gq in range(8):
            pp = psA if gq < 4 else psB
            c = gq // 2
            MM(pp[:, (gq % 4) * 128:(gq % 4) * 128 + 128], vb[:, c * 128:(c + 1) * 128], wsT[:, gq, :], True, True, [big, wsT], [pp])
        for gq in range(8):
            pp = psA if gq < 4 else psB
            c = gq // 2
            rows = slice((gq % 2) * 64, (gq % 2) * 64 + 64)
            tmp = acc[rows, 1, 1, 0:128]
            TTe("dve", tmp, pp[rows, (gq % 4) * 128:(gq % 4) * 128 + 128], bsT[rows, c, :], ALU.add, [pp, bsT], [acc])
            TTe("dve", YB[2][rows, c, blk * 128:(blk + 1) * 128], tmp, scr[rows, c, blk * 128:(blk + 1) * 128], ALU.mult, [acc, scr], [YB[2]])


def rwkv_tile(P, L):
    g = L
    PSF, PSB, MM, TR, ACT, TTe, TS, STT, CP, MSET = (g[k] for k in ("PSF", "PSB", "MM", "TR", "ACT", "TTe", "TS", "STT", "CP", "MSET"))
    big, scr, acc, YB, H0, H0b = g["big"], g["scr"], g["acc"], g["YB"], g["H0"], g["H0b"]
    rwp, omka, wupb, aupb, gupb, gnw, gnb = g["rwp"], g["omka"], g["wupb"], g["aupb"], g["gupb"], g["gnw"], g["gnb"]
    cst, cstb, mrw, mlo, sel = g["cst"], g["cstb"], g["mrw"], g["mlo"], g["sel"]
    chk = g["chk"]
    identb, identf, blkones = g["identb"], g["identf"], g["blkones"]
    rw = getattr(P, "_rw", None)
    if rw is None:
        sb = P.sb
        rw = dict(
            th=sb([128, 128], BF16), sgx=sb([128, 128], BF16),
            **{nm: View(scr[:, i_, :].rearrange("p (a b) -> p a b", a=4), scr) for i_, nm in enumerate(("lw", "cum", "cum2", "av", "kk", "km", "t1", "t2"))},
            AR=sb([128, 4, 2, 128], BF16), BK=sb([128, 4, 2, 128], BF16),
            wc=sb([128, 4], F32),
            tok=View(g["QT"][:, 0:3, :], g["QT"]),
            Xs=View(g["QT"][:, 3, :], g["QT"]),
            **{nm: View(g["V3g"][0][:, i_ // 2, (i_ % 2) * 2:(i_ % 2) * 2 + 2, :].rearrange("p a (h t) -> p (a h) t", h=4), g["V3g"][0])
               for i_, nm in enumerate(("N1", "L1", "N2", "L2"))},
            **{nm: View(YB[2][:, 2 * i_:2 * i_ + 2, :].rearrange("p a (h t) -> p (a h) t", h=4), YB[2]) for i_, nm in enumerate(("IL", "Mm"))},
            **{nm: View(big[:, 14 + 2 * i_:16 + 2 * i_, :].rearrange("p a (h t) -> p (a h) t", h=4), big)
               for i_, nm in enumerate(("Mm2", "Arb", "Aak", "Ark"))},
            Y=View(acc[:, 0, 0, :], acc), Y2=View(acc[:, 0, 1, :], acc), st=sb([128, 4, 8], F32), gt=View(acc[:, 1, 0, :], acc),
            Ht=sb([128, 4, 64], F32),
            Noff=View(g["rstd"][:].bitcast(BF16).rearrange("p (h t) -> p h t", h=8), g["rstd"]),
        )
        P._rw = rw
    rw = P._rw
    th, sgx, lw, cum, cum2, av, kk, km, t1, t2 = (rw[k] for k in ("th", "sgx", "lw", "cum", "cum2", "av", "kk", "km", "t1", "t2"))
    AR, BK, wc, tok = rw["AR"], rw["BK"], rw["wc"], rw["tok"]
    N1, L1, N2, L2, IL, Mm, Mm2, Arb, Aak, Ark = (rw[k] for k in ("N1", "L1", "N2", "L2", "IL", "Mm", "Mm2", "Arb", "Aak", "Ark"))
    V1r_ = g["V1r"]
    fs_ = (4, 5, 6) if g["it"] % 2 == 0 else (0, 1, 2)
    Us = View(V1r_[:, fs_[0], :], V1r_)
    yb = View(V1r_[:, fs_[1], :], V1r_)
    prod = View(V1r_[:, fs_[2], :].rearrange("p (a b) -> p a b", a=4), V1r_)
    Xs, Y, Y2, st, gt, Ht = (rw[k] for k in ("Xs", "Y", "Y2", "st", "gt", "Ht"))
    W0, A0, KK_, KA, RK = (rwp[:, i, :] for i in range(5))
    EH = 0.6065306597126334

    for blk in range(4):
        ts = slice(blk * 128, (blk + 1) * 128)
        zr = lambda c: big[:, 0 + c, ts]
        zk = lambda c: big[:, 4 + c, ts]
        zv = lambda c: big[:, 8 + c, ts]
        zwa = big[:, 12, ts]
        zg = big[:, 13, ts]
        ACT(th[0:64, :], big[0:64, 12, ts], AF.Tanh, [big], [th])
        ACT(sgx[:], zg, AF.Sigmoid, [big], [sgx])
        ps = PSF()
        psa = PSF()
        for c in range(4):
            MM(ps[:, c * 128:(c + 1) * 128], wupb[0:64, c * 128:(c + 1) * 128], th[0:64, :], True, True, [wupb, th], [ps])
            MM(psa[:, c * 128:(c + 1) * 128], aupb[64:128, c * 128:(c + 1) * 128], big[64:128, 12, ts], True, True, [aupb, big], [psa])
        for c in range(4):
            ACT(lw[:, c, :], ps[:, c * 128:(c + 1) * 128], AF.Exp, [ps, g["nw0"]], [lw], bias=g["nw0"][:, c:c + 1], scale=-1.0)
            ACT(av[:, c, :], psa[:, c * 128:(c + 1) * 128], AF.Sigmoid, [psa, rwp], [av], bias=A0[:, c:c + 1])
        TS("dve", lw[:], lw[:], 1.0, None, ALU.add, None, [lw], [lw])
        P.op("dve", lambda e: e.reciprocal(out=lw[:], in_=lw[:]), ar=[lw[:]], aw=[lw[:]])
        TS("dve", lw[:], lw[:], -EH, None, ALU.mult, None, [lw], [lw])
        psg = PSF()
        MM(psg[:], sgx[:], gupb[:], True, True, [sgx, gupb], [psg])
        CP("act", gt[:], psg[:], [psg], [gt])
        chk('rw1')
        for c in range(4):
            TS("dve", kk[:, c, :], zk(c), KK_[:, c:c + 1], None, ALU.mult, None, [big, rwp], [kk])
        ACT(t1[:], kk[:], AF.Square, [kk], [t1])
        ps = PSF()
        for c in range(4):
            MM(ps[:, c * 128:(c + 1) * 128], blkones, t1[:, c, :], True, True, [cst, t1], [ps])
        ACT(t1[:].rearrange("p a b -> p (a b)"), ps[:], AF.Sqrt, [ps], [t1])
        TS("dve", t1[:], t1[:], 1e-12, None, ALU.max, None, [t1], [t1])
        P.op("dve", lambda e: e.reciprocal(out=t1[:], in_=t1[:]), ar=[t1[:]], aw=[t1[:]])
        TTe("dve", kk[:], kk[:], t1[:], ALU.mult, [kk, t1], [kk])
        for c in range(4):
            TS("dve", t2[:, c, :], av[:, c, :], KA[:, c:c + 1], omka[:, c:c + 1], ALU.mult, ALU.add, [av, rwp, omka], [t2])
            TTe("dve", km[:, c, :], t2[:, c, :], zk(c), ALU.mult, [t2, big], [km])
        for c in range(4):
            STT("dve", prod[:, c, :], zr(c), RK[:, c:c + 1], km[:, c, :], ALU.mult, ALU.mult, [big, rwp, km], [prod])
        psr = PSF()
        for c in range(4):
            MM(psr[:, 0:8], prod[:, c, :], sel[:, c, :], c == 0, c == 3, [prod, sel], [psr])
        CP("act", st[:, 0, :], psr[:, 0:8], [psr], [st])
        chk('rw2')
        src, dst = lw, cum
        CP("pool", cum[:], lw[:], [lw], [cum])
        a_, b_ = cum, cum2
        for s in (1, 2, 4, 8, 16, 32, 64):
            CP("pool", b_[:, :, 0:s], a_[:, :, 0:s], [a_], [b_])
            TTe("dve", b_[:, :, s:128], a_[:, :, s:128], a_[:, :, 0:128 - s], ALU.add, [a_], [b_])
            a_, b_ = b_, a_
        cm = a_
        ot = b_
        ACT(t1[:], cm[:], AF.Exp, [cm], [t1])
        for c in range(4):
            TTe("dve", AR[:, c, 1, :], t1[:, c, :], zr(c), ALU.mult, [t1, big], [AR])
        ACT(wc[:], cm[:, :, 127], AF.Exp, [cm], [wc])
        TTe("dve", ot[:], cm[:], lw[:], ALU.subtract, [cm, lw], [ot])
        ACT(t1[:], ot[:], AF.Exp, [ot], [t1])
        STT("dve", AR[:, :, 0, :], kk[:], -1.0, t1[:], ALU.mult, ALU.mult, [kk, t1], [AR])
        ACT(t1[:], cm[:], AF.Exp, [cm], [t1], scale=-1.0)
        TTe("dve", t2[:], kk[:], av[:], ALU.mult, [kk, av], [t2])
        TTe("dve", BK[:, :, 0, :], t2[:], t1[:], ALU.mult, [t2, t1], [BK])
        TTe("dve", BK[:, :, 1, :], km[:], t1[:], ALU.mult, [km, t1], [BK])
        chk('rw3')
        pb = PSB()
        for c in range(4):
            TR(pb[:, c * 128:(c + 1) * 128], zv(c), identb, [big, cstb], [pb])
        CP("act", tok[:, 0, :], pb[:, 0:512], [pb], [tok])
        for q in range(2):
            pb = PSB()
            for c in range(4):
                TR(pb[:, c * 128:(c + 1) * 128], BK[:, c, q, :], identb, [BK, cstb], [pb])
            CP("act", tok[:, 1 + q, :], pb[:, 0:512], [pb], [tok])
        chk('rw4')
        for hg in range(2):
            p1 = PSF(); p2 = PSF(); p3 = PSF(); p4 = PSF()
            for hq in range(4):
                c, hh = hq, hg
                rows = slice(hh * 64, hh * 64 + 64)
                half = hq % 2
                pa = p1 if hq < 2 else p2
                MM(pa[:, half * 256:half * 256 + 256], BK[rows, c, 0, :], AR[rows, c, :, :].rearrange("p a b -> p (a b)"), True, True, [BK, AR], [pa])
                pk = p3 if hq < 2 else p4
                MM(pk[:, half * 256:half * 256 + 256], BK[rows, c, 1, :], AR[rows, c, :, :].rearrange("p a b -> p (a b)"), True, True, [BK, AR], [pk])
            chk('rw4a')
            for i2, (pa, pk) in enumerate(((p1, p3), (p2, p4))):
                h0 = hg * 4 + i2 * 2
                sa = pa[:].rearrange("p (h z t) -> p h z t", h=2, z=2)
                sk = pk[:].rearrange("p (h z t) -> p h z t", h=2, z=2)
                TTe("dve", N1[:, h0:h0 + 2, :], sa[:, :, 0, :], mrw[:, 0:2, 0, :], ALU.mult, [pa, mrw], [N1])
                TTe("dve", Arb[:, h0:h0 + 2, :], sa[:, :, 1, :], mrw[:, 0:2, 1, :], ALU.mult, [pa, mrw], [Arb])
                TTe("dve", Aak[:, h0:h0 + 2, :], sk[:, :, 0, :], mrw[:, 0:2, 0, :], ALU.mult, [pk, mrw], [Aak])
                TTe("dve", Ark[:, h0:h0 + 2, :], sk[:, :, 1, :], mrw[:, 0:2, 1, :], ALU.mult, [pk, mrw], [Ark])
            chk('rw4b')
            pl = PSF()
            for hq in range(4):
                c, hh = hq, hg
                rows = slice(hh * 64, hh * 64 + 64)
                MM(pl[:, hq * 128:(hq + 1) * 128], AR[rows, c, 0, :], BK[rows, c, 0, :], True, True, [AR, BK], [pl])
            TTe("dve", L1[:, hg * 4:hg * 4 + 4, :], pl[:].rearrange("p (h t) -> p h t", h=4), mlo[:], ALU.mult, [pl, mlo], [L1])
        chk('rw5')
        Noff = rw["Noff"]
        MSET("pool", Noff[:], 0.0, [Noff])
        CP("pool", Noff[0:64, :, 64:128], N1[0:64, :, 64:128], [N1], [Noff])
        MSET("pool", N1[0:64, :, 64:128], 0.0, [N1])
        MSET("pool", L1[64:128, :, 0:64], 0.0, [L1])
        for hg in range(2):
            hs = slice(hg * 4, hg * 4 + 4)
            TTe("pool", Mm[:, hs, :], N1[:, hs, :], cstb[:, 4:8, :], ALU.add, [N1, cstb], [Mm])
        Nk, Lk, Nn, Ln = N1, L1, N2, L2
        Mc, Mn = Mm, Mm2
        for lev in range(5):
            last = lev == 4
            for hg in range(2):
                hs = slice(hg * 4, hg * 4 + 4)
                pn = PSF()
                for hq in range(4):
                    h = hg * 4 + hq
                    MM(pn[:, hq * 128:(hq + 1) * 128], Lk[:, h, :], Nk[:, h, :], True, True, [Lk, Nk], [pn])
                if not last:
                    CP("act", Nn[:, hs, :], pn[:].rearrange("p (h t) -> p h t", h=4), [pn], [Nn])
                pL = PSF()
                for hq in range(4):
                    h = hg * 4 + hq
                    MM(pL[:, hq * 128:(hq + 1) * 128], Nk[:, h, :], Lk[:, h, :], True, True, [Lk, Nk], [pL])
                if not last:
                    CP("act", Ln[:, hs, :], pL[:].rearrange("p (h t) -> p h t", h=4), [pL], [Ln])
                TTe("dve", IL[:, hs, :], pL[:].rearrange("p (h t) -> p h t", h=4), cstb[:, 4:8, :], ALU.add, [pL, cstb], [IL])
                pm = PSF()
                for hq in range(4):
                    h = hg * 4 + hq
                    MM(pm[:, hq * 128:(hq + 1) * 128], IL[:, h, :], Mc[:, h, :], True, True, [IL, Mc], [pm])
                CP("act", Mn[:, hs, :], pm[:].rearrange("p (h t) -> p h t", h=4), [pm], [Mn])
            Nk, Nn = Nn, Nk
            Lk, Ln = Ln, Lk
            Mc, Mn = Mn, Mc
        Mf = Mc
        chk('rw6')
        px = PSF()
        for h in range(8):
            c, hh = h % 4, h // 4
            hn = 2 * c + hh
            rows = slice(hh * 64, hh * 64 + 64)
            o = px[:, hn * 64:(hn + 1) * 64]
            MM(o, AR[rows, c, 0, :], H0b[rows, c, :], True, False, [AR, H0b], [px])
            MM(o, Aak[:, h, :], tok[:, 0, hn * 64:(hn + 1) * 64], False, True, [Aak, tok], [px])
        CP("act", Xs[:], px[:], [px], [Xs])
        Noff = rw["Noff"]
        for rnd in range(2):
            pu = PSF()
            for h in range(8):
                c, hh = h % 4, h // 4
                hn = 2 * c + hh
                MM(pu[:, hn * 64:(hn + 1) * 64], Mf[:, h, :], Xs[:, hn * 64:(hn + 1) * 64], True, True, [Mf, Xs], [pu])
            CP("act", Us[:], pu[:], [pu], [Us])
            if rnd == 0:
                pw = PSF()
                for h in range(8):
                    c, hh = h % 4, h // 4
                    hn = 2 * c + hh
                    MM(pw[:, hn * 64:(hn + 1) * 64], Noff[:, h, :], Us[:, hn * 64:(hn + 1) * 64], True, True, [Noff, Us], [pw])
                TTe("dve", Xs[:], pw[:], Xs[:], ALU.add, [pw, Xs], [Xs])
        py = PSF()
        for h in range(8):
            c, hh = h % 4, h // 4
            hn = 2 * c + hh
            rows = slice(hh * 64, hh * 64 + 64)
            o = py[:, hn * 64:(hn + 1) * 64]
            MM(o, AR[rows, c, 1, :], H0b[rows, c, :], True, False, [AR, H0b], [py])
            MM(o, Arb[:, h, :], Us[:, hn * 64:(hn + 1) * 64], False, False, [Arb, Us], [py])
            MM(o, Ark[:, h, :], tok[:, 0, hn * 64:(hn + 1) * 64], False, True, [Ark, tok], [py])
        CP("act", Y[:], py[:], [py], [Y])
        ph = PSF()
        for c in range(4):
            o = ph[:, c * 128:(c + 1) * 128]
            MM(o, tok[:, 1, c * 128:(c + 1) * 128], Us[:, c * 128:(c + 1) * 128], True, False, [tok, Us], [ph])
            MM(o, tok[:, 2, c * 128:(c + 1) * 128], tok[:, 0, c * 128:(c + 1) * 128], False, True, [tok], [ph])
        for hh in range(2):
            rows = slice(hh * 64, hh * 64 + 64)
            src = ph[rows, :].rearrange("p (c x i) -> p c x i", c=4, x=2)[:, :, hh, :]
            TTe("dve", Ht[rows, :, :], src, H0[rows, :, :], ALU.add, [ph, H0], [Ht])
        for c in range(4):
            TS("dve", H0[:, c, :], Ht[:, c, :], wc[:, c:c + 1], None, ALU.mult, None, [Ht, wc], [H0])
        CP("act", H0b[:], H0[:], [H0], [H0b])
        chk('rw7')
        Y3 = Y[:].rearrange("p (h i) -> p h i", h=8)
        P.op("dve", lambda e: e.tensor_reduce(out=st[:, 1, :], in_=Y3, axis=AX.X, op=ALU.add), ar=[Y3], aw=[st[:, 1, :]])
        ACT(Y2[:], Y[:], AF.Square, [Y], [Y2])
        P.op("dve", lambda e: e.tensor_reduce(out=st[:, 2, :], in_=Y2[:].rearrange("p (h i) -> p h i", h=8), axis=AX.X, op=ALU.add), ar=[Y2[:]], aw=[st[:, 2, :]])
        TS("dve", st[:, 1, :], st[:, 1, :], 1.0 / 64, None, ALU.mult, None, [st], [st])
        TTe("dve", st[:, 3, :], st[:, 1, :], st[:, 1, :], ALU.mult, [st], [st])
        STT("dve", st[:, 2, :], st[:, 2, :], 1.0 / 64, st[:, 3, :], ALU.mult, ALU.subtract, [st], [st])
        ACT(st[:, 2, :], st[:, 2, :], AF.Sqrt, [st], [st], bias=64e-5)
        P.op("dve", lambda e: e.reciprocal(out=st[:, 2, :], in_=st[:, 2, :]), ar=[st[:, 2, :]], aw=[st[:, 2, :]])
        for h in range(8):
            TS("dve", Y2[:, h * 64:(h + 1) * 64], Y[:, h * 64:(h + 1) * 64], st[:, 1, h:h + 1], st[:, 2, h:h + 1], ALU.subtract, ALU.mult, [Y, st], [Y2])
        TTe("dve", Y2[:], Y2[:], gnw[:], ALU.mult, [Y2, gnw], [Y2])
        TTe("dve", Y2[:], Y2[:], gnb[:], ALU.add, [Y2, gnb], [Y2])
        for h in range(8):
            STT("dve", Y2[:, h * 64:(h + 1) * 64], tok[:, 0, h * 64:(h + 1) * 64], st[:, 0, h:h + 1], Y2[:, h * 64:(h + 1) * 64], ALU.mult, ALU.add, [tok, st, Y2], [Y2])
        TTe("dve", yb[:], Y2[:], gt[:], ALU.mult, [Y2, gt], [yb])
        pb = PSB()
        for c in range(4):
            TR(pb[:, c * 128:(c + 1) * 128], yb[:, c * 128:(c + 1) * 128], identb, [yb, cstb], [pb])
        CP("act", YB[1][:, :, ts], pb[:, 0:512].rearrange("p (c t) -> p c t", c=4), [pb], [YB[1]])


def _consts(T):
    bf = ml_dtypes.bfloat16
    cst = np.zeros((128, 5, 128), np.float32)
    cst[:, 0, :] = np.eye(128)
    cst[:, 1, :] = 1.0
    cst[0:64, 2, 0:64] = 1.0
    cst[64:128, 2, 64:128] = 1.0
    cstb = np.zeros((128, 8, 128), np.float32)
    cstb[:, 0, :] = np.eye(128)
    for m in range(128):
        cstb[(m // 64) * 64 + ((m % 64) + 32) % 64, 1, m] = 1.0
    cstb[:, 2, :] = 1.0
    for q_ in range(4):
        cstb[:, 4 + q_, :] = np.eye(128)
    ki = np.arange(128)[:, None]
    qi = np.arange(128)[None, :]
    m12 = np.zeros((128, 2, 256), np.float32)
    for u in range(2):
        m12[:, u, 0:128] = (ki >= qi)
        m12[:, u, 128:256] = (ki <= qi)
    m3 = np.zeros((128, 4, 4, 64), np.float32)
    for j in range(4):
        q = 32 * j + np.arange(32)[None, :]
        for rr in range(4):
            m3[:, j, rr, 0:32] = (ki >= q)
            m3[:, j, rr, 32:64] = (ki <= q)
    mrw = np.zeros((128, 4, 2, 128), np.float32)
    mrw[:, :, 0, :] = (ki < qi)[:, None, :]
    mrw[:, :, 1, :] = (ki <= qi)[:, None, :]
    mlo = np.zeros((128, 4, 128), np.float32)
    mlo[:, :, :] = (qi < ki)[:, None, :]
    sel = np.zeros((128, 4, 8), np.float32)
    for p in range(128):
        for c in range(4):
            sel[p, c, 2 * c + p // 64] = 1.0
    inv = (1.0 / (np.float32(10000.0) ** (np.arange(0, 64, 2, dtype=np.float32) / np.float32(64)))).astype(np.float32)
    ang = (np.arange(T, dtype=np.float32)[:, None] * inv[None, :]).astype(np.float32)
    cosv, sinv = np.cos(ang).astype(np.float32), np.sin(ang).astype(np.float32)
    cosT = np.zeros((128, T), np.float32)
    sinT = np.zeros((128, T), np.float32)
    for p in range(128):
        d = p % 64
        cosT[p] = cosv[:, d % 32]
        sinT[p] = sinv[:, d % 32] * (-1.0 if d < 32 else 1.0)
    return dict(cst_f32=cst, cst_bf=cstb.astype(bf), m12=m12.astype(bf), m3=m3.astype(bf), mrw=mrw.astype(bf),
                mlo=mlo.astype(bf), sel=sel.astype(bf), cosT=cosT, sinT=sinT)


def _col(v, n):
    return np.ascontiguousarray(np.asarray(v, np.float32).reshape(n, 128).T)


def _shared_inputs(inp, T):
    f = lambda a: np.ascontiguousarray(np.asarray(a, np.float32)[:NL]) if np.asarray(a).shape[0] == 2 and np.asarray(a).ndim >= 2 else np.ascontiguousarray(np.asarray(a, np.float32))
    d = dict(_consts(T))
    d["w_ada"] = f(inp["w_ada"])
    d["b_ada"] = np.stack([_col(inp["b_ada"][l], 48) for l in range(NL)])
    d["g_mix"] = np.stack([_col(inp["norm_mix"][l], 8) for l in range(NL)])
    d["g_ffn"] = np.stack([_col(inp["norm_ffn"][l], 8) for l in range(NL)])
    d["g_fin"] = _col(inp["norm_final"], 8)
    d["w_in"] = f(inp["w_in"])
    d["mu"] = np.stack([_col(inp["rwkv_mu"][l], 14) for l in range(NL)])
    d["rwp"] = np.stack([np.stack([_col(np.asarray(inp[k][l]).reshape(-1), 4) for k in
                                   ("rwkv_w0", "rwkv_a0", "rwkv_k_k", "rwkv_k_a", "rwkv_r_k")], axis=1) for l in range(NL)])
    d["r_wup"] = f(inp["rwkv_w_up"])
    d["r_aup"] = f(inp["rwkv_a_up"])
    d["r_gup"] = f(inp["rwkv_g_up"])
    bc = lambda a: np.ascontiguousarray(np.broadcast_to(f(a).reshape(NL, 1, 512), (NL, 128, 512)))
    d["gnw"] = bc(inp["rwkv_gn_w"])
    d["gnb"] = bc(inp["rwkv_gn_b"])
    d["lng"] = bc(inp["gmlp_ln_g"])
    d["lnb"] = bc(inp["gmlp_ln_b"])
    d["wsT"] = np.ascontiguousarray(np.transpose(f(inp["gmlp_w_s"]), (0, 3, 1, 2)))
    bsv = f(inp["gmlp_b_s"])
    bsT = np.zeros((NL, 128, 4, 128), np.float32)
    for g_ in range(8):
        bsT[:, (g_ % 2) * 64:(g_ % 2) * 64 + 64, g_ // 2, :] = bsv[:, g_, None, :]
    d["bsT"] = bsT
    d["w_branch"] = f(inp["w_branch"])
    d["w_out"] = f(inp["w_out"])
    d["ffn_up"] = f(inp["ffn_w_up"])
    d["convw"] = np.stack([np.ascontiguousarray(np.transpose(np.asarray(inp["ffn_conv_w"][l], np.float32).reshape(3, NF, 128), (2, 1, 0))) for l in range(NL)])
    d["convb"] = np.stack([_col(inp["ffn_conv_b"][l], NF) for l in range(NL)])
    d["ffn_down"] = f(inp["ffn_w_down"])
    return d


_NC_CACHE = {}


def run(inp, T=None):
    x = np.asarray(inp["x"], np.float32)
    B, S, _ = x.shape
    T = S
    if T not in _NC_CACHE:
        _NC_CACHE[T] = build(T)
    nc = _NC_CACHE[T]
    shared = _shared_inputs(inp, T)
    c = np.asarray(inp["c"], np.float32)
    in_maps = []
    for core in range(8):
        b = core % B
        m = dict(shared)
        m["xT"] = np.ascontiguousarray(x[b].T)
        m["c128"] = _col(c[b], 8)
        in_maps.append(m)
    res = run_bass_kernel_spmd(nc, in_maps, core_ids=list(range(8)))
    out = np.stack([np.ascontiguousarray(res.results[b]["yT"].T) for b in range(B)])
    return out.astype(np.float32)


def kernel(**inputs):
    return run(inputs)
```

```python
import numpy as np
import ml_dtypes
import concourse.bass as bass
import concourse.mybir as mybir
from concourse.bass_utils import run_bass_kernel_spmd

F32 = mybir.dt.float32
BF16 = mybir.dt.bfloat16
AF = mybir.ActivationFunctionType
ALU = mybir.AluOpType
AX = mybir.AxisListType

D = 1024
import os as _os
NL = int(_os.environ.get("KNL", "2"))
TT = 512
IN_COLS = 7424
DFF = 2816
NF = 22


class Buf:
    __slots__ = ("name", "lw", "rd", "excl")

    def __init__(self, name="", excl=False):
        self.name = name
        self.lw = None
        self.rd = []
        self.excl = excl


class Tl:
    __slots__ = ("t", "b")

    def __init__(self, t, b):
        self.t = t
        self.b = b

    def __getitem__(self, k):
        return self.t[k]


def _b(x):
    return x.b if isinstance(x, Tl) else x


def View(ap, buf):
    return Tl(ap, _b(buf))


class Prog:
    ENGS = ("pe", "dve", "act", "pool", "sp")
    NDMA = 8

    def __init__(self):
        self.nc = bass.Bass("TRN2", target_bir_lowering=False)
        self.ops = {e: [] for e in self.ENGS}
        self.cnt = {e: 0 for e in self.ENGS}
        self.seen = {e: {} for e in self.ENGS}
        self.sems = {}
        self._stack = []
        nc = self.nc
        for e in ("pe", "dve", "act", "pool"):
            self.sems[e] = self._enter(nc.semaphore("s_" + e))
        self.dslots = {}
        for q in ("sp", "act", "pool"):
            sl = []
            for i in range(self.NDMA):
                key = "d_%s%d" % (q, i)
                self.sems[key] = self._enter(nc.semaphore(key))
                sl.append([key, 0])
            self.dslots[q] = [sl, 0]
        self.n_sb = 0
        self.reg = {}

    def _enter(self, cm):
        v = cm.__enter__()
        self._stack.append(cm)
        return v

    def sb(self, shape, dtype, name=None):
        self.n_sb += 1
        t = self._enter(self.nc.sbuf_tensor("s_%s_%d" % (name or "t", self.n_sb), list(shape), dtype))
        return Tl(t, Buf(name or ""))

    def ps(self, shape, dtype=F32, name=None):
        self.n_sb += 1
        t = self._enter(self.nc.psum_tensor(name or ("ps%d" % self.n_sb), list(shape), dtype))
        return Tl(t, Buf(name or "", excl=True))

    def _deps(self, eng, reads, writes):
        toks = []
        for b in reads:
            b = _b(b)
            if b.lw is not None:
                toks.append(b.lw)
        for b in writes:
            b = _b(b)
            if b.lw is not None:
                toks.append(b.lw)
            toks.extend(b.rd)
        need = {}
        for (k, v) in toks:
            if k == "pe" and eng == "pe":
                continue
            if self.seen[eng].get(k, 0) >= v:
                continue
            if need.get(k, 0) < v:
                need[k] = v
        for k, v in need.items():
            self.seen[eng][k] = v
        return list(need.items())

    def _commit(self, tok, reads, writes):
        for b in reads:
            b = _b(b)
            b.rd.append(tok)
            if len(b.rd) > 64:
                mx = {}
                for (k, v) in b.rd:
                    if mx.get(k, 0) < v:
                        mx[k] = v
                b.rd = list(mx.items())
        for b in writes:
            b = _b(b)
            b.lw = tok
            b.rd = []

    @staticmethod
    def _onchip(ap):
        n = ap.name
        return n.startswith("s_") or n.startswith("ps")

    def _reg(self, ap):
        pstep, pcnt = ap.ap[0]
        off = ap.offset
        if ap.name.startswith("ps"):
            return (ap.name, 0, 128, 0, 1 << 30, True)
        p0 = off // pstep
        f0 = off % pstep
        span = 0
        for st, cnt in ap.ap[1:]:
            span += (cnt - 1) * abs(st)
        esz = 2 if ap.dtype == BF16 else 4
        return (ap.name, p0, p0 + pcnt, f0 * esz, (f0 + span + 1) * esz, False)

    def _rdeps(self, eng, ars, aws):
        toks = []
        new = []
        for (aps, isw) in ((ars, False), (aws, True)):
            for ap in aps:
                name, p0, p1, b0, b1, isps = self._reg(ap)
                w = isw or (isps and eng != "pe")
                recs = self.reg.setdefault(name, [])
                for r in recs:
                    if r[0] < p1 and p0 < r[1] and r[2] < b1 and b0 < r[3] and (w or r[5]):
                        toks.append(r[4])
                new.append((name, [p0, p1, b0, b1, None, w]))
        return toks, new

    def _rcommit(self, tok, new):
        for name, rec in new:
            rec[4] = tok
            recs = self.reg[name]
            if rec[5]:
                recs[:] = [r for r in recs if not (rec[0] <= r[0] and r[1] <= rec[1] and rec[2] <= r[2] and r[3] <= rec[3])]
            else:
                recs[:] = [r for r in recs if not ((not r[5]) and r[4][0] == tok[0] and rec[0] <= r[0] and r[1] <= rec[1]
                                                   and rec[2] <= r[2] and r[3] <= rec[3])]
            recs.append(rec)

    def _filter(self, eng, toks):
        need = {}
        for (k, v) in toks:
            if k == "pe" and eng == "pe":
                continue
            if self.seen[eng].get(k, 0) >= v:
                continue
            if need.get(k, 0) < v:
                need[k] = v
        for k, v in need.items():
            self.seen[eng][k] = v
        return list(need.items())

    def _buftoks(self, reads, writes):
        toks = []
        for b in reads:
            if b.lw is not None:
                toks.append(b.lw)
        for b in writes:
            if b.lw is not None:
                toks.append(b.lw)
            toks.extend(b.rd)
        return toks

    def op(self, eng, fn, reads=(), writes=(), ar=None, aw=None):
        if eng == "pool":
            eng = "dve"
        assert ar is not None and aw is not None
        reads = [b for b in reads if isinstance(b, Buf)]
        writes = [b for b in writes if isinstance(b, Buf)]
        rt, new = self._rdeps(eng, ar, aw)
        waits = self._filter(eng, self._buftoks(reads, writes) + rt)
        self.cnt[eng] += 1
        tok = (eng, self.cnt[eng])
        self.ops[eng].append((waits, fn, (eng, 1)))
        self._commit(tok, reads, writes)
        self._rcommit(tok, new)
        return tok

    def dma(self, q, out, in_, reads=(), writes=()):
        sl, idx = self.dslots[q]
        slot = sl[idx % self.NDMA]
        self.dslots[q][1] = idx + 1
        key = slot[0]
        reads = [b for b in reads if isinstance(b, Buf)]
        writes = [b for b in writes if isinstance(b, Buf)]
        ar = [in_] if self._onchip(in_) else []
        aw = [out] if self._onchip(out) else []
        rt, new = self._rdeps(q, ar, aw)
        waits = self._filter(q, self._buftoks(reads, writes) + rt)
        if slot[1] > 0 and self.seen[q].get(key, 0) < slot[1]:
            waits.append((key, slot[1]))
            self.seen[q][key] = slot[1]
        slot[1] += 16
        tok = (key, slot[1])

        def fn(e, out=out, in_=in_):
            return e.dma_start(out=out, in_=in_)
        self.ops[q].append((waits, fn, (key, 16)))
        self._commit(tok, reads, writes)
        self._rcommit(tok, new)
        return tok

    def fence(self, src, dst):
        return
        toks = []
        for s in src:
            s = _b(s)
            if s.lw is not None:
                toks.append(s.lw)
            toks.extend(s.rd)
        for d in dst:
            _b(d).rd.extend(toks)

    def finish(self, final_bufs):
        waits = self._filter("sp", self._buftoks([_b(b) for b in final_bufs], []))
        self.ops["sp"].append((waits, None, None))
        nc = self.nc
        sems = self.sems
        ops = self.ops
        with nc.Block() as block:
            def mk(e):
                def body(eng):
                    for (waits, fn, inc) in ops[e]:
                        for (k, v) in waits:
                            eng.wait_ge(sems[k], v)
                        if fn is not None:
                            ins = fn(eng)
                            ins.then_inc(sems[inc[0]], inc[1])
                return body
            block.tensor(mk("pe"))
            block.vector(mk("dve"))
            block.scalar(mk("act"))
            block.gpsimd(mk("pool"))
            block.sync(mk("sp"))
        for cm in reversed(self._stack):
            cm.__exit__(None, None, None)
        self._stack = []
        return nc


class _Stop(Exception):
    pass


def build(T, dbg=False):
    import os
    STOP = os.environ.get("KSTOP")
    P = Prog()

    def chk(name):
        if STOP == name:
            raise _Stop()
    nc = P.nc
    NTI = T // TT
    NB = T // 128

    def din(name, shape, dt=F32):
        return nc.dram_tensor(name, list(shape), dt, kind="ExternalInput").ap()

    xT_in = din("xT", [D, T])
    c_in = din("c128", [128, 8])
    w_ada = din("w_ada", [NL, D, 6 * D])
    b_ada = din("b_ada", [NL, 128, 48])
    g_mix = din("g_mix", [NL, 128, 8])
    g_ffn = din("g_ffn", [NL, 128, 8])
    g_fin = din("g_fin", [128, 8])
    w_in = din("w_in", [NL, D, IN_COLS])
    mu_d = din("mu", [NL, 128, 14])
    pr_d = din("rwp", [NL, 128, 5, 4])
    wup_d = din("r_wup", [NL, 64, 512])
    aup_d = din("r_aup", [NL, 64, 512])
    gup_d = din("r_gup", [NL, 128, 512])
    gnw_d = din("gnw", [NL, 128, 512])
    gnb_d = din("gnb", [NL, 128, 512])
    lng_d = din("lng", [NL, 128, 512])
    lnb_d = din("lnb", [NL, 128, 512])
    wsT_d = din("wsT", [NL, 128, 8, 128])
    bs_d = din("bsT", [NL, 128, 4, 128])
    wbr_d = din("w_branch", [NL, 3, 512, D])
    wout_d = din("w_out", [NL, D, D])
    wfu_d = din("ffn_up", [NL, D, 2 * DFF])
    cw_d = din("convw", [NL, 128, NF, 3])
    cb_d = din("convb", [NL, 128, NF])
    wfd_d = din("ffn_down", [NL, DFF, D])
    cos_d = din("cosT", [128, T])
    sin_d = din("sinT", [128, T])
    cst_d = din("cst_f32", [128, 5, 128])
    cstb_d = din("cst_bf", [128, 8, 128], BF16)
    m12_d = din("m12", [128, 2, 256], BF16)
    m3_d = din("m3", [128, 4, 4, 64], BF16)
    mrw_d = din("mrw", [128, 4, 2, 128], BF16)
    mlo_d = din("mlo", [128, 4, 128], BF16)
    sel_d = din("sel", [128, 4, 8], BF16)
    yT_out = nc.dram_tensor("yT", [D, T], F32, kind="ExternalOutput").ap()
    x1T = nc.dram_tensor("x1T", [D, T], F32, kind="Internal").ap()
    Vd = [nc.dram_tensor("Vd%d" % l, [T, 512], BF16, kind="Internal").ap() for l in range(NL)]
    b_x1 = Buf("x1T")
    b_vd = [Buf("vd0"), Buf("vd1")]
    b_out = Buf("out")

    def MM(out, lhsT, rhs, start, stop, r, w):
        P.op("pe", lambda e: e.matmul(out, lhsT, rhs, start=start, stop=stop), reads=r, writes=w, ar=[lhsT, rhs], aw=[out])

    def TR(out, in_, ident, r, w):
        P.op("pe", lambda e: e.transpose(out, in_, ident), reads=r, writes=w, ar=[in_, ident], aw=[out])

    def ACT(out, in_, func, r, w, bias=0.0, scale=1.0):
        P.op("act", lambda e: e.activation(out=out, in_=in_, func=func, bias=bias, scale=scale), reads=r, writes=w,
             ar=[in_] + [x for x in (bias, scale) if hasattr(x, "offset")], aw=[out])

    def TTe(eng, out, in0, in1, op, r, w):
        P.op(eng, lambda e: e.tensor_tensor(out=out, in0=in0, in1=in1, op=op), reads=r, writes=w, ar=[in0, in1], aw=[out])

    def TS(eng, out, in0, s1, s2, op0, op1, r, w):
        if s2 is None:
            P.op(eng, lambda e: e.tensor_scalar(out=out, in0=in0, scalar1=s1, scalar2=None, op0=op0), reads=r, writes=w,
                 ar=[in0] + [x for x in (s1,) if hasattr(x, "offset")], aw=[out])
        else:
            P.op(eng, lambda e: e.tensor_scalar(out=out, in0=in0, scalar1=s1, scalar2=s2, op0=op0, op1=op1), reads=r, writes=w,
                 ar=[in0] + [x for x in (s1, s2) if hasattr(x, "offset")], aw=[out])

    def STT(eng, out, in0, sc, in1, op0, op1, r, w):
        P.op(eng, lambda e: e.scalar_tensor_tensor(out=out, in0=in0, scalar=sc, in1=in1, op0=op0, op1=op1), reads=r, writes=w,
             ar=[in0, in1] + [x for x in (sc,) if hasattr(x, "offset")], aw=[out])

    def CP(eng, out, in_, r, w):
        if eng == "act":
            P.op("act", lambda e: e.copy(out=out, in_=in_), reads=r, writes=w, ar=[in_], aw=[out])
        else:
            P.op(eng, lambda e: e.tensor_copy(out=out, in_=in_), reads=r, writes=w, ar=[in_], aw=[out])

    def MSET(eng, ap, val, w):
        P.op(eng, lambda e: e.memset(ap, val), writes=w, ar=[], aw=[ap])

    psf = [P.ps([128, 512], F32) for _ in range(6)]
    psb = [P.ps([128, 1024], BF16) for _ in range(2)]
    psi = [0, 0]

    def PSF():
        psi[0] += 1
        return psf[psi[0] % 6]

    def PSB():
        psi[1] += 1
        return psb[psi[1] % 2]

    cst = P.sb([128, 5, 128], F32)
    cstb = P.sb([128, 8, 128], BF16)
    m12 = P.sb([128, 2, 256], BF16)
    m3 = P.sb([128, 4, 4, 64], BF16)
    mrw = P.sb([128, 4, 2, 128], BF16)
    mlo = P.sb([128, 4, 128], BF16)
    sel = P.sb([128, 4, 8], BF16)
    for (t_, d_) in ((cst, cst_d), (cstb, cstb_d), (m12, m12_d), (m3, m3_d), (mrw, mrw_d), (mlo, mlo_d), (sel, sel_d)):
        P.dma("sp", t_[:], d_, writes=[t_])
    identf = cst[:, 0, :]
    onesf = cst[:, 1, :]
    blkones = cst[:, 2, :]
    identb = cstb[:, 0, :]
    permb = cstb[:, 1, :]
    onesb = cstb[:, 2, :]

    xt = P.sb([128, 8, TT], F32, "xt")
    scr = P.sb([128, 8, TT], F32, "scr")
    hT = P.sb([128, 8, TT], BF16, "hT")
    rstd = P.sb([128, TT], F32, "rstd")
    NWB = 2
    wb = [P.sb([128, 8, 512], BF16, "wb%d" % i) for i in range(NWB)]
    wbi = [0]
    QT = P.sb([128, 4, TT], BF16, "QT")
    KT = P.sb([128, 4, T], BF16, "KT")
    V1r = P.sb([128, 8, 512], BF16, "V1r")
    V2 = P.sb([128, 2, 4, 512], BF16, "V2")
    V3g = [P.sb([128, 2, 4, 512], BF16, "V3g0")] * 2
    acc = P.sb([128, 4, 2, TT], F32, "acc")
    big = P.sb([128, NF, TT], BF16, "big")
    YB = [P.sb([128, 4, TT], BF16, "YB%d" % i) for i in range(3)]
    cosb = P.sb([128, TT], F32, "cos")
    sinb = P.sb([128, TT], F32, "sin")
    zcar = P.sb([128, 14], F32, "zcar")
    H0 = P.sb([128, 4, 64], F32, "H0")
    H0b = P.sb([128, 4, 64], BF16, "H0b")
    ccar = P.sb([128, NF, 2], F32, "ccar")
    MSET("pool", KT[:], 0.0, [KT])

    modp = P.sb([128, NL, 48], F32, "mod")
    A1 = P.sb([128, NL, 8], F32, "A1")
    A2 = P.sb([128, NL, 8], F32, "A2")
    gfin = P.sb([128, 8], F32, "gfin")
    cact = P.sb([128, 8], F32, "cact")
    wtmp = scr
    bad = P.sb([128, NL, 48], F32, "bad")
    gm = P.sb([128, NL, 8], F32, "gm")
    gf = P.sb([128, NL, 8], F32, "gf")
    P.dma("sp", cact[:], c_in, writes=[cact])
    P.dma("sp", gfin[:], g_fin, writes=[gfin])
    for l in range(NL):
        P.dma("sp", bad[:, l, :], b_ada[l], writes=[bad])
        P.dma("sp", gm[:, l, :], g_mix[l], writes=[gm])
        P.dma("sp", gf[:, l, :], g_ffn[l], writes=[gf])
    ACT(cact[:], cact[:], AF.Silu, [cact], [cact])
    for l in range(NL):
        for pc in range(12):
            P.dma("sp", wtmp[:], w_ada[l][:, pc * 512:(pc + 1) * 512].rearrange("(k p) n -> p k n", p=128), writes=[wtmp])
            ps = PSF()
            for mc in range(4):
                for k in range(8):
                    MM(ps[:, mc:mc + 1], wtmp[:, k, mc * 128:(mc + 1) * 128], cact[:, k:k + 1], k == 0, k == 7, [wtmp, cact], [ps])
            TTe("dve", modp[:, l, pc * 4:(pc + 1) * 4], ps[:, 0:4], bad[:, l, pc * 4:(pc + 1) * 4], ALU.add, [ps, bad], [modp])
        STT("dve", A1[:, l, :], modp[:, l, 8:16], 1.0, gm[:, l, :], ALU.add, ALU.mult, [modp, gm], [A1])
        STT("dve", A2[:, l, :], modp[:, l, 32:40], 1.0, gf[:, l, :], ALU.add, ALU.mult, [modp, gf], [A2])

    mu = P.sb([128, 14], F32, "mu")
    omu = P.sb([128, 14], F32, "omu")
    rwp = P.sb([128, 5, 4], F32, "rwp")
    omka = P.sb([128, 4], F32, "omka")
    nw0 = P.sb([128, 4], F32, "nw0")
    wupb = P.sb([128, 512], BF16, "wupb")
    aupb = P.sb([128, 512], BF16, "aupb")
    gupb = P.sb([128, 512], BF16, "gupb")
    gnw = P.sb([128, 512], F32, "gnw")
    gnb = P.sb([128, 512], F32, "gnb")
    lng = P.sb([128, 512], F32, "lng")
    lnb = P.sb([128, 512], F32, "lnb")
    wsT = P.sb([128, 8, 128], BF16, "wsT")
    wsTf = View(scr[:, 0:2, :].rearrange("p a (g i) -> p (a g) i", g=4), scr)
    bsT = P.sb([128, 4, 128], F32, "bsT")
    cw = P.sb([128, NF, 3], F32, "cw")
    cb = P.sb([128, NF], F32, "cb")

    def load_w(src_ap, kc, ncols):
        wbi[0] += 1
        w = wb[wbi[0] % NWB]
        P.dma("pool", w[:, 0:kc, 0:ncols], src_ap.rearrange("(k p) n -> p k n", p=128), writes=[w])
        return w

    def rmsnorm_mod(Acol, shcol, l):
        ACT(scr[:], xt[:], AF.Square, [xt], [scr])
        ps = PSF()
        for k in range(8):
            MM(ps[:], onesf, scr[:, k, :], k == 0, k == 7, [cst, scr], [ps])
        ACT(rstd[:], ps[:], AF.Sqrt, [ps], [rstd], bias=1e-6, scale=1.0 / D)
        P.op("dve", lambda e: e.reciprocal(out=rstd[:], in_=rstd[:]), ar=[rstd[:]], aw=[rstd[:]])
        for k in range(8):
            STT("dve", scr[:, k, :], xt[:, k, :], Acol[:, k:k + 1], rstd[:], ALU.mult, ALU.mult, [xt, rstd, A1, A2, gfin], [scr])
            if shcol is not None:
                TS("pool", hT[:, k, :], scr[:, k, :], shcol[:, k:k + 1], None, ALU.add, None, [scr, modp], [hT])

    GC = 0.7978845608028654

    def gelu_from(out, src, rd, wr, tmpa, tmpb):
        ACT(out, src, AF.Gelu_apprx_tanh, rd, wr)

    try:
        for l in range(NL):
            x_src = xT_in if l == 0 else x1T
            P.dma("sp", mu[:], mu_d[l], writes=[mu])
            TS("dve", omu[:], mu[:], -1.0, 1.0, ALU.mult, ALU.add, [mu], [omu])
            P.dma("sp", rwp[:], pr_d[l], writes=[rwp])
            TS("dve", omka[:], rwp[:, 3, :], -1.0, 1.0, ALU.mult, ALU.add, [rwp], [omka])
            TS("dve", nw0[:], rwp[:, 0, :], -1.0, None, ALU.mult, None, [rwp], [nw0])
            P.dma("pool", wupb[0:64, :], wup_d[l], writes=[wupb])
            P.dma("pool", aupb[64:128, :], aup_d[l], writes=[aupb])
            P.dma("pool", gupb[:], gup_d[l], writes=[gupb])
            for (t_, d_) in ((gnw, gnw_d), (gnb, gnb_d), (lng, lng_d), (lnb, lnb_d)):
                P.dma("sp", t_[:], d_[l], writes=[t_])
            P.dma("sp", wsTf[:], wsT_d[l], writes=[wsTf])
            for g in range(8):
                TTe("dve", wsT[:, g, :], wsTf[:, g, :], mrw[:, 0, 1, :], ALU.mult, [wsTf, mrw], [wsT])
            P.dma("sp", bsT[:], bs_d[l], writes=[bsT])
            P.dma("sp", cw[:], cw_d[l], writes=[cw])
            P.dma("sp", cb[:], cb_d[l], writes=[cb])
            MSET("pool", zcar[:], 0.0, [zcar])
            MSET("pool", H0[:], 0.0, [H0])
            MSET("pool", H0b[:], 0.0, [H0b])
            MSET("pool", ccar[:], 0.0, [ccar])
            MSET("pool", big[:, 0:4, :], 0.0, [big])
            for q in range(T // 512):
                P.dma("sp", Vd[l][q * 512:(q + 1) * 512, :].rearrange("(p a) c -> p (a c)", p=128), big[:, 0:4, :].rearrange("p a b -> p (a b)"), reads=[big], writes=[b_vd[l]])

            for it in range(NTI):
                t0 = it * TT
                P.dma("sp", xt[:], x_src[:, t0:t0 + TT].rearrange("(k p) t -> p k t", p=128), reads=[b_x1] if l else [], writes=[xt])
                P.dma("sp", cosb[:], cos_d[:, t0:t0 + TT], writes=[cosb])
                P.dma("sp", sinb[:], sin_d[:, t0:t0 + TT], writes=[sinb])
                chk('pro')
                rmsnorm_mod(A1[:, l, :], modp[:, l, 0:8], l)
                chk('norm')

                for pc in range(2):
                    w = load_w(w_in[l][:, pc * 512:(pc + 1) * 512], 8, 512)
                    for mc in range(4):
                        ps = PSF()
                        for k in range(8):
                            MM(ps[:], w[:, k, mc * 128:(mc + 1) * 128], hT[:, k, :], k == 0, k == 7, [w, hT], [ps])
                        chk('qk1')
                        qs = scr[:, mc, :]
                        qbf = big[:, 14 + mc, :]
                        CP("act", qbf, ps[:], [ps], [big])
                        ps2 = PSF()
                        MM(ps2[:], permb, qbf, True, True, [cstb, big], [ps2])
                        chk('qk2')
                        TTe("dve", qs, ps[:], cosb[:], ALU.mult, [ps, cosb, big, ps2], [scr])
                        TTe("dve", scr[:, 4 + mc, :], ps2[:], sinb[:], ALU.mult, [ps2, sinb], [scr])
                        chk('qk3')
                        dst = QT[:, mc, :] if pc == 0 else KT[:, mc, t0:t0 + TT]
                        TTe("pool", dst, qs, scr[:, 4 + mc, :], ALU.add, [scr], [QT if pc == 0 else KT])
                chk('qk')
                w = load_w(w_in[l][:, 1024:1536], 8, 512)
                for blk in range(4):
                    ps = PSF()
                    for k in range(8):
                        MM(ps[:], hT[:, k, blk * 128:(blk + 1) * 128], w[:, k, :], k == 0, k == 7, [w, hT], [ps])
                    gb = 4 * it + blk
                    CP("act", V1r[:, gb % 8, :], ps[:], [ps], [V1r])
                    P.dma("sp", Vd[l][t0 + blk * 128:t0 + (blk + 1) * 128, :], V1r[:, gb % 8, :], reads=[V1r], writes=[b_vd[l]])
                P.dma("sp", V2[:, it % 2, :, :], Vd[l][t0:t0 + TT, :].rearrange("(i r) c -> i r c", r=4), reads=[b_vd[l]], writes=[V2])

                chk('v')
                if "EP" not in P.__dict__:
                    P.EP = [(View(big[:, 16 + 2 * i_, :], Buf("E%d" % i_)), View(big[:, 17 + 2 * i_, :], Buf("P%d" % i_))) for i_ in range(3)]
                    P.epi = 0
                EPb = [x for pr in P.EP for x in pr]
                P.fence([big], EPb)
                for cfg in range(3):
                    n_sp = t0 // min(2048, T)
                    j3 = (t0 % min(2048, T)) // TT
                    if cfg < 2:
                        items = [(None, c, hh) for c in range(4) for hh in range(2)]
                    else:
                        items = [(rg_, c, hh) for rg_ in range(4) for c in range(4) for hh in range(2)]
                    for (rg_, c, hh) in items:
                        if True:
                            rows = slice(hh * 64, hh * 64 + 64)
                            if cfg < 2:
                                for half in range(2):
                                    psS = PSF()
                                    hp = []
                                    for u in range(2):
                                        qi = half * 2 + u
                                        if cfg == 0:
                                            gb = 4 * it + qi
                                            q_ap = QT[rows, c, qi * 128:(qi + 1) * 128]
                                            kc_ap = KT[rows, c, gb * 128:(gb + 1) * 128]
                                            kp_ap = KT[rows, c, (gb - 1) * 128:gb * 128] if gb > 0 else None
                                            vc_ap = V1r[:, gb % 8, c * 128:(c + 1) * 128]
                                            vp_ap = V1r[:, (gb - 1) % 8, c * 128:(c + 1) * 128]
                                            vb = V1r
                                        else:
                                            q_ap = QT[rows, c, qi:TT:4]
                                            kc_ap = KT[rows, c, t0 + qi:t0 + TT:4]
                                            kp_ap = KT[rows, c, t0 - TT + qi:t0:4] if it > 0 else None
                                            vc_ap = V2[:, it % 2, qi, c * 128:(c + 1) * 128]
                                            vp_ap = V2[:, (it - 1) % 2, qi, c * 128:(c + 1) * 128]
                                            vb = V2
                                        hp.append(kp_ap is not None)
                                        if kp_ap is not None:
                                            MM(psS[:, u * 256:u * 256 + 128], kp_ap, q_ap, True, True, [KT, QT], [psS])
                                        MM(psS[:, u * 256 + 128:u * 256 + 256], kc_ap, q_ap, True, True, [KT, QT], [psS])
                                        hp.append((vp_ap, vc_ap, vb))
                                    P.epi += 1
                                    Et, Pt = P.EP[P.epi % 3]
                                    for u in range(2):
                                        if hp[2 * u]:
                                            ACT(Et[:, u * 256:(u + 1) * 256], psS[:, u * 256:(u + 1) * 256], AF.Exp, [psS], [Et], scale=0.125)
                                        else:
                                            MSET("pool", Et[:, u * 256:u * 256 + 128], 0.0, [Et])
                                            ACT(Et[:, u * 256 + 128:u * 256 + 256], psS[:, u * 256 + 128:u * 256 + 256], AF.Exp, [psS], [Et], scale=0.125)
                                    Pm = Pt[:]
                                    TTe("dve", Pm, Et[:], m12[:].rearrange("p a b -> p (a b)"), ALU.mult, [Et, m12], [Pt])
                                    psN = PSF()
                                    for u in range(2):
                                        vp_ap, vc_ap, vb = hp[2 * u + 1]
                                        has_prev = hp[2 * u]
                                        for z in range(2):
                                            o = psN[:, z * 256 + u * 128:z * 256 + u * 128 + 128]
                                            if has_prev:
                                                MM(o, vp_ap if z == 0 else onesb, Pm[:, u * 256:u * 256 + 128], True, False, [vb, Pt, cstb], [psN])
                                            MM(o, vc_ap if z == 0 else onesb, Pm[:, u * 256 + 128:u * 256 + 256], not has_prev, True, [vb, Pt, cstb], [psN])
                                    src = psN[rows, :].rearrange("p (z u q) -> p z u q", z=2, u=2)
                                    if cfg == 0:
                                        dsta = acc[rows, c, :, half * 256:(half + 1) * 256].rearrange("p z (u q) -> p z u q", u=2)
                                        CP("act", dsta, src, [psN], [acc])
                                    else:
                                        dsta = acc[rows, c, :, :].rearrange("p z (q r) -> p z r q", r=4)[:, :, half * 2:half * 2 + 2, :]
                                        TTe("dve", dsta, src, dsta, ALU.add, [psN, acc], [acc])
                            else:
                                SPAN = min(2048, T)
                                nk3 = SPAN // 16
                                kp_ = slice(0, nk3)
                                for rg in (rg_,):
                                    if c == 0 and hh == 0:
                                        v3 = V3g[rg % 2]
                                        for sp_ in range(2):
                                            nn = n_sp - 1 + sp_
                                            if nn < 0:
                                                continue
                                            P.dma("sp", v3[kp_, sp_, :, :],
                                                  Vd[l][nn * SPAN:(nn + 1) * SPAN, :].rearrange("(i r) c -> i r c", r=16)[:, rg * 4:rg * 4 + 4, :],
                                                  reads=[b_vd[l]], writes=[v3])
                                    v3 = V3g[rg % 2]
                                    psS = PSF()
                                    hasp = n_sp > 0
                                    for rr in range(4):
                                        r = rg * 4 + rr
                                        q_ap = QT[rows, c, r:TT:16]
                                        kc_ap = KT[rows, c, n_sp * SPAN + r:(n_sp + 1) * SPAN:16]
                                        if hasp:
                                            kp_ap = KT[rows, c, (n_sp - 1) * SPAN + r:n_sp * SPAN:16]
                                            MM(psS[kp_, rr * 64:rr * 64 + 32], kp_ap, q_ap, True, True, [KT, QT], [psS])
                                        MM(psS[kp_, rr * 64 + 32:rr * 64 + 64], kc_ap, q_ap, True, True, [KT, QT], [psS])
                                    P.epi += 1
                                    Et, Pt = P.EP[P.epi % 3]
                                    if not hasp:
                                        MSET("pool", Et[kp_, 0:256], 0.0, [Et])
                                        for rr in range(4):
                                            ACT(Et[kp_, rr * 64 + 32:rr * 64 + 64], psS[kp_, rr * 64 + 32:rr * 64 + 64], AF.Exp, [psS], [Et], scale=0.125)
                                    else:
                                        ACT(Et[kp_, 0:256], psS[kp_, 0:256], AF.Exp, [psS], [Et], scale=0.125)
                                    Pm = Pt[kp_, 0:256]
                                    TTe("dve", Pm, Et[kp_, 0:256], m3[kp_, j3, :, :].rearrange("p a b -> p (a b)"), ALU.mult, [Et, m3], [Pt])
                                    psN = PSF()
                                    for rr in range(4):
                                        for z in range(2):
                                            o = psN[:, z * 128 + rr * 32:z * 128 + rr * 32 + 32]
                                            if hasp:
                                                MM(o, v3[kp_, 0, rr, c * 128:(c + 1) * 128] if z == 0 else onesb[kp_, :], Pm[:, rr * 64:rr * 64 + 32], True, False, [v3, Pt, cstb], [psN])
                                            MM(o, v3[kp_, 1, rr, c * 128:(c + 1) * 128] if z == 0 else onesb[kp_, :], Pm[:, rr * 64 + 32:rr * 64 + 64], not hasp, True, [v3, Pt, cstb], [psN])
                                    src = psN[rows, 0:256].rearrange("p (z r q) -> p z r q", z=2, r=4)
                                    dsta = acc[rows, c, :, :].rearrange("p z (q r) -> p z r q", r=16)[:, :, rg * 4:rg * 4 + 4, :]
                                    TTe("dve", dsta, src, dsta, ALU.add, [psN, acc], [acc])
                P.fence(EPb, [big])
                for c in range(4):
                    P.op("dve", lambda e, c=c: e.reciprocal(out=acc[:, c, 1, :], in_=acc[:, c, 1, :]), ar=[acc[:, c, 1, :]], aw=[acc[:, c, 1, :]])
                    TTe("dve", YB[0][:, c, :], acc[:, c, 0, :], acc[:, c, 1, :], ALU.mult, [acc], [YB[0]])
                chk('att')

                for pc in range(4):
                    ncol = 512 if pc < 3 else 256
                    w = load_w(w_in[l][:, 1536 + pc * 512:1536 + pc * 512 + ncol], 8, ncol)
                    for mc in range(ncol // 128):
                        ch = pc * 4 + mc
                        ps = PSF()
                        for k in range(8):
                            MM(ps[:], w[:, k, mc * 128:(mc + 1) * 128], hT[:, k, :], k == 0, k == 7, [w, hT], [ps])
                        zb = scr[:, ch % 4, :]
                        zp = scr[:, 4 + ch % 4, :]
                        CP("act", zb, ps[:], [ps], [scr])
                        CP("pool", zp[:, 1:TT], zb[:, 0:TT - 1], [scr], [scr])
                        CP("pool", zp[:, 0:1], zcar[:, ch:ch + 1], [zcar, scr], [scr])
                        CP("pool", zcar[:, ch:ch + 1], zb[:, TT - 1:TT], [scr], [zcar])
                        TS("dve", zb, zb, omu[:, ch:ch + 1], None, ALU.mult, None, [scr, omu], [scr])
                        STT("dve", big[:, ch, :], zp, mu[:, ch:ch + 1], zb, ALU.mult, ALU.add, [scr, mu], [big])
                chk('rwin'); rwkv_tile(P, locals()); chk('rw')

                gmlp_tile(P, locals()); chk('gmlp')

                wbrs = []
                for br in range(3):
                    for half in range(2):
                        wg = load_w(w_in[l][:, 4352 + br * 1024 + half * 512:4352 + br * 1024 + (half + 1) * 512], 8, 512)
                        wbt = load_w(wbr_d[l, br][:, half * 512:(half + 1) * 512], 4, 512)
                        for mc in range(4):
                            m = half * 4 + mc
                            psg = PSF()
                            for k in range(8):
                                MM(psg[:], wg[:, k, mc * 128:(mc + 1) * 128], hT[:, k, :], k == 0, k == 7, [wg, hT], [psg])
                            psbr = PSF()
                            for k in range(4):
                                MM(psbr[:], wbt[:, k, mc * 128:(mc + 1) * 128], YB[br][:, k, :], k == 0, k == 3, [wbt, YB[br]], [psbr])
                            sg = acc[:, (br * 8 + m) % 4, 0, :]
                            ACT(sg, psg[:], AF.Sigmoid, [psg], [acc])
                            if br == 0:
                                TTe("dve", scr[:, m, :], sg, psbr[:], ALU.mult, [acc, psbr], [scr])
                            else:
                                TTe("dve", sg, sg, psbr[:], ALU.mult, [acc, psbr], [acc])
                                TTe("pool", scr[:, m, :], scr[:, m, :], sg, ALU.add, [acc, scr], [scr])
                chk('br')
                for m in range(8):
                    CP("act", hT[:, m, :], scr[:, m, :], [scr], [hT])
                for half in range(2):
                    w = load_w(wout_d[l][:, half * 512:(half + 1) * 512], 8, 512)
                    for mc in range(4):
                        m = half * 4 + mc
                        ps = PSF()
                        for k in range(8):
                            MM(ps[:], w[:, k, mc * 128:(mc + 1) * 128], hT[:, k, :], k == 0, k == 7, [w, hT], [ps])
                        STT("dve", xt[:, m, :], ps[:], modp[:, l, 16 + m:17 + m], xt[:, m, :], ALU.mult, ALU.add, [ps, modp, xt], [xt])

                chk('out')
                rmsnorm_mod(A2[:, l, :], modp[:, l, 24:32], l)
                for fg in range(6):
                    nf = 4 if fg < 5 else 2
                    wa = load_w(wfu_d[l][:, fg * 512:fg * 512 + nf * 128], 8, nf * 128)
                    wg = load_w(wfu_d[l][:, DFF + fg * 512:DFF + fg * 512 + nf * 128], 8, nf * 128)
                    for mc in range(nf):
                        f = fg * 4 + mc
                        psa = PSF()
                        for k in range(8):
                            MM(psa[:], wa[:, k, mc * 128:(mc + 1) * 128], hT[:, k, :], k == 0, k == 7, [wa, hT], [psa])
                        psg = PSF()
                        for k in range(8):
                            MM(psg[:], wg[:, k, mc * 128:(mc + 1) * 128], hT[:, k, :], k == 0, k == 7, [wg, hT], [psg])
                        fb_ = 2 * (f % 2)
                        ab = acc[:, fb_, :, :].rearrange("p a b -> p (a b)")
                        CP("act", ab[:, 2:2 + TT], psa[:], [psa], [acc])
                        CP("pool", ab[:, 0:2], ccar[:, f, :], [ccar, acc], [acc])
                        CP("pool", ccar[:, f, :], ab[:, TT:TT + 2], [acc], [ccar])
                        c1 = acc[:, fb_ + 1, 0, :]
                        TS("dve", c1, ab[:, 2:2 + TT], cw[:, f, 2:3], cb[:, f:f + 1], ALU.mult, ALU.add, [acc, cw, cb], [acc])
                        STT("dve", c1, ab[:, 1:1 + TT], cw[:, f, 1:2], c1, ALU.mult, ALU.add, [acc, cw], [acc])
                        STT("dve", c1, ab[:, 0:TT], cw[:, f, 0:1], c1, ALU.mult, ALU.add, [acc, cw], [acc])
                        ge = acc[:, fb_ + 1, 1, :]
                        gelu_from(ge, c1, [acc], [acc], None, None)
                        TTe("dve", big[:, f, :], ge, psg[:], ALU.mult, [acc, psg], [big])
                for m in range(8):
                    wbi[0] += 1
                    wd = wb[wbi[0] % NWB]
                    wdv = wd[:].rearrange("p a b -> p (a b)")[:, 0:NF * 128].rearrange("p (f n) -> p f n", f=NF)
                    P.dma("pool", wdv, wfd_d[l][:, m * 128:(m + 1) * 128].rearrange("(f p) n -> p f n", p=128), writes=[wd])
                    ps = PSF()
                    for f in range(NF):
                        MM(ps[:], wdv[:, f, :], big[:, f, :], f == 0, f == NF - 1, [wd, big], [ps])
                    STT("dve", xt[:, m, :], ps[:], modp[:, l, 40 + m:41 + m], xt[:, m, :], ALU.mult, ALU.add, [ps, modp, xt], [xt])

                chk('ffn')
                if l < NL - 1:
                    P.dma("sp", x1T[:, t0:t0 + TT].rearrange("(k p) t -> p k t", p=128), xt[:], reads=[xt], writes=[b_x1])
                else:
                    rmsnorm_mod(gfin[:], None, l)
                    P.dma("sp", yT_out[:, t0:t0 + TT].rearrange("(k p) t -> p k t", p=128), scr[:], reads=[scr], writes=[b_out])
    except _Stop:
        src_t = xt
        ybi = {"att": 0, "rw": 1, "gmlp": 2}.get(STOP)
        if ybi is not None:
            MSET("pool", scr[:], 0.0, [scr])
            for c_ in range(4):
                CP("dve", scr[:, c_, :], YB[ybi][:, c_, :], [YB[ybi]], [scr])
            src_t = scr
        elif STOP == "br":
            src_t = scr
        elif STOP == "rw7":
            MSET("pool", scr[:], 0.0, [scr])
            CP("dve", scr[:, 0, :], acc[:, 0, 0, :], [acc], [scr])
            src_t = scr
        P.dma("sp", yT_out[:, 0:TT].rearrange("(k p) t -> p k t", p=128), src_t[:], reads=[src_t], writes=[b_out])
    return P.finish([b_out])


def gmlp_tile(P, L):
    g = L
    PSF, MM, ACT, TTe, TS, STT, CP = g["PSF"], g["MM"], g["ACT"], g["TTe"], g["TS"], g["STT"], g["CP"]
    hT, big, acc, scr, YB, wsT, bsT, lng, lnb = g["hT"], g["big"], g["acc"], g["scr"], g["YB"], g["wsT"], g["bsT"], g["lng"], g["lnb"]
    load_w, w_in, l, gelu_from = g["load_w"], g["w_in"], g["l"], g["gelu_from"]
    w = load_w(w_in[l][:, 3328:3840], 8, 512)
    for mc in range(4):
        ps = PSF()
        for k in range(8):
            MM(ps[:], w[:, k, mc * 128:(mc + 1) * 128], hT[:, k, :], k == 0, k == 7, [w, hT], [ps])
        gelu_from(scr[:, mc, :], ps[:], [ps], [scr], scr[:, 4, :], scr[:, 5, :])
    w = load_w(w_in[l][:, 3840:4352], 8, 512)
    stats = acc[:, 3, 1, 0:8]
    for blk in range(4):
        ps = PSF()
        for k in range(8):
            MM(ps[:], hT[:, k, blk * 128:(blk + 1) * 128], w[:, k, :], k == 0, k == 7, [w, hT], [ps])
        v = acc[:, 0, 0, :]
        gelu_from(v, ps[:], [ps], [acc], acc[:, 0, 1, :], acc[:, 1, 0, :])
        P.op("dve", lambda e: e.bn_stats(out=acc[:, 3, 1, 0:6], in_=v), ar=[v], aw=[acc[:, 3, 1, 0:6]])
        P.op("dve", lambda e: e.bn_aggr(out=acc[:, 3, 1, 6:8], in_=acc[:, 3, 1, 0:6]), ar=[acc[:, 3, 1, 0:6]], aw=[acc[:, 3, 1, 6:8]])
        ACT(acc[:, 3, 1, 7:8], acc[:, 3, 1, 7:8], AF.Sqrt, [acc], [acc], bias=1e-5)
        P.op("dve", lambda e: e.reciprocal(out=acc[:, 3, 1, 7:8], in_=acc[:, 3, 1, 7:8]), ar=[acc[:, 3, 1, 7:8]], aw=[acc[:, 3, 1, 7:8]])
        TS("dve", v, v, acc[:, 3, 1, 6:7], acc[:, 3, 1, 7:8], ALU.subtract, ALU.mult, [acc], [acc])
        TTe("dve", v, v, lng[:], ALU.mult, [acc, lng], [acc])
        vb = big[:, 18, :]
        TTe("dve", vb, v, lnb[:], ALU.add, [acc, lnb], [big])
        psA = PSF()
        psB = PSF()
        for gq in range(8):
            pp = psA if gq < 4 else psB
            c = gq // 2
            MM(pp[:, (gq % 4) * 128:(gq % 4) * 128 + 128], vb[:, c * 128:(c + 1) * 128], wsT[:, gq, :], True, True, [big, wsT], [pp])
        for gq in range(8):
            pp = psA if gq < 4 else psB
            c = gq // 2
            rows = slice((gq % 2) * 64, (gq % 2) * 64 + 64)
            tmp = acc[rows, 1, 1, 0:128]
            TTe("dve", tmp, pp[rows, (gq % 4) * 128:(gq % 4) * 128 + 128], bsT[rows, c, :], ALU.add, [pp, bsT], [acc])
            TTe("dve", YB[2][rows, c, blk * 128:(blk + 1) * 128], tmp, scr[rows, c, blk * 128:(blk + 1) * 128], ALU.mult, [acc, scr], [YB[2]])


def rwkv_tile(P, L):
    g = L
    PSF, PSB, MM, TR, ACT, TTe, TS, STT, CP, MSET = (g[k] for k in ("PSF", "PSB", "MM", "TR", "ACT", "TTe", "TS", "STT", "CP", "MSET"))
    big, scr, acc, YB, H0, H0b = g["big"], g["scr"], g["acc"], g["YB"], g["H0"], g["H0b"]
    rwp, omka, wupb, aupb, gupb, gnw, gnb = g["rwp"], g["omka"], g["wupb"], g["aupb"], g["gupb"], g["gnw"], g["gnb"]
    cst, cstb, mrw, mlo, sel = g["cst"], g["cstb"], g["mrw"], g["mlo"], g["sel"]
    chk = g["chk"]
    identb, identf, blkones = g["identb"], g["identf"], g["blkones"]
    rw = getattr(P, "_rw", None)
    if rw is None:
        sb = P.sb
        rw = dict(
            th=sb([128, 128], BF16), sgx=sb([128, 128], BF16),
            **{nm: View(scr[:, i_, :].rearrange("p (a b) -> p a b", a=4), scr) for i_, nm in enumerate(("lw", "cum", "cum2", "av", "kk", "km", "t1", "t2"))},
            AR=sb([128, 4, 2, 128], BF16), BK=sb([128, 4, 2, 128], BF16),
            wc=sb([128, 4], F32),
            tok=View(g["QT"][:, 0:3, :], g["QT"]),
            Xs=View(g["QT"][:, 3, :], g["QT"]),
            **{nm: View(g["V3g"][0][:, i_ // 2, (i_ % 2) * 2:(i_ % 2) * 2 + 2, :].rearrange("p a (h t) -> p (a h) t", h=4), g["V3g"][0])
               for i_, nm in enumerate(("N1", "L1", "N2", "L2"))},
            **{nm: View(YB[2][:, 2 * i_:2 * i_ + 2, :].rearrange("p a (h t) -> p (a h) t", h=4), YB[2]) for i_, nm in enumerate(("IL", "Mm"))},
            **{nm: View(big[:, 14 + 2 * i_:16 + 2 * i_, :].rearrange("p a (h t) -> p (a h) t", h=4), big)
               for i_, nm in enumerate(("Mm2", "Arb", "Aak", "Ark"))},
            Y=View(acc[:, 0, 0, :], acc), Y2=View(acc[:, 0, 1, :], acc), st=sb([128, 4, 8], F32), gt=View(acc[:, 1, 0, :], acc),
            Ht=sb([128, 4, 64], F32),
            Noff=View(g["rstd"][:].bitcast(BF16).rearrange("p (h t) -> p h t", h=8), g["rstd"]),
        )
        P._rw = rw
    rw = P._rw
    th, sgx, lw, cum, cum2, av, kk, km, t1, t2 = (rw[k] for k in ("th", "sgx", "lw", "cum", "cum2", "av", "kk", "km", "t1", "t2"))
    AR, BK, wc, tok = rw["AR"], rw["BK"], rw["wc"], rw["tok"]
    N1, L1, N2, L2, IL, Mm, Mm2, Arb, Aak, Ark = (rw[k] for k in ("N1", "L1", "N2", "L2", "IL", "Mm", "Mm2", "Arb", "Aak", "Ark"))
    V1r_ = g["V1r"]
    fs_ = (4, 5, 6) if g["it"] % 2 == 0 else (0, 1, 2)
    Us = View(V1r_[:, fs_[0], :], V1r_)
    yb = View(V1r_[:, fs_[1], :], V1r_)
    prod = View(V1r_[:, fs_[2], :].rearrange("p (a b) -> p a b", a=4), V1r_)
    Xs, Y, Y2, st, gt, Ht = (rw[k] for k in ("Xs", "Y", "Y2", "st", "gt", "Ht"))
    W0, A0, KK_, KA, RK = (rwp[:, i, :] for i in range(5))
    EH = 0.6065306597126334

    for blk in range(4):
        ts = slice(blk * 128, (blk + 1) * 128)
        zr = lambda c: big[:, 0 + c, ts]
        zk = lambda c: big[:, 4 + c, ts]
        zv = lambda c: big[:, 8 + c, ts]
        zwa = big[:, 12, ts]
        zg = big[:, 13, ts]
        ACT(th[0:64, :], big[0:64, 12, ts], AF.Tanh, [big], [th])
        ACT(sgx[:], zg, AF.Sigmoid, [big], [sgx])
        ps = PSF()
        psa = PSF()
        for c in range(4):
            MM(ps[:, c * 128:(c + 1) * 128], wupb[0:64, c * 128:(c + 1) * 128], th[0:64, :], True, True, [wupb, th], [ps])
            MM(psa[:, c * 128:(c + 1) * 128], aupb[64:128, c * 128:(c + 1) * 128], big[64:128, 12, ts], True, True, [aupb, big], [psa])
        for c in range(4):
            ACT(lw[:, c, :], ps[:, c * 128:(c + 1) * 128], AF.Exp, [ps, g["nw0"]], [lw], bias=g["nw0"][:, c:c + 1], scale=-1.0)
            ACT(av[:, c, :], psa[:, c * 128:(c + 1) * 128], AF.Sigmoid, [psa, rwp], [av], bias=A0[:, c:c + 1])
        TS("dve", lw[:], lw[:], 1.0, None, ALU.add, None, [lw], [lw])
        P.op("dve", lambda e: e.reciprocal(out=lw[:], in_=lw[:]), ar=[lw[:]], aw=[lw[:]])
        TS("dve", lw[:], lw[:], -EH, None, ALU.mult, None, [lw], [lw])
        psg = PSF()
        MM(psg[:], sgx[:], gupb[:], True, True, [sgx, gupb], [psg])
        CP("act", gt[:], psg[:], [psg], [gt])
        chk('rw1')
        for c in range(4):
            TS("dve", kk[:, c, :], zk(c), KK_[:, c:c + 1], None, ALU.mult, None, [big, rwp], [kk])
        ACT(t1[:], kk[:], AF.Square, [kk], [t1])
        ps = PSF()
        for c in range(4):
            MM(ps[:, c * 128:(c + 1) * 128], blkones, t1[:, c, :], True, True, [cst, t1], [ps])
        ACT(t1[:].rearrange("p a b -> p (a b)"), ps[:], AF.Sqrt, [ps], [t1])
        TS("dve", t1[:], t1[:], 1e-12, None, ALU.max, None, [t1], [t1])
        P.op("dve", lambda e: e.reciprocal(out=t1[:], in_=t1[:]), ar=[t1[:]], aw=[t1[:]])
        TTe("dve", kk[:], kk[:], t1[:], ALU.mult, [kk, t1], [kk])
        for c in range(4):
            TS("dve", t2[:, c, :], av[:, c, :], KA[:, c:c + 1], omka[:, c:c + 1], ALU.mult, ALU.add, [av, rwp, omka], [t2])
            TTe("dve", km[:, c, :], t2[:, c, :], zk(c), ALU.mult, [t2, big], [km])
        for c in range(4):
            STT("dve", prod[:, c, :], zr(c), RK[:, c:c + 1], km[:, c, :], ALU.mult, ALU.mult, [big, rwp, km], [prod])
        psr = PSF()
        for c in range(4):
            MM(psr[:, 0:8], prod[:, c, :], sel[:, c, :], c == 0, c == 3, [prod, sel], [psr])
        CP("act", st[:, 0, :], psr[:, 0:8], [psr], [st])
        chk('rw2')
        src, dst = lw, cum
        CP("pool", cum[:], lw[:], [lw], [cum])
        a_, b_ = cum, cum2
        for s in (1, 2, 4, 8, 16, 32, 64):
            CP("pool", b_[:, :, 0:s], a_[:, :, 0:s], [a_], [b_])
            TTe("dve", b_[:, :, s:128], a_[:, :, s:128], a_[:, :, 0:128 - s], ALU.add, [a_], [b_])
            a_, b_ = b_, a_
        cm = a_
        ot = b_
        ACT(t1[:], cm[:], AF.Exp, [cm], [t1])
        for c in range(4):
            TTe("dve", AR[:, c, 1, :], t1[:, c, :], zr(c), ALU.mult, [t1, big], [AR])
        ACT(wc[:], cm[:, :, 127], AF.Exp, [cm], [wc])
        TTe("dve", ot[:], cm[:], lw[:], ALU.subtract, [cm, lw], [ot])
        ACT(t1[:], ot[:], AF.Exp, [ot], [t1])
        STT("dve", AR[:, :, 0, :], kk[:], -1.0, t1[:], ALU.mult, ALU.mult, [kk, t1], [AR])
        ACT(t1[:], cm[:], AF.Exp, [cm], [t1], scale=-1.0)
        TTe("dve", t2[:], kk[:], av[:], ALU.mult, [kk, av], [t2])
        TTe("dve", BK[:, :, 0, :], t2[:], t1[:], ALU.mult, [t2, t1], [BK])
        TTe("dve", BK[:, :, 1, :], km[:], t1[:], ALU.mult, [km, t1], [BK])
        chk('rw3')
        pb = PSB()
        for c in range(4):
            TR(pb[:, c * 128:(c + 1) * 128], zv(c), identb, [big, cstb], [pb])
        CP("act", tok[:, 0, :], pb[:, 0:512], [pb], [tok])
        for q in range(2):
            pb = PSB()
            for c in range(4):
                TR(pb[:, c * 128:(c + 1) * 128], BK[:, c, q, :], identb, [BK, cstb], [pb])
            CP("act", tok[:, 1 + q, :], pb[:, 0:512], [pb], [tok])
        chk('rw4')
        for hg in range(2):
            p1 = PSF(); p2 = PSF(); p3 = PSF(); p4 = PSF()
            for hq in range(4):
                c, hh = hq, hg
                rows = slice(hh * 64, hh * 64 + 64)
                half = hq % 2
                pa = p1 if hq < 2 else p2
                MM(pa[:, half * 256:half * 256 + 256], BK[rows, c, 0, :], AR[rows, c, :, :].rearrange("p a b -> p (a b)"), True, True, [BK, AR], [pa])
                pk = p3 if hq < 2 else p4
                MM(pk[:, half * 256:half * 256 + 256], BK[rows, c, 1, :], AR[rows, c, :, :].rearrange("p a b -> p (a b)"), True, True, [BK, AR], [pk])
            chk('rw4a')
            for i2, (pa, pk) in enumerate(((p1, p3), (p2, p4))):
                h0 = hg * 4 + i2 * 2
                sa = pa[:].rearrange("p (h z t) -> p h z t", h=2, z=2)
                sk = pk[:].rearrange("p (h z t) -> p h z t", h=2, z=2)
                TTe("dve", N1[:, h0:h0 + 2, :], sa[:, :, 0, :], mrw[:, 0:2, 0, :], ALU.mult, [pa, mrw], [N1])
                TTe("dve", Arb[:, h0:h0 + 2, :], sa[:, :, 1, :], mrw[:, 0:2, 1, :], ALU.mult, [pa, mrw], [Arb])
                TTe("dve", Aak[:, h0:h0 + 2, :], sk[:, :, 0, :], mrw[:, 0:2, 0, :], ALU.mult, [pk, mrw], [Aak])
                TTe("dve", Ark[:, h0:h0 + 2, :], sk[:, :, 1, :], mrw[:, 0:2, 1, :], ALU.mult, [pk, mrw], [Ark])
            chk('rw4b')
            pl = PSF()
            for hq in range(4):
                c, hh = hq, hg
                rows = slice(hh * 64, hh * 64 + 64)
                MM(pl[:, hq * 128:(hq + 1) * 128], AR[rows, c, 0, :], BK[rows, c, 0, :], True, True, [AR, BK], [pl])
            TTe("dve", L1[:, hg * 4:hg * 4 + 4, :], pl[:].rearrange("p (h t) -> p h t", h=4), mlo[:], ALU.mult, [pl, mlo], [L1])
        chk('rw5')
        Noff = rw["Noff"]
        MSET("pool", Noff[:], 0.0, [Noff])
        CP("pool", Noff[0:64, :, 64:128], N1[0:64, :, 64:128], [N1], [Noff])
        MSET("pool", N1[0:64, :, 64:128], 0.0, [N1])
        MSET("pool", L1[64:128, :, 0:64], 0.0, [L1])
        for hg in range(2):
            hs = slice(hg * 4, hg * 4 + 4)
            TTe("pool", Mm[:, hs, :], N1[:, hs, :], cstb[:, 4:8, :], ALU.add, [N1, cstb], [Mm])
        Nk, Lk, Nn, Ln = N1, L1, N2, L2
        Mc, Mn = Mm, Mm2
        for lev in range(5):
            last = lev == 4
            for hg in range(2):
                hs = slice(hg * 4, hg * 4 + 4)
                pn = PSF()
                for hq in range(4):
                    h = hg * 4 + hq
                    MM(pn[:, hq * 128:(hq + 1) * 128], Lk[:, h, :], Nk[:, h, :], True, True, [Lk, Nk], [pn])
                if not last:
                    CP("act", Nn[:, hs, :], pn[:].rearrange("p (h t) -> p h t", h=4), [pn], [Nn])
                pL = PSF()
                for hq in range(4):
                    h = hg * 4 + hq
                    MM(pL[:, hq * 128:(hq + 1) * 128], Nk[:, h, :], Lk[:, h, :], True, True, [Lk, Nk], [pL])
                if not last:
                    CP("act", Ln[:, hs, :], pL[:].rearrange("p (h t) -> p h t", h=4), [pL], [Ln])
                TTe("dve", IL[:, hs, :], pL[:].rearrange("p (h t) -> p h t", h=4), cstb[:, 4:8, :], ALU.add, [pL, cstb], [IL])
                pm = PSF()
                for hq in range(4):
                    h = hg * 4 + hq
                    MM(pm[:, hq * 128:(hq + 1) * 128], IL[:, h, :], Mc[:, h, :], True, True, [IL, Mc], [pm])
                CP("act", Mn[:, hs, :], pm[:].rearrange("p (h t) -> p h t", h=4), [pm], [Mn])
            Nk, Nn = Nn, Nk
            Lk, Ln = Ln, Lk
            Mc, Mn = Mn, Mc
        Mf = Mc
        chk('rw6')
        px = PSF()
        for h in range(8):
            c, hh = h % 4, h // 4
            hn = 2 * c + hh
            rows = slice(hh * 64, hh * 64 + 64)
            o = px[:, hn * 64:(hn + 1) * 64]
            MM(o, AR[rows, c, 0, :], H0b[rows, c, :], True, False, [AR, H0b], [px])
            MM(o, Aak[:, h, :], tok[:, 0, hn * 64:(hn + 1) * 64], False, True, [Aak, tok], [px])
        CP("act", Xs[:], px[:], [px], [Xs])
        Noff = rw["Noff"]
        for rnd in range(2):
            pu = PSF()
            for h in range(8):
                c, hh = h % 4, h // 4
                hn = 2 * c + hh
                MM(pu[:, hn * 64:(hn + 1) * 64], Mf[:, h, :], Xs[:, hn * 64:(hn + 1) * 64], True, True, [Mf, Xs], [pu])
            CP("act", Us[:], pu[:], [pu], [Us])
            if rnd == 0:
                pw = PSF()
                for h in range(8):
                    c, hh = h % 4, h // 4
                    hn = 2 * c + hh
                    MM(pw[:, hn * 64:(hn + 1) * 64], Noff[:, h, :], Us[:, hn * 64:(hn + 1) * 64], True, True, [Noff, Us], [pw])
                TTe("dve", Xs[:], pw[:], Xs[:], ALU.add, [pw, Xs], [Xs])
        py = PSF()
        for h in range(8):
            c, hh = h % 4, h // 4
            hn = 2 * c + hh
            rows = slice(hh * 64, hh * 64 + 64)
            o = py[:, hn * 64:(hn + 1) * 64]
            MM(o, AR[rows, c, 1, :], H0b[rows, c, :], True, False, [AR, H0b], [py])
            MM(o, Arb[:, h, :], Us[:, hn * 64:(hn + 1) * 64], False, False, [Arb, Us], [py])
            MM(o, Ark[:, h, :], tok[:, 0, hn * 64:(hn + 1) * 64], False, True, [Ark, tok], [py])
        CP("act", Y[:], py[:], [py], [Y])
        ph = PSF()
        for c in range(4):
            o = ph[:, c * 128:(c + 1) * 128]
            MM(o, tok[:, 1, c * 128:(c + 1) * 128], Us[:, c * 128:(c + 1) * 128], True, False, [tok, Us], [ph])
            MM(o, tok[:, 2, c * 128:(c + 1) * 128], tok[:, 0, c * 128:(c + 1) * 128], False, True, [tok], [ph])
        for hh in range(2):
            rows = slice(hh * 64, hh * 64 + 64)
            src = ph[rows, :].rearrange("p (c x i) -> p c x i", c=4, x=2)[:, :, hh, :]
            TTe("dve", Ht[rows, :, :], src, H0[rows, :, :], ALU.add, [ph, H0], [Ht])
        for c in range(4):
            TS("dve", H0[:, c, :], Ht[:, c, :], wc[:, c:c + 1], None, ALU.mult, None, [Ht, wc], [H0])
        CP("act", H0b[:], H0[:], [H0], [H0b])
        chk('rw7')
        Y3 = Y[:].rearrange("p (h i) -> p h i", h=8)
        P.op("dve", lambda e: e.tensor_reduce(out=st[:, 1, :], in_=Y3, axis=AX.X, op=ALU.add), ar=[Y3], aw=[st[:, 1, :]])
        ACT(Y2[:], Y[:], AF.Square, [Y], [Y2])
        P.op("dve", lambda e: e.tensor_reduce(out=st[:, 2, :], in_=Y2[:].rearrange("p (h i) -> p h i", h=8), axis=AX.X, op=ALU.add), ar=[Y2[:]], aw=[st[:, 2, :]])
        TS("dve", st[:, 1, :], st[:, 1, :], 1.0 / 64, None, ALU.mult, None, [st], [st])
        TTe("dve", st[:, 3, :], st[:, 1, :], st[:, 1, :], ALU.mult, [st], [st])
        STT("dve", st[:, 2, :], st[:, 2, :], 1.0 / 64, st[:, 3, :], ALU.mult, ALU.subtract, [st], [st])
        ACT(st[:, 2, :], st[:, 2, :], AF.Sqrt, [st], [st], bias=64e-5)
        P.op("dve", lambda e: e.reciprocal(out=st[:, 2, :], in_=st[:, 2, :]), ar=[st[:, 2, :]], aw=[st[:, 2, :]])
        for h in range(8):
            TS("dve", Y2[:, h * 64:(h + 1) * 64], Y[:, h * 64:(h + 1) * 64], st[:, 1, h:h + 1], st[:, 2, h:h + 1], ALU.subtract, ALU.mult, [Y, st], [Y2])
        TTe("dve", Y2[:], Y2[:], gnw[:], ALU.mult, [Y2, gnw], [Y2])
        TTe("dve", Y2[:], Y2[:], gnb[:], ALU.add, [Y2, gnb], [Y2])
        for h in range(8):
            STT("dve", Y2[:, h * 64:(h + 1) * 64], tok[:, 0, h * 64:(h + 1) * 64], st[:, 0, h:h + 1], Y2[:, h * 64:(h + 1) * 64], ALU.mult, ALU.add, [tok, st, Y2], [Y2])
        TTe("dve", yb[:], Y2[:], gt[:], ALU.mult, [Y2, gt], [yb])
        pb = PSB()
        for c in range(4):
            TR(pb[:, c * 128:(c + 1) * 128], yb[:, c * 128:(c + 1) * 128], identb, [yb, cstb], [pb])
        CP("act", YB[1][:, :, ts], pb[:, 0:512].rearrange("p (c t) -> p c t", c=4), [pb], [YB[1]])


def _consts(T):
    bf = ml_dtypes.bfloat16
    cst = np.zeros((128, 5, 128), np.float32)
    cst[:, 0, :] = np.eye(128)
    cst[:, 1, :] = 1.0
    cst[0:64, 2, 0:64] = 1.0
    cst[64:128, 2, 64:128] = 1.0
    cstb = np.zeros((128, 8, 128), np.float32)
    cstb[:, 0, :] = np.eye(128)
    for m in range(128):
        cstb[(m // 64) * 64 + ((m % 64) + 32) % 64, 1, m] = 1.0
    cstb[:, 2, :] = 1.0
    for q_ in range(4):
        cstb[:, 4 + q_, :] = np.eye(128)
    ki = np.arange(128)[:, None]
    qi = np.arange(128)[None, :]
    m12 = np.zeros((128, 2, 256), np.float32)
    for u in range(2):
        m12[:, u, 0:128] = (ki >= qi)
        m12[:, u, 128:256] = (ki <= qi)
    m3 = np.zeros((128, 4, 4, 64), np.float32)
    for j in range(4):
        q = 32 * j + np.arange(32)[None, :]
        for rr in range(4):
            m3[:, j, rr, 0:32] = (ki >= q)
            m3[:, j, rr, 32:64] = (ki <= q)
    mrw = np.zeros((128, 4, 2, 128), np.float32)
    mrw[:, :, 0, :] = (ki < qi)[:, None, :]
    mrw[:, :, 1, :] = (ki <= qi)[:, None, :]
    mlo = np.zeros((128, 4, 128), np.float32)
    mlo[:, :, :] = (qi < ki)[:, None, :]
    sel = np.zeros((128, 4, 8), np.float32)
    for p in range(128):
        for c in range(4):
            sel[p, c, 2 * c + p // 64] = 1.0
    inv = (1.0 / (np.float32(10000.0) ** (np.arange(0, 64, 2, dtype=np.float32) / np.float32(64)))).astype(np.float32)
    ang = (np.arange(T, dtype=np.float32)[:, None] * inv[None, :]).astype(np.float32)
    cosv, sinv = np.cos(ang).astype(np.float32), np.sin(ang).astype(np.float32)
    cosT = np.zeros((128, T), np.float32)
    sinT = np.zeros((128, T), np.float32)
    for p in range(128):
        d = p % 64
        cosT[p] = cosv[:, d % 32]
        sinT[p] = sinv[:, d % 32] * (-1.0 if d < 32 else 1.0)
    return dict(cst_f32=cst, cst_bf=cstb.astype(bf), m12=m12.astype(bf), m3=m3.astype(bf), mrw=mrw.astype(bf),
                mlo=mlo.astype(bf), sel=sel.astype(bf), cosT=cosT, sinT=sinT)


def _col(v, n):
    return np.ascontiguousarray(np.asarray(v, np.float32).reshape(n, 128).T)


def _shared_inputs(inp, T):
    f = lambda a: np.ascontiguousarray(np.asarray(a, np.float32)[:NL]) if np.asarray(a).shape[0] == 2 and np.asarray(a).ndim >= 2 else np.ascontiguousarray(np.asarray(a, np.float32))
    d = dict(_consts(T))
    d["w_ada"] = f(inp["w_ada"])
    d["b_ada"] = np.stack([_col(inp["b_ada"][l], 48) for l in range(NL)])
    d["g_mix"] = np.stack([_col(inp["norm_mix"][l], 8) for l in range(NL)])
    d["g_ffn"] = np.stack([_col(inp["norm_ffn"][l], 8) for l in range(NL)])
    d["g_fin"] = _col(inp["norm_final"], 8)
    d["w_in"] = f(inp["w_in"])
    d["mu"] = np.stack([_col(inp["rwkv_mu"][l], 14) for l in range(NL)])
    d["rwp"] = np.stack([np.stack([_col(np.asarray(inp[k][l]).reshape(-1), 4) for k in
                                   ("rwkv_w0", "rwkv_a0", "rwkv_k_k", "rwkv_k_a", "rwkv_r_k")], axis=1) for l in range(NL)])
    d["r_wup"] = f(inp["rwkv_w_up"])
    d["r_aup"] = f(inp["rwkv_a_up"])
    d["r_gup"] = f(inp["rwkv_g_up"])
    bc = lambda a: np.ascontiguousarray(np.broadcast_to(f(a).reshape(NL, 1, 512), (NL, 128, 512)))
    d["gnw"] = bc(inp["rwkv_gn_w"])
    d["gnb"] = bc(inp["rwkv_gn_b"])
    d["lng"] = bc(inp["gmlp_ln_g"])
    d["lnb"] = bc(inp["gmlp_ln_b"])
    d["wsT"] = np.ascontiguousarray(np.transpose(f(inp["gmlp_w_s"]), (0, 3, 1, 2)))
    bsv = f(inp["gmlp_b_s"])
    bsT = np.zeros((NL, 128, 4, 128), np.float32)
    for g_ in range(8):
        bsT[:, (g_ % 2) * 64:(g_ % 2) * 64 + 64, g_ // 2, :] = bsv[:, g_, None, :]
    d["bsT"] = bsT
    d["w_branch"] = f(inp["w_branch"])
    d["w_out"] = f(inp["w_out"])
    d["ffn_up"] = f(inp["ffn_w_up"])
    d["convw"] = np.stack([np.ascontiguousarray(np.transpose(np.asarray(inp["ffn_conv_w"][l], np.float32).reshape(3, NF, 128), (2, 1, 0))) for l in range(NL)])
    d["convb"] = np.stack([_col(inp["ffn_conv_b"][l], NF) for l in range(NL)])
    d["ffn_down"] = f(inp["ffn_w_down"])
    return d


_NC_CACHE = {}


def run(inp, T=None):
    x = np.asarray(inp["x"], np.float32)
    B, S, _ = x.shape
    T = S
    if T not in _NC_CACHE:
        _NC_CACHE[T] = build(T)
    nc = _NC_CACHE[T]
    shared = _shared_inputs(inp, T)
    c = np.asarray(inp["c"], np.float32)
    in_maps = []
    for core in range(8):
        b = core % B
        m = dict(shared)
        m["xT"] = np.ascontiguousarray(x[b].T)
        m["c128"] = _col(c[b], 8)
        in_maps.append(m)
    res = run_bass_kernel_spmd(nc, in_maps, core_ids=list(range(8)))
    out = np.stack([np.ascontiguousarray(res.results[b]["yT"].T) for b in range(B)])
    return out.astype(np.float32)


def kernel(**inputs):
    return run(inputs)
```

```python
import numpy as np
import ml_dtypes
import concourse.bass as bass
import concourse.mybir as mybir
from concourse.bass_utils import run_bass_kernel_spmd

F32 = mybir.dt.float32
BF16 = mybir.dt.bfloat16
AF = mybir.ActivationFunctionType
ALU = mybir.AluOpType
AX = mybir.AxisListType

D = 1024
import os as _os
NL = int(_os.environ.get("KNL", "2"))
TT = 512
IN_COLS = 7424
DFF = 2816
NF = 22


class Buf:
    __slots__ = ("name", "lw", "rd", "excl")

    def __init__(self, name="", excl=False):
        self.name = name
        self.lw = None
        self.rd = []
        self.excl = excl


class Tl:
    __slots__ = ("t", "b")

    def __init__(self, t, b):
        self.t = t
        self.b = b

    def __getitem__(self, k):
        return self.t[k]


def _b(x):
    return x.b if isinstance(x, Tl) else x


def View(ap, buf):
    return Tl(ap, _b(buf))


class Prog:
    ENGS = ("pe", "dve", "act", "pool", "sp")
    NDMA = 8

    def __init__(self):
        self.nc = bass.Bass("TRN2", target_bir_lowering=False)
        self.ops = {e: [] for e in self.ENGS}
        self.cnt = {e: 0 for e in self.ENGS}
        self.seen = {e: {} for e in self.ENGS}
        self.sems = {}
        self._stack = []
        nc = self.nc
        for e in ("pe", "dve", "act", "pool"):
            self.sems[e] = self._enter(nc.semaphore("s_" + e))
        self.dslots = {}
        for q in ("sp", "act", "pool"):
            sl = []
            for i in range(self.NDMA):
                key = "d_%s%d" % (q, i)
                self.sems[key] = self._enter(nc.semaphore(key))
                sl.append([key, 0])
            self.dslots[q] = [sl, 0]
        self.n_sb = 0
        self.reg = {}

    def _enter(self, cm):
        v = cm.__enter__()
        self._stack.append(cm)
        return v

    def sb(self, shape, dtype, name=None):
        self.n_sb += 1
        t = self._enter(self.nc.sbuf_tensor("s_%s_%d" % (name or "t", self.n_sb), list(shape), dtype))
        return Tl(t, Buf(name or ""))

    def ps(self, shape, dtype=F32, name=None):
        self.n_sb += 1
        t = self._enter(self.nc.psum_tensor(name or ("ps%d" % self.n_sb), list(shape), dtype))
        return Tl(t, Buf(name or "", excl=True))

    def _deps(self, eng, reads, writes):
        toks = []
        for b in reads:
            b = _b(b)
            if b.lw is not None:
                toks.append(b.lw)
        for b in writes:
            b = _b(b)
            if b.lw is not None:
                toks.append(b.lw)
            toks.extend(b.rd)
        need = {}
        for (k, v) in toks:
            if k == "pe" and eng == "pe":
                continue
            if self.seen[eng].get(k, 0) >= v:
                continue
            if need.get(k, 0) < v:
                need[k] = v
        for k, v in need.items():
            self.seen[eng][k] = v
        return list(need.items())

    def _commit(self, tok, reads, writes):
        for b in reads:
            b = _b(b)
            b.rd.append(tok)
            if len(b.rd) > 64:
                mx = {}
                for (k, v) in b.rd:
                    if mx.get(k, 0) < v:
                        mx[k] = v
                b.rd = list(mx.items())
        for b in writes:
            b = _b(b)
            b.lw = tok
            b.rd = []

    @staticmethod
    def _onchip(ap):
        n = ap.name
        return n.startswith("s_") or n.startswith("ps")

    def _reg(self, ap):
        pstep, pcnt = ap.ap[0]
        off = ap.offset
        if ap.name.startswith("ps"):
            return (ap.name, 0, 128, 0, 1 << 30, True)
        p0 = off // pstep
        f0 = off % pstep
        span = 0
        for st, cnt in ap.ap[1:]:
            span += (cnt - 1) * abs(st)
        esz = 2 if ap.dtype == BF16 else 4
        return (ap.name, p0, p0 + pcnt, f0 * esz, (f0 + span + 1) * esz, False)

    def _rdeps(self, eng, ars, aws):
        toks = []
        new = []
        for (aps, isw) in ((ars, False), (aws, True)):
            for ap in aps:
                name, p0, p1, b0, b1, isps = self._reg(ap)
                w = isw or (isps and eng != "pe")
                recs = self.reg.setdefault(name, [])
                for r in recs:
                    if r[0] < p1 and p0 < r[1] and r[2] < b1 and b0 < r[3] and (w or r[5]):
                        toks.append(r[4])
                new.append((name, [p0, p1, b0, b1, None, w]))
        return toks, new

    def _rcommit(self, tok, new):
        for name, rec in new:
            rec[4] = tok
            recs = self.reg[name]
            if rec[5]:
                recs[:] = [r for r in recs if not (rec[0] <= r[0] and r[1] <= rec[1] and rec[2] <= r[2] and r[3] <= rec[3])]
            else:
                recs[:] = [r for r in recs if not ((not r[5]) and r[4][0] == tok[0] and rec[0] <= r[0] and r[1] <= rec[1]
                                                   and rec[2] <= r[2] and r[3] <= rec[3])]
            recs.append(rec)

    def _filter(self, eng, toks):
        need = {}
        for (k, v) in toks:
            if k == "pe" and eng == "pe":
                continue
            if self.seen[eng].get(k, 0) >= v:
                continue
            if need.get(k, 0) < v:
                need[k] = v
        for k, v in need.items():
            self.seen[eng][k] = v
        return list(need.items())

    def _buftoks(self, reads, writes):
        toks = []
        for b in reads:
            if b.lw is not None:
                toks.append(b.lw)
        for b in writes:
            if b.lw is not None:
                toks.append(b.lw)
            toks.extend(b.rd)
        return toks

    def op(self, eng, fn, reads=(), writes=(), ar=None, aw=None):
        if eng == "pool":
            eng = "dve"
        assert ar is not None and aw is not None
        reads = [b for b in reads if isinstance(b, Buf)]
        writes = [b for b in writes if isinstance(b, Buf)]
        rt, new = self._rdeps(eng, ar, aw)
        waits = self._filter(eng, self._buftoks(reads, writes) + rt)
        self.cnt[eng] += 1
        tok = (eng, self.cnt[eng])
        self.ops[eng].append((waits, fn, (eng, 1)))
        self._commit(tok, reads, writes)
        self._rcommit(tok, new)
        return tok

    def dma(self, q, out, in_, reads=(), writes=()):
        sl, idx = self.dslots[q]
        slot = sl[idx % self.NDMA]
        self.dslots[q][1] = idx + 1
        key = slot[0]
        reads = [b for b in reads if isinstance(b, Buf)]
        writes = [b for b in writes if isinstance(b, Buf)]
        ar = [in_] if self._onchip(in_) else []
        aw = [out] if self._onchip(out) else []
        rt, new = self._rdeps(q, ar, aw)
        waits = self._filter(q, self._buftoks(reads, writes) + rt)
        if slot[1] > 0 and self.seen[q].get(key, 0) < slot[1]:
            waits.append((key, slot[1]))
            self.seen[q][key] = slot[1]
        slot[1] += 16
        tok = (key, slot[1])

        def fn(e, out=out, in_=in_):
            return e.dma_start(out=out, in_=in_)
        self.ops[q].append((waits, fn, (key, 16)))
        self._commit(tok, reads, writes)
        self._rcommit(tok, new)
        return tok

    def fence(self, src, dst):
        return
        toks = []
        for s in src:
            s = _b(s)
            if s.lw is not None:
                toks.append(s.lw)
            toks.extend(s.rd)
        for d in dst:
            _b(d).rd.extend(toks)

    def finish(self, final_bufs):
        waits = self._filter("sp", self._buftoks([_b(b) for b in final_bufs], []))
        self.ops["sp"].append((waits, None, None))
        nc = self.nc
        sems = self.sems
        ops = self.ops
        with nc.Block() as block:
            def mk(e):
                def body(eng):
                    for (waits, fn, inc) in ops[e]:
                        for (k, v) in waits:
                            eng.wait_ge(sems[k], v)
                        if fn is not None:
                            ins = fn(eng)
                            ins.then_inc(sems[inc[0]], inc[1])
                return body
            block.tensor(mk("pe"))
            block.vector(mk("dve"))
            block.scalar(mk("act"))
            block.gpsimd(mk("pool"))
            block.sync(mk("sp"))
        for cm in reversed(self._stack):
            cm.__exit__(None, None, None)
        self._stack = []
        return nc


class _Stop(Exception):
    pass


def build(T, dbg=False):
    import os
    STOP = os.environ.get("KSTOP")
    P = Prog()

    def chk(name):
        if STOP == name:
            raise _Stop()
    nc = P.nc
    NTI = T // TT
    NB = T // 128

    def din(name, shape, dt=F32):
        return nc.dram_tensor(name, list(shape), dt, kind="ExternalInput").ap()

    xT_in = din("xT", [D, T])
    c_in = din("c128", [128, 8])
    w_ada = din("w_ada", [NL, D, 6 * D])
    b_ada = din("b_ada", [NL, 128, 48])
    g_mix = din("g_mix", [NL, 128, 8])
    g_ffn = din("g_ffn", [NL, 128, 8])
    g_fin = din("g_fin", [128, 8])
    w_in = din("w_in", [NL, D, IN_COLS])
    mu_d = din("mu", [NL, 128, 14])
    pr_d = din("rwp", [NL, 128, 5, 4])
    wup_d = din("r_wup", [NL, 64, 512])
    aup_d = din("r_aup", [NL, 64, 512])
    gup_d = din("r_gup", [NL, 128, 512])
    gnw_d = din("gnw", [NL, 128, 512])
    gnb_d = din("gnb", [NL, 128, 512])
    lng_d = din("lng", [NL, 128, 512])
    lnb_d = din("lnb", [NL, 128, 512])
    wsT_d = din("wsT", [NL, 128, 8, 128])
    bs_d = din("bsT", [NL, 128, 4, 128])
    wbr_d = din("w_branch", [NL, 3, 512, D])
    wout_d = din("w_out", [NL, D, D])
    wfu_d = din("ffn_up", [NL, D, 2 * DFF])
    cw_d = din("convw", [NL, 128, NF, 3])
    cb_d = din("convb", [NL, 128, NF])
    wfd_d = din("ffn_down", [NL, DFF, D])
    cos_d = din("cosT", [128, T])
    sin_d = din("sinT", [128, T])
    cst_d = din("cst_f32", [128, 5, 128])
    cstb_d = din("cst_bf", [128, 8, 128], BF16)
    m12_d = din("m12", [128, 2, 256], BF16)
    m3_d = din("m3", [128, 4, 4, 64], BF16)
    mrw_d = din("mrw", [128, 4, 2, 128], BF16)
    mlo_d = din("mlo", [128, 4, 128], BF16)
    sel_d = din("sel", [128, 4, 8], BF16)
    yT_out = nc.dram_tensor("yT", [D, T], F32, kind="ExternalOutput").ap()
    x1T = nc.dram_tensor("x1T", [D, T], F32, kind="Internal").ap()
    Vd = [nc.dram_tensor("Vd%d" % l, [T, 512], BF16, kind="Internal").ap() for l in range(NL)]
    b_x1 = Buf("x1T")
    b_vd = [Buf("vd0"), Buf("vd1")]
    b_out = Buf("out")

    def MM(out, lhsT, rhs, start, stop, r, w):
        P.op("pe", lambda e: e.matmul(out, lhsT, rhs, start=start, stop=stop), reads=r, writes=w, ar=[lhsT, rhs], aw=[out])

    def TR(out, in_, ident, r, w):
        P.op("pe", lambda e: e.transpose(out, in_, ident), reads=r, writes=w, ar=[in_, ident], aw=[out])

    def ACT(out, in_, func, r, w, bias=0.0, scale=1.0):
        P.op("act", lambda e: e.activation(out=out, in_=in_, func=func, bias=bias, scale=scale), reads=r, writes=w,
             ar=[in_] + [x for x in (bias, scale) if hasattr(x, "offset")], aw=[out])

    def TTe(eng, out, in0, in1, op, r, w):
        P.op(eng, lambda e: e.tensor_tensor(out=out, in0=in0, in1=in1, op=op), reads=r, writes=w, ar=[in0, in1], aw=[out])

    def TS(eng, out, in0, s1, s2, op0, op1, r, w):
        if s2 is None:
            P.op(eng, lambda e: e.tensor_scalar(out=out, in0=in0, scalar1=s1, scalar2=None, op0=op0), reads=r, writes=w,
                 ar=[in0] + [x for x in (s1,) if hasattr(x, "offset")], aw=[out])
        else:
            P.op(eng, lambda e: e.tensor_scalar(out=out, in0=in0, scalar1=s1, scalar2=s2, op0=op0, op1=op1), reads=r, writes=w,
                 ar=[in0] + [x for x in (s1, s2) if hasattr(x, "offset")], aw=[out])

    def STT(eng, out, in0, sc, in1, op0, op1, r, w):
        P.op(eng, lambda e: e.scalar_tensor_tensor(out=out, in0=in0, scalar=sc, in1=in1, op0=op0, op1=op1), reads=r, writes=w,
             ar=[in0, in1] + [x for x in (sc,) if hasattr(x, "offset")], aw=[out])

    def CP(eng, out, in_, r, w):
        if eng == "act":
            P.op("act", lambda e: e.copy(out=out, in_=in_), reads=r, writes=w, ar=[in_], aw=[out])
        else:
            P.op(eng, lambda e: e.tensor_copy(out=out, in_=in_), reads=r, writes=w, ar=[in_], aw=[out])

    def MSET(eng, ap, val, w):
        P.op(eng, lambda e: e.memset(ap, val), writes=w, ar=[], aw=[ap])

    psf = [P.ps([128, 512], F32) for _ in range(6)]
    psb = [P.ps([128, 1024], BF16) for _ in range(2)]
    psi = [0, 0]

    def PSF():
        psi[0] += 1
        return psf[psi[0] % 6]

    def PSB():
        psi[1] += 1
        return psb[psi[1] % 2]

    cst = P.sb([128, 5, 128], F32)
    cstb = P.sb([128, 8, 128], BF16)
    m12 = P.sb([128, 2, 256], BF16)
    m3 = P.sb([128, 4, 4, 64], BF16)
    mrw = P.sb([128, 4, 2, 128], BF16)
    mlo = P.sb([128, 4, 128], BF16)
    sel = P.sb([128, 4, 8], BF16)
    for (t_, d_) in ((cst, cst_d), (cstb, cstb_d), (m12, m12_d), (m3, m3_d), (mrw, mrw_d), (mlo, mlo_d), (sel, sel_d)):
        P.dma("sp", t_[:], d_, writes=[t_])
    identf = cst[:, 0, :]
    onesf = cst[:, 1, :]
    blkones = cst[:, 2, :]
    identb = cstb[:, 0, :]
    permb = cstb[:, 1, :]
    onesb = cstb[:, 2, :]

    xt = P.sb([128, 8, TT], F32, "xt")
    scr = P.sb([128, 8, TT], F32, "scr")
    hT = P.sb([128, 8, TT], BF16, "hT")
    rstd = P.sb([128, TT], F32, "rstd")
    NWB = 2
    wb = [P.sb([128, 8, 512], BF16, "wb%d" % i) for i in range(NWB)]
    wbi = [0]
    QT = P.sb([128, 4, TT], BF16, "QT")
    KT = P.sb([128, 4, T], BF16, "KT")
    V1r = P.sb([128, 8, 512], BF16, "V1r")
    V2 = P.sb([128, 2, 4, 512], BF16, "V2")
    V3g = [P.sb([128, 2, 4, 512], BF16, "V3g0")] * 2
    acc = P.sb([128, 4, 2, TT], F32, "acc")
    big = P.sb([128, NF, TT], BF16, "big")
    YB = [P.sb([128, 4, TT], BF16, "YB%d" % i) for i in range(3)]
    cosb = P.sb([128, TT], F32, "cos")
    sinb = P.sb([128, TT], F32, "sin")
    zcar = P.sb([128, 14], F32, "zcar")
    H0 = P.sb([128, 4, 64], F32, "H0")
    H0b = P.sb([128, 4, 64], BF16, "H0b")
    ccar = P.sb([128, NF, 2], F32, "ccar")
    MSET("pool", KT[:], 0.0, [KT])

    modp = P.sb([128, NL, 48], F32, "mod")
    A1 = P.sb([128, NL, 8], F32, "A1")
    A2 = P.sb([128, NL, 8], F32, "A2")
    gfin = P.sb([128, 8], F32, "gfin")
    cact = P.sb([128, 8], F32, "cact")
    wtmp = scr
    bad = P.sb([128, NL, 48], F32, "bad")
    gm = P.sb([128, NL, 8], F32, "gm")
    gf = P.sb([128, NL, 8], F32, "gf")
    P.dma("sp", cact[:], c_in, writes=[cact])
    P.dma("sp", gfin[:], g_fin, writes=[gfin])
    for l in range(NL):
        P.dma("sp", bad[:, l, :], b_ada[l], writes=[bad])
        P.dma("sp", gm[:, l, :], g_mix[l], writes=[gm])
        P.dma("sp", gf[:, l, :], g_ffn[l], writes=[gf])
    ACT(cact[:], cact[:], AF.Silu, [cact], [cact])
    for l in range(NL):
        for pc in range(12):
            P.dma("sp", wtmp[:], w_ada[l][:, pc * 512:(pc + 1) * 512].rearrange("(k p) n -> p k n", p=128), writes=[wtmp])
            ps = PSF()
            for mc in range(4):
                for k in range(8):
                    MM(ps[:, mc:mc + 1], wtmp[:, k, mc * 128:(mc + 1) * 128], cact[:, k:k + 1], k == 0, k == 7, [wtmp, cact], [ps])
            TTe("dve", modp[:, l, pc * 4:(pc + 1) * 4], ps[:, 0:4], bad[:, l, pc * 4:(pc + 1) * 4], ALU.add, [ps, bad], [modp])
        STT("dve", A1[:, l, :], modp[:, l, 8:16], 1.0, gm[:, l, :], ALU.add, ALU.mult, [modp, gm], [A1])
        STT("dve", A2[:, l, :], modp[:, l, 32:40], 1.0, gf[:, l, :], ALU.add, ALU.mult, [modp, gf], [A2])

    mu = P.sb([128, 14], F32, "mu")
    omu = P.sb([128, 14], F32, "omu")
    rwp = P.sb([128, 5, 4], F32, "rwp")
    omka = P.sb([128, 4], F32, "omka")
    nw0 = P.sb([128, 4], F32, "nw0")
    wupb = P.sb([128, 512], BF16, "wupb")
    aupb = P.sb([128, 512], BF16, "aupb")
    gupb = P.sb([128, 512], BF16, "gupb")
    gnw = P.sb([128, 512], F32, "gnw")
    gnb = P.sb([128, 512], F32, "gnb")
    lng = P.sb([128, 512], F32, "lng")
    lnb = P.sb([128, 512], F32, "lnb")
    wsT = P.sb([128, 8, 128], BF16, "wsT")
    wsTf = View(scr[:, 0:2, :].rearrange("p a (g i) -> p (a g) i", g=4), scr)
    bsT = P.sb([128, 4, 128], F32, "bsT")
    cw = P.sb([128, NF, 3], F32, "cw")
    cb = P.sb([128, NF], F32, "cb")

    def load_w(src_ap, kc, ncols):
        wbi[0] += 1
        w = wb[wbi[0] % NWB]
        P.dma("pool", w[:, 0:kc, 0:ncols], src_ap.rearrange("(k p) n -> p k n", p=128), writes=[w])
        return w

    whi = [0]

    def load_wh(src_ap, kc, ncols):
        whi[0] += 1
        i_ = whi[0] % (2 * NWB)
        flat = wb[i_ // 2][:].rearrange("p a b -> p (a b)")[:, (i_ % 2) * 2048:(i_ % 2) * 2048 + kc * ncols]
        w = View(flat.rearrange("p (k n) -> p k n", k=kc), wb[i_ // 2])
        P.dma("pool", w[:], src_ap.rearrange("(k p) n -> p k n", p=128))
        return w

    def rmsnorm_mod(Acol, shcol, l):
        ACT(scr[:], xt[:], AF.Square, [xt], [scr])
        ps = PSF()
        for k in range(8):
            MM(ps[:], onesf, scr[:, k, :], k == 0, k == 7, [cst, scr], [ps])
        ACT(rstd[:], ps[:], AF.Sqrt, [ps], [rstd], bias=1e-6, scale=1.0 / D)
        P.op("dve", lambda e: e.reciprocal(out=rstd[:], in_=rstd[:]), ar=[rstd[:]], aw=[rstd[:]])
        for k in range(8):
            STT("dve", scr[:, k, :], xt[:, k, :], Acol[:, k:k + 1], rstd[:], ALU.mult, ALU.mult, [xt, rstd, A1, A2, gfin], [scr])
            if shcol is not None:
                TS("pool", hT[:, k, :], scr[:, k, :], shcol[:, k:k + 1], None, ALU.add, None, [scr, modp], [hT])

    GC = 0.7978845608028654

    def gelu_from(out, src, rd, wr, tmpa, tmpb):
        ACT(out, src, AF.Gelu_apprx_tanh, rd, wr)

    try:
        for l in range(NL):
            x_src = xT_in if l == 0 else x1T
            P.dma("sp", mu[:], mu_d[l], writes=[mu])
            TS("dve", omu[:], mu[:], -1.0, 1.0, ALU.mult, ALU.add, [mu], [omu])
            P.dma("sp", rwp[:], pr_d[l], writes=[rwp])
            TS("dve", omka[:], rwp[:, 3, :], -1.0, 1.0, ALU.mult, ALU.add, [rwp], [omka])
            TS("dve", nw0[:], rwp[:, 0, :], -1.0, None, ALU.mult, None, [rwp], [nw0])
            P.dma("pool", wupb[0:64, :], wup_d[l], writes=[wupb])
            P.dma("pool", aupb[64:128, :], aup_d[l], writes=[aupb])
            P.dma("pool", gupb[:], gup_d[l], writes=[gupb])
            for (t_, d_) in ((gnw, gnw_d), (gnb, gnb_d), (lng, lng_d), (lnb, lnb_d)):
                P.dma("sp", t_[:], d_[l], writes=[t_])
            P.dma("sp", wsTf[:], wsT_d[l], writes=[wsTf])
            for g in range(8):
                TTe("dve", wsT[:, g, :], wsTf[:, g, :], mrw[:, 0, 1, :], ALU.mult, [wsTf, mrw], [wsT])
            P.dma("sp", bsT[:], bs_d[l], writes=[bsT])
            P.dma("sp", cw[:], cw_d[l], writes=[cw])
            P.dma("sp", cb[:], cb_d[l], writes=[cb])
            MSET("pool", zcar[:], 0.0, [zcar])
            MSET("pool", H0[:], 0.0, [H0])
            MSET("pool", H0b[:], 0.0, [H0b])
            MSET("pool", ccar[:], 0.0, [ccar])
            MSET("pool", big[:, 0:4, :], 0.0, [big])
            for q in range(T // 512):
                P.dma("sp", Vd[l][q * 512:(q + 1) * 512, :].rearrange("(p a) c -> p (a c)", p=128), big[:, 0:4, :].rearrange("p a b -> p (a b)"), reads=[big], writes=[b_vd[l]])

            for it in range(NTI):
                t0 = it * TT
                P.dma("sp", xt[:], x_src[:, t0:t0 + TT].rearrange("(k p) t -> p k t", p=128), reads=[b_x1] if l else [], writes=[xt])
                P.dma("sp", cosb[:], cos_d[:, t0:t0 + TT], writes=[cosb])
                P.dma("sp", sinb[:], sin_d[:, t0:t0 + TT], writes=[sinb])
                chk('pro')
                rmsnorm_mod(A1[:, l, :], modp[:, l, 0:8], l)
                chk('norm')

                for pc in range(2):
                    w = load_w(w_in[l][:, pc * 512:(pc + 1) * 512], 8, 512)
                    for mc in range(4):
                        ps = PSF()
                        for k in range(8):
                            MM(ps[:], w[:, k, mc * 128:(mc + 1) * 128], hT[:, k, :], k == 0, k == 7, [w, hT], [ps])
                        chk('qk1')
                        qs = scr[:, mc, :]
                        qbf = big[:, 14 + mc, :]
                        CP("act", qbf, ps[:], [ps], [big])
                        ps2 = PSF()
                        MM(ps2[:], permb, qbf, True, True, [cstb, big], [ps2])
                        chk('qk2')
                        TTe("dve", qs, ps[:], cosb[:], ALU.mult, [ps, cosb, big, ps2], [scr])
                        TTe("dve", scr[:, 4 + mc, :], ps2[:], sinb[:], ALU.mult, [ps2, sinb], [scr])
                        chk('qk3')
                        dst = QT[:, mc, :] if pc == 0 else KT[:, mc, t0:t0 + TT]
                        TTe("pool", dst, qs, scr[:, 4 + mc, :], ALU.add, [scr], [QT if pc == 0 else KT])
                chk('qk')
                w = load_w(w_in[l][:, 1024:1536], 8, 512)
                for blk in range(4):
                    ps = PSF()
                    for k in range(8):
                        MM(ps[:], hT[:, k, blk * 128:(blk + 1) * 128], w[:, k, :], k == 0, k == 7, [w, hT], [ps])
                    gb = 4 * it + blk
                    CP("act", V1r[:, gb % 8, :], ps[:], [ps], [V1r])
                    P.dma("sp", Vd[l][t0 + blk * 128:t0 + (blk + 1) * 128, :], V1r[:, gb % 8, :], reads=[V1r], writes=[b_vd[l]])
                P.dma("sp", V2[:, it % 2, :, :], Vd[l][t0:t0 + TT, :].rearrange("(i r) c -> i r c", r=4), reads=[b_vd[l]], writes=[V2])

                chk('v')
                if "EP" not in P.__dict__:
                    P.EP = [(View(big[:, 16 + 2 * i_, :], Buf("E%d" % i_)), View(big[:, 17 + 2 * i_, :], Buf("P%d" % i_))) for i_ in range(3)]
                    P.epi = 0
                EPb = [x for pr in P.EP for x in pr]
                P.fence([big], EPb)
                for cfg in range(3):
                    n_sp = t0 // min(2048, T)
                    j3 = (t0 % min(2048, T)) // TT
                    if cfg < 2:
                        items = [(None, c, hh) for c in range(4) for hh in range(2)]
                    else:
                        items = [(rg_, c, hh) for rg_ in range(4) for c in range(4) for hh in range(2)]
                    for (rg_, c, hh) in items:
                        if True:
                            rows = slice(hh * 64, hh * 64 + 64)
                            if cfg < 2:
                                for half in range(2):
                                    psS = PSF()
                                    hp = []
                                    for u in range(2):
                                        qi = half * 2 + u
                                        if cfg == 0:
                                            gb = 4 * it + qi
                                            q_ap = QT[rows, c, qi * 128:(qi + 1) * 128]
                                            kc_ap = KT[rows, c, gb * 128:(gb + 1) * 128]
                                            kp_ap = KT[rows, c, (gb - 1) * 128:gb * 128] if gb > 0 else None
                                            vc_ap = V1r[:, gb % 8, c * 128:(c + 1) * 128]
                                            vp_ap = V1r[:, (gb - 1) % 8, c * 128:(c + 1) * 128]
                                            vb = V1r
                                        else:
                                            q_ap = QT[rows, c, qi:TT:4]
                                            kc_ap = KT[rows, c, t0 + qi:t0 + TT:4]
                                            kp_ap = KT[rows, c, t0 - TT + qi:t0:4] if it > 0 else None
                                            vc_ap = V2[:, it % 2, qi, c * 128:(c + 1) * 128]
                                            vp_ap = V2[:, (it - 1) % 2, qi, c * 128:(c + 1) * 128]
                                            vb = V2
                                        hp.append(kp_ap is not None)
                                        if kp_ap is not None:
                                            MM(psS[:, u * 256:u * 256 + 128], kp_ap, q_ap, True, True, [KT, QT], [psS])
                                        MM(psS[:, u * 256 + 128:u * 256 + 256], kc_ap, q_ap, True, True, [KT, QT], [psS])
                                        hp.append((vp_ap, vc_ap, vb))
                                    P.epi += 1
                                    Et, Pt = P.EP[P.epi % 3]
                                    for u in range(2):
                                        if hp[2 * u]:
                                            ACT(Et[:, u * 256:(u + 1) * 256], psS[:, u * 256:(u + 1) * 256], AF.Exp, [psS], [Et], scale=0.125)
                                        else:
                                            MSET("pool", Et[:, u * 256:u * 256 + 128], 0.0, [Et])
                                            ACT(Et[:, u * 256 + 128:u * 256 + 256], psS[:, u * 256 + 128:u * 256 + 256], AF.Exp, [psS], [Et], scale=0.125)
                                    Pm = Pt[:]
                                    TTe("dve", Pm, Et[:], m12[:].rearrange("p a b -> p (a b)"), ALU.mult, [Et, m12], [Pt])
                                    psN = PSF()
                                    for u in range(2):
                                        vp_ap, vc_ap, vb = hp[2 * u + 1]
                                        has_prev = hp[2 * u]
                                        for z in range(2):
                                            o = psN[:, z * 256 + u * 128:z * 256 + u * 128 + 128]
                                            if has_prev:
                                                MM(o, vp_ap if z == 0 else onesb, Pm[:, u * 256:u * 256 + 128], True, False, [vb, Pt, cstb], [psN])
                                            MM(o, vc_ap if z == 0 else onesb, Pm[:, u * 256 + 128:u * 256 + 256], not has_prev, True, [vb, Pt, cstb], [psN])
                                    src = psN[rows, :].rearrange("p (z u q) -> p z u q", z=2, u=2)
                                    if cfg == 0:
                                        dsta = acc[rows, c, :, half * 256:(half + 1) * 256].rearrange("p z (u q) -> p z u q", u=2)
                                        CP("act", dsta, src, [psN], [acc])
                                    else:
                                        dsta = acc[rows, c, :, :].rearrange("p z (q r) -> p z r q", r=4)[:, :, half * 2:half * 2 + 2, :]
                                        TTe("dve", dsta, src, dsta, ALU.add, [psN, acc], [acc])
                            else:
                                SPAN = min(2048, T)
                                nk3 = SPAN // 16
                                kp_ = slice(0, nk3)
                                for rg in (rg_,):
                                    if c == 0 and hh == 0:
                                        v3 = V3g[rg % 2]
                                        for sp_ in range(2):
                                            nn = n_sp - 1 + sp_
                                            if nn < 0:
                                                continue
                                            P.dma("sp", v3[kp_, sp_, :, :],
                                                  Vd[l][nn * SPAN:(nn + 1) * SPAN, :].rearrange("(i r) c -> i r c", r=16)[:, rg * 4:rg * 4 + 4, :],
                                                  reads=[b_vd[l]], writes=[v3])
                                    v3 = V3g[rg % 2]
                                    psS = PSF()
                                    hasp = n_sp > 0
                                    for rr in range(4):
                                        r = rg * 4 + rr
                                        q_ap = QT[rows, c, r:TT:16]
                                        kc_ap = KT[rows, c, n_sp * SPAN + r:(n_sp + 1) * SPAN:16]
                                        if hasp:
                                            kp_ap = KT[rows, c, (n_sp - 1) * SPAN + r:n_sp * SPAN:16]
                                            MM(psS[kp_, rr * 64:rr * 64 + 32], kp_ap, q_ap, True, True, [KT, QT], [psS])
                                        MM(psS[kp_, rr * 64 + 32:rr * 64 + 64], kc_ap, q_ap, True, True, [KT, QT], [psS])
                                    P.epi += 1
                                    Et, Pt = P.EP[P.epi % 3]
                                    if not hasp:
                                        MSET("pool", Et[kp_, 0:256], 0.0, [Et])
                                        for rr in range(4):
                                            ACT(Et[kp_, rr * 64 + 32:rr * 64 + 64], psS[kp_, rr * 64 + 32:rr * 64 + 64], AF.Exp, [psS], [Et], scale=0.125)
                                    else:
                                        ACT(Et[kp_, 0:256], psS[kp_, 0:256], AF.Exp, [psS], [Et], scale=0.125)
                                    Pm = Pt[kp_, 0:256]
                                    TTe("dve", Pm, Et[kp_, 0:256], m3[kp_, j3, :, :].rearrange("p a b -> p (a b)"), ALU.mult, [Et, m3], [Pt])
                                    psN = PSF()
                                    for rr in range(4):
                                        for z in range(2):
                                            o = psN[:, z * 128 + rr * 32:z * 128 + rr * 32 + 32]
                                            if hasp:
                                                MM(o, v3[kp_, 0, rr, c * 128:(c + 1) * 128] if z == 0 else onesb[kp_, :], Pm[:, rr * 64:rr * 64 + 32], True, False, [v3, Pt, cstb], [psN])
                                            MM(o, v3[kp_, 1, rr, c * 128:(c + 1) * 128] if z == 0 else onesb[kp_, :], Pm[:, rr * 64 + 32:rr * 64 + 64], not hasp, True, [v3, Pt, cstb], [psN])
                                    src = psN[rows, 0:256].rearrange("p (z r q) -> p z r q", z=2, r=4)
                                    dsta = acc[rows, c, :, :].rearrange("p z (q r) -> p z r q", r=16)[:, :, rg * 4:rg * 4 + 4, :]
                                    TTe("dve", dsta, src, dsta, ALU.add, [psN, acc], [acc])
                P.fence(EPb, [big])
                for c in range(4):
                    P.op("dve", lambda e, c=c: e.reciprocal(out=acc[:, c, 1, :], in_=acc[:, c, 1, :]), ar=[acc[:, c, 1, :]], aw=[acc[:, c, 1, :]])
                    TTe("dve", YB[0][:, c, :], acc[:, c, 0, :], acc[:, c, 1, :], ALU.mult, [acc], [YB[0]])
                chk('att')

                for pc in range(4):
                    ncol = 512 if pc < 3 else 256
                    w = load_w(w_in[l][:, 1536 + pc * 512:1536 + pc * 512 + ncol], 8, ncol)
                    for mc in range(ncol // 128):
                        ch = pc * 4 + mc
                        ps = PSF()
                        for k in range(8):
                            MM(ps[:], w[:, k, mc * 128:(mc + 1) * 128], hT[:, k, :], k == 0, k == 7, [w, hT], [ps])
                        zb = scr[:, ch % 4, :]
                        zp = scr[:, 4 + ch % 4, :]
                        CP("act", zb, ps[:], [ps], [scr])
                        CP("pool", zp[:, 1:TT], zb[:, 0:TT - 1], [scr], [scr])
                        CP("pool", zp[:, 0:1], zcar[:, ch:ch + 1], [zcar, scr], [scr])
                        CP("pool", zcar[:, ch:ch + 1], zb[:, TT - 1:TT], [scr], [zcar])
                        TS("dve", zb, zb, omu[:, ch:ch + 1], None, ALU.mult, None, [scr, omu], [scr])
                        STT("dve", big[:, ch, :], zp, mu[:, ch:ch + 1], zb, ALU.mult, ALU.add, [scr, mu], [big])
                chk('rwin'); rwkv_tile(P, locals()); chk('rw')

                gmlp_tile(P, locals()); chk('gmlp')

                wbrs = []
                for br in range(3):
                    for half in range(2):
                        wbt = load_wh(wbr_d[l, br][:, half * 512:(half + 1) * 512], 4, 512)
                        for mc in range(4):
                            m = half * 4 + mc
                            if mc % 2 == 0:
                                c0_ = 4352 + br * 1024 + half * 512 + (mc // 2) * 256
                                wg = load_wh(w_in[l][:, c0_:c0_ + 256], 8, 256)
                            psg = PSF()
                            for k in range(8):
                                MM(psg[:], wg[:, k, (mc % 2) * 128:(mc % 2 + 1) * 128], hT[:, k, :], k == 0, k == 7, [wg, hT], [psg])
                            psbr = PSF()
                            for k in range(4):
                                MM(psbr[:], wbt[:, k, mc * 128:(mc + 1) * 128], YB[br][:, k, :], k == 0, k == 3, [wbt, YB[br]], [psbr])
                            sg = acc[:, (br * 8 + m) % 4, 0, :]
                            ACT(sg, psg[:], AF.Sigmoid, [psg], [acc])
                            if br == 0:
                                TTe("dve", scr[:, m, :], sg, psbr[:], ALU.mult, [acc, psbr], [scr])
                            else:
                                TTe("dve", sg, sg, psbr[:], ALU.mult, [acc, psbr], [acc])
                                TTe("pool", scr[:, m, :], scr[:, m, :], sg, ALU.add, [acc, scr], [scr])
                chk('br')
                for m in range(8):
                    CP("act", hT[:, m, :], scr[:, m, :], [scr], [hT])
                for half in range(2):
                    w = load_w(wout_d[l][:, half * 512:(half + 1) * 512], 8, 512)
                    for mc in range(4):
                        m = half * 4 + mc
                        ps = PSF()
                        for k in range(8):
                            MM(ps[:], w[:, k, mc * 128:(mc + 1) * 128], hT[:, k, :], k == 0, k == 7, [w, hT], [ps])
                        STT("dve", xt[:, m, :], ps[:], modp[:, l, 16 + m:17 + m], xt[:, m, :], ALU.mult, ALU.add, [ps, modp, xt], [xt])

                chk('out')
                rmsnorm_mod(A2[:, l, :], modp[:, l, 24:32], l)
                for fg in range(11):
                    nf = 2
                    wa = load_wh(wfu_d[l][:, fg * 256:fg * 256 + 256], 8, 256)
                    wg = load_wh(wfu_d[l][:, DFF + fg * 256:DFF + fg * 256 + 256], 8, 256)
                    for mc in range(nf):
                        f = fg * 2 + mc
                        psa = PSF()
                        for k in range(8):
                            MM(psa[:], wa[:, k, mc * 128:(mc + 1) * 128], hT[:, k, :], k == 0, k == 7, [wa, hT], [psa])
                        psg = PSF()
                        for k in range(8):
                            MM(psg[:], wg[:, k, mc * 128:(mc + 1) * 128], hT[:, k, :], k == 0, k == 7, [wg, hT], [psg])
                        fb_ = 2 * (f % 2)
                        ab = acc[:, fb_, :, :].rearrange("p a b -> p (a b)")
                        CP("act", ab[:, 2:2 + TT], psa[:], [psa], [acc])
                        CP("pool", ab[:, 0:2], ccar[:, f, :], [ccar, acc], [acc])
                        CP("pool", ccar[:, f, :], ab[:, TT:TT + 2], [acc], [ccar])
                        c1 = acc[:, fb_ + 1, 0, :]
                        TS("dve", c1, ab[:, 2:2 + TT], cw[:, f, 2:3], cb[:, f:f + 1], ALU.mult, ALU.add, [acc, cw, cb], [acc])
                        STT("dve", c1, ab[:, 1:1 + TT], cw[:, f, 1:2], c1, ALU.mult, ALU.add, [acc, cw], [acc])
                        STT("dve", c1, ab[:, 0:TT], cw[:, f, 0:1], c1, ALU.mult, ALU.add, [acc, cw], [acc])
                        ge = acc[:, fb_ + 1, 1, :]
                        gelu_from(ge, c1, [acc], [acc], None, None)
                        TTe("dve", big[:, f, :], ge, psg[:], ALU.mult, [acc, psg], [big])
                for m in range(8):
                    wbi[0] += 1
                    wd = wb[wbi[0] % NWB]
                    wdv = wd[:].rearrange("p a b -> p (a b)")[:, 0:NF * 128].rearrange("p (f n) -> p f n", f=NF)
                    P.dma("pool", wdv, wfd_d[l][:, m * 128:(m + 1) * 128].rearrange("(f p) n -> p f n", p=128), writes=[wd])
                    ps = PSF()
                    for f in range(NF):
                        MM(ps[:], wdv[:, f, :], big[:, f, :], f == 0, f == NF - 1, [wd, big], [ps])
                    STT("dve", xt[:, m, :], ps[:], modp[:, l, 40 + m:41 + m], xt[:, m, :], ALU.mult, ALU.add, [ps, modp, xt], [xt])

                chk('ffn')
                if l < NL - 1:
                    P.dma("sp", x1T[:, t0:t0 + TT].rearrange("(k p) t -> p k t", p=128), xt[:], reads=[xt], writes=[b_x1])
                else:
                    rmsnorm_mod(gfin[:], None, l)
                    P.dma("sp", yT_out[:, t0:t0 + TT].rearrange("(k p) t -> p k t", p=128), scr[:], reads=[scr], writes=[b_out])
    except _Stop:
        src_t = xt
        ybi = {"att": 0, "rw": 1, "gmlp": 2}.get(STOP)
        if ybi is not None:
            MSET("pool", scr[:], 0.0, [scr])
            for c_ in range(4):
                CP("dve", scr[:, c_, :], YB[ybi][:, c_, :], [YB[ybi]], [scr])
            src_t = scr
        elif STOP == "br":
            src_t = scr
        elif STOP == "rw7":
            MSET("pool", scr[:], 0.0, [scr])
            CP("dve", scr[:, 0, :], acc[:, 0, 0, :], [acc], [scr])
            src_t = scr
        P.dma("sp", yT_out[:, 0:TT].rearrange("(k p) t -> p k t", p=128), src_t[:], reads=[src_t], writes=[b_out])
    return P.finish([b_out])


def gmlp_tile(P, L):
    g = L
    PSF, MM, ACT, TTe, TS, STT, CP = g["PSF"], g["MM"], g["ACT"], g["TTe"], g["TS"], g["STT"], g["CP"]
    hT, big, acc, scr, YB, wsT, bsT, lng, lnb = g["hT"], g["big"], g["acc"], g["scr"], g["YB"], g["wsT"], g["bsT"], g["lng"], g["lnb"]
    load_w, w_in, l, gelu_from = g["load_w"], g["w_in"], g["l"], g["gelu_from"]
    w = load_w(w_in[l][:, 3328:3840], 8, 512)
    for mc in range(4):
        ps = PSF()
        for k in range(8):
            MM(ps[:], w[:, k, mc * 128:(mc + 1) * 128], hT[:, k, :], k == 0, k == 7, [w, hT], [ps])
        gelu_from(scr[:, mc, :], ps[:], [ps], [scr], scr[:, 4, :], scr[:, 5, :])
    w = load_w(w_in[l][:, 3840:4352], 8, 512)
    stats = acc[:, 3, 1, 0:8]
    for blk in range(4):
        ps = PSF()
        for k in range(8):
            MM(ps[:], hT[:, k, blk * 128:(blk + 1) * 128], w[:, k, :], k == 0, k == 7, [w, hT], [ps])
        v = acc[:, 0, 0, :]
        gelu_from(v, ps[:], [ps], [acc], acc[:, 0, 1, :], acc[:, 1, 0, :])
        P.op("dve", lambda e: e.bn_stats(out=acc[:, 3, 1, 0:6], in_=v), ar=[v], aw=[acc[:, 3, 1, 0:6]])
        P.op("dve", lambda e: e.bn_aggr(out=acc[:, 3, 1, 6:8], in_=acc[:, 3, 1, 0:6]), ar=[acc[:, 3, 1, 0:6]], aw=[acc[:, 3, 1, 6:8]])
        ACT(acc[:, 3, 1, 7:8], acc[:, 3, 1, 7:8], AF.Sqrt, [acc], [acc], bias=1e-5)
        P.op("dve", lambda e: e.reciprocal(out=acc[:, 3, 1, 7:8], in_=acc[:, 3, 1, 7:8]), ar=[acc[:, 3, 1, 7:8]], aw=[acc[:, 3, 1, 7:8]])
        TS("dve", v, v, acc[:, 3, 1, 6:7], acc[:, 3, 1, 7:8], ALU.subtract, ALU.mult, [acc], [acc])
        TTe("dve", v, v, lng[:], ALU.mult, [acc, lng], [acc])
        vb = big[:, 18, :]
        TTe("dve", vb, v, lnb[:], ALU.add, [acc, lnb], [big])
        psA = PSF()
        psB = PSF()
        for gq in range(8):
            pp = psA if gq < 4 else psB
            c = gq // 2
            MM(pp[:, (gq % 4) * 128:(gq % 4) * 128 + 128], vb[:, c * 128:(c + 1) * 128], wsT[:, gq, :], True, True, [big, wsT], [pp])
        for gq in range(8):
            pp = psA if gq < 4 else psB
            c = gq // 2
            rows = slice((gq % 2) * 64, (gq % 2) * 64 + 64)
            tmp = acc[rows, 1, 1, 0:128]
            TTe("dve", tmp, pp[rows, (gq % 4) * 128:(gq % 4) * 128 + 128], bsT[rows, c, :], ALU.add, [pp, bsT], [acc])
            TTe("dve", YB[2][rows, c, blk * 128:(blk + 1) * 128], tmp, scr[rows, c, blk * 128:(blk + 1) * 128], ALU.mult, [acc, scr], [YB[2]])


def rwkv_tile(P, L):
    g = L
    PSF, PSB, MM, TR, ACT, TTe, TS, STT, CP, MSET = (g[k] for k in ("PSF", "PSB", "MM", "TR", "ACT", "TTe", "TS", "STT", "CP", "MSET"))
    big, scr, acc, YB, H0, H0b = g["big"], g["scr"], g["acc"], g["YB"], g["H0"], g["H0b"]
    rwp, omka, wupb, aupb, gupb, gnw, gnb = g["rwp"], g["omka"], g["wupb"], g["aupb"], g["gupb"], g["gnw"], g["gnb"]
    cst, cstb, mrw, mlo, sel = g["cst"], g["cstb"], g["mrw"], g["mlo"], g["sel"]
    chk = g["chk"]
    identb, identf, blkones = g["identb"], g["identf"], g["blkones"]
    rw = getattr(P, "_rw", None)
    if rw is None:
        sb = P.sb
        rw = dict(
            th=sb([128, 128], BF16), sgx=sb([128, 128], BF16),
            **{nm: View(scr[:, i_, :].rearrange("p (a b) -> p a b", a=4), scr) for i_, nm in enumerate(("lw", "cum", "cum2", "av", "kk", "km", "t1", "t2"))},
            AR=sb([128, 4, 2, 128], BF16), BK=sb([128, 4, 2, 128], BF16),
            wc=sb([128, 4], F32),
            tok=View(g["QT"][:, 0:3, :], g["QT"]),
            Xs=View(g["QT"][:, 3, :], g["QT"]),
            **{nm: View(g["V3g"][0][:, i_ // 2, (i_ % 2) * 2:(i_ % 2) * 2 + 2, :].rearrange("p a (h t) -> p (a h) t", h=4), g["V3g"][0])
               for i_, nm in enumerate(("N1", "L1", "N2", "L2"))},
            **{nm: View(YB[2][:, 2 * i_:2 * i_ + 2, :].rearrange("p a (h t) -> p (a h) t", h=4), YB[2]) for i_, nm in enumerate(("IL", "Mm"))},
            **{nm: View(big[:, 14 + 2 * i_:16 + 2 * i_, :].rearrange("p a (h t) -> p (a h) t", h=4), big)
               for i_, nm in enumerate(("Mm2", "Arb", "Aak", "Ark"))},
            Y=View(acc[:, 0, 0, :], acc), Y2=View(acc[:, 0, 1, :], acc), st=sb([128, 4, 8], F32), gt=View(acc[:, 1, 0, :], acc),
            Ht=sb([128, 4, 64], F32),
            Noff=View(g["rstd"][:].bitcast(BF16).rearrange("p (h t) -> p h t", h=8), g["rstd"]),
        )
        P._rw = rw
    rw = P._rw
    th, sgx, lw, cum, cum2, av, kk, km, t1, t2 = (rw[k] for k in ("th", "sgx", "lw", "cum", "cum2", "av", "kk", "km", "t1", "t2"))
    AR, BK, wc, tok = rw["AR"], rw["BK"], rw["wc"], rw["tok"]
    N1, L1, N2, L2, IL, Mm, Mm2, Arb, Aak, Ark = (rw[k] for k in ("N1", "L1", "N2", "L2", "IL", "Mm", "Mm2", "Arb", "Aak", "Ark"))
    V1r_ = g["V1r"]
    fs_ = (4, 5, 6) if g["it"] % 2 == 0 else (0, 1, 2)
    Us = View(V1r_[:, fs_[0], :], V1r_)
    yb = View(V1r_[:, fs_[1], :], V1r_)
    prod = View(V1r_[:, fs_[2], :].rearrange("p (a b) -> p a b", a=4), V1r_)
    Xs, Y, Y2, st, gt, Ht = (rw[k] for k in ("Xs", "Y", "Y2", "st", "gt", "Ht"))
    W0, A0, KK_, KA, RK = (rwp[:, i, :] for i in range(5))
    EH = 0.6065306597126334

    for blk in range(4):
        ts = slice(blk * 128, (blk + 1) * 128)
        zr = lambda c: big[:, 0 + c, ts]
        zk = lambda c: big[:, 4 + c, ts]
        zv = lambda c: big[:, 8 + c, ts]
        zwa = big[:, 12, ts]
        zg = big[:, 13, ts]
        ACT(th[0:64, :], big[0:64, 12, ts], AF.Tanh, [big], [th])
        ACT(sgx[:], zg, AF.Sigmoid, [big], [sgx])
        ps = PSF()
        psa = PSF()
        for c in range(4):
            MM(ps[:, c * 128:(c + 1) * 128], wupb[0:64, c * 128:(c + 1) * 128], th[0:64, :], True, True, [wupb, th], [ps])
            MM(psa[:, c * 128:(c + 1) * 128], aupb[64:128, c * 128:(c + 1) * 128], big[64:128, 12, ts], True, True, [aupb, big], [psa])
        for c in range(4):
            ACT(lw[:, c, :], ps[:, c * 128:(c + 1) * 128], AF.Exp, [ps, g["nw0"]], [lw], bias=g["nw0"][:, c:c + 1], scale=-1.0)
            ACT(av[:, c, :], psa[:, c * 128:(c + 1) * 128], AF.Sigmoid, [psa, rwp], [av], bias=A0[:, c:c + 1])
        TS("dve", lw[:], lw[:], 1.0, None, ALU.add, None, [lw], [lw])
        P.op("dve", lambda e: e.reciprocal(out=lw[:], in_=lw[:]), ar=[lw[:]], aw=[lw[:]])
        TS("dve", lw[:], lw[:], -EH, None, ALU.mult, None, [lw], [lw])
        psg = PSF()
        MM(psg[:], sgx[:], gupb[:], True, True, [sgx, gupb], [psg])
        CP("act", gt[:], psg[:], [psg], [gt])
        chk('rw1')
        for c in range(4):
            TS("dve", kk[:, c, :], zk(c), KK_[:, c:c + 1], None, ALU.mult, None, [big, rwp], [kk])
        ACT(t1[:], kk[:], AF.Square, [kk], [t1])
        ps = PSF()
        for c in range(4):
            MM(ps[:, c * 128:(c + 1) * 128], blkones, t1[:, c, :], True, True, [cst, t1], [ps])
        ACT(t1[:].rearrange("p a b -> p (a b)"), ps[:], AF.Sqrt, [ps], [t1])
        TS("dve", t1[:], t1[:], 1e-12, None, ALU.max, None, [t1], [t1])
        P.op("dve", lambda e: e.reciprocal(out=t1[:], in_=t1[:]), ar=[t1[:]], aw=[t1[:]])
        TTe("dve", kk[:], kk[:], t1[:], ALU.mult, [kk, t1], [kk])
        for c in range(4):
            TS("dve", t2[:, c, :], av[:, c, :], KA[:, c:c + 1], omka[:, c:c + 1], ALU.mult, ALU.add, [av, rwp, omka], [t2])
            TTe("dve", km[:, c, :], t2[:, c, :], zk(c), ALU.mult, [t2, big], [km])
        for c in range(4):
            STT("dve", prod[:, c, :], zr(c), RK[:, c:c + 1], km[:, c, :], ALU.mult, ALU.mult, [big, rwp, km], [prod])
        psr = PSF()
        for c in range(4):
            MM(psr[:, 0:8], prod[:, c, :], sel[:, c, :], c == 0, c == 3, [prod, sel], [psr])
        CP("act", st[:, 0, :], psr[:, 0:8], [psr], [st])
        chk('rw2')
        src, dst = lw, cum
        CP("pool", cum[:], lw[:], [lw], [cum])
        a_, b_ = cum, cum2
        for s in (1, 2, 4, 8, 16, 32, 64):
            CP("pool", b_[:, :, 0:s], a_[:, :, 0:s], [a_], [b_])
            TTe("dve", b_[:, :, s:128], a_[:, :, s:128], a_[:, :, 0:128 - s], ALU.add, [a_], [b_])
            a_, b_ = b_, a_
        cm = a_
        ot = b_
        ACT(t1[:], cm[:], AF.Exp, [cm], [t1])
        for c in range(4):
            TTe("dve", AR[:, c, 1, :], t1[:, c, :], zr(c), ALU.mult, [t1, big], [AR])
        ACT(wc[:], cm[:, :, 127], AF.Exp, [cm], [wc])
        TTe("dve", ot[:], cm[:], lw[:], ALU.subtract, [cm, lw], [ot])
        ACT(t1[:], ot[:], AF.Exp, [ot], [t1])
        STT("dve", AR[:, :, 0, :], kk[:], -1.0, t1[:], ALU.mult, ALU.mult, [kk, t1], [AR])
        ACT(t1[:], cm[:], AF.Exp, [cm], [t1], scale=-1.0)
        TTe("dve", t2[:], kk[:], av[:], ALU.mult, [kk, av], [t2])
        TTe("dve", BK[:, :, 0, :], t2[:], t1[:], ALU.mult, [t2, t1], [BK])
        TTe("dve", BK[:, :, 1, :], km[:], t1[:], ALU.mult, [km, t1], [BK])
        chk('rw3')
        pb = PSB()
        for c in range(4):
            TR(pb[:, c * 128:(c + 1) * 128], zv(c), identb, [big, cstb], [pb])
        CP("act", tok[:, 0, :], pb[:, 0:512], [pb], [tok])
        for q in range(2):
            pb = PSB()
            for c in range(4):
                TR(pb[:, c * 128:(c + 1) * 128], BK[:, c, q, :], identb, [BK, cstb], [pb])
            CP("act", tok[:, 1 + q, :], pb[:, 0:512], [pb], [tok])
        chk('rw4')
        for hg in range(2):
            p1 = PSF(); p2 = PSF(); p3 = PSF(); p4 = PSF()
            for hq in range(4):
                c, hh = hq, hg
                rows = slice(hh * 64, hh * 64 + 64)
                half = hq % 2
                pa = p1 if hq < 2 else p2
                MM(pa[:, half * 256:half * 256 + 256], BK[rows, c, 0, :], AR[rows, c, :, :].rearrange("p a b -> p (a b)"), True, True, [BK, AR], [pa])
                pk = p3 if hq < 2 else p4
                MM(pk[:, half * 256:half * 256 + 256], BK[rows, c, 1, :], AR[rows, c, :, :].rearrange("p a b -> p (a b)"), True, True, [BK, AR], [pk])
            chk('rw4a')
            for i2, (pa, pk) in enumerate(((p1, p3), (p2, p4))):
                h0 = hg * 4 + i2 * 2
                sa = pa[:].rearrange("p (h z t) -> p h z t", h=2, z=2)
                sk = pk[:].rearrange("p (h z t) -> p h z t", h=2, z=2)
                TTe("dve", N1[:, h0:h0 + 2, :], sa[:, :, 0, :], mrw[:, 0:2, 0, :], ALU.mult, [pa, mrw], [N1])
                TTe("dve", Arb[:, h0:h0 + 2, :], sa[:, :, 1, :], mrw[:, 0:2, 1, :], ALU.mult, [pa, mrw], [Arb])
                TTe("dve", Aak[:, h0:h0 + 2, :], sk[:, :, 0, :], mrw[:, 0:2, 0, :], ALU.mult, [pk, mrw], [Aak])
                TTe("dve", Ark[:, h0:h0 + 2, :], sk[:, :, 1, :], mrw[:, 0:2, 1, :], ALU.mult, [pk, mrw], [Ark])
            chk('rw4b')
            pl = PSF()
            for hq in range(4):
                c, hh = hq, hg
                rows = slice(hh * 64, hh * 64 + 64)
                MM(pl[:, hq * 128:(hq + 1) * 128], AR[rows, c, 0, :], BK[rows, c, 0, :], True, True, [AR, BK], [pl])
            TTe("dve", L1[:, hg * 4:hg * 4 + 4, :], pl[:].rearrange("p (h t) -> p h t", h=4), mlo[:], ALU.mult, [pl, mlo], [L1])
        chk('rw5')
        Noff = rw["Noff"]
        MSET("pool", Noff[:], 0.0, [Noff])
        CP("pool", Noff[0:64, :, 64:128], N1[0:64, :, 64:128], [N1], [Noff])
        MSET("pool", N1[0:64, :, 64:128], 0.0, [N1])
        MSET("pool", L1[64:128, :, 0:64], 0.0, [L1])
        for hg in range(2):
            hs = slice(hg * 4, hg * 4 + 4)
            TTe("pool", Mm[:, hs, :], N1[:, hs, :], cstb[:, 4:8, :], ALU.add, [N1, cstb], [Mm])
        Nk, Lk, Nn, Ln = N1, L1, N2, L2
        Mc, Mn = Mm, Mm2
        for lev in range(5):
            last = lev == 4
            for hg in range(2):
                hs = slice(hg * 4, hg * 4 + 4)
                pn = PSF()
                for hq in range(4):
                    h = hg * 4 + hq
                    MM(pn[:, hq * 128:(hq + 1) * 128], Lk[:, h, :], Nk[:, h, :], True, True, [Lk, Nk], [pn])
                if not last:
                    CP("act", Nn[:, hs, :], pn[:].rearrange("p (h t) -> p h t", h=4), [pn], [Nn])
                pL = PSF()
                for hq in range(4):
                    h = hg * 4 + hq
                    MM(pL[:, hq * 128:(hq + 1) * 128], Nk[:, h, :], Lk[:, h, :], True, True, [Lk, Nk], [pL])
                if not last:
                    CP("act", Ln[:, hs, :], pL[:].rearrange("p (h t) -> p h t", h=4), [pL], [Ln])
                TTe("dve", IL[:, hs, :], pL[:].rearrange("p (h t) -> p h t", h=4), cstb[:, 4:8, :], ALU.add, [pL, cstb], [IL])
                pm = PSF()
                for hq in range(4):
                    h = hg * 4 + hq
                    MM(pm[:, hq * 128:(hq + 1) * 128], IL[:, h, :], Mc[:, h, :], True, True, [IL, Mc], [pm])
                CP("act", Mn[:, hs, :], pm[:].rearrange("p (h t) -> p h t", h=4), [pm], [Mn])
            Nk, Nn = Nn, Nk
            Lk, Ln = Ln, Lk
            Mc, Mn = Mn, Mc
        Mf = Mc
        chk('rw6')
        px = PSF()
        for h in range(8):
            c, hh = h % 4, h // 4
            hn = 2 * c + hh
            rows = slice(hh * 64, hh * 64 + 64)
            o = px[:, hn * 64:(hn + 1) * 64]
            MM(o, AR[rows, c, 0, :], H0b[rows, c, :], True, False, [AR, H0b], [px])
            MM(o, Aak[:, h, :], tok[:, 0, hn * 64:(hn + 1) * 64], False, True, [Aak, tok], [px])
        CP("act", Xs[:], px[:], [px], [Xs])
        Noff = rw["Noff"]
        for rnd in range(2):
            pu = PSF()
            for h in range(8):
                c, hh = h % 4, h // 4
                hn = 2 * c + hh
                MM(pu[:, hn * 64:(hn + 1) * 64], Mf[:, h, :], Xs[:, hn * 64:(hn + 1) * 64], True, True, [Mf, Xs], [pu])
            CP("act", Us[:], pu[:], [pu], [Us])
            if rnd == 0:
                pw = PSF()
                for h in range(8):
                    c, hh = h % 4, h // 4
                    hn = 2 * c + hh
                    MM(pw[:, hn * 64:(hn + 1) * 64], Noff[:, h, :], Us[:, hn * 64:(hn + 1) * 64], True, True, [Noff, Us], [pw])
                TTe("dve", Xs[:], pw[:], Xs[:], ALU.add, [pw, Xs], [Xs])
        py = PSF()
        for h in range(8):
            c, hh = h % 4, h // 4
            hn = 2 * c + hh
            rows = slice(hh * 64, hh * 64 + 64)
            o = py[:, hn * 64:(hn + 1) * 64]
            MM(o, AR[rows, c, 1, :], H0b[rows, c, :], True, False, [AR, H0b], [py])
            MM(o, Arb[:, h, :], Us[:, hn * 64:(hn + 1) * 64], False, False, [Arb, Us], [py])
            MM(o, Ark[:, h, :], tok[:, 0, hn * 64:(hn + 1) * 64], False, True, [Ark, tok], [py])
        CP("act", Y[:], py[:], [py], [Y])
        ph = PSF()
        for c in range(4):
            o = ph[:, c * 128:(c + 1) * 128]
            MM(o, tok[:, 1, c * 128:(c + 1) * 128], Us[:, c * 128:(c + 1) * 128], True, False, [tok, Us], [ph])
            MM(o, tok[:, 2, c * 128:(c + 1) * 128], tok[:, 0, c * 128:(c + 1) * 128], False, True, [tok], [ph])
        for hh in range(2):
            rows = slice(hh * 64, hh * 64 + 64)
            src = ph[rows, :].rearrange("p (c x i) -> p c x i", c=4, x=2)[:, :, hh, :]
            TTe("dve", Ht[rows, :, :], src, H0[rows, :, :], ALU.add, [ph, H0], [Ht])
        for c in range(4):
            TS("dve", H0[:, c, :], Ht[:, c, :], wc[:, c:c + 1], None, ALU.mult, None, [Ht, wc], [H0])
        CP("act", H0b[:], H0[:], [H0], [H0b])
        chk('rw7')
        Y3 = Y[:].rearrange("p (h i) -> p h i", h=8)
        P.op("dve", lambda e: e.tensor_reduce(out=st[:, 1, :], in_=Y3, axis=AX.X, op=ALU.add), ar=[Y3], aw=[st[:, 1, :]])
        ACT(Y2[:], Y[:], AF.Square, [Y], [Y2])
        P.op("dve", lambda e: e.tensor_reduce(out=st[:, 2, :], in_=Y2[:].rearrange("p (h i) -> p h i", h=8), axis=AX.X, op=ALU.add), ar=[Y2[:]], aw=[st[:, 2, :]])
        TS("dve", st[:, 1, :], st[:, 1, :], 1.0 / 64, None, ALU.mult, None, [st], [st])
        TTe("dve", st[:, 3, :], st[:, 1, :], st[:, 1, :], ALU.mult, [st], [st])
        STT("dve", st[:, 2, :], st[:, 2, :], 1.0 / 64, st[:, 3, :], ALU.mult, ALU.subtract, [st], [st])
        ACT(st[:, 2, :], st[:, 2, :], AF.Sqrt, [st], [st], bias=64e-5)
        P.op("dve", lambda e: e.reciprocal(out=st[:, 2, :], in_=st[:, 2, :]), ar=[st[:, 2, :]], aw=[st[:, 2, :]])
        for h in range(8):
            TS("dve", Y2[:, h * 64:(h + 1) * 64], Y[:, h * 64:(h + 1) * 64], st[:, 1, h:h + 1], st[:, 2, h:h + 1], ALU.subtract, ALU.mult, [Y, st], [Y2])
        TTe("dve", Y2[:], Y2[:], gnw[:], ALU.mult, [Y2, gnw], [Y2])
        TTe("dve", Y2[:], Y2[:], gnb[:], ALU.add, [Y2, gnb], [Y2])
        for h in range(8):
            STT("dve", Y2[:, h * 64:(h + 1) * 64], tok[:, 0, h * 64:(h + 1) * 64], st[:, 0, h:h + 1], Y2[:, h * 64:(h + 1) * 64], ALU.mult, ALU.add, [tok, st, Y2], [Y2])
        TTe("dve", yb[:], Y2[:], gt[:], ALU.mult, [Y2, gt], [yb])
        pb = PSB()
        for c in range(4):
            TR(pb[:, c * 128:(c + 1) * 128], yb[:, c * 128:(c + 1) * 128], identb, [yb, cstb], [pb])
        CP("act", YB[1][:, :, ts], pb[:, 0:512].rearrange("p (c t) -> p c t", c=4), [pb], [YB[1]])


def _consts(T):
    bf = ml_dtypes.bfloat16
    cst = np.zeros((128, 5, 128), np.float32)
    cst[:, 0, :] = np.eye(128)
    cst[:, 1, :] = 1.0
    cst[0:64, 2, 0:64] = 1.0
    cst[64:128, 2, 64:128] = 1.0
    cstb = np.zeros((128, 8, 128), np.float32)
    cstb[:, 0, :] = np.eye(128)
    for m in range(128):
        cstb[(m // 64) * 64 + ((m % 64) + 32) % 64, 1, m] = 1.0
    cstb[:, 2, :] = 1.0
    for q_ in range(4):
        cstb[:, 4 + q_, :] = np.eye(128)
    ki = np.arange(128)[:, None]
    qi = np.arange(128)[None, :]
    m12 = np.zeros((128, 2, 256), np.float32)
    for u in range(2):
        m12[:, u, 0:128] = (ki >= qi)
        m12[:, u, 128:256] = (ki <= qi)
    m3 = np.zeros((128, 4, 4, 64), np.float32)
    for j in range(4):
        q = 32 * j + np.arange(32)[None, :]
        for rr in range(4):
            m3[:, j, rr, 0:32] = (ki >= q)
            m3[:, j, rr, 32:64] = (ki <= q)
    mrw = np.zeros((128, 4, 2, 128), np.float32)
    mrw[:, :, 0, :] = (ki < qi)[:, None, :]
    mrw[:, :, 1, :] = (ki <= qi)[:, None, :]
    mlo = np.zeros((128, 4, 128), np.float32)
    mlo[:, :, :] = (qi < ki)[:, None, :]
    sel = np.zeros((128, 4, 8), np.float32)
    for p in range(128):
        for c in range(4):
            sel[p, c, 2 * c + p // 64] = 1.0
    inv = (1.0 / (np.float32(10000.0) ** (np.arange(0, 64, 2, dtype=np.float32) / np.float32(64)))).astype(np.float32)
    ang = (np.arange(T, dtype=np.float32)[:, None] * inv[None, :]).astype(np.float32)
    cosv, sinv = np.cos(ang).astype(np.float32), np.sin(ang).astype(np.float32)
    cosT = np.zeros((128, T), np.float32)
    sinT = np.zeros((128, T), np.float32)
    for p in range(128):
        d = p % 64
        cosT[p] = cosv[:, d % 32]
        sinT[p] = sinv[:, d % 32] * (-1.0 if d < 32 else 1.0)
    return dict(cst_f32=cst, cst_bf=cstb.astype(bf), m12=m12.astype(bf), m3=m3.astype(bf), mrw=mrw.astype(bf),
                mlo=mlo.astype(bf), sel=sel.astype(bf), cosT=cosT, sinT=sinT)


def _col(v, n):
    return np.ascontiguousarray(np.asarray(v, np.float32).reshape(n, 128).T)


def _shared_inputs(inp, T):
    f = lambda a: np.ascontiguousarray(np.asarray(a, np.float32)[:NL]) if np.asarray(a).shape[0] == 2 and np.asarray(a).ndim >= 2 else np.ascontiguousarray(np.asarray(a, np.float32))
    d = dict(_consts(T))
    d["w_ada"] = f(inp["w_ada"])
    d["b_ada"] = np.stack([_col(inp["b_ada"][l], 48) for l in range(NL)])
    d["g_mix"] = np.stack([_col(inp["norm_mix"][l], 8) for l in range(NL)])
    d["g_ffn"] = np.stack([_col(inp["norm_ffn"][l], 8) for l in range(NL)])
    d["g_fin"] = _col(inp["norm_final"], 8)
    d["w_in"] = f(inp["w_in"])
    d["mu"] = np.stack([_col(inp["rwkv_mu"][l], 14) for l in range(NL)])
    d["rwp"] = np.stack([np.stack([_col(np.asarray(inp[k][l]).reshape(-1), 4) for k in
                                   ("rwkv_w0", "rwkv_a0", "rwkv_k_k", "rwkv_k_a", "rwkv_r_k")], axis=1) for l in range(NL)])
    d["r_wup"] = f(inp["rwkv_w_up"])
    d["r_aup"] = f(inp["rwkv_a_up"])
    d["r_gup"] = f(inp["rwkv_g_up"])
    bc = lambda a: np.ascontiguousarray(np.broadcast_to(f(a).reshape(NL, 1, 512), (NL, 128, 512)))
    d["gnw"] = bc(inp["rwkv_gn_w"])
    d["gnb"] = bc(inp["rwkv_gn_b"])
    d["lng"] = bc(inp["gmlp_ln_g"])
    d["lnb"] = bc(inp["gmlp_ln_b"])
    d["wsT"] = np.ascontiguousarray(np.transpose(f(inp["gmlp_w_s"]), (0, 3, 1, 2)))
    bsv = f(inp["gmlp_b_s"])
    bsT = np.zeros((NL, 128, 4, 128), np.float32)
    for g_ in range(8):
        bsT[:, (g_ % 2) * 64:(g_ % 2) * 64 + 64, g_ // 2, :] = bsv[:, g_, None, :]
    d["bsT"] = bsT
    d["w_branch"] = f(inp["w_branch"])
    d["w_out"] = f(inp["w_out"])
    d["ffn_up"] = f(inp["ffn_w_up"])
    d["convw"] = np.stack([np.ascontiguousarray(np.transpose(np.asarray(inp["ffn_conv_w"][l], np.float32).reshape(3, NF, 128), (2, 1, 0))) for l in range(NL)])
    d["convb"] = np.stack([_col(inp["ffn_conv_b"][l], NF) for l in range(NL)])
    d["ffn_down"] = f(inp["ffn_w_down"])
    return d


_NC_CACHE = {}


def run(inp, T=None):
    x = np.asarray(inp["x"], np.float32)
    B, S, _ = x.shape
    T = S
    if T not in _NC_CACHE:
        _NC_CACHE[T] = build(T)
    nc = _NC_CACHE[T]
    shared = _shared_inputs(inp, T)
    c = np.asarray(inp["c"], np.float32)
    in_maps = []
    for core in range(8):
        b = core % B
        m = dict(shared)
        m["xT"] = np.ascontiguousarray(x[b].T)
        m["c128"] = _col(c[b], 8)
        in_maps.append(m)
    res = run_bass_kernel_spmd(nc, in_maps, core_ids=list(range(8)))
    out = np.stack([np.ascontiguousarray(res.results[b]["yT"].T) for b in range(B)])
    return out.astype(np.float32)


def kernel(**inputs):
    return run(inputs)
```

```python
import numpy as np
import ml_dtypes
import concourse.bass as bass
import concourse.mybir as mybir
from concourse.bass_utils import run_bass_kernel_spmd

F32 = mybir.dt.float32
BF16 = mybir.dt.bfloat16
AF = mybir.ActivationFunctionType
ALU = mybir.AluOpType
AX = mybir.AxisListType

D = 1024
import os as _os
NL = int(_os.environ.get("KNL", "2"))
TT = 512
IN_COLS = 7424
DFF = 2816
NF = 22


class Buf:
    __slots__ = ("name", "lw", "rd", "excl")

    def __init__(self, name="", excl=False):
        self.name = name
        self.lw = None
        self.rd = []
        self.excl = excl


class Tl:
    __slots__ = ("t", "b")

    def __init__(self, t, b):
        self.t = t
        self.b = b

    def __getitem__(self, k):
        return self.t[k]


def _b(x):
    return x.b if isinstance(x, Tl) else x


def View(ap, buf):
    return Tl(ap, _b(buf))


class Prog:
    ENGS = ("pe", "dve", "act", "pool", "sp")
    NDMA = 8

    def __init__(self):
        self.nc = bass.Bass("TRN2", target_bir_lowering=False)
        self.ops = {e: [] for e in self.ENGS}
        self.cnt = {e: 0 for e in self.ENGS}
        self.seen = {e: {} for e in self.ENGS}
        self.sems = {}
        self._stack = []
        nc = self.nc
        for e in ("pe", "dve", "act", "pool"):
            self.sems[e] = self._enter(nc.semaphore("s_" + e))
        self.dslots = {}
        for q in ("sp", "act", "pool"):
            sl = []
            for i in range(self.NDMA):
                key = "d_%s%d" % (q, i)
                self.sems[key] = self._enter(nc.semaphore(key))
                sl.append([key, 0])
            self.dslots[q] = [sl, 0]
        self.n_sb = 0
        self.reg = {}

    def _enter(self, cm):
        v = cm.__enter__()
        self._stack.append(cm)
        return v

    def sb(self, shape, dtype, name=None):
        self.n_sb += 1
        t = self._enter(self.nc.sbuf_tensor("s_%s_%d" % (name or "t", self.n_sb), list(shape), dtype))
        return Tl(t, Buf(name or ""))

    def ps(self, shape, dtype=F32, name=None):
        self.n_sb += 1
        t = self._enter(self.nc.psum_tensor(name or ("ps%d" % self.n_sb), list(shape), dtype))
        return Tl(t, Buf(name or "", excl=True))

    def _deps(self, eng, reads, writes):
        toks = []
        for b in reads:
            b = _b(b)
            if b.lw is not None:
                toks.append(b.lw)
        for b in writes:
            b = _b(b)
            if b.lw is not None:
                toks.append(b.lw)
            toks.extend(b.rd)
        need = {}
        for (k, v) in toks:
            if k == "pe" and eng == "pe":
                continue
            if self.seen[eng].get(k, 0) >= v:
                continue
            if need.get(k, 0) < v:
                need[k] = v
        for k, v in need.items():
            self.seen[eng][k] = v
        return list(need.items())

    def _commit(self, tok, reads, writes):
        for b in reads:
            b = _b(b)
            b.rd.append(tok)
            if len(b.rd) > 64:
                mx = {}
                for (k, v) in b.rd:
                    if mx.get(k, 0) < v:
                        mx[k] = v
                b.rd = list(mx.items())
        for b in writes:
            b = _b(b)
            b.lw = tok
            b.rd = []

    @staticmethod
    def _onchip(ap):
        n = ap.name
        return n.startswith("s_") or n.startswith("ps")

    def _reg(self, ap):
        pstep, pcnt = ap.ap[0]
        off = ap.offset
        if ap.name.startswith("ps"):
            return (ap.name, 0, 128, 0, 1 << 30, True)
        p0 = off // pstep
        f0 = off % pstep
        span = 0
        for st, cnt in ap.ap[1:]:
            span += (cnt - 1) * abs(st)
        esz = 2 if ap.dtype == BF16 else 4
        return (ap.name, p0, p0 + pcnt, f0 * esz, (f0 + span + 1) * esz, False)

    def _rdeps(self, eng, ars, aws):
        toks = []
        new = []
        for (aps, isw) in ((ars, False), (aws, True)):
            for ap in aps:
                name, p0, p1, b0, b1, isps = self._reg(ap)
                w = isw or (isps and eng != "pe")
                recs = self.reg.setdefault(name, [])
                for r in recs:
                    if r[0] < p1 and p0 < r[1] and r[2] < b1 and b0 < r[3] and (w or r[5]):
                        toks.append(r[4])
                new.append((name, [p0, p1, b0, b1, None, w]))
        return toks, new

    def _rcommit(self, tok, new):
        for name, rec in new:
            rec[4] = tok
            recs = self.reg[name]
            if rec[5]:
                recs[:] = [r for r in recs if not (rec[0] <= r[0] and r[1] <= rec[1] and rec[2] <= r[2] and r[3] <= rec[3])]
            else:
                recs[:] = [r for r in recs if not ((not r[5]) and r[4][0] == tok[0] and rec[0] <= r[0] and r[1] <= rec[1]
                                                   and rec[2] <= r[2] and r[3] <= rec[3])]
            recs.append(rec)

    def _filter(self, eng, toks):
        need = {}
        for (k, v) in toks:
            if k == "pe" and eng == "pe":
                continue
            if self.seen[eng].get(k, 0) >= v:
                continue
            if need.get(k, 0) < v:
                need[k] = v
        for k, v in need.items():
            self.seen[eng][k] = v
        return list(need.items())

    def _buftoks(self, reads, writes):
        toks = []
        for b in reads:
            if b.lw is not None:
                toks.append(b.lw)
        for b in writes:
            if b.lw is not None:
                toks.append(b.lw)
            toks.extend(b.rd)
        return toks

    def op(self, eng, fn, reads=(), writes=(), ar=None, aw=None):
        if eng == "pool":
            eng = "dve"
        assert ar is not None and aw is not None
        reads = [b for b in reads if isinstance(b, Buf)]
        writes = [b for b in writes if isinstance(b, Buf)]
        rt, new = self._rdeps(eng, ar, aw)
        waits = self._filter(eng, self._buftoks(reads, writes) + rt)
        self.cnt[eng] += 1
        tok = (eng, self.cnt[eng])
        self.ops[eng].append((waits, fn, (eng, 1)))
        self._commit(tok, reads, writes)
        self._rcommit(tok, new)
        return tok

    def dma(self, q, out, in_, reads=(), writes=()):
        sl, idx = self.dslots[q]
        slot = sl[idx % self.NDMA]
        self.dslots[q][1] = idx + 1
        key = slot[0]
        reads = [b for b in reads if isinstance(b, Buf)]
        writes = [b for b in writes if isinstance(b, Buf)]
        ar = [in_] if self._onchip(in_) else []
        aw = [out] if self._onchip(out) else []
        rt, new = self._rdeps(q, ar, aw)
        waits = self._filter(q, self._buftoks(reads, writes) + rt)
        if slot[1] > 0 and self.seen[q].get(key, 0) < slot[1]:
            waits.append((key, slot[1]))
            self.seen[q][key] = slot[1]
        slot[1] += 16
        tok = (key, slot[1])

        def fn(e, out=out, in_=in_):
            return e.dma_start(out=out, in_=in_)
        self.ops[q].append((waits, fn, (key, 16)))
        self._commit(tok, reads, writes)
        self._rcommit(tok, new)
        return tok

    def fence(self, src, dst):
        return
        toks = []
        for s in src:
            s = _b(s)
            if s.lw is not None:
                toks.append(s.lw)
            toks.extend(s.rd)
        for d in dst:
            _b(d).rd.extend(toks)

    def finish(self, final_bufs):
        waits = self._filter("sp", self._buftoks([_b(b) for b in final_bufs], []))
        self.ops["sp"].append((waits, None, None))
        nc = self.nc
        sems = self.sems
        ops = self.ops
        with nc.Block() as block:
            def mk(e):
                def body(eng):
                    for (waits, fn, inc) in ops[e]:
                        for (k, v) in waits:
                            eng.wait_ge(sems[k], v)
                        if fn is not None:
                            ins = fn(eng)
                            ins.then_inc(sems[inc[0]], inc[1])
                return body
            block.tensor(mk("pe"))
            block.vector(mk("dve"))
            block.scalar(mk("act"))
            block.gpsimd(mk("pool"))
            block.sync(mk("sp"))
        for cm in reversed(self._stack):
            cm.__exit__(None, None, None)
        self._stack = []
        return nc


class _Stop(Exception):
    pass


def build(T, dbg=False):
    import os
    STOP = os.environ.get("KSTOP")
    P = Prog()

    def chk(name):
        if STOP == name:
            raise _Stop()
    nc = P.nc
    NTI = T // TT
    NB = T // 128

    def din(name, shape, dt=F32):
        return nc.dram_tensor(name, list(shape), dt, kind="ExternalInput").ap()

    xT_in = din("xT", [D, T])
    c_in = din("c128", [128, 8])
    w_ada = din("w_ada", [NL, D, 6 * D])
    b_ada = din("b_ada", [NL, 128, 48])
    g_mix = din("g_mix", [NL, 128, 8])
    g_ffn = din("g_ffn", [NL, 128, 8])
    g_fin = din("g_fin", [128, 8])
    w_in = din("w_in", [NL, D, IN_COLS])
    mu_d = din("mu", [NL, 128, 14])
    pr_d = din("rwp", [NL, 128, 5, 4])
    wup_d = din("r_wup", [NL, 64, 512])
    aup_d = din("r_aup", [NL, 64, 512])
    gup_d = din("r_gup", [NL, 128, 512])
    gnw_d = din("gnw", [NL, 128, 512])
    gnb_d = din("gnb", [NL, 128, 512])
    lng_d = din("lng", [NL, 128, 512])
    lnb_d = din("lnb", [NL, 128, 512])
    wsT_d = din("wsT", [NL, 128, 8, 128])
    bs_d = din("bsT", [NL, 128, 4, 128])
    wbr_d = din("w_branch", [NL, 3, 512, D])
    wout_d = din("w_out", [NL, D, D])
    wfu_d = din("ffn_up", [NL, D, 2 * DFF])
    cw_d = din("convw", [NL, 128, NF, 3])
    cb_d = din("convb", [NL, 128, NF])
    wfd_d = din("ffn_down", [NL, DFF, D])
    cos_d = din("cosT", [128, T])
    sin_d = din("sinT", [128, T])
    cst_d = din("cst_f32", [128, 5, 128])
    cstb_d = din("cst_bf", [128, 8, 128], BF16)
    m12_d = din("m12", [128, 2, 256], BF16)
    m3_d = din("m3", [128, 4, 4, 64], BF16)
    mrw_d = din("mrw", [128, 4, 2, 128], BF16)
    mlo_d = din("mlo", [128, 4, 128], BF16)
    sel_d = din("sel", [128, 4, 8], BF16)
    yT_out = nc.dram_tensor("yT", [D, T], F32, kind="ExternalOutput").ap()
    x1T = nc.dram_tensor("x1T", [D, T], F32, kind="Internal").ap()
    Vd = [nc.dram_tensor("Vd%d" % l, [T, 512], BF16, kind="Internal").ap() for l in range(NL)]
    b_x1 = Buf("x1T")
    b_vd = [Buf("vd0"), Buf("vd1")]
    b_out = Buf("out")

    def MM(out, lhsT, rhs, start, stop, r, w):
        P.op("pe", lambda e: e.matmul(out, lhsT, rhs, start=start, stop=stop), reads=r, writes=w, ar=[lhsT, rhs], aw=[out])

    def TR(out, in_, ident, r, w):
        P.op("pe", lambda e: e.transpose(out, in_, ident), reads=r, writes=w, ar=[in_, ident], aw=[out])

    def ACT(out, in_, func, r, w, bias=0.0, scale=1.0):
        P.op("act", lambda e: e.activation(out=out, in_=in_, func=func, bias=bias, scale=scale), reads=r, writes=w,
             ar=[in_] + [x for x in (bias, scale) if hasattr(x, "offset")], aw=[out])

    def TTe(eng, out, in0, in1, op, r, w):
        P.op(eng, lambda e: e.tensor_tensor(out=out, in0=in0, in1=in1, op=op), reads=r, writes=w, ar=[in0, in1], aw=[out])

    def TS(eng, out, in0, s1, s2, op0, op1, r, w):
        if s2 is None:
            P.op(eng, lambda e: e.tensor_scalar(out=out, in0=in0, scalar1=s1, scalar2=None, op0=op0), reads=r, writes=w,
                 ar=[in0] + [x for x in (s1,) if hasattr(x, "offset")], aw=[out])
        else:
            P.op(eng, lambda e: e.tensor_scalar(out=out, in0=in0, scalar1=s1, scalar2=s2, op0=op0, op1=op1), reads=r, writes=w,
                 ar=[in0] + [x for x in (s1, s2) if hasattr(x, "offset")], aw=[out])

    def STT(eng, out, in0, sc, in1, op0, op1, r, w):
        P.op(eng, lambda e: e.scalar_tensor_tensor(out=out, in0=in0, scalar=sc, in1=in1, op0=op0, op1=op1), reads=r, writes=w,
             ar=[in0, in1] + [x for x in (sc,) if hasattr(x, "offset")], aw=[out])

    def CP(eng, out, in_, r, w):
        if eng == "act":
            P.op("act", lambda e: e.copy(out=out, in_=in_), reads=r, writes=w, ar=[in_], aw=[out])
        else:
            P.op(eng, lambda e: e.tensor_copy(out=out, in_=in_), reads=r, writes=w, ar=[in_], aw=[out])

    def MSET(eng, ap, val, w):
        P.op(eng, lambda e: e.memset(ap, val), writes=w, ar=[], aw=[ap])

    psf = [P.ps([128, 512], F32) for _ in range(6)]
    psb = [P.ps([128, 1024], BF16) for _ in range(2)]
    psi = [0, 0]

    def PSF():
        psi[0] += 1
        return psf[psi[0] % 6]

    def PSB():
        psi[1] += 1
        return psb[psi[1] % 2]

    cst = P.sb([128, 5, 128], F32)
    cstb = P.sb([128, 8, 128], BF16)
    m12 = P.sb([128, 2, 256], BF16)
    m3 = P.sb([128, 4, 4, 64], BF16)
    mrw = P.sb([128, 4, 2, 128], BF16)
    mlo = P.sb([128, 4, 128], BF16)
    sel = P.sb([128, 4, 8], BF16)
    for (t_, d_) in ((cst, cst_d), (cstb, cstb_d), (m12, m12_d), (m3, m3_d), (mrw, mrw_d), (mlo, mlo_d), (sel, sel_d)):
        P.dma("sp", t_[:], d_, writes=[t_])
    identf = cst[:, 0, :]
    onesf = cst[:, 1, :]
    blkones = cst[:, 2, :]
    identb = cstb[:, 0, :]
    permb = cstb[:, 1, :]
    onesb = cstb[:, 2, :]

    xt = P.sb([128, 8, TT], F32, "xt")
    scr = P.sb([128, 8, TT], F32, "scr")
    hT = P.sb([128, 8, TT], BF16, "hT")
    rstd = P.sb([128, TT], F32, "rstd")
    NWB = 2
    wb = [P.sb([128, 8, 512], BF16, "wb%d" % i) for i in range(NWB)]
    wbi = [0]
    QT = P.sb([128, 4, TT], BF16, "QT")
    KT = P.sb([128, 4, T], BF16, "KT")
    V1r = P.sb([128, 8, 512], BF16, "V1r")
    V2 = P.sb([128, 2, 4, 512], BF16, "V2")
    V3g = [P.sb([128, 2, 4, 512], BF16, "V3g0")] * 2
    acc = P.sb([128, 4, 2, TT], F32, "acc")
    big = P.sb([128, NF, TT], BF16, "big")
    YB = [P.sb([128, 4, TT], BF16, "YB%d" % i) for i in range(3)]
    cosb = P.sb([128, TT], F32, "cos")
    sinb = P.sb([128, TT], F32, "sin")
    zcar = P.sb([128, 14], F32, "zcar")
    H0 = P.sb([128, 4, 64], F32, "H0")
    H0b = P.sb([128, 4, 64], BF16, "H0b")
    ccar = P.sb([128, NF, 2], F32, "ccar")
    MSET("pool", KT[:], 0.0, [KT])

    modp = P.sb([128, NL, 48], F32, "mod")
    A1 = P.sb([128, NL, 8], F32, "A1")
    A2 = P.sb([128, NL, 8], F32, "A2")
    gfin = P.sb([128, 8], F32, "gfin")
    cact = P.sb([128, 8], F32, "cact")
    wtmp = scr
    bad = P.sb([128, NL, 48], F32, "bad")
    gm = P.sb([128, NL, 8], F32, "gm")
    gf = P.sb([128, NL, 8], F32, "gf")
    P.dma("sp", cact[:], c_in, writes=[cact])
    P.dma("sp", gfin[:], g_fin, writes=[gfin])
    for l in range(NL):
        P.dma("sp", bad[:, l, :], b_ada[l], writes=[bad])
        P.dma("sp", gm[:, l, :], g_mix[l], writes=[gm])
        P.dma("sp", gf[:, l, :], g_ffn[l], writes=[gf])
    ACT(cact[:], cact[:], AF.Silu, [cact], [cact])
    for l in range(NL):
        for pc in range(12):
            P.dma("sp", wtmp[:], w_ada[l][:, pc * 512:(pc + 1) * 512].rearrange("(k p) n -> p k n", p=128), writes=[wtmp])
            ps = PSF()
            for mc in range(4):
                for k in range(8):
                    MM(ps[:, mc:mc + 1], wtmp[:, k, mc * 128:(mc + 1) * 128], cact[:, k:k + 1], k == 0, k == 7, [wtmp, cact], [ps])
            TTe("dve", modp[:, l, pc * 4:(pc + 1) * 4], ps[:, 0:4], bad[:, l, pc * 4:(pc + 1) * 4], ALU.add, [ps, bad], [modp])
        STT("dve", A1[:, l, :], modp[:, l, 8:16], 1.0, gm[:, l, :], ALU.add, ALU.mult, [modp, gm], [A1])
        STT("dve", A2[:, l, :], modp[:, l, 32:40], 1.0, gf[:, l, :], ALU.add, ALU.mult, [modp, gf], [A2])

    mu = P.sb([128, 14], F32, "mu")
    omu = P.sb([128, 14], F32, "omu")
    rwp = P.sb([128, 5, 4], F32, "rwp")
    omka = P.sb([128, 4], F32, "omka")
    nw0 = P.sb([128, 4], F32, "nw0")
    wupb = P.sb([128, 512], BF16, "wupb")
    aupb = P.sb([128, 512], BF16, "aupb")
    gupb = P.sb([128, 512], BF16, "gupb")
    gnw = P.sb([128, 512], F32, "gnw")
    gnb = P.sb([128, 512], F32, "gnb")
    lng = P.sb([128, 512], F32, "lng")
    lnb = P.sb([128, 512], F32, "lnb")
    wsT = P.sb([128, 8, 128], BF16, "wsT")
    wsTf = View(scr[:, 0:2, :].rearrange("p a (g i) -> p (a g) i", g=4), scr)
    bsT = P.sb([128, 4, 128], F32, "bsT")
    cw = P.sb([128, NF, 3], F32, "cw")
    cb = P.sb([128, NF], F32, "cb")

    def load_w(src_ap, kc, ncols):
        wbi[0] += 1
        w = wb[wbi[0] % NWB]
        P.dma("pool", w[:, 0:kc, 0:ncols], src_ap.rearrange("(k p) n -> p k n", p=128), writes=[w])
        return w

    whi = [0]

    def load_wh(src_ap, kc, ncols):
        whi[0] += 1
        i_ = whi[0] % (2 * NWB)
        flat = wb[i_ // 2][:].rearrange("p a b -> p (a b)")[:, (i_ % 2) * 2048:(i_ % 2) * 2048 + kc * ncols]
        w = View(flat.rearrange("p (k n) -> p k n", k=kc), wb[i_ // 2])
        P.dma("pool", w[:], src_ap.rearrange("(k p) n -> p k n", p=128))
        return w

    def rmsnorm_mod(Acol, shcol, l):
        ACT(scr[:], xt[:], AF.Square, [xt], [scr])
        ps = PSF()
        for k in range(8):
            MM(ps[:], onesf, scr[:, k, :], k == 0, k == 7, [cst, scr], [ps])
        ACT(rstd[:], ps[:], AF.Sqrt, [ps], [rstd], bias=1e-6, scale=1.0 / D)
        P.op("dve", lambda e: e.reciprocal(out=rstd[:], in_=rstd[:]), ar=[rstd[:]], aw=[rstd[:]])
        for k in range(8):
            STT("dve", scr[:, k, :], xt[:, k, :], Acol[:, k:k + 1], rstd[:], ALU.mult, ALU.mult, [xt, rstd, A1, A2, gfin], [scr])
            if shcol is not None:
                TS("pool", hT[:, k, :], scr[:, k, :], shcol[:, k:k + 1], None, ALU.add, None, [scr, modp], [hT])

    GC = 0.7978845608028654

    def gelu_from(out, src, rd, wr, tmpa, tmpb):
        ACT(out, src, AF.Gelu_apprx_tanh, rd, wr)

    try:
        for l in range(NL):
            x_src = xT_in if l == 0 else x1T
            P.dma("sp", mu[:], mu_d[l], writes=[mu])
            TS("dve", omu[:], mu[:], -1.0, 1.0, ALU.mult, ALU.add, [mu], [omu])
            P.dma("sp", rwp[:], pr_d[l], writes=[rwp])
            TS("dve", omka[:], rwp[:, 3, :], -1.0, 1.0, ALU.mult, ALU.add, [rwp], [omka])
            TS("dve", nw0[:], rwp[:, 0, :], -1.0, None, ALU.mult, None, [rwp], [nw0])
            P.dma("pool", wupb[0:64, :], wup_d[l], writes=[wupb])
            P.dma("pool", aupb[64:128, :], aup_d[l], writes=[aupb])
            P.dma("pool", gupb[:], gup_d[l], writes=[gupb])
            for (t_, d_) in ((gnw, gnw_d), (gnb, gnb_d), (lng, lng_d), (lnb, lnb_d)):
                P.dma("sp", t_[:], d_[l], writes=[t_])
            P.dma("sp", wsTf[:], wsT_d[l], writes=[wsTf])
            for g in range(8):
                TTe("dve", wsT[:, g, :], wsTf[:, g, :], mrw[:, 0, 1, :], ALU.mult, [wsTf, mrw], [wsT])
            P.dma("sp", bsT[:], bs_d[l], writes=[bsT])
            P.dma("sp", cw[:], cw_d[l], writes=[cw])
            P.dma("sp", cb[:], cb_d[l], writes=[cb])
            MSET("pool", zcar[:], 0.0, [zcar])
            MSET("pool", H0[:], 0.0, [H0])
            MSET("pool", H0b[:], 0.0, [H0b])
            MSET("pool", ccar[:], 0.0, [ccar])
            MSET("pool", big[:, 0:4, :], 0.0, [big])
            for q in range(T // 512):
                P.dma("sp", Vd[l][q * 512:(q + 1) * 512, :].rearrange("(p a) c -> p (a c)", p=128), big[:, 0:4, :].rearrange("p a b -> p (a b)"), reads=[big], writes=[b_vd[l]])

            for it in range(NTI):
                t0 = it * TT
                P.dma("sp", xt[:], x_src[:, t0:t0 + TT].rearrange("(k p) t -> p k t", p=128), reads=[b_x1] if l else [], writes=[xt])
                P.dma("sp", cosb[:], cos_d[:, t0:t0 + TT], writes=[cosb])
                P.dma("sp", sinb[:], sin_d[:, t0:t0 + TT], writes=[sinb])
                chk('pro')
                rmsnorm_mod(A1[:, l, :], modp[:, l, 0:8], l)
                chk('norm')

                for pc in range(2):
                    w = load_w(w_in[l][:, pc * 512:(pc + 1) * 512], 8, 512)
                    for mc in range(4):
                        ps = PSF()
                        for k in range(8):
                            MM(ps[:], w[:, k, mc * 128:(mc + 1) * 128], hT[:, k, :], k == 0, k == 7, [w, hT], [ps])
                        chk('qk1')
                        qs = scr[:, mc, :]
                        qbf = big[:, 14 + mc, :]
                        CP("act", qbf, ps[:], [ps], [big])
                        ps2 = PSF()
                        MM(ps2[:], permb, qbf, True, True, [cstb, big], [ps2])
                        chk('qk2')
                        TTe("dve", qs, ps[:], cosb[:], ALU.mult, [ps, cosb, big, ps2], [scr])
                        TTe("dve", scr[:, 4 + mc, :], ps2[:], sinb[:], ALU.mult, [ps2, sinb], [scr])
                        chk('qk3')
                        dst = QT[:, mc, :] if pc == 0 else KT[:, mc, t0:t0 + TT]
                        TTe("pool", dst, qs, scr[:, 4 + mc, :], ALU.add, [scr], [QT if pc == 0 else KT])
                chk('qk')
                w = load_w(w_in[l][:, 1024:1536], 8, 512)
                for blk in range(4):
                    ps = PSF()
                    for k in range(8):
                        MM(ps[:], hT[:, k, blk * 128:(blk + 1) * 128], w[:, k, :], k == 0, k == 7, [w, hT], [ps])
                    gb = 4 * it + blk
                    CP("act", V1r[:, gb % 8, :], ps[:], [ps], [V1r])
                    P.dma("sp", Vd[l][t0 + blk * 128:t0 + (blk + 1) * 128, :], V1r[:, gb % 8, :], reads=[V1r], writes=[b_vd[l]])
                P.dma("sp", V2[:, it % 2, :, :], Vd[l][t0:t0 + TT, :].rearrange("(i r) c -> i r c", r=4), reads=[b_vd[l]], writes=[V2])

                chk('v')
                if "EP" not in P.__dict__:
                    P.EP = [(View(big[:, 16 + 2 * i_, :], Buf("E%d" % i_)), View(big[:, 17 + 2 * i_, :], Buf("P%d" % i_))) for i_ in range(3)]
                    P.epi = 0
                EPb = [x for pr in P.EP for x in pr]
                P.fence([big], EPb)
                pend = []

                def defer(fn_):
                    if pend:
                        pend.pop(0)()
                    pend.append(fn_)

                def flush():
                    while pend:
                        pend.pop(0)()
                for cfg in range(3):
                    n_sp = t0 // min(2048, T)
                    j3 = (t0 % min(2048, T)) // TT
                    if cfg < 2:
                        items = [(None, c, hh) for c in range(4) for hh in range(2)]
                    else:
                        items = [(rg_, c, hh) for rg_ in range(4) for c in range(4) for hh in range(2)]
                    for (rg_, c, hh) in items:
                        if True:
                            rows = slice(hh * 64, hh * 64 + 64)
                            if cfg < 2:
                                for half in range(2):
                                    psS = PSF()
                                    hp = []
                                    for u in range(2):
                                        qi = half * 2 + u
                                        if cfg == 0:
                                            gb = 4 * it + qi
                                            q_ap = QT[rows, c, qi * 128:(qi + 1) * 128]
                                            kc_ap = KT[rows, c, gb * 128:(gb + 1) * 128]
                                            kp_ap = KT[rows, c, (gb - 1) * 128:gb * 128] if gb > 0 else None
                                            vc_ap = V1r[:, gb % 8, c * 128:(c + 1) * 128]
                                            vp_ap = V1r[:, (gb - 1) % 8, c * 128:(c + 1) * 128]
                                            vb = V1r
                                        else:
                                            q_ap = QT[rows, c, qi:TT:4]
                                            kc_ap = KT[rows, c, t0 + qi:t0 + TT:4]
                                            kp_ap = KT[rows, c, t0 - TT + qi:t0:4] if it > 0 else None
                                            vc_ap = V2[:, it % 2, qi, c * 128:(c + 1) * 128]
                                            vp_ap = V2[:, (it - 1) % 2, qi, c * 128:(c + 1) * 128]
                                            vb = V2
                                        hp.append(kp_ap is not None)
                                        if kp_ap is not None:
                                            MM(psS[:, u * 256:u * 256 + 128], kp_ap, q_ap, True, True, [KT, QT], [psS])
                                        MM(psS[:, u * 256 + 128:u * 256 + 256], kc_ap, q_ap, True, True, [KT, QT], [psS])
                                        hp.append((vp_ap, vc_ap, vb))
                                    P.epi += 1
                                    Et, Pt = P.EP[P.epi % 3]
                                    for u in range(2):
                                        if hp[2 * u]:
                                            ACT(Et[:, u * 256:(u + 1) * 256], psS[:, u * 256:(u + 1) * 256], AF.Exp, [psS], [Et], scale=0.125)
                                        else:
                                            MSET("pool", Et[:, u * 256:u * 256 + 128], 0.0, [Et])
                                            ACT(Et[:, u * 256 + 128:u * 256 + 256], psS[:, u * 256 + 128:u * 256 + 256], AF.Exp, [psS], [Et], scale=0.125)
                                    Pm = Pt[:]
                                    TTe("dve", Pm, Et[:], m12[:].rearrange("p a b -> p (a b)"), ALU.mult, [Et, m12], [Pt])
                                    def Bp(hp=hp, Pm=Pm, Pt=Pt, rows=rows, c=c, cfg=cfg, half=half):
                                        psN = PSF()
                                        for u in range(2):
                                            vp_ap, vc_ap, vb = hp[2 * u + 1]
                                            has_prev = hp[2 * u]
                                            for z in range(2):
                                                o = psN[:, z * 256 + u * 128:z * 256 + u * 128 + 128]
                                                if has_prev:
                                                    MM(o, vp_ap if z == 0 else onesb, Pm[:, u * 256:u * 256 + 128], True, False, [vb, Pt, cstb], [psN])
                                                MM(o, vc_ap if z == 0 else onesb, Pm[:, u * 256 + 128:u * 256 + 256], not has_prev, True, [vb, Pt, cstb], [psN])
                                        src = psN[rows, :].rearrange("p (z u q) -> p z u q", z=2, u=2)
                                        if cfg == 0:
                                            dsta = acc[rows, c, :, half * 256:(half + 1) * 256].rearrange("p z (u q) -> p z u q", u=2)
                                            CP("act", dsta, src, [psN], [acc])
                                        else:
                                            dsta = acc[rows, c, :, :].rearrange("p z (q r) -> p z r q", r=4)[:, :, half * 2:half * 2 + 2, :]
                                            TTe("dve", dsta, src, dsta, ALU.add, [psN, acc], [acc])
                                    defer(Bp)
                            else:
                                SPAN = min(2048, T)
                                nk3 = SPAN // 16
                                kp_ = slice(0, nk3)
                                for rg in (rg_,):
                                    if c == 0 and hh == 0:
                                        flush()
                                        v3 = V3g[rg % 2]
                                        for sp_ in range(2):
                                            nn = n_sp - 1 + sp_
                                            if nn < 0:
                                                continue
                                            P.dma("sp", v3[kp_, sp_, :, :],
                                                  Vd[l][nn * SPAN:(nn + 1) * SPAN, :].rearrange("(i r) c -> i r c", r=16)[:, rg * 4:rg * 4 + 4, :],
                                                  reads=[b_vd[l]], writes=[v3])
                                    v3 = V3g[rg % 2]
                                    psS = PSF()
                                    hasp = n_sp > 0
                                    for rr in range(4):
                                        r = rg * 4 + rr
                                        q_ap = QT[rows, c, r:TT:16]
                                        kc_ap = KT[rows, c, n_sp * SPAN + r:(n_sp + 1) * SPAN:16]
                                        if hasp:
                                            kp_ap = KT[rows, c, (n_sp - 1) * SPAN + r:n_sp * SPAN:16]
                                            MM(psS[kp_, rr * 64:rr * 64 + 32], kp_ap, q_ap, True, True, [KT, QT], [psS])
                                        MM(psS[kp_, rr * 64 + 32:rr * 64 + 64], kc_ap, q_ap, True, True, [KT, QT], [psS])
                                    P.epi += 1
                                    Et, Pt = P.EP[P.epi % 3]
                                    if not hasp:
                                        MSET("pool", Et[kp_, 0:256], 0.0, [Et])
                                        for rr in range(4):
                                            ACT(Et[kp_, rr * 64 + 32:rr * 64 + 64], psS[kp_, rr * 64 + 32:rr * 64 + 64], AF.Exp, [psS], [Et], scale=0.125)
                                    else:
                                        ACT(Et[kp_, 0:256], psS[kp_, 0:256], AF.Exp, [psS], [Et], scale=0.125)
                                    Pm = Pt[kp_, 0:256]
                                    TTe("dve", Pm, Et[kp_, 0:256], m3[kp_, j3, :, :].rearrange("p a b -> p (a b)"), ALU.mult, [Et, m3], [Pt])
                                    def Bq(v3=v3, Pm=Pm, Pt=Pt, rows=rows, c=c, rg=rg, hasp=hasp, kp_=kp_):
                                        psN = PSF()
                                        for rr in range(4):
                                            for z in range(2):
                                                o = psN[:, z * 128 + rr * 32:z * 128 + rr * 32 + 32]
                                                if hasp:
                                                    MM(o, v3[kp_, 0, rr, c * 128:(c + 1) * 128] if z == 0 else onesb[kp_, :], Pm[:, rr * 64:rr * 64 + 32], True, False, [v3, Pt, cstb], [psN])
                                                MM(o, v3[kp_, 1, rr, c * 128:(c + 1) * 128] if z == 0 else onesb[kp_, :], Pm[:, rr * 64 + 32:rr * 64 + 64], not hasp, True, [v3, Pt, cstb], [psN])
                                        src = psN[rows, 0:256].rearrange("p (z r q) -> p z r q", z=2, r=4)
                                        dsta = acc[rows, c, :, :].rearrange("p z (q r) -> p z r q", r=16)[:, :, rg * 4:rg * 4 + 4, :]
                                        TTe("dve", dsta, src, dsta, ALU.add, [psN, acc], [acc])
                                    defer(Bq)
                flush()
                P.fence(EPb, [big])
                for c in range(4):
                    P.op("dve", lambda e, c=c: e.reciprocal(out=acc[:, c, 1, :], in_=acc[:, c, 1, :]), ar=[acc[:, c, 1, :]], aw=[acc[:, c, 1, :]])
                    TTe("dve", YB[0][:, c, :], acc[:, c, 0, :], acc[:, c, 1, :], ALU.mult, [acc], [YB[0]])
                chk('att')

                for pc in range(4):
                    ncol = 512 if pc < 3 else 256
                    w = load_w(w_in[l][:, 1536 + pc * 512:1536 + pc * 512 + ncol], 8, ncol)
                    for mc in range(ncol // 128):
                        ch = pc * 4 + mc
                        ps = PSF()
                        for k in range(8):
                            MM(ps[:], w[:, k, mc * 128:(mc + 1) * 128], hT[:, k, :], k == 0, k == 7, [w, hT], [ps])
                        zb = scr[:, ch % 4, :]
                        zp = scr[:, 4 + ch % 4, :]
                        CP("act", zb, ps[:], [ps], [scr])
                        CP("pool", zp[:, 1:TT], zb[:, 0:TT - 1], [scr], [scr])
                        CP("pool", zp[:, 0:1], zcar[:, ch:ch + 1], [zcar, scr], [scr])
                        CP("pool", zcar[:, ch:ch + 1], zb[:, TT - 1:TT], [scr], [zcar])
                        TS("dve", zb, zb, omu[:, ch:ch + 1], None, ALU.mult, None, [scr, omu], [scr])
                        STT("dve", big[:, ch, :], zp, mu[:, ch:ch + 1], zb, ALU.mult, ALU.add, [scr, mu], [big])
                chk('rwin'); rwkv_tile(P, locals()); chk('rw')

                gmlp_tile(P, locals()); chk('gmlp')

                wbrs = []
                for br in range(3):
                    for half in range(2):
                        wbt = load_wh(wbr_d[l, br][:, half * 512:(half + 1) * 512], 4, 512)
                        for mc in range(4):
                            m = half * 4 + mc
                            if mc % 2 == 0:
                                c0_ = 4352 + br * 1024 + half * 512 + (mc // 2) * 256
                                wg = load_wh(w_in[l][:, c0_:c0_ + 256], 8, 256)
                            psg = PSF()
                            for k in range(8):
                                MM(psg[:], wg[:, k, (mc % 2) * 128:(mc % 2 + 1) * 128], hT[:, k, :], k == 0, k == 7, [wg, hT], [psg])
                            psbr = PSF()
                            for k in range(4):
                                MM(psbr[:], wbt[:, k, mc * 128:(mc + 1) * 128], YB[br][:, k, :], k == 0, k == 3, [wbt, YB[br]], [psbr])
                            sg = acc[:, (br * 8 + m) % 4, 0, :]
                            ACT(sg, psg[:], AF.Sigmoid, [psg], [acc])
                            if br == 0:
                                TTe("dve", scr[:, m, :], sg, psbr[:], ALU.mult, [acc, psbr], [scr])
                            else:
                                TTe("dve", sg, sg, psbr[:], ALU.mult, [acc, psbr], [acc])
                                TTe("pool", scr[:, m, :], scr[:, m, :], sg, ALU.add, [acc, scr], [scr])
                chk('br')
                for m in range(8):
                    CP("act", hT[:, m, :], scr[:, m, :], [scr], [hT])
                for half in range(2):
                    w = load_w(wout_d[l][:, half * 512:(half + 1) * 512], 8, 512)
                    for mc in range(4):
                        m = half * 4 + mc
                        ps = PSF()
                        for k in range(8):
                            MM(ps[:], w[:, k, mc * 128:(mc + 1) * 128], hT[:, k, :], k == 0, k == 7, [w, hT], [ps])
                        STT("dve", xt[:, m, :], ps[:], modp[:, l, 16 + m:17 + m], xt[:, m, :], ALU.mult, ALU.add, [ps, modp, xt], [xt])

                chk('out')
                rmsnorm_mod(A2[:, l, :], modp[:, l, 24:32], l)
                for fg in range(11):
                    nf = 2
                    wa = load_wh(wfu_d[l][:, fg * 256:fg * 256 + 256], 8, 256)
                    wg = load_wh(wfu_d[l][:, DFF + fg * 256:DFF + fg * 256 + 256], 8, 256)
                    for mc in range(nf):
                        f = fg * 2 + mc
                        psa = PSF()
                        for k in range(8):
                            MM(psa[:], wa[:, k, mc * 128:(mc + 1) * 128], hT[:, k, :], k == 0, k == 7, [wa, hT], [psa])
                        psg = PSF()
                        for k in range(8):
                            MM(psg[:], wg[:, k, mc * 128:(mc + 1) * 128], hT[:, k, :], k == 0, k == 7, [wg, hT], [psg])
                        fb_ = 2 * (f % 2)
                        ab = acc[:, fb_, :, :].rearrange("p a b -> p (a b)")
                        CP("act", ab[:, 2:2 + TT], psa[:], [psa], [acc])
                        CP("pool", ab[:, 0:2], ccar[:, f, :], [ccar, acc], [acc])
                        CP("pool", ccar[:, f, :], ab[:, TT:TT + 2], [acc], [ccar])
                        c1 = acc[:, fb_ + 1, 0, :]
                        TS("dve", c1, ab[:, 2:2 + TT], cw[:, f, 2:3], cb[:, f:f + 1], ALU.mult, ALU.add, [acc, cw, cb], [acc])
                        STT("dve", c1, ab[:, 1:1 + TT], cw[:, f, 1:2], c1, ALU.mult, ALU.add, [acc, cw], [acc])
                        STT("dve", c1, ab[:, 0:TT], cw[:, f, 0:1], c1, ALU.mult, ALU.add, [acc, cw], [acc])
                        ge = acc[:, fb_ + 1, 1, :]
                        gelu_from(ge, c1, [acc], [acc], None, None)
                        TTe("dve", big[:, f, :], ge, psg[:], ALU.mult, [acc, psg], [big])
                for m in range(8):
                    wbi[0] += 1
                    wd = wb[wbi[0] % NWB]
                    wdv = wd[:].rearrange("p a b -> p (a b)")[:, 0:NF * 128].rearrange("p (f n) -> p f n", f=NF)
                    P.dma("pool", wdv, wfd_d[l][:, m * 128:(m + 1) * 128].rearrange("(f p) n -> p f n", p=128), writes=[wd])
                    ps = PSF()
                    for f in range(NF):
                        MM(ps[:], wdv[:, f, :], big[:, f, :], f == 0, f == NF - 1, [wd, big], [ps])
                    STT("dve", xt[:, m, :], ps[:], modp[:, l, 40 + m:41 + m], xt[:, m, :], ALU.mult, ALU.add, [ps, modp, xt], [xt])

                chk('ffn')
                if l < NL - 1:
                    P.dma("sp", x1T[:, t0:t0 + TT].rearrange("(k p) t -> p k t", p=128), xt[:], reads=[xt], writes=[b_x1])
                else:
                    rmsnorm_mod(gfin[:], None, l)
                    P.dma("sp", yT_out[:, t0:t0 + TT].rearrange("(k p) t -> p k t", p=128), scr[:], reads=[scr], writes=[b_out])
    except _Stop:
        src_t = xt
        ybi = {"att": 0, "rw": 1, "gmlp": 2}.get(STOP)
        if ybi is not None:
            MSET("pool", scr[:], 0.0, [scr])
            for c_ in range(4):
                CP("dve", scr[:, c_, :], YB[ybi][:, c_, :], [YB[ybi]], [scr])
            src_t = scr
        elif STOP == "br":
            src_t = scr
        elif STOP == "rw7":
            MSET("pool", scr[:], 0.0, [scr])
            CP("dve", scr[:, 0, :], acc[:, 0, 0, :], [acc], [scr])
            src_t = scr
        P.dma("sp", yT_out[:, 0:TT].rearrange("(k p) t -> p k t", p=128), src_t[:], reads=[src_t], writes=[b_out])
    return P.finish([b_out])


def gmlp_tile(P, L):
    g = L
    PSF, MM, ACT, TTe, TS, STT, CP = g["PSF"], g["MM"], g["ACT"], g["TTe"], g["TS"], g["STT"], g["CP"]
    hT, big, acc, scr, YB, wsT, bsT, lng, lnb = g["hT"], g["big"], g["acc"], g["scr"], g["YB"], g["wsT"], g["bsT"], g["lng"], g["lnb"]
    load_w, w_in, l, gelu_from = g["load_w"], g["w_in"], g["l"], g["gelu_from"]
    w = load_w(w_in[l][:, 3328:3840], 8, 512)
    for mc in range(4):
        ps = PSF()
        for k in range(8):
            MM(ps[:], w[:, k, mc * 128:(mc + 1) * 128], hT[:, k, :], k == 0, k == 7, [w, hT], [ps])
        gelu_from(scr[:, mc, :], ps[:], [ps], [scr], scr[:, 4, :], scr[:, 5, :])
    w = load_w(w_in[l][:, 3840:4352], 8, 512)
    stats = acc[:, 3, 1, 0:8]
    for blk in range(4):
        ps = PSF()
        for k in range(8):
            MM(ps[:], hT[:, k, blk * 128:(blk + 1) * 128], w[:, k, :], k == 0, k == 7, [w, hT], [ps])
        v = acc[:, 0, 0, :]
        gelu_from(v, ps[:], [ps], [acc], acc[:, 0, 1, :], acc[:, 1, 0, :])
        P.op("dve", lambda e: e.bn_stats(out=acc[:, 3, 1, 0:6], in_=v), ar=[v], aw=[acc[:, 3, 1, 0:6]])
        P.op("dve", lambda e: e.bn_aggr(out=acc[:, 3, 1, 6:8], in_=acc[:, 3, 1, 0:6]), ar=[acc[:, 3, 1, 0:6]], aw=[acc[:, 3, 1, 6:8]])
        ACT(acc[:, 3, 1, 7:8], acc[:, 3, 1, 7:8], AF.Sqrt, [acc], [acc], bias=1e-5)
        P.op("dve", lambda e: e.reciprocal(out=acc[:, 3, 1, 7:8], in_=acc[:, 3, 1, 7:8]), ar=[acc[:, 3, 1, 7:8]], aw=[acc[:, 3, 1, 7:8]])
        TS("dve", v, v, acc[:, 3, 1, 6:7], acc[:, 3, 1, 7:8], ALU.subtract, ALU.mult, [acc], [acc])
        TTe("dve", v, v, lng[:], ALU.mult, [acc, lng], [acc])
        vb = big[:, 18, :]
        TTe("dve", vb, v, lnb[:], ALU.add, [acc, lnb], [big])
        psA = PSF()
        psB = PSF()
        for gq in range(8):
            pp = psA if gq < 4 else psB
            c = gq // 2
            MM(pp[:, (gq % 4) * 128:(gq % 4) * 128 + 128], vb[:, c * 128:(c + 1) * 128], wsT[:, gq, :], True, True, [big, wsT], [pp])
        for gq in range(8):
            pp = psA if gq < 4 else psB
            c = gq // 2
            rows = slice((gq % 2) * 64, (gq % 2) * 64 + 64)
            tmp = acc[rows, 1, 1, 0:128]
            TTe("dve", tmp, pp[rows, (gq % 4) * 128:(gq % 4) * 128 + 128], bsT[rows, c, :], ALU.add, [pp, bsT], [acc])
            TTe("dve", YB[2][rows, c, blk * 128:(blk + 1) * 128], tmp, scr[rows, c, blk * 128:(blk + 1) * 128], ALU.mult, [acc, scr], [YB[2]])


def rwkv_tile(P, L):
    g = L
    PSF, PSB, MM, TR, ACT, TTe, TS, STT, CP, MSET = (g[k] for k in ("PSF", "PSB", "MM", "TR", "ACT", "TTe", "TS", "STT", "CP", "MSET"))
    big, scr, acc, YB, H0, H0b = g["big"], g["scr"], g["acc"], g["YB"], g["H0"], g["H0b"]
    rwp, omka, wupb, aupb, gupb, gnw, gnb = g["rwp"], g["omka"], g["wupb"], g["aupb"], g["gupb"], g["gnw"], g["gnb"]
    cst, cstb, mrw, mlo, sel = g["cst"], g["cstb"], g["mrw"], g["mlo"], g["sel"]
    chk = g["chk"]
    identb, identf, blkones = g["identb"], g["identf"], g["blkones"]
    rw = getattr(P, "_rw", None)
    if rw is None:
        sb = P.sb
        rw = dict(
            th=sb([128, 128], BF16), sgx=sb([128, 128], BF16),
            **{nm: View(scr[:, i_, :].rearrange("p (a b) -> p a b", a=4), scr) for i_, nm in enumerate(("lw", "cum", "cum2", "av", "kk", "km", "t1", "t2"))},
            AR=sb([128, 4, 2, 128], BF16), BK=sb([128, 4, 2, 128], BF16),
            wc=sb([128, 4], F32),
            tok=View(g["QT"][:, 0:3, :], g["QT"]),
            Xs=View(g["QT"][:, 3, :], g["QT"]),
            **{nm: View(g["V3g"][0][:, i_ // 2, (i_ % 2) * 2:(i_ % 2) * 2 + 2, :].rearrange("p a (h t) -> p (a h) t", h=4), g["V3g"][0])
               for i_, nm in enumerate(("N1", "L1", "N2", "L2"))},
            **{nm: View(YB[2][:, 2 * i_:2 * i_ + 2, :].rearrange("p a (h t) -> p (a h) t", h=4), YB[2]) for i_, nm in enumerate(("IL", "Mm"))},
            **{nm: View(big[:, 14 + 2 * i_:16 + 2 * i_, :].rearrange("p a (h t) -> p (a h) t", h=4), big)
               for i_, nm in enumerate(("Mm2", "Arb", "Aak", "Ark"))},
            Y=View(acc[:, 0, 0, :], acc), Y2=View(acc[:, 0, 1, :], acc), st=sb([128, 4, 8], F32), gt=View(acc[:, 1, 0, :], acc),
            Ht=sb([128, 4, 64], F32),
            Noff=View(g["rstd"][:].bitcast(BF16).rearrange("p (h t) -> p h t", h=8), g["rstd"]),
        )
        P._rw = rw
    rw = P._rw
    th, sgx, lw, cum, cum2, av, kk, km, t1, t2 = (rw[k] for k in ("th", "sgx", "lw", "cum", "cum2", "av", "kk", "km", "t1", "t2"))
    AR, BK, wc, tok = rw["AR"], rw["BK"], rw["wc"], rw["tok"]
    N1, L1, N2, L2, IL, Mm, Mm2, Arb, Aak, Ark = (rw[k] for k in ("N1", "L1", "N2", "L2", "IL", "Mm", "Mm2", "Arb", "Aak", "Ark"))
    V1r_ = g["V1r"]
    fs_ = (4, 5, 6) if g["it"] % 2 == 0 else (0, 1, 2)
    Us = View(V1r_[:, fs_[0], :], V1r_)
    yb = View(V1r_[:, fs_[1], :], V1r_)
    prod = View(V1r_[:, fs_[2], :].rearrange("p (a b) -> p a b", a=4), V1r_)
    Xs, Y, Y2, st, gt, Ht = (rw[k] for k in ("Xs", "Y", "Y2", "st", "gt", "Ht"))
    W0, A0, KK_, KA, RK = (rwp[:, i, :] for i in range(5))
    EH = 0.6065306597126334

    for blk in range(4):
        ts = slice(blk * 128, (blk + 1) * 128)
        zr = lambda c: big[:, 0 + c, ts]
        zk = lambda c: big[:, 4 + c, ts]
        zv = lambda c: big[:, 8 + c, ts]
        zwa = big[:, 12, ts]
        zg = big[:, 13, ts]
        ACT(th[0:64, :], big[0:64, 12, ts], AF.Tanh, [big], [th])
        ACT(sgx[:], zg, AF.Sigmoid, [big], [sgx])
        ps = PSF()
        psa = PSF()
        for c in range(4):
            MM(ps[:, c * 128:(c + 1) * 128], wupb[0:64, c * 128:(c + 1) * 128], th[0:64, :], True, True, [wupb, th], [ps])
            MM(psa[:, c * 128:(c + 1) * 128], aupb[64:128, c * 128:(c + 1) * 128], big[64:128, 12, ts], True, True, [aupb, big], [psa])
        for c in range(4):
            ACT(lw[:, c, :], ps[:, c * 128:(c + 1) * 128], AF.Exp, [ps, g["nw0"]], [lw], bias=g["nw0"][:, c:c + 1], scale=-1.0)
            ACT(av[:, c, :], psa[:, c * 128:(c + 1) * 128], AF.Sigmoid, [psa, rwp], [av], bias=A0[:, c:c + 1])
        TS("dve", lw[:], lw[:], 1.0, None, ALU.add, None, [lw], [lw])
        P.op("dve", lambda e: e.reciprocal(out=lw[:], in_=lw[:]), ar=[lw[:]], aw=[lw[:]])
        TS("dve", lw[:], lw[:], -EH, None, ALU.mult, None, [lw], [lw])
        psg = PSF()
        MM(psg[:], sgx[:], gupb[:], True, True, [sgx, gupb], [psg])
        CP("act", gt[:], psg[:], [psg], [gt])
        chk('rw1')
        for c in range(4):
            TS("dve", kk[:, c, :], zk(c), KK_[:, c:c + 1], None, ALU.mult, None, [big, rwp], [kk])
        ACT(t1[:], kk[:], AF.Square, [kk], [t1])
        ps = PSF()
        for c in range(4):
            MM(ps[:, c * 128:(c + 1) * 128], blkones, t1[:, c, :], True, True, [cst, t1], [ps])
        ACT(t1[:].rearrange("p a b -> p (a b)"), ps[:], AF.Sqrt, [ps], [t1])
        TS("dve", t1[:], t1[:], 1e-12, None, ALU.max, None, [t1], [t1])
        P.op("dve", lambda e: e.reciprocal(out=t1[:], in_=t1[:]), ar=[t1[:]], aw=[t1[:]])
        TTe("dve", kk[:], kk[:], t1[:], ALU.mult, [kk, t1], [kk])
        for c in range(4):
            TS("dve", t2[:, c, :], av[:, c, :], KA[:, c:c + 1], omka[:, c:c + 1], ALU.mult, ALU.add, [av, rwp, omka], [t2])
            TTe("dve", km[:, c, :], t2[:, c, :], zk(c), ALU.mult, [t2, big], [km])
        for c in range(4):
            STT("dve", prod[:, c, :], zr(c), RK[:, c:c + 1], km[:, c, :], ALU.mult, ALU.mult, [big, rwp, km], [prod])
        psr = PSF()
        for c in range(4):
            MM(psr[:, 0:8], prod[:, c, :], sel[:, c, :], c == 0, c == 3, [prod, sel], [psr])
        CP("act", st[:, 0, :], psr[:, 0:8], [psr], [st])
        chk('rw2')
        src, dst = lw, cum
        CP("pool", cum[:], lw[:], [lw], [cum])
        a_, b_ = cum, cum2
        for s in (1, 2, 4, 8, 16, 32, 64):
            CP("pool", b_[:, :, 0:s], a_[:, :, 0:s], [a_], [b_])
            TTe("dve", b_[:, :, s:128], a_[:, :, s:128], a_[:, :, 0:128 - s], ALU.add, [a_], [b_])
            a_, b_ = b_, a_
        cm = a_
        ot = b_
        ACT(t1[:], cm[:], AF.Exp, [cm], [t1])
        for c in range(4):
            TTe("dve", AR[:, c, 1, :], t1[:, c, :], zr(c), ALU.mult, [t1, big], [AR])
        ACT(wc[:], cm[:, :, 127], AF.Exp, [cm], [wc])
        TTe("dve", ot[:], cm[:], lw[:], ALU.subtract, [cm, lw], [ot])
        ACT(t1[:], ot[:], AF.Exp, [ot], [t1])
        STT("dve", AR[:, :, 0, :], kk[:], -1.0, t1[:], ALU.mult, ALU.mult, [kk, t1], [AR])
        ACT(t1[:], cm[:], AF.Exp, [cm], [t1], scale=-1.0)
        TTe("dve", t2[:], kk[:], av[:], ALU.mult, [kk, av], [t2])
        TTe("dve", BK[:, :, 0, :], t2[:], t1[:], ALU.mult, [t2, t1], [BK])
        TTe("dve", BK[:, :, 1, :], km[:], t1[:], ALU.mult, [km, t1], [BK])
        chk('rw3')
        pb = PSB()
        for c in range(4):
            TR(pb[:, c * 128:(c + 1) * 128], zv(c), identb, [big, cstb], [pb])
        CP("act", tok[:, 0, :], pb[:, 0:512], [pb], [tok])
        for q in range(2):
            pb = PSB()
            for c in range(4):
                TR(pb[:, c * 128:(c + 1) * 128], BK[:, c, q, :], identb, [BK, cstb], [pb])
            CP("act", tok[:, 1 + q, :], pb[:, 0:512], [pb], [tok])
        chk('rw4')
        for hg in range(2):
            p1 = PSF(); p2 = PSF(); p3 = PSF(); p4 = PSF()
            for hq in range(4):
                c, hh = hq, hg
                rows = slice(hh * 64, hh * 64 + 64)
                half = hq % 2
                pa = p1 if hq < 2 else p2
                MM(pa[:, half * 256:half * 256 + 256], BK[rows, c, 0, :], AR[rows, c, :, :].rearrange("p a b -> p (a b)"), True, True, [BK, AR], [pa])
                pk = p3 if hq < 2 else p4
                MM(pk[:, half * 256:half * 256 + 256], BK[rows, c, 1, :], AR[rows, c, :, :].rearrange("p a b -> p (a b)"), True, True, [BK, AR], [pk])
            chk('rw4a')
            for i2, (pa, pk) in enumerate(((p1, p3), (p2, p4))):
                h0 = hg * 4 + i2 * 2
                sa = pa[:].rearrange("p (h z t) -> p h z t", h=2, z=2)
                sk = pk[:].rearrange("p (h z t) -> p h z t", h=2, z=2)
                TTe("dve", N1[:, h0:h0 + 2, :], sa[:, :, 0, :], mrw[:, 0:2, 0, :], ALU.mult, [pa, mrw], [N1])
                TTe("dve", Arb[:, h0:h0 + 2, :], sa[:, :, 1, :], mrw[:, 0:2, 1, :], ALU.mult, [pa, mrw], [Arb])
                TTe("dve", Aak[:, h0:h0 + 2, :], sk[:, :, 0, :], mrw[:, 0:2, 0, :], ALU.mult, [pk, mrw], [Aak])
                TTe("dve", Ark[:, h0:h0 + 2, :], sk[:, :, 1, :], mrw[:, 0:2, 1, :], ALU.mult, [pk, mrw], [Ark])
            chk('rw4b')
            pl = PSF()
            for hq in range(4):
                c, hh = hq, hg
                rows = slice(hh * 64, hh * 64 + 64)
                MM(pl[:, hq * 128:(hq + 1) * 128], AR[rows, c, 0, :], BK[rows, c, 0, :], True, True, [AR, BK], [pl])
            TTe("dve", L1[:, hg * 4:hg * 4 + 4, :], pl[:].rearrange("p (h t) -> p h t", h=4), mlo[:], ALU.mult, [pl, mlo], [L1])
        chk('rw5')
        Noff = rw["Noff"]
        MSET("pool", Noff[:], 0.0, [Noff])
        CP("pool", Noff[0:64, :, 64:128], N1[0:64, :, 64:128], [N1], [Noff])
        MSET("pool", N1[0:64, :, 64:128], 0.0, [N1])
        MSET("pool", L1[64:128, :, 0:64], 0.0, [L1])
        for hg in range(2):
            hs = slice(hg * 4, hg * 4 + 4)
            TTe("pool", Mm[:, hs, :], N1[:, hs, :], cstb[:, 4:8, :], ALU.add, [N1, cstb], [Mm])
        Nk, Lk, Nn, Ln = N1, L1, N2, L2
        Mc, Mn = Mm, Mm2
        for lev in range(5):
            last = lev == 4
            for hg in range(2):
                hs = slice(hg * 4, hg * 4 + 4)
                pn = PSF()
                for hq in range(4):
                    h = hg * 4 + hq
                    MM(pn[:, hq * 128:(hq + 1) * 128], Lk[:, h, :], Nk[:, h, :], True, True, [Lk, Nk], [pn])
                if not last:
                    CP("act", Nn[:, hs, :], pn[:].rearrange("p (h t) -> p h t", h=4), [pn], [Nn])
                pL = PSF()
                for hq in range(4):
                    h = hg * 4 + hq
                    MM(pL[:, hq * 128:(hq + 1) * 128], Nk[:, h, :], Lk[:, h, :], True, True, [Lk, Nk], [pL])
                if not last:
                    CP("act", Ln[:, hs, :], pL[:].rearrange("p (h t) -> p h t", h=4), [pL], [Ln])
                TTe("dve", IL[:, hs, :], pL[:].rearrange("p (h t) -> p h t", h=4), cstb[:, 4:8, :], ALU.add, [pL, cstb], [IL])
                pm = PSF()
                for hq in range(4):
                    h = hg * 4 + hq
                    MM(pm[:, hq * 128:(hq + 1) * 128], IL[:, h, :], Mc[:, h, :], True, True, [IL, Mc], [pm])
                CP("act", Mn[:, hs, :], pm[:].rearrange("p (h t) -> p h t", h=4), [pm], [Mn])
            Nk, Nn = Nn, Nk
            Lk, Ln = Ln, Lk
            Mc, Mn = Mn, Mc
        Mf = Mc
        chk('rw6')
        px = PSF()
        for h in range(8):
            c, hh = h % 4, h // 4
            hn = 2 * c + hh
            rows = slice(hh * 64, hh * 64 + 64)
            o = px[:, hn * 64:(hn + 1) * 64]
            MM(o, AR[rows, c, 0, :], H0b[rows, c, :], True, False, [AR, H0b], [px])
            MM(o, Aak[:, h, :], tok[:, 0, hn * 64:(hn + 1) * 64], False, True, [Aak, tok], [px])
        CP("act", Xs[:], px[:], [px], [Xs])
        Noff = rw["Noff"]
        for rnd in range(2):
            pu = PSF()
            for h in range(8):
                c, hh = h % 4, h // 4
                hn = 2 * c + hh
                MM(pu[:, hn * 64:(hn + 1) * 64], Mf[:, h, :], Xs[:, hn * 64:(hn + 1) * 64], True, True, [Mf, Xs], [pu])
            CP("act", Us[:], pu[:], [pu], [Us])
            if rnd == 0:
                pw = PSF()
                for h in range(8):
                    c, hh = h % 4, h // 4
                    hn = 2 * c + hh
                    MM(pw[:, hn * 64:(hn + 1) * 64], Noff[:, h, :], Us[:, hn * 64:(hn + 1) * 64], True, True, [Noff, Us], [pw])
                TTe("dve", Xs[:], pw[:], Xs[:], ALU.add, [pw, Xs], [Xs])
        py = PSF()
        for h in range(8):
            c, hh = h % 4, h // 4
            hn = 2 * c + hh
            rows = slice(hh * 64, hh * 64 + 64)
            o = py[:, hn * 64:(hn + 1) * 64]
            MM(o, AR[rows, c, 1, :], H0b[rows, c, :], True, False, [AR, H0b], [py])
            MM(o, Arb[:, h, :], Us[:, hn * 64:(hn + 1) * 64], False, False, [Arb, Us], [py])
            MM(o, Ark[:, h, :], tok[:, 0, hn * 64:(hn + 1) * 64], False, True, [Ark, tok], [py])
        CP("act", Y[:], py[:], [py], [Y])
        ph = PSF()
        for c in range(4):
            o = ph[:, c * 128:(c + 1) * 128]
            MM(o, tok[:, 1, c * 128:(c + 1) * 128], Us[:, c * 128:(c + 1) * 128], True, False, [tok, Us], [ph])
            MM(o, tok[:, 2, c * 128:(c + 1) * 128], tok[:, 0, c * 128:(c + 1) * 128], False, True, [tok], [ph])
        for hh in range(2):
            rows = slice(hh * 64, hh * 64 + 64)
            src = ph[rows, :].rearrange("p (c x i) -> p c x i", c=4, x=2)[:, :, hh, :]
            TTe("dve", Ht[rows, :, :], src, H0[rows, :, :], ALU.add, [ph, H0], [Ht])
        for c in range(4):
            TS("dve", H0[:, c, :], Ht[:, c, :], wc[:, c:c + 1], None, ALU.mult, None, [Ht, wc], [H0])
        CP("act", H0b[:], H0[:], [H0], [H0b])
        chk('rw7')
        Y3 = Y[:].rearrange("p (h i) -> p h i", h=8)
        P.op("dve", lambda e: e.tensor_reduce(out=st[:, 1, :], in_=Y3, axis=AX.X, op=ALU.add), ar=[Y3], aw=[st[:, 1, :]])
        ACT(Y2[:], Y[:], AF.Square, [Y], [Y2])
        P.op("dve", lambda e: e.tensor_reduce(out=st[:, 2, :], in_=Y2[:].rearrange("p (h i) -> p h i", h=8), axis=AX.X, op=ALU.add), ar=[Y2[:]], aw=[st[:, 2, :]])
        TS("dve", st[:, 1, :], st[:, 1, :], 1.0 / 64, None, ALU.mult, None, [st], [st])
        TTe("dve", st[:, 3, :], st[:, 1, :], st[:, 1, :], ALU.mult, [st], [st])
        STT("dve", st[:, 2, :], st[:, 2, :], 1.0 / 64, st[:, 3, :], ALU.mult, ALU.subtract, [st], [st])
        ACT(st[:, 2, :], st[:, 2, :], AF.Sqrt, [st], [st], bias=64e-5)
        P.op("dve", lambda e: e.reciprocal(out=st[:, 2, :], in_=st[:, 2, :]), ar=[st[:, 2, :]], aw=[st[:, 2, :]])
        for h in range(8):
            TS("dve", Y2[:, h * 64:(h + 1) * 64], Y[:, h * 64:(h + 1) * 64], st[:, 1, h:h + 1], st[:, 2, h:h + 1], ALU.subtract, ALU.mult, [Y, st], [Y2])
        TTe("dve", Y2[:], Y2[:], gnw[:], ALU.mult, [Y2, gnw], [Y2])
        TTe("dve", Y2[:], Y2[:], gnb[:], ALU.add, [Y2, gnb], [Y2])
        for h in range(8):
            STT("dve", Y2[:, h * 64:(h + 1) * 64], tok[:, 0, h * 64:(h + 1) * 64], st[:, 0, h:h + 1], Y2[:, h * 64:(h + 1) * 64], ALU.mult, ALU.add, [tok, st, Y2], [Y2])
        TTe("dve", yb[:], Y2[:], gt[:], ALU.mult, [Y2, gt], [yb])
        pb = PSB()
        for c in range(4):
            TR(pb[:, c * 128:(c + 1) * 128], yb[:, c * 128:(c + 1) * 128], identb, [yb, cstb], [pb])
        CP("act", YB[1][:, :, ts], pb[:, 0:512].rearrange("p (c t) -> p c t", c=4), [pb], [YB[1]])


def _consts(T):
    bf = ml_dtypes.bfloat16
    cst = np.zeros((128, 5, 128), np.float32)
    cst[:, 0, :] = np.eye(128)
    cst[:, 1, :] = 1.0
    cst[0:64, 2, 0:64] = 1.0
    cst[64:128, 2, 64:128] = 1.0
    cstb = np.zeros((128, 8, 128), np.float32)
    cstb[:, 0, :] = np.eye(128)
    for m in range(128):
        cstb[(m // 64) * 64 + ((m % 64) + 32) % 64, 1, m] = 1.0
    cstb[:, 2, :] = 1.0
    for q_ in range(4):
        cstb[:, 4 + q_, :] = np.eye(128)
    ki = np.arange(128)[:, None]
    qi = np.arange(128)[None, :]
    m12 = np.zeros((128, 2, 256), np.float32)
    for u in range(2):
        m12[:, u, 0:128] = (ki >= qi)
        m12[:, u, 128:256] = (ki <= qi)
    m3 = np.zeros((128, 4, 4, 64), np.float32)
    for j in range(4):
        q = 32 * j + np.arange(32)[None, :]
        for rr in range(4):
            m3[:, j, rr, 0:32] = (ki >= q)
            m3[:, j, rr, 32:64] = (ki <= q)
    mrw = np.zeros((128, 4, 2, 128), np.float32)
    mrw[:, :, 0, :] = (ki < qi)[:, None, :]
    mrw[:, :, 1, :] = (ki <= qi)[:, None, :]
    mlo = np.zeros((128, 4, 128), np.float32)
    mlo[:, :, :] = (qi < ki)[:, None, :]
    sel = np.zeros((128, 4, 8), np.float32)
    for p in range(128):
        for c in range(4):
            sel[p, c, 2 * c + p // 64] = 1.0
    inv = (1.0 / (np.float32(10000.0) ** (np.arange(0, 64, 2, dtype=np.float32) / np.float32(64)))).astype(np.float32)
    ang = (np.arange(T, dtype=np.float32)[:, None] * inv[None, :]).astype(np.float32)
    cosv, sinv = np.cos(ang).astype(np.float32), np.sin(ang).astype(np.float32)
    cosT = np.zeros((128, T), np.float32)
    sinT = np.zeros((128, T), np.float32)
    for p in range(128):
        d = p % 64
        cosT[p] = cosv[:, d % 32]
        sinT[p] = sinv[:, d % 32] * (-1.0 if d < 32 else 1.0)
    return dict(cst_f32=cst, cst_bf=cstb.astype(bf), m12=m12.astype(bf), m3=m3.astype(bf), mrw=mrw.astype(bf),
                mlo=mlo.astype(bf), sel=sel.astype(bf), cosT=cosT, sinT=sinT)


def _col(v, n):
    return np.ascontiguousarray(np.asarray(v, np.float32).reshape(n, 128).T)


def _shared_inputs(inp, T):
    f = lambda a: np.ascontiguousarray(np.asarray(a, np.float32)[:NL]) if np.asarray(a).shape[0] == 2 and np.asarray(a).ndim >= 2 else np.ascontiguousarray(np.asarray(a, np.float32))
    d = dict(_consts(T))
    d["w_ada"] = f(inp["w_ada"])
    d["b_ada"] = np.stack([_col(inp["b_ada"][l], 48) for l in range(NL)])
    d["g_mix"] = np.stack([_col(inp["norm_mix"][l], 8) for l in range(NL)])
    d["g_ffn"] = np.stack([_col(inp["norm_ffn"][l], 8) for l in range(NL)])
    d["g_fin"] = _col(inp["norm_final"], 8)
    d["w_in"] = f(inp["w_in"])
    d["mu"] = np.stack([_col(inp["rwkv_mu"][l], 14) for l in range(NL)])
    d["rwp"] = np.stack([np.stack([_col(np.asarray(inp[k][l]).reshape(-1), 4) for k in
                                   ("rwkv_w0", "rwkv_a0", "rwkv_k_k", "rwkv_k_a", "rwkv_r_k")], axis=1) for l in range(NL)])
    d["r_wup"] = f(inp["rwkv_w_up"])
    d["r_aup"] = f(inp["rwkv_a_up"])
    d["r_gup"] = f(inp["rwkv_g_up"])
    bc = lambda a: np.ascontiguousarray(np.broadcast_to(f(a).reshape(NL, 1, 512), (NL, 128, 512)))
    d["gnw"] = bc(inp["rwkv_gn_w"])
    d["gnb"] = bc(inp["rwkv_gn_b"])
    d["lng"] = bc(inp["gmlp_ln_g"])
    d["lnb"] = bc(inp["gmlp_ln_b"])
    d["wsT"] = np.ascontiguousarray(np.transpose(f(inp["gmlp_w_s"]), (0, 3, 1, 2)))
    bsv = f(inp["gmlp_b_s"])
    bsT = np.zeros((NL, 128, 4, 128), np.float32)
    for g_ in range(8):
        bsT[:, (g_ % 2) * 64:(g_ % 2) * 64 + 64, g_ // 2, :] = bsv[:, g_, None, :]
    d["bsT"] = bsT
    d["w_branch"] = f(inp["w_branch"])
    d["w_out"] = f(inp["w_out"])
    d["ffn_up"] = f(inp["ffn_w_up"])
    d["convw"] = np.stack([np.ascontiguousarray(np.transpose(np.asarray(inp["ffn_conv_w"][l], np.float32).reshape(3, NF, 128), (2, 1, 0))) for l in range(NL)])
    d["convb"] = np.stack([_col(inp["ffn_conv_b"][l], NF) for l in range(NL)])
    d["ffn_down"] = f(inp["ffn_w_down"])
    return d


_NC_CACHE = {}


def run(inp, T=None):
    x = np.asarray(inp["x"], np.float32)
    B, S, _ = x.shape
    T = S
    if T not in _NC_CACHE:
        _NC_CACHE[T] = build(T)
    nc = _NC_CACHE[T]
    shared = _shared_inputs(inp, T)
    c = np.asarray(inp["c"], np.float32)
    in_maps = []
    for core in range(8):
        b = core % B
        m = dict(shared)
        m["xT"] = np.ascontiguousarray(x[b].T)
        m["c128"] = _col(c[b], 8)
        in_maps.append(m)
    res = run_bass_kernel_spmd(nc, in_maps, core_ids=list(range(8)))
    out = np.stack([np.ascontiguousarray(res.results[b]["yT"].T) for b in range(B)])
    return out.astype(np.float32)


def kernel(**inputs):
    return run(inputs)
```

```python
import numpy as np
import ml_dtypes
import concourse.bass as bass
import concourse.mybir as mybir
from concourse.bass_utils import run_bass_kernel_spmd

F32 = mybir.dt.float32
BF16 = mybir.dt.bfloat16
AF = mybir.ActivationFunctionType
ALU = mybir.AluOpType
AX = mybir.AxisListType

D = 1024
import os as _os
NL = int(_os.environ.get("KNL", "2"))
TT = 512
IN_COLS = 7424
DFF = 2816
NF = 22


class Buf:
    __slots__ = ("name", "lw", "rd", "excl")

    def __init__(self, name="", excl=False):
        self.name = name
        self.lw = None
        self.rd = []
        self.excl = excl


class Tl:
    __slots__ = ("t", "b")

    def __init__(self, t, b):
        self.t = t
        self.b = b

    def __getitem__(self, k):
        return self.t[k]


def _b(x):
    return x.b if isinstance(x, Tl) else x


def View(ap, buf):
    return Tl(ap, _b(buf))


class Prog:
    ENGS = ("pe", "dve", "act", "pool", "sp")
    NDMA = 8

    def __init__(self):
        self.nc = bass.Bass("TRN2", target_bir_lowering=False)
        self.ops = {e: [] for e in self.ENGS}
        self.cnt = {e: 0 for e in self.ENGS}
        self.seen = {e: {} for e in self.ENGS}
        self.sems = {}
        self._stack = []
        nc = self.nc
        for e in ("pe", "dve", "act", "pool"):
            self.sems[e] = self._enter(nc.semaphore("s_" + e))
        self.dslots = {}
        for q in ("sp", "act", "pool"):
            sl = []
            for i in range(self.NDMA):
                key = "d_%s%d" % (q, i)
                self.sems[key] = self._enter(nc.semaphore(key))
                sl.append([key, 0])
            self.dslots[q] = [sl, 0]
        self.n_sb = 0
        self.reg = {}

    def _enter(self, cm):
        v = cm.__enter__()
        self._stack.append(cm)
        return v

    def sb(self, shape, dtype, name=None):
        self.n_sb += 1
        t = self._enter(self.nc.sbuf_tensor("s_%s_%d" % (name or "t", self.n_sb), list(shape), dtype))
        return Tl(t, Buf(name or ""))

    def ps(self, shape, dtype=F32, name=None):
        self.n_sb += 1
        t = self._enter(self.nc.psum_tensor(name or ("ps%d" % self.n_sb), list(shape), dtype))
        return Tl(t, Buf(name or "", excl=True))

    def _deps(self, eng, reads, writes):
        toks = []
        for b in reads:
            b = _b(b)
            if b.lw is not None:
                toks.append(b.lw)
        for b in writes:
            b = _b(b)
            if b.lw is not None:
                toks.append(b.lw)
            toks.extend(b.rd)
        need = {}
        for (k, v) in toks:
            if k == "pe" and eng == "pe":
                continue
            if self.seen[eng].get(k, 0) >= v:
                continue
            if need.get(k, 0) < v:
                need[k] = v
        for k, v in need.items():
            self.seen[eng][k] = v
        return list(need.items())

    def _commit(self, tok, reads, writes):
        for b in reads:
            b = _b(b)
            b.rd.append(tok)
            if len(b.rd) > 64:
                mx = {}
                for (k, v) in b.rd:
                    if mx.get(k, 0) < v:
                        mx[k] = v
                b.rd = list(mx.items())
        for b in writes:
            b = _b(b)
            b.lw = tok
            b.rd = []

    @staticmethod
    def _onchip(ap):
        n = ap.name
        return n.startswith("s_") or n.startswith("ps")

    def _reg(self, ap):
        pstep, pcnt = ap.ap[0]
        off = ap.offset
        if ap.name.startswith("ps"):
            return (ap.name, 0, 128, 0, 1 << 30, True)
        p0 = off // pstep
        f0 = off % pstep
        span = 0
        for st, cnt in ap.ap[1:]:
            span += (cnt - 1) * abs(st)
        esz = 2 if ap.dtype == BF16 else 4
        return (ap.name, p0, p0 + pcnt, f0 * esz, (f0 + span + 1) * esz, False)

    def _rdeps(self, eng, ars, aws):
        toks = []
        new = []
        for (aps, isw) in ((ars, False), (aws, True)):
            for ap in aps:
                name, p0, p1, b0, b1, isps = self._reg(ap)
                w = isw or (isps and eng != "pe")
                recs = self.reg.setdefault(name, [])
                for r in recs:
                    if r[0] < p1 and p0 < r[1] and r[2] < b1 and b0 < r[3] and (w or r[5]):
                        toks.append(r[4])
                new.append((name, [p0, p1, b0, b1, None, w]))
        return toks, new

    def _rcommit(self, tok, new):
        for name, rec in new:
            rec[4] = tok
            recs = self.reg[name]
            if rec[5]:
                recs[:] = [r for r in recs if not (rec[0] <= r[0] and r[1] <= rec[1] and rec[2] <= r[2] and r[3] <= rec[3])]
            else:
                recs[:] = [r for r in recs if not ((not r[5]) and r[4][0] == tok[0] and rec[0] <= r[0] and r[1] <= rec[1]
                                                   and rec[2] <= r[2] and r[3] <= rec[3])]
            recs.append(rec)

    def _filter(self, eng, toks):
        need = {}
        for (k, v) in toks:
            if k == "pe" and eng == "pe":
                continue
            if self.seen[eng].get(k, 0) >= v:
                continue
            if need.get(k, 0) < v:
                need[k] = v
        for k, v in need.items():
            self.seen[eng][k] = v
        return list(need.items())

    def _buftoks(self, reads, writes):
        toks = []
        for b in reads:
            if b.lw is not None:
                toks.append(b.lw)
        for b in writes:
            if b.lw is not None:
                toks.append(b.lw)
            toks.extend(b.rd)
        return toks

    def op(self, eng, fn, reads=(), writes=(), ar=None, aw=None):
        if eng == "pool":
            eng = "dve"
        assert ar is not None and aw is not None
        reads = [b for b in reads if isinstance(b, Buf)]
        writes = [b for b in writes if isinstance(b, Buf)]
        rt, new = self._rdeps(eng, ar, aw)
        waits = self._filter(eng, self._buftoks(reads, writes) + rt)
        self.cnt[eng] += 1
        tok = (eng, self.cnt[eng])
        self.ops[eng].append((waits, fn, (eng, 1)))
        self._commit(tok, reads, writes)
        self._rcommit(tok, new)
        return tok

    def dma(self, q, out, in_, reads=(), writes=()):
        sl, idx = self.dslots[q]
        slot = sl[idx % self.NDMA]
        self.dslots[q][1] = idx + 1
        key = slot[0]
        reads = [b for b in reads if isinstance(b, Buf)]
        writes = [b for b in writes if isinstance(b, Buf)]
        ar = [in_] if self._onchip(in_) else []
        aw = [out] if self._onchip(out) else []
        rt, new = self._rdeps(q, ar, aw)
        waits = self._filter(q, self._buftoks(reads, writes) + rt)
        if slot[1] > 0 and self.seen[q].get(key, 0) < slot[1]:
            waits.append((key, slot[1]))
            self.seen[q][key] = slot[1]
        slot[1] += 16
        tok = (key, slot[1])

        def fn(e, out=out, in_=in_):
            return e.dma_start(out=out, in_=in_)
        self.ops[q].append((waits, fn, (key, 16)))
        self._commit(tok, reads, writes)
        self._rcommit(tok, new)
        return tok

    def fence(self, src, dst):
        return
        toks = []
        for s in src:
            s = _b(s)
            if s.lw is not None:
                toks.append(s.lw)
            toks.extend(s.rd)
        for d in dst:
            _b(d).rd.extend(toks)

    def finish(self, final_bufs):
        waits = self._filter("sp", self._buftoks([_b(b) for b in final_bufs], []))
        self.ops["sp"].append((waits, None, None))
        nc = self.nc
        sems = self.sems
        ops = self.ops
        with nc.Block() as block:
            def mk(e):
                def body(eng):
                    for (waits, fn, inc) in ops[e]:
                        for (k, v) in waits:
                            eng.wait_ge(sems[k], v)
                        if fn is not None:
                            ins = fn(eng)
                            ins.then_inc(sems[inc[0]], inc[1])
                return body
            block.tensor(mk("pe"))
            block.vector(mk("dve"))
            block.scalar(mk("act"))
            block.gpsimd(mk("pool"))
            block.sync(mk("sp"))
        for cm in reversed(self._stack):
            cm.__exit__(None, None, None)
        self._stack = []
        return nc


class _Stop(Exception):
    pass


def build(T, dbg=False):
    import os
    STOP = os.environ.get("KSTOP")
    P = Prog()

    def chk(name):
        if STOP == name:
            raise _Stop()
    nc = P.nc
    NTI = T // TT
    NB = T // 128

    def din(name, shape, dt=F32):
        return nc.dram_tensor(name, list(shape), dt, kind="ExternalInput").ap()

    xT_in = din("xT", [D, T])
    c_in = din("c128", [128, 8])
    w_ada = din("w_ada", [NL, D, 6 * D])
    b_ada = din("b_ada", [NL, 128, 48])
    g_mix = din("g_mix", [NL, 128, 8])
    g_ffn = din("g_ffn", [NL, 128, 8])
    g_fin = din("g_fin", [128, 8])
    w_in = din("w_in", [NL, D, IN_COLS])
    mu_d = din("mu", [NL, 128, 14])
    pr_d = din("rwp", [NL, 128, 5, 4])
    wup_d = din("r_wup", [NL, 64, 512])
    aup_d = din("r_aup", [NL, 64, 512])
    gup_d = din("r_gup", [NL, 128, 512])
    gnw_d = din("gnw", [NL, 128, 512])
    gnb_d = din("gnb", [NL, 128, 512])
    lng_d = din("lng", [NL, 128, 512])
    lnb_d = din("lnb", [NL, 128, 512])
    wsT_d = din("wsT", [NL, 128, 8, 128])
    bs_d = din("bsT", [NL, 128, 4, 128])
    wbr_d = din("w_branch", [NL, 3, 512, D])
    wout_d = din("w_out", [NL, D, D])
    wfu_d = din("ffn_up", [NL, D, 2 * DFF])
    cw_d = din("convw", [NL, 128, NF, 3])
    cb_d = din("convb", [NL, 128, NF])
    wfd_d = din("ffn_down", [NL, DFF, D])
    cos_d = din("cosT", [128, T])
    sin_d = din("sinT", [128, T])
    cst_d = din("cst_f32", [128, 5, 128])
    cstb_d = din("cst_bf", [128, 8, 128], BF16)
    m12_d = din("m12", [128, 2, 256], BF16)
    m3_d = din("m3", [128, 4, 4, 64], BF16)
    mrw_d = din("mrw", [128, 4, 2, 128], BF16)
    mlo_d = din("mlo", [128, 4, 128], BF16)
    sel_d = din("sel", [128, 4, 8], BF16)
    yT_out = nc.dram_tensor("yT", [D, T], F32, kind="ExternalOutput").ap()
    x1T = nc.dram_tensor("x1T", [D, T], F32, kind="Internal").ap()
    Vd = [nc.dram_tensor("Vd%d" % l, [T, 512], BF16, kind="Internal").ap() for l in range(NL)]
    b_x1 = Buf("x1T")
    b_vd = [Buf("vd0"), Buf("vd1")]
    b_out = Buf("out")

    def MM(out, lhsT, rhs, start, stop, r, w):
        P.op("pe", lambda e: e.matmul(out, lhsT, rhs, start=start, stop=stop), reads=r, writes=w, ar=[lhsT, rhs], aw=[out])

    def TR(out, in_, ident, r, w):
        P.op("pe", lambda e: e.transpose(out, in_, ident), reads=r, writes=w, ar=[in_, ident], aw=[out])

    def ACT(out, in_, func, r, w, bias=0.0, scale=1.0):
        P.op("act", lambda e: e.activation(out=out, in_=in_, func=func, bias=bias, scale=scale), reads=r, writes=w,
             ar=[in_] + [x for x in (bias, scale) if hasattr(x, "offset")], aw=[out])

    def TTe(eng, out, in0, in1, op, r, w):
        P.op(eng, lambda e: e.tensor_tensor(out=out, in0=in0, in1=in1, op=op), reads=r, writes=w, ar=[in0, in1], aw=[out])

    def TS(eng, out, in0, s1, s2, op0, op1, r, w):
        if s2 is None:
            P.op(eng, lambda e: e.tensor_scalar(out=out, in0=in0, scalar1=s1, scalar2=None, op0=op0), reads=r, writes=w,
                 ar=[in0] + [x for x in (s1,) if hasattr(x, "offset")], aw=[out])
        else:
            P.op(eng, lambda e: e.tensor_scalar(out=out, in0=in0, scalar1=s1, scalar2=s2, op0=op0, op1=op1), reads=r, writes=w,
                 ar=[in0] + [x for x in (s1, s2) if hasattr(x, "offset")], aw=[out])

    def STT(eng, out, in0, sc, in1, op0, op1, r, w):
        P.op(eng, lambda e: e.scalar_tensor_tensor(out=out, in0=in0, scalar=sc, in1=in1, op0=op0, op1=op1), reads=r, writes=w,
             ar=[in0, in1] + [x for x in (sc,) if hasattr(x, "offset")], aw=[out])

    def CP(eng, out, in_, r, w):
        if eng == "act":
            P.op("act", lambda e: e.copy(out=out, in_=in_), reads=r, writes=w, ar=[in_], aw=[out])
        else:
            P.op(eng, lambda e: e.tensor_copy(out=out, in_=in_), reads=r, writes=w, ar=[in_], aw=[out])

    def MSET(eng, ap, val, w):
        P.op(eng, lambda e: e.memset(ap, val), writes=w, ar=[], aw=[ap])

    psf = [P.ps([128, 512], F32) for _ in range(6)]
    psb = [P.ps([128, 1024], BF16) for _ in range(2)]
    psi = [0, 0]

    def PSF():
        psi[0] += 1
        return psf[psi[0] % 6]

    def PSB():
        psi[1] += 1
        return psb[psi[1] % 2]

    cst = P.sb([128, 5, 128], F32)
    cstb = P.sb([128, 8, 128], BF16)
    m12 = P.sb([128, 2, 256], BF16)
    m3 = P.sb([128, 4, 4, 64], BF16)
    mrw = P.sb([128, 4, 2, 128], BF16)
    mlo = P.sb([128, 4, 128], BF16)
    sel = P.sb([128, 4, 8], BF16)
    for (t_, d_) in ((cst, cst_d), (cstb, cstb_d), (m12, m12_d), (m3, m3_d), (mrw, mrw_d), (mlo, mlo_d), (sel, sel_d)):
        P.dma("sp", t_[:], d_, writes=[t_])
    identf = cst[:, 0, :]
    onesf = cst[:, 1, :]
    blkones = cst[:, 2, :]
    identb = cstb[:, 0, :]
    permb = cstb[:, 1, :]
    onesb = cstb[:, 2, :]

    xt = P.sb([128, 8, TT], F32, "xt")
    scr = P.sb([128, 8, TT], F32, "scr")
    hT = P.sb([128, 8, TT], BF16, "hT")
    rstd = P.sb([128, TT], F32, "rstd")
    NWB = 2
    wb = [P.sb([128, 8, 512], BF16, "wb%d" % i) for i in range(NWB)]
    wbi = [0]
    QT = P.sb([128, 4, TT], BF16, "QT")
    KT = P.sb([128, 4, T], BF16, "KT")
    V1r = P.sb([128, 8, 512], BF16, "V1r")
    V2 = P.sb([128, 2, 4, 512], BF16, "V2")
    V3g = [P.sb([128, 2, 4, 512], BF16, "V3g0")] * 2
    acc = P.sb([128, 4, 2, TT], F32, "acc")
    big = P.sb([128, NF, TT], BF16, "big")
    YB = [P.sb([128, 4, TT], BF16, "YB%d" % i) for i in range(3)]
    cosb = P.sb([128, TT], F32, "cos")
    sinb = P.sb([128, TT], F32, "sin")
    zcar = P.sb([128, 14], F32, "zcar")
    H0 = P.sb([128, 4, 64], F32, "H0")
    H0b = P.sb([128, 4, 64], BF16, "H0b")
    ccar = P.sb([128, NF, 2], F32, "ccar")
    MSET("pool", KT[:], 0.0, [KT])

    modp = P.sb([128, NL, 48], F32, "mod")
    A1 = P.sb([128, NL, 8], F32, "A1")
    A2 = P.sb([128, NL, 8], F32, "A2")
    gfin = P.sb([128, 8], F32, "gfin")
    cact = P.sb([128, 8], F32, "cact")
    wtmp = scr
    bad = P.sb([128, NL, 48], F32, "bad")
    gm = P.sb([128, NL, 8], F32, "gm")
    gf = P.sb([128, NL, 8], F32, "gf")
    P.dma("sp", cact[:], c_in, writes=[cact])
    P.dma("sp", gfin[:], g_fin, writes=[gfin])
    for l in range(NL):
        P.dma("sp", bad[:, l, :], b_ada[l], writes=[bad])
        P.dma("sp", gm[:, l, :], g_mix[l], writes=[gm])
        P.dma("sp", gf[:, l, :], g_ffn[l], writes=[gf])
    ACT(cact[:], cact[:], AF.Silu, [cact], [cact])
    for l in range(NL):
        for pc in range(12):
            P.dma("sp", wtmp[:], w_ada[l][:, pc * 512:(pc + 1) * 512].rearrange("(k p) n -> p k n", p=128), writes=[wtmp])
            ps = PSF()
            for mc in range(4):
                for k in range(8):
                    MM(ps[:, mc:mc + 1], wtmp[:, k, mc * 128:(mc + 1) * 128], cact[:, k:k + 1], k == 0, k == 7, [wtmp, cact], [ps])
            TTe("dve", modp[:, l, pc * 4:(pc + 1) * 4], ps[:, 0:4], bad[:, l, pc * 4:(pc + 1) * 4], ALU.add, [ps, bad], [modp])
        STT("dve", A1[:, l, :], modp[:, l, 8:16], 1.0, gm[:, l, :], ALU.add, ALU.mult, [modp, gm], [A1])
        STT("dve", A2[:, l, :], modp[:, l, 32:40], 1.0, gf[:, l, :], ALU.add, ALU.mult, [modp, gf], [A2])

    mu = P.sb([128, 14], F32, "mu")
    omu = P.sb([128, 14], F32, "omu")
    rwp = P.sb([128, 5, 4], F32, "rwp")
    omka = P.sb([128, 4], F32, "omka")
    nw0 = P.sb([128, 4], F32, "nw0")
    wupb = P.sb([128, 512], BF16, "wupb")
    aupb = P.sb([128, 512], BF16, "aupb")
    gupb = P.sb([128, 512], BF16, "gupb")
    gnw = P.sb([128, 512], F32, "gnw")
    gnb = P.sb([128, 512], F32, "gnb")
    lng = P.sb([128, 512], F32, "lng")
    lnb = P.sb([128, 512], F32, "lnb")
    wsT = P.sb([128, 8, 128], BF16, "wsT")
    wsTf = View(scr[:, 0:2, :].rearrange("p a (g i) -> p (a g) i", g=4), scr)
    bsT = P.sb([128, 4, 128], F32, "bsT")
    cw = P.sb([128, NF, 3], F32, "cw")
    cb = P.sb([128, NF], F32, "cb")

    def load_w(src_ap, kc, ncols):
        wbi[0] += 1
        w = wb[wbi[0] % NWB]
        P.dma("pool", w[:, 0:kc, 0:ncols], src_ap.rearrange("(k p) n -> p k n", p=128), writes=[w])
        return w

    whi = [0]

    def load_wh(src_ap, kc, ncols):
        whi[0] += 1
        i_ = whi[0] % (2 * NWB)
        flat = wb[i_ // 2][:].rearrange("p a b -> p (a b)")[:, (i_ % 2) * 2048:(i_ % 2) * 2048 + kc * ncols]
        w = View(flat.rearrange("p (k n) -> p k n", k=kc), wb[i_ // 2])
        P.dma("pool", w[:], src_ap.rearrange("(k p) n -> p k n", p=128))
        return w

    def rmsnorm_mod(Acol, shcol, l):
        ACT(scr[:], xt[:], AF.Square, [xt], [scr])
        ps = PSF()
        for k in range(8):
            MM(ps[:], onesf, scr[:, k, :], k == 0, k == 7, [cst, scr], [ps])
        ACT(rstd[:], ps[:], AF.Sqrt, [ps], [rstd], bias=1e-6, scale=1.0 / D)
        P.op("dve", lambda e: e.reciprocal(out=rstd[:], in_=rstd[:]), ar=[rstd[:]], aw=[rstd[:]])
        for k in range(8):
            STT("dve", scr[:, k, :], xt[:, k, :], Acol[:, k:k + 1], rstd[:], ALU.mult, ALU.mult, [xt, rstd, A1, A2, gfin], [scr])
            if shcol is not None:
                TS("pool", hT[:, k, :], scr[:, k, :], shcol[:, k:k + 1], None, ALU.add, None, [scr, modp], [hT])

    GC = 0.7978845608028654

    def gelu_from(out, src, rd, wr, tmpa, tmpb):
        ACT(out, src, AF.Gelu_apprx_tanh, rd, wr)

    try:
        for l in range(NL):
            x_src = xT_in if l == 0 else x1T
            P.dma("sp", mu[:], mu_d[l], writes=[mu])
            TS("dve", omu[:], mu[:], -1.0, 1.0, ALU.mult, ALU.add, [mu], [omu])
            P.dma("sp", rwp[:], pr_d[l], writes=[rwp])
            TS("dve", omka[:], rwp[:, 3, :], -1.0, 1.0, ALU.mult, ALU.add, [rwp], [omka])
            TS("dve", nw0[:], rwp[:, 0, :], -1.0, None, ALU.mult, None, [rwp], [nw0])
            P.dma("pool", wupb[0:64, :], wup_d[l], writes=[wupb])
            P.dma("pool", aupb[64:128, :], aup_d[l], writes=[aupb])
            P.dma("pool", gupb[:], gup_d[l], writes=[gupb])
            for (t_, d_) in ((gnw, gnw_d), (gnb, gnb_d), (lng, lng_d), (lnb, lnb_d)):
                P.dma("sp", t_[:], d_[l], writes=[t_])
            P.dma("sp", wsTf[:], wsT_d[l], writes=[wsTf])
            for g in range(8):
                TTe("dve", wsT[:, g, :], wsTf[:, g, :], mrw[:, 0, 1, :], ALU.mult, [wsTf, mrw], [wsT])
            P.dma("sp", bsT[:], bs_d[l], writes=[bsT])
            P.dma("sp", cw[:], cw_d[l], writes=[cw])
            P.dma("sp", cb[:], cb_d[l], writes=[cb])
            MSET("pool", zcar[:], 0.0, [zcar])
            MSET("pool", H0[:], 0.0, [H0])
            MSET("pool", H0b[:], 0.0, [H0b])
            MSET("pool", ccar[:], 0.0, [ccar])
            MSET("pool", big[:, 0:4, :], 0.0, [big])
            for q in range(T // 512):
                P.dma("sp", Vd[l][q * 512:(q + 1) * 512, :].rearrange("(p a) c -> p (a c)", p=128), big[:, 0:4, :].rearrange("p a b -> p (a b)"), reads=[big], writes=[b_vd[l]])

            for it in range(NTI):
                t0 = it * TT
                P.dma("sp", xt[:], x_src[:, t0:t0 + TT].rearrange("(k p) t -> p k t", p=128), reads=[b_x1] if l else [], writes=[xt])
                P.dma("sp", cosb[:], cos_d[:, t0:t0 + TT], writes=[cosb])
                P.dma("sp", sinb[:], sin_d[:, t0:t0 + TT], writes=[sinb])
                chk('pro')
                rmsnorm_mod(A1[:, l, :], modp[:, l, 0:8], l)
                chk('norm')

                for pc in range(2):
                    w = load_w(w_in[l][:, pc * 512:(pc + 1) * 512], 8, 512)
                    for mc in range(4):
                        ps = PSF()
                        for k in range(8):
                            MM(ps[:], w[:, k, mc * 128:(mc + 1) * 128], hT[:, k, :], k == 0, k == 7, [w, hT], [ps])
                        chk('qk1')
                        qs = scr[:, mc, :]
                        qbf = big[:, 14 + mc, :]
                        CP("act", qbf, ps[:], [ps], [big])
                        ps2 = PSF()
                        MM(ps2[:], permb, qbf, True, True, [cstb, big], [ps2])
                        chk('qk2')
                        TTe("dve", qs, ps[:], cosb[:], ALU.mult, [ps, cosb, big, ps2], [scr])
                        TTe("dve", scr[:, 4 + mc, :], ps2[:], sinb[:], ALU.mult, [ps2, sinb], [scr])
                        chk('qk3')
                        dst = QT[:, mc, :] if pc == 0 else KT[:, mc, t0:t0 + TT]
                        TTe("pool", dst, qs, scr[:, 4 + mc, :], ALU.add, [scr], [QT if pc == 0 else KT])
                chk('qk')
                w = load_w(w_in[l][:, 1024:1536], 8, 512)
                for blk in range(4):
                    ps = PSF()
                    for k in range(8):
                        MM(ps[:], hT[:, k, blk * 128:(blk + 1) * 128], w[:, k, :], k == 0, k == 7, [w, hT], [ps])
                    gb = 4 * it + blk
                    CP("act", V1r[:, gb % 8, :], ps[:], [ps], [V1r])
                    P.dma("sp", Vd[l][t0 + blk * 128:t0 + (blk + 1) * 128, :], V1r[:, gb % 8, :], reads=[V1r], writes=[b_vd[l]])
                P.dma("sp", V2[:, it % 2, :, :], Vd[l][t0:t0 + TT, :].rearrange("(i r) c -> i r c", r=4), reads=[b_vd[l]], writes=[V2])

                chk('v')
                if "EP" not in P.__dict__:
                    P.EP = [(View(big[:, 16 + 2 * i_, :], Buf("E%d" % i_)), View(big[:, 17 + 2 * i_, :], Buf("P%d" % i_))) for i_ in range(3)]
                    P.epi = 0
                EPb = [x for pr in P.EP for x in pr]
                P.fence([big], EPb)
                pend = []

                def defer(fn_):
                    if pend:
                        pend.pop(0)()
                    pend.append(fn_)

                def flush():
                    while pend:
                        pend.pop(0)()
                for cfg in range(3):
                    n_sp = t0 // min(2048, T)
                    j3 = (t0 % min(2048, T)) // TT
                    if cfg < 2:
                        items = [(None, c, hh) for c in range(4) for hh in range(2)]
                    else:
                        items = [(rg_, c, hh) for rg_ in range(4) for c in range(4) for hh in range(2)]
                    for (rg_, c, hh) in items:
                        if True:
                            rows = slice(hh * 64, hh * 64 + 64)
                            if cfg < 2:
                                for half in range(2):
                                    psS = PSF()
                                    hp = []
                                    for u in range(2):
                                        qi = half * 2 + u
                                        if cfg == 0:
                                            gb = 4 * it + qi
                                            q_ap = QT[rows, c, qi * 128:(qi + 1) * 128]
                                            kc_ap = KT[rows, c, gb * 128:(gb + 1) * 128]
                                            kp_ap = KT[rows, c, (gb - 1) * 128:gb * 128] if gb > 0 else None
                                            vc_ap = V1r[:, gb % 8, c * 128:(c + 1) * 128]
                                            vp_ap = V1r[:, (gb - 1) % 8, c * 128:(c + 1) * 128]
                                            vb = V1r
                                        else:
                                            q_ap = QT[rows, c, qi:TT:4]
                                            kc_ap = KT[rows, c, t0 + qi:t0 + TT:4]
                                            kp_ap = KT[rows, c, t0 - TT + qi:t0:4] if it > 0 else None
                                            vc_ap = V2[:, it % 2, qi, c * 128:(c + 1) * 128]
                                            vp_ap = V2[:, (it - 1) % 2, qi, c * 128:(c + 1) * 128]
                                            vb = V2
                                        hp.append(kp_ap is not None)
                                        if kp_ap is not None:
                                            MM(psS[:, u * 256:u * 256 + 128], kp_ap, q_ap, True, True, [KT, QT], [psS])
                                        MM(psS[:, u * 256 + 128:u * 256 + 256], kc_ap, q_ap, True, True, [KT, QT], [psS])
                                        hp.append((vp_ap, vc_ap, vb))
                                    P.epi += 1
                                    Et, Pt = P.EP[P.epi % 3]
                                    for u in range(2):
                                        if hp[2 * u]:
                                            ACT(Et[:, u * 256:(u + 1) * 256], psS[:, u * 256:(u + 1) * 256], AF.Exp, [psS], [Et], scale=0.125)
                                        else:
                                            MSET("pool", Et[:, u * 256:u * 256 + 128], 0.0, [Et])
                                            ACT(Et[:, u * 256 + 128:u * 256 + 256], psS[:, u * 256 + 128:u * 256 + 256], AF.Exp, [psS], [Et], scale=0.125)
                                    Pm = Pt[:]
                                    TTe("dve", Pm, Et[:], m12[:].rearrange("p a b -> p (a b)"), ALU.mult, [Et, m12], [Pt])
                                    def Bp(hp=hp, Pm=Pm, Pt=Pt, rows=rows, c=c, cfg=cfg, half=half):
                                        psN = PSF()
                                        for u in range(2):
                                            vp_ap, vc_ap, vb = hp[2 * u + 1]
                                            has_prev = hp[2 * u]
                                            for z in range(2):
                                                o = psN[:, z * 256 + u * 128:z * 256 + u * 128 + 128]
                                                if has_prev:
                                                    MM(o, vp_ap if z == 0 else onesb, Pm[:, u * 256:u * 256 + 128], True, False, [vb, Pt, cstb], [psN])
                                                MM(o, vc_ap if z == 0 else onesb, Pm[:, u * 256 + 128:u * 256 + 256], not has_prev, True, [vb, Pt, cstb], [psN])
                                        src = psN[rows, :].rearrange("p (z u q) -> p z u q", z=2, u=2)
                                        if cfg == 0:
                                            dsta = acc[rows, c, :, half * 256:(half + 1) * 256].rearrange("p z (u q) -> p z u q", u=2)
                                            CP("act", dsta, src, [psN], [acc])
                                        else:
                                            dsta = acc[rows, c, :, :].rearrange("p z (q r) -> p z r q", r=4)[:, :, half * 2:half * 2 + 2, :]
                                            TTe("dve", dsta, src, dsta, ALU.add, [psN, acc], [acc])
                                    defer(Bp)
                            else:
                                SPAN = min(2048, T)
                                nk3 = SPAN // 16
                                kp_ = slice(0, nk3)
                                for rg in (rg_,):
                                    if c == 0 and hh == 0:
                                        flush()
                                        v3 = V3g[rg % 2]
                                        for sp_ in range(2):
                                            nn = n_sp - 1 + sp_
                                            if nn < 0:
                                                continue
                                            P.dma("sp", v3[kp_, sp_, :, :],
                                                  Vd[l][nn * SPAN:(nn + 1) * SPAN, :].rearrange("(i r) c -> i r c", r=16)[:, rg * 4:rg * 4 + 4, :],
                                                  reads=[b_vd[l]], writes=[v3])
                                    v3 = V3g[rg % 2]
                                    psS = PSF()
                                    hasp = n_sp > 0
                                    for rr in range(4):
                                        r = rg * 4 + rr
                                        q_ap = QT[rows, c, r:TT:16]
                                        kc_ap = KT[rows, c, n_sp * SPAN + r:(n_sp + 1) * SPAN:16]
                                        if hasp:
                                            kp_ap = KT[rows, c, (n_sp - 1) * SPAN + r:n_sp * SPAN:16]
                                            MM(psS[kp_, rr * 64:rr * 64 + 32], kp_ap, q_ap, True, True, [KT, QT], [psS])
                                        MM(psS[kp_, rr * 64 + 32:rr * 64 + 64], kc_ap, q_ap, True, True, [KT, QT], [psS])
                                    P.epi += 1
                                    Et, Pt = P.EP[P.epi % 3]
                                    if not hasp:
                                        MSET("pool", Et[kp_, 0:256], 0.0, [Et])
                                        for rr in range(4):
                                            ACT(Et[kp_, rr * 64 + 32:rr * 64 + 64], psS[kp_, rr * 64 + 32:rr * 64 + 64], AF.Exp, [psS], [Et], scale=0.125)
                                    else:
                                        ACT(Et[kp_, 0:256], psS[kp_, 0:256], AF.Exp, [psS], [Et], scale=0.125)
                                    Pm = Pt[kp_, 0:256]
                                    TTe("dve", Pm, Et[kp_, 0:256], m3[kp_, j3, :, :].rearrange("p a b -> p (a b)"), ALU.mult, [Et, m3], [Pt])
                                    def Bq(v3=v3, Pm=Pm, Pt=Pt, rows=rows, c=c, rg=rg, hasp=hasp, kp_=kp_):
                                        psN = PSF()
                                        for rr in range(4):
                                            for z in range(2):
                                                o = psN[:, z * 128 + rr * 32:z * 128 + rr * 32 + 32]
                                                if hasp:
                                                    MM(o, v3[kp_, 0, rr, c * 128:(c + 1) * 128] if z == 0 else onesb[kp_, :], Pm[:, rr * 64:rr * 64 + 32], True, False, [v3, Pt, cstb], [psN])
                                                MM(o, v3[kp_, 1, rr, c * 128:(c + 1) * 128] if z == 0 else onesb[kp_, :], Pm[:, rr * 64 + 32:rr * 64 + 64], not hasp, True, [v3, Pt, cstb], [psN])
                                        src = psN[rows, 0:256].rearrange("p (z r q) -> p z r q", z=2, r=4)
                                        dsta = acc[rows, c, :, :].rearrange("p z (q r) -> p z r q", r=16)[:, :, rg * 4:rg * 4 + 4, :]
                                        TTe("dve", dsta, src, dsta, ALU.add, [psN, acc], [acc])
                                    defer(Bq)
                flush()
                P.fence(EPb, [big])
                for c in range(4):
                    P.op("dve", lambda e, c=c: e.reciprocal(out=acc[:, c, 1, :], in_=acc[:, c, 1, :]), ar=[acc[:, c, 1, :]], aw=[acc[:, c, 1, :]])
                    TTe("dve", YB[0][:, c, :], acc[:, c, 0, :], acc[:, c, 1, :], ALU.mult, [acc], [YB[0]])
                chk('att')

                for pc in range(4):
                    ncol = 512 if pc < 3 else 256
                    w = load_w(w_in[l][:, 1536 + pc * 512:1536 + pc * 512 + ncol], 8, ncol)
                    for mc in range(ncol // 128):
                        ch = pc * 4 + mc
                        ps = PSF()
                        for k in range(8):
                            MM(ps[:], w[:, k, mc * 128:(mc + 1) * 128], hT[:, k, :], k == 0, k == 7, [w, hT], [ps])
                        zb = scr[:, ch % 4, :]
                        zp = scr[:, 4 + ch % 4, :]
                        CP("act", zb, ps[:], [ps], [scr])
                        CP("pool", zp[:, 1:TT], zb[:, 0:TT - 1], [scr], [scr])
                        CP("pool", zp[:, 0:1], zcar[:, ch:ch + 1], [zcar, scr], [scr])
                        CP("pool", zcar[:, ch:ch + 1], zb[:, TT - 1:TT], [scr], [zcar])
                        TS("dve", zb, zb, omu[:, ch:ch + 1], None, ALU.mult, None, [scr, omu], [scr])
                        STT("dve", big[:, ch, :], zp, mu[:, ch:ch + 1], zb, ALU.mult, ALU.add, [scr, mu], [big])
                chk('rwin'); rwkv_tile(P, locals()); chk('rw')

                gmlp_tile(P, locals()); chk('gmlp')

                wbrs = []
                for br in range(3):
                    for half in range(2):
                        wbt = load_wh(wbr_d[l, br][:, half * 512:(half + 1) * 512], 4, 512)
                        for mc in range(4):
                            m = half * 4 + mc
                            if mc % 2 == 0:
                                c0_ = 4352 + br * 1024 + half * 512 + (mc // 2) * 256
                                wg = load_wh(w_in[l][:, c0_:c0_ + 256], 8, 256)
                            psg = PSF()
                            for k in range(8):
                                MM(psg[:], wg[:, k, (mc % 2) * 128:(mc % 2 + 1) * 128], hT[:, k, :], k == 0, k == 7, [wg, hT], [psg])
                            psbr = PSF()
                            for k in range(4):
                                MM(psbr[:], wbt[:, k, mc * 128:(mc + 1) * 128], YB[br][:, k, :], k == 0, k == 3, [wbt, YB[br]], [psbr])
                            sg = acc[:, (br * 8 + m) % 4, 0, :]
                            ACT(sg, psg[:], AF.Sigmoid, [psg], [acc])
                            if br == 0:
                                TTe("dve", scr[:, m, :], sg, psbr[:], ALU.mult, [acc, psbr], [scr])
                            else:
                                TTe("dve", sg, sg, psbr[:], ALU.mult, [acc, psbr], [acc])
                                TTe("pool", scr[:, m, :], scr[:, m, :], sg, ALU.add, [acc, scr], [scr])
                chk('br')
                for m in range(8):
                    CP("act", hT[:, m, :], scr[:, m, :], [scr], [hT])
                for half in range(2):
                    w = load_w(wout_d[l][:, half * 512:(half + 1) * 512], 8, 512)
                    for mc in range(4):
                        m = half * 4 + mc
                        ps = PSF()
                        for k in range(8):
                            MM(ps[:], w[:, k, mc * 128:(mc + 1) * 128], hT[:, k, :], k == 0, k == 7, [w, hT], [ps])
                        STT("dve", xt[:, m, :], ps[:], modp[:, l, 16 + m:17 + m], xt[:, m, :], ALU.mult, ALU.add, [ps, modp, xt], [xt])

                chk('out')
                rmsnorm_mod(A2[:, l, :], modp[:, l, 24:32], l)
                for fg in range(11):
                    nf = 2
                    wa = load_wh(wfu_d[l][:, fg * 256:fg * 256 + 256], 8, 256)
                    wg = load_wh(wfu_d[l][:, DFF + fg * 256:DFF + fg * 256 + 256], 8, 256)
                    for mc in range(nf):
                        f = fg * 2 + mc
                        psa = PSF()
                        for k in range(8):
                            MM(psa[:], wa[:, k, mc * 128:(mc + 1) * 128], hT[:, k, :], k == 0, k == 7, [wa, hT], [psa])
                        psg = PSF()
                        for k in range(8):
                            MM(psg[:], wg[:, k, mc * 128:(mc + 1) * 128], hT[:, k, :], k == 0, k == 7, [wg, hT], [psg])
                        fb_ = 2 * (f % 2)
                        ab = acc[:, fb_, :, :].rearrange("p a b -> p (a b)")
                        CP("act", ab[:, 2:2 + TT], psa[:], [psa], [acc])
                        CP("pool", ab[:, 0:2], ccar[:, f, :], [ccar, acc], [acc])
                        CP("pool", ccar[:, f, :], ab[:, TT:TT + 2], [acc], [ccar])
                        c1 = acc[:, fb_ + 1, 0, :]
                        TS("dve", c1, ab[:, 2:2 + TT], cw[:, f, 2:3], cb[:, f:f + 1], ALU.mult, ALU.add, [acc, cw, cb], [acc])
                        STT("dve", c1, ab[:, 1:1 + TT], cw[:, f, 1:2], c1, ALU.mult, ALU.add, [acc, cw], [acc])
                        STT("dve", c1, ab[:, 0:TT], cw[:, f, 0:1], c1, ALU.mult, ALU.add, [acc, cw], [acc])
                        ge = acc[:, fb_ + 1, 1, :]
                        gelu_from(ge, c1, [acc], [acc], None, None)
                        TTe("dve", big[:, f, :], ge, psg[:], ALU.mult, [acc, psg], [big])
                for m in range(8):
                    wbi[0] += 1
                    wd = wb[wbi[0] % NWB]
                    wdv = wd[:].rearrange("p a b -> p (a b)")[:, 0:NF * 128].rearrange("p (f n) -> p f n", f=NF)
                    P.dma("pool", wdv, wfd_d[l][:, m * 128:(m + 1) * 128].rearrange("(f p) n -> p f n", p=128), writes=[wd])
                    ps = PSF()
                    for f in range(NF):
                        MM(ps[:], wdv[:, f, :], big[:, f, :], f == 0, f == NF - 1, [wd, big], [ps])
                    STT("dve", xt[:, m, :], ps[:], modp[:, l, 40 + m:41 + m], xt[:, m, :], ALU.mult, ALU.add, [ps, modp, xt], [xt])

                chk('ffn')
                if l < NL - 1:
                    P.dma("sp", x1T[:, t0:t0 + TT].rearrange("(k p) t -> p k t", p=128), xt[:], reads=[xt], writes=[b_x1])
                else:
                    rmsnorm_mod(gfin[:], None, l)
                    P.dma("sp", yT_out[:, t0:t0 + TT].rearrange("(k p) t -> p k t", p=128), scr[:], reads=[scr], writes=[b_out])
    except _Stop:
        src_t = xt
        ybi = {"att": 0, "rw": 1, "gmlp": 2}.get(STOP)
        if ybi is not None:
            MSET("pool", scr[:], 0.0, [scr])
            for c_ in range(4):
                CP("dve", scr[:, c_, :], YB[ybi][:, c_, :], [YB[ybi]], [scr])
            src_t = scr
        elif STOP == "br":
            src_t = scr
        elif STOP == "rw7":
            MSET("pool", scr[:], 0.0, [scr])
            CP("dve", scr[:, 0, :], acc[:, 0, 0, :], [acc], [scr])
            src_t = scr
        P.dma("sp", yT_out[:, 0:TT].rearrange("(k p) t -> p k t", p=128), src_t[:], reads=[src_t], writes=[b_out])
    return P.finish([b_out])


def gmlp_tile(P, L):
    g = L
    PSF, MM, ACT, TTe, TS, STT, CP = g["PSF"], g["MM"], g["ACT"], g["TTe"], g["TS"], g["STT"], g["CP"]
    hT, big, acc, scr, YB, wsT, bsT, lng, lnb = g["hT"], g["big"], g["acc"], g["scr"], g["YB"], g["wsT"], g["bsT"], g["lng"], g["lnb"]
    load_w, w_in, l, gelu_from = g["load_w"], g["w_in"], g["l"], g["gelu_from"]
    w = load_w(w_in[l][:, 3328:3840], 8, 512)
    for mc in range(4):
        ps = PSF()
        for k in range(8):
            MM(ps[:], w[:, k, mc * 128:(mc + 1) * 128], hT[:, k, :], k == 0, k == 7, [w, hT], [ps])
        gelu_from(scr[:, mc, :], ps[:], [ps], [scr], scr[:, 4, :], scr[:, 5, :])
    w = load_w(w_in[l][:, 3840:4352], 8, 512)
    stats = acc[:, 3, 1, 0:8]
    for blk in range(4):
        ps = PSF()
        for k in range(8):
            MM(ps[:], hT[:, k, blk * 128:(blk + 1) * 128], w[:, k, :], k == 0, k == 7, [w, hT], [ps])
        v = acc[:, 0, 0, :]
        gelu_from(v, ps[:], [ps], [acc], acc[:, 0, 1, :], acc[:, 1, 0, :])
        P.op("dve", lambda e: e.bn_stats(out=acc[:, 3, 1, 0:6], in_=v), ar=[v], aw=[acc[:, 3, 1, 0:6]])
        P.op("dve", lambda e: e.bn_aggr(out=acc[:, 3, 1, 6:8], in_=acc[:, 3, 1, 0:6]), ar=[acc[:, 3, 1, 0:6]], aw=[acc[:, 3, 1, 6:8]])
        ACT(acc[:, 3, 1, 7:8], acc[:, 3, 1, 7:8], AF.Sqrt, [acc], [acc], bias=1e-5)
        P.op("dve", lambda e: e.reciprocal(out=acc[:, 3, 1, 7:8], in_=acc[:, 3, 1, 7:8]), ar=[acc[:, 3, 1, 7:8]], aw=[acc[:, 3, 1, 7:8]])
        TS("dve", v, v, acc[:, 3, 1, 6:7], acc[:, 3, 1, 7:8], ALU.subtract, ALU.mult, [acc], [acc])
        TTe("dve", v, v, lng[:], ALU.mult, [acc, lng], [acc])
        vb = big[:, 18, :]
        TTe("dve", vb, v, lnb[:], ALU.add, [acc, lnb], [big])
        psA = PSF()
        psB = PSF()
        for gq in range(8):
            pp = psA if gq < 4 else psB
            c = gq // 2
            MM(pp[:, (gq % 4) * 128:(gq % 4) * 128 + 128], vb[:, c * 128:(c + 1) * 128], wsT[:, gq, :], True, True, [big, wsT], [pp])
        for gq in range(8):
            pp = psA if gq < 4 else psB
            c = gq // 2
            rows = slice((gq % 2) * 64, (gq % 2) * 64 + 64)
            tmp = acc[rows, 1, 1, 0:128]
            TTe("dve", tmp, pp[rows, (gq % 4) * 128:(gq % 4) * 128 + 128], bsT[rows, c, :], ALU.add, [pp, bsT], [acc])
            TTe("dve", YB[2][rows, c, blk * 128:(blk + 1) * 128], tmp, scr[rows, c, blk * 128:(blk + 1) * 128], ALU.mult, [acc, scr], [YB[2]])


def rwkv_tile(P, L):
    g = L
    PSF, PSB, MM, TR, ACT, TTe, TS, STT, CP, MSET = (g[k] for k in ("PSF", "PSB", "MM", "TR", "ACT", "TTe", "TS", "STT", "CP", "MSET"))
    big, scr, acc, YB, H0, H0b = g["big"], g["scr"], g["acc"], g["YB"], g["H0"], g["H0b"]
    rwp, omka, wupb, aupb, gupb, gnw, gnb = g["rwp"], g["omka"], g["wupb"], g["aupb"], g["gupb"], g["gnw"], g["gnb"]
    cst, cstb, mrw, mlo, sel = g["cst"], g["cstb"], g["mrw"], g["mlo"], g["sel"]
    chk = g["chk"]
    identb, identf, blkones = g["identb"], g["identf"], g["blkones"]
    rw = getattr(P, "_rw", None)
    if rw is None:
        sb = P.sb
        rw = dict(
            th=sb([128, 128], BF16), sgx=sb([128, 128], BF16),
            **{nm: View(scr[:, i_, :].rearrange("p (a b) -> p a b", a=4), scr) for i_, nm in enumerate(("lw", "cum", "cum2", "av", "kk", "km", "t1", "t2"))},
            AR=sb([128, 4, 2, 128], BF16), BK=sb([128, 4, 2, 128], BF16),
            wc=sb([128, 4], F32),
            tok=View(g["QT"][:, 0:3, :], g["QT"]),
            Xs=View(g["QT"][:, 3, :], g["QT"]),
            **{nm: View(g["V3g"][0][:, i_ // 2, (i_ % 2) * 2:(i_ % 2) * 2 + 2, :].rearrange("p a (h t) -> p (a h) t", h=4), g["V3g"][0])
               for i_, nm in enumerate(("N1", "L1", "N2", "L2"))},
            **{nm: View(YB[2][:, 2 * i_:2 * i_ + 2, :].rearrange("p a (h t) -> p (a h) t", h=4), YB[2]) for i_, nm in enumerate(("IL", "Mm"))},
            **{nm: View(big[:, 14 + 2 * i_:16 + 2 * i_, :].rearrange("p a (h t) -> p (a h) t", h=4), big)
               for i_, nm in enumerate(("Mm2", "Arb", "Aak", "Ark"))},
            Y=View(acc[:, 0, 0, :], acc), Y2=View(acc[:, 0, 1, :], acc), st=sb([128, 4, 8], F32), gt=View(acc[:, 1, 0, :], acc),
            Ht=sb([128, 4, 64], F32),
            Noff=View(g["rstd"][:].bitcast(BF16).rearrange("p (h t) -> p h t", h=8), g["rstd"]),
        )
        P._rw = rw
    rw = P._rw
    th, sgx, lw, cum, cum2, av, kk, km, t1, t2 = (rw[k] for k in ("th", "sgx", "lw", "cum", "cum2", "av", "kk", "km", "t1", "t2"))
    AR, BK, wc, tok = rw["AR"], rw["BK"], rw["wc"], rw["tok"]
    N1, L1, N2, L2, IL, Mm, Mm2, Arb, Aak, Ark = (rw[k] for k in ("N1", "L1", "N2", "L2", "IL", "Mm", "Mm2", "Arb", "Aak", "Ark"))
    V1r_ = g["V1r"]
    fs_ = (4, 5, 6) if g["it"] % 2 == 0 else (0, 1, 2)
    Us = View(V1r_[:, fs_[0], :], V1r_)
    yb = View(V1r_[:, fs_[1], :], V1r_)
    prod = View(V1r_[:, fs_[2], :].rearrange("p (a b) -> p a b", a=4), V1r_)
    Xs, Y, Y2, st, gt, Ht = (rw[k] for k in ("Xs", "Y", "Y2", "st", "gt", "Ht"))
    W0, A0, KK_, KA, RK = (rwp[:, i, :] for i in range(5))
    EH = 0.6065306597126334

    for blk in range(4):
        ts = slice(blk * 128, (blk + 1) * 128)
        zr = lambda c: big[:, 0 + c, ts]
        zk = lambda c: big[:, 4 + c, ts]
        zv = lambda c: big[:, 8 + c, ts]
        zwa = big[:, 12, ts]
        zg = big[:, 13, ts]
        ACT(th[0:64, :], big[0:64, 12, ts], AF.Tanh, [big], [th])
        ACT(sgx[:], zg, AF.Sigmoid, [big], [sgx])
        ps = PSF()
        psa = PSF()
        for c in range(4):
            MM(ps[:, c * 128:(c + 1) * 128], wupb[0:64, c * 128:(c + 1) * 128], th[0:64, :], True, True, [wupb, th], [ps])
            MM(psa[:, c * 128:(c + 1) * 128], aupb[64:128, c * 128:(c + 1) * 128], big[64:128, 12, ts], True, True, [aupb, big], [psa])
        for c in range(4):
            ACT(lw[:, c, :], ps[:, c * 128:(c + 1) * 128], AF.Exp, [ps, g["nw0"]], [lw], bias=g["nw0"][:, c:c + 1], scale=-1.0)
            ACT(av[:, c, :], psa[:, c * 128:(c + 1) * 128], AF.Sigmoid, [psa, rwp], [av], bias=A0[:, c:c + 1])
        TS("dve", lw[:], lw[:], 1.0, None, ALU.add, None, [lw], [lw])
        P.op("dve", lambda e: e.reciprocal(out=lw[:], in_=lw[:]), ar=[lw[:]], aw=[lw[:]])
        TS("dve", lw[:], lw[:], -EH, None, ALU.mult, None, [lw], [lw])
        psg = PSF()
        MM(psg[:], sgx[:], gupb[:], True, True, [sgx, gupb], [psg])
        CP("act", gt[:], psg[:], [psg], [gt])
        chk('rw1')
        for c in range(4):
            TS("dve", kk[:, c, :], zk(c), KK_[:, c:c + 1], None, ALU.mult, None, [big, rwp], [kk])
        ACT(t1[:], kk[:], AF.Square, [kk], [t1])
        ps = PSF()
        for c in range(4):
            MM(ps[:, c * 128:(c + 1) * 128], blkones, t1[:, c, :], True, True, [cst, t1], [ps])
        ACT(t1[:].rearrange("p a b -> p (a b)"), ps[:], AF.Sqrt, [ps], [t1])
        TS("dve", t1[:], t1[:], 1e-12, None, ALU.max, None, [t1], [t1])
        P.op("dve", lambda e: e.reciprocal(out=t1[:], in_=t1[:]), ar=[t1[:]], aw=[t1[:]])
        TTe("dve", kk[:], kk[:], t1[:], ALU.mult, [kk, t1], [kk])
        for c in range(4):
            TS("dve", t2[:, c, :], av[:, c, :], KA[:, c:c + 1], omka[:, c:c + 1], ALU.mult, ALU.add, [av, rwp, omka], [t2])
            TTe("dve", km[:, c, :], t2[:, c, :], zk(c), ALU.mult, [t2, big], [km])
        for c in range(4):
            STT("dve", prod[:, c, :], zr(c), RK[:, c:c + 1], km[:, c, :], ALU.mult, ALU.mult, [big, rwp, km], [prod])
        psr = PSF()
        for c in range(4):
            MM(psr[:, 0:8], prod[:, c, :], sel[:, c, :], c == 0, c == 3, [prod, sel], [psr])
        CP("act", st[:, 0, :], psr[:, 0:8], [psr], [st])
        chk('rw2')
        src, dst = lw, cum
        CP("pool", cum[:], lw[:], [lw], [cum])
        a_, b_ = cum, cum2
        for s in (1, 2, 4, 8, 16, 32, 64):
            CP("pool", b_[:, :, 0:s], a_[:, :, 0:s], [a_], [b_])
            TTe("dve", b_[:, :, s:128], a_[:, :, s:128], a_[:, :, 0:128 - s], ALU.add, [a_], [b_])
            a_, b_ = b_, a_
        cm = a_
        ot = b_
        ACT(t1[:], cm[:], AF.Exp, [cm], [t1])
        for c in range(4):
            TTe("dve", AR[:, c, 1, :], t1[:, c, :], zr(c), ALU.mult, [t1, big], [AR])
        ACT(wc[:], cm[:, :, 127], AF.Exp, [cm], [wc])
        TTe("dve", ot[:], cm[:], lw[:], ALU.subtract, [cm, lw], [ot])
        ACT(t1[:], ot[:], AF.Exp, [ot], [t1])
        STT("dve", AR[:, :, 0, :], kk[:], -1.0, t1[:], ALU.mult, ALU.mult, [kk, t1], [AR])
        ACT(t1[:], cm[:], AF.Exp, [cm], [t1], scale=-1.0)
        TTe("dve", t2[:], kk[:], av[:], ALU.mult, [kk, av], [t2])
        TTe("dve", BK[:, :, 0, :], t2[:], t1[:], ALU.mult, [t2, t1], [BK])
        TTe("dve", BK[:, :, 1, :], km[:], t1[:], ALU.mult, [km, t1], [BK])
        chk('rw3')
        pb = PSB()
        for c in range(4):
            TR(pb[:, c * 128:(c + 1) * 128], zv(c), identb, [big, cstb], [pb])
        CP("act", tok[:, 0, :], pb[:, 0:512], [pb], [tok])
        for q in range(2):
            pb = PSB()
            for c in range(4):
                TR(pb[:, c * 128:(c + 1) * 128], BK[:, c, q, :], identb, [BK, cstb], [pb])
            CP("act", tok[:, 1 + q, :], pb[:, 0:512], [pb], [tok])
        chk('rw4')
        for hg in range(2):
            p1 = PSF(); p2 = PSF(); p3 = PSF(); p4 = PSF()
            for hq in range(4):
                c, hh = hq, hg
                rows = slice(hh * 64, hh * 64 + 64)
                half = hq % 2
                pa = p1 if hq < 2 else p2
                MM(pa[:, half * 256:half * 256 + 256], BK[rows, c, 0, :], AR[rows, c, :, :].rearrange("p a b -> p (a b)"), True, True, [BK, AR], [pa])
                pk = p3 if hq < 2 else p4
                MM(pk[:, half * 256:half * 256 + 256], BK[rows, c, 1, :], AR[rows, c, :, :].rearrange("p a b -> p (a b)"), True, True, [BK, AR], [pk])
            chk('rw4a')
            for i2, (pa, pk) in enumerate(((p1, p3), (p2, p4))):
                h0 = hg * 4 + i2 * 2
                sa = pa[:].rearrange("p (h z t) -> p h z t", h=2, z=2)
                sk = pk[:].rearrange("p (h z t) -> p h z t", h=2, z=2)
                TTe("dve", N1[:, h0:h0 + 2, :], sa[:, :, 0, :], mrw[:, 0:2, 0, :], ALU.mult, [pa, mrw], [N1])
                TTe("dve", Arb[:, h0:h0 + 2, :], sa[:, :, 1, :], mrw[:, 0:2, 1, :], ALU.mult, [pa, mrw], [Arb])
                TTe("dve", Aak[:, h0:h0 + 2, :], sk[:, :, 0, :], mrw[:, 0:2, 0, :], ALU.mult, [pk, mrw], [Aak])
                TTe("dve", Ark[:, h0:h0 + 2, :], sk[:, :, 1, :], mrw[:, 0:2, 1, :], ALU.mult, [pk, mrw], [Ark])
            chk('rw4b')
            pl = PSF()
            for hq in range(4):
                c, hh = hq, hg
                rows = slice(hh * 64, hh * 64 + 64)
                MM(pl[:, hq * 128:(hq + 1) * 128], AR[rows, c, 0, :], BK[rows, c, 0, :], True, True, [AR, BK], [pl])
            TTe("dve", L1[:, hg * 4:hg * 4 + 4, :], pl[:].rearrange("p (h t) -> p h t", h=4), mlo[:], ALU.mult, [pl, mlo], [L1])
        chk('rw5')
        Noff = rw["Noff"]
        MSET("pool", Noff[:], 0.0, [Noff])
        CP("pool", Noff[0:64, :, 64:128], N1[0:64, :, 64:128], [N1], [Noff])
        MSET("pool", N1[0:64, :, 64:128], 0.0, [N1])
        MSET("pool", L1[64:128, :, 0:64], 0.0, [L1])
        for hg in range(2):
            hs = slice(hg * 4, hg * 4 + 4)
            TTe("pool", Mm[:, hs, :], N1[:, hs, :], cstb[:, 4:8, :], ALU.add, [N1, cstb], [Mm])
        Nk, Lk, Nn, Ln = N1, L1, N2, L2
        Mc, Mn = Mm, Mm2
        for lev in range(5):
            last = lev == 4
            for hg in range(2):
                hs = slice(hg * 4, hg * 4 + 4)
                pn = PSF()
                for hq in range(4):
                    h = hg * 4 + hq
                    MM(pn[:, hq * 128:(hq + 1) * 128], Lk[:, h, :], Nk[:, h, :], True, True, [Lk, Nk], [pn])
                if not last:
                    CP("act", Nn[:, hs, :], pn[:].rearrange("p (h t) -> p h t", h=4), [pn], [Nn])
                pL = PSF()
                for hq in range(4):
                    h = hg * 4 + hq
                    MM(pL[:, hq * 128:(hq + 1) * 128], Nk[:, h, :], Lk[:, h, :], True, True, [Lk, Nk], [pL])
                if not last:
                    CP("act", Ln[:, hs, :], pL[:].rearrange("p (h t) -> p h t", h=4), [pL], [Ln])
                TTe("dve", IL[:, hs, :], pL[:].rearrange("p (h t) -> p h t", h=4), cstb[:, 4:8, :], ALU.add, [pL, cstb], [IL])
            for hg in range(2):
                hs = slice(hg * 4, hg * 4 + 4)
                pm = PSF()
                for hq in range(4):
                    h = hg * 4 + hq
                    MM(pm[:, hq * 128:(hq + 1) * 128], IL[:, h, :], Mc[:, h, :], True, True, [IL, Mc], [pm])
                CP("act", Mn[:, hs, :], pm[:].rearrange("p (h t) -> p h t", h=4), [pm], [Mn])
            Nk, Nn = Nn, Nk
            Lk, Ln = Ln, Lk
            Mc, Mn = Mn, Mc
        Mf = Mc
        chk('rw6')
        px = PSF()
        for h in range(8):
            c, hh = h % 4, h // 4
            hn = 2 * c + hh
            rows = slice(hh * 64, hh * 64 + 64)
            o = px[:, hn * 64:(hn + 1) * 64]
            MM(o, AR[rows, c, 0, :], H0b[rows, c, :], True, False, [AR, H0b], [px])
            MM(o, Aak[:, h, :], tok[:, 0, hn * 64:(hn + 1) * 64], False, True, [Aak, tok], [px])
        CP("act", Xs[:], px[:], [px], [Xs])
        Noff = rw["Noff"]
        for rnd in range(2):
            pu = PSF()
            for h in range(8):
                c, hh = h % 4, h // 4
                hn = 2 * c + hh
                MM(pu[:, hn * 64:(hn + 1) * 64], Mf[:, h, :], Xs[:, hn * 64:(hn + 1) * 64], True, True, [Mf, Xs], [pu])
            CP("act", Us[:], pu[:], [pu], [Us])
            if rnd == 0:
                pw = PSF()
                for h in range(8):
                    c, hh = h % 4, h // 4
                    hn = 2 * c + hh
                    MM(pw[:, hn * 64:(hn + 1) * 64], Noff[:, h, :], Us[:, hn * 64:(hn + 1) * 64], True, True, [Noff, Us], [pw])
                TTe("dve", Xs[:], pw[:], Xs[:], ALU.add, [pw, Xs], [Xs])
        py = PSF()
        for h in range(8):
            c, hh = h % 4, h // 4
            hn = 2 * c + hh
            rows = slice(hh * 64, hh * 64 + 64)
            o = py[:, hn * 64:(hn + 1) * 64]
            MM(o, AR[rows, c, 1, :], H0b[rows, c, :], True, False, [AR, H0b], [py])
            MM(o, Arb[:, h, :], Us[:, hn * 64:(hn + 1) * 64], False, False, [Arb, Us], [py])
            MM(o, Ark[:, h, :], tok[:, 0, hn * 64:(hn + 1) * 64], False, True, [Ark, tok], [py])
        CP("act", Y[:], py[:], [py], [Y])
        ph = PSF()
        for c in range(4):
            o = ph[:, c * 128:(c + 1) * 128]
            MM(o, tok[:, 1, c * 128:(c + 1) * 128], Us[:, c * 128:(c + 1) * 128], True, False, [tok, Us], [ph])
            MM(o, tok[:, 2, c * 128:(c + 1) * 128], tok[:, 0, c * 128:(c + 1) * 128], False, True, [tok], [ph])
        for hh in range(2):
            rows = slice(hh * 64, hh * 64 + 64)
            src = ph[rows, :].rearrange("p (c x i) -> p c x i", c=4, x=2)[:, :, hh, :]
            TTe("dve", Ht[rows, :, :], src, H0[rows, :, :], ALU.add, [ph, H0], [Ht])
        for c in range(4):
            TS("dve", H0[:, c, :], Ht[:, c, :], wc[:, c:c + 1], None, ALU.mult, None, [Ht, wc], [H0])
        CP("act", H0b[:], H0[:], [H0], [H0b])
        chk('rw7')
        Y3 = Y[:].rearrange("p (h i) -> p h i", h=8)
        P.op("dve", lambda e: e.tensor_reduce(out=st[:, 1, :], in_=Y3, axis=AX.X, op=ALU.add), ar=[Y3], aw=[st[:, 1, :]])
        ACT(Y2[:], Y[:], AF.Square, [Y], [Y2])
        P.op("dve", lambda e: e.tensor_reduce(out=st[:, 2, :], in_=Y2[:].rearrange("p (h i) -> p h i", h=8), axis=AX.X, op=ALU.add), ar=[Y2[:]], aw=[st[:, 2, :]])
        TS("dve", st[:, 1, :], st[:, 1, :], 1.0 / 64, None, ALU.mult, None, [st], [st])
        TTe("dve", st[:, 3, :], st[:, 1, :], st[:, 1, :], ALU.mult, [st], [st])
        STT("dve", st[:, 2, :], st[:, 2, :], 1.0 / 64, st[:, 3, :], ALU.mult, ALU.subtract, [st], [st])
        ACT(st[:, 2, :], st[:, 2, :], AF.Sqrt, [st], [st], bias=64e-5)
        P.op("dve", lambda e: e.reciprocal(out=st[:, 2, :], in_=st[:, 2, :]), ar=[st[:, 2, :]], aw=[st[:, 2, :]])
        for h in range(8):
            TS("dve", Y2[:, h * 64:(h + 1) * 64], Y[:, h * 64:(h + 1) * 64], st[:, 1, h:h + 1], st[:, 2, h:h + 1], ALU.subtract, ALU.mult, [Y, st], [Y2])
        TTe("dve", Y2[:], Y2[:], gnw[:], ALU.mult, [Y2, gnw], [Y2])
        TTe("dve", Y2[:], Y2[:], gnb[:], ALU.add, [Y2, gnb], [Y2])
        for h in range(8):
            STT("dve", Y2[:, h * 64:(h + 1) * 64], tok[:, 0, h * 64:(h + 1) * 64], st[:, 0, h:h + 1], Y2[:, h * 64:(h + 1) * 64], ALU.mult, ALU.add, [tok, st, Y2], [Y2])
        TTe("dve", yb[:], Y2[:], gt[:], ALU.mult, [Y2, gt], [yb])
        pb = PSB()
        for c in range(4):
            TR(pb[:, c * 128:(c + 1) * 128], yb[:, c * 128:(c + 1) * 128], identb, [yb, cstb], [pb])
        CP("act", YB[1][:, :, ts], pb[:, 0:512].rearrange("p (c t) -> p c t", c=4), [pb], [YB[1]])


def _consts(T):
    bf = ml_dtypes.bfloat16
    cst = np.zeros((128, 5, 128), np.float32)
    cst[:, 0, :] = np.eye(128)
    cst[:, 1, :] = 1.0
    cst[0:64, 2, 0:64] = 1.0
    cst[64:128, 2, 64:128] = 1.0
    cstb = np.zeros((128, 8, 128), np.float32)
    cstb[:, 0, :] = np.eye(128)
    for m in range(128):
        cstb[(m // 64) * 64 + ((m % 64) + 32) % 64, 1, m] = 1.0
    cstb[:, 2, :] = 1.0
    for q_ in range(4):
        cstb[:, 4 + q_, :] = np.eye(128)
    ki = np.arange(128)[:, None]
    qi = np.arange(128)[None, :]
    m12 = np.zeros((128, 2, 256), np.float32)
    for u in range(2):
        m12[:, u, 0:128] = (ki >= qi)
        m12[:, u, 128:256] = (ki <= qi)
    m3 = np.zeros((128, 4, 4, 64), np.float32)
    for j in range(4):
        q = 32 * j + np.arange(32)[None, :]
        for rr in range(4):
            m3[:, j, rr, 0:32] = (ki >= q)
            m3[:, j, rr, 32:64] = (ki <= q)
    mrw = np.zeros((128, 4, 2, 128), np.float32)
    mrw[:, :, 0, :] = (ki < qi)[:, None, :]
    mrw[:, :, 1, :] = (ki <= qi)[:, None, :]
    mlo = np.zeros((128, 4, 128), np.float32)
    mlo[:, :, :] = (qi < ki)[:, None, :]
    sel = np.zeros((128, 4, 8), np.float32)
    for p in range(128):
        for c in range(4):
            sel[p, c, 2 * c + p // 64] = 1.0
    inv = (1.0 / (np.float32(10000.0) ** (np.arange(0, 64, 2, dtype=np.float32) / np.float32(64)))).astype(np.float32)
    ang = (np.arange(T, dtype=np.float32)[:, None] * inv[None, :]).astype(np.float32)
    cosv, sinv = np.cos(ang).astype(np.float32), np.sin(ang).astype(np.float32)
    cosT = np.zeros((128, T), np.float32)
    sinT = np.zeros((128, T), np.float32)
    for p in range(128):
        d = p % 64
        cosT[p] = cosv[:, d % 32]
        sinT[p] = sinv[:, d % 32] * (-1.0 if d < 32 else 1.0)
    return dict(cst_f32=cst, cst_bf=cstb.astype(bf), m12=m12.astype(bf), m3=m3.astype(bf), mrw=mrw.astype(bf),
                mlo=mlo.astype(bf), sel=sel.astype(bf), cosT=cosT, sinT=sinT)


def _col(v, n):
    return np.ascontiguousarray(np.asarray(v, np.float32).reshape(n, 128).T)


def _shared_inputs(inp, T):
    f = lambda a: np.ascontiguousarray(np.asarray(a, np.float32)[:NL]) if np.asarray(a).shape[0] == 2 and np.asarray(a).ndim >= 2 else np.ascontiguousarray(np.asarray(a, np.float32))
    d = dict(_consts(T))
    d["w_ada"] = f(inp["w_ada"])
    d["b_ada"] = np.stack([_col(inp["b_ada"][l], 48) for l in range(NL)])
    d["g_mix"] = np.stack([_col(inp["norm_mix"][l], 8) for l in range(NL)])
    d["g_ffn"] = np.stack([_col(inp["norm_ffn"][l], 8) for l in range(NL)])
    d["g_fin"] = _col(inp["norm_final"], 8)
    d["w_in"] = f(inp["w_in"])
    d["mu"] = np.stack([_col(inp["rwkv_mu"][l], 14) for l in range(NL)])
    d["rwp"] = np.stack([np.stack([_col(np.asarray(inp[k][l]).reshape(-1), 4) for k in
                                   ("rwkv_w0", "rwkv_a0", "rwkv_k_k", "rwkv_k_a", "rwkv_r_k")], axis=1) for l in range(NL)])
    d["r_wup"] = f(inp["rwkv_w_up"])
    d["r_aup"] = f(inp["rwkv_a_up"])
    d["r_gup"] = f(inp["rwkv_g_up"])
    bc = lambda a: np.ascontiguousarray(np.broadcast_to(f(a).reshape(NL, 1, 512), (NL, 128, 512)))
    d["gnw"] = bc(inp["rwkv_gn_w"])
    d["gnb"] = bc(inp["rwkv_gn_b"])
    d["lng"] = bc(inp["gmlp_ln_g"])
    d["lnb"] = bc(inp["gmlp_ln_b"])
    d["wsT"] = np.ascontiguousarray(np.transpose(f(inp["gmlp_w_s"]), (0, 3, 1, 2)))
    bsv = f(inp["gmlp_b_s"])
    bsT = np.zeros((NL, 128, 4, 128), np.float32)
    for g_ in range(8):
        bsT[:, (g_ % 2) * 64:(g_ % 2) * 64 + 64, g_ // 2, :] = bsv[:, g_, None, :]
    d["bsT"] = bsT
    d["w_branch"] = f(inp["w_branch"])
    d["w_out"] = f(inp["w_out"])
    d["ffn_up"] = f(inp["ffn_w_up"])
    d["convw"] = np.stack([np.ascontiguousarray(np.transpose(np.asarray(inp["ffn_conv_w"][l], np.float32).reshape(3, NF, 128), (2, 1, 0))) for l in range(NL)])
    d["convb"] = np.stack([_col(inp["ffn_conv_b"][l], NF) for l in range(NL)])
    d["ffn_down"] = f(inp["ffn_w_down"])
    return d


_NC_CACHE = {}


def run(inp, T=None):
    x = np.asarray(inp["x"], np.float32)
    B, S, _ = x.shape
    T = S
    if T not in _NC_CACHE:
        _NC_CACHE[T] = build(T)
    nc = _NC_CACHE[T]
    shared = _shared_inputs(inp, T)
    c = np.asarray(inp["c"], np.float32)
    in_maps = []
    for core in range(8):
        b = core % B
        m = dict(shared)
        m["xT"] = np.ascontiguousarray(x[b].T)
        m["c128"] = _col(c[b], 8)
        in_maps.append(m)
    res = run_bass_kernel_spmd(nc, in_maps, core_ids=list(range(8)))
    out = np.stack([np.ascontiguousarray(res.results[b]["yT"].T) for b in range(B)])
    return out.astype(np.float32)


def kernel(**inputs):
    return run(inputs)
```

```python
import numpy as np
import ml_dtypes
import concourse.bass as bass
import concourse.mybir as mybir
from concourse.bass_utils import run_bass_kernel_spmd

F32 = mybir.dt.float32
BF16 = mybir.dt.bfloat16
AF = mybir.ActivationFunctionType
ALU = mybir.AluOpType
AX = mybir.AxisListType

D = 1024
import os as _os
NL = int(_os.environ.get("KNL", "2"))
TT = 512
IN_COLS = 7424
DFF = 2816
NF = 22


class Buf:
    __slots__ = ("name", "lw", "rd", "excl")

    def __init__(self, name="", excl=False):
        self.name = name
        self.lw = None
        self.rd = []
        self.excl = excl


class Tl:
    __slots__ = ("t", "b")

    def __init__(self, t, b):
        self.t = t
        self.b = b

    def __getitem__(self, k):
        return self.t[k]


def _b(x):
    return x.b if isinstance(x, Tl) else x


def View(ap, buf):
    return Tl(ap, _b(buf))


class Prog:
    ENGS = ("pe", "dve", "act", "pool", "sp")
    NDMA = 8

    def __init__(self):
        self.nc = bass.Bass("TRN2", target_bir_lowering=False)
        self.ops = {e: [] for e in self.ENGS}
        self.cnt = {e: 0 for e in self.ENGS}
        self.seen = {e: {} for e in self.ENGS}
        self.sems = {}
        self._stack = []
        nc = self.nc
        for e in ("pe", "dve", "act", "pool"):
            self.sems[e] = self._enter(nc.semaphore("s_" + e))
        self.dslots = {}
        for q in ("sp", "act", "pool"):
            sl = []
            for i in range(self.NDMA):
                key = "d_%s%d" % (q, i)
                self.sems[key] = self._enter(nc.semaphore(key))
                sl.append([key, 0])
            self.dslots[q] = [sl, 0]
        self.n_sb = 0
        self.reg = {}

    def _enter(self, cm):
        v = cm.__enter__()
        self._stack.append(cm)
        return v

    def sb(self, shape, dtype, name=None):
        self.n_sb += 1
        t = self._enter(self.nc.sbuf_tensor("s_%s_%d" % (name or "t", self.n_sb), list(shape), dtype))
        return Tl(t, Buf(name or ""))

    def ps(self, shape, dtype=F32, name=None):
        self.n_sb += 1
        t = self._enter(self.nc.psum_tensor(name or ("ps%d" % self.n_sb), list(shape), dtype))
        return Tl(t, Buf(name or "", excl=True))

    def _deps(self, eng, reads, writes):
        toks = []
        for b in reads:
            b = _b(b)
            if b.lw is not None:
                toks.append(b.lw)
        for b in writes:
            b = _b(b)
            if b.lw is not None:
                toks.append(b.lw)
            toks.extend(b.rd)
        need = {}
        for (k, v) in toks:
            if k == "pe" and eng == "pe":
                continue
            if self.seen[eng].get(k, 0) >= v:
                continue
            if need.get(k, 0) < v:
                need[k] = v
        for k, v in need.items():
            self.seen[eng][k] = v
        return list(need.items())

    def _commit(self, tok, reads, writes):
        for b in reads:
            b = _b(b)
            b.rd.append(tok)
            if len(b.rd) > 64:
                mx = {}
                for (k, v) in b.rd:
                    if mx.get(k, 0) < v:
                        mx[k] = v
                b.rd = list(mx.items())
        for b in writes:
            b = _b(b)
            b.lw = tok
            b.rd = []

    @staticmethod
    def _onchip(ap):
        n = ap.name
        return n.startswith("s_") or n.startswith("ps")

    def _reg(self, ap):
        pstep, pcnt = ap.ap[0]
        off = ap.offset
        if ap.name.startswith("ps"):
            return (ap.name, 0, 128, 0, 1 << 30, True)
        p0 = off // pstep
        f0 = off % pstep
        span = 0
        for st, cnt in ap.ap[1:]:
            span += (cnt - 1) * abs(st)
        esz = 2 if ap.dtype == BF16 else 4
        return (ap.name, p0, p0 + pcnt, f0 * esz, (f0 + span + 1) * esz, False)

    def _rdeps(self, eng, ars, aws):
        toks = []
        new = []
        for (aps, isw) in ((ars, False), (aws, True)):
            for ap in aps:
                name, p0, p1, b0, b1, isps = self._reg(ap)
                w = isw or (isps and eng != "pe")
                recs = self.reg.setdefault(name, [])
                for r in recs:
                    if r[0] < p1 and p0 < r[1] and r[2] < b1 and b0 < r[3] and (w or r[5]):
                        toks.append(r[4])
                new.append((name, [p0, p1, b0, b1, None, w]))
        return toks, new

    def _rcommit(self, tok, new):
        for name, rec in new:
            rec[4] = tok
            recs = self.reg[name]
            if rec[5]:
                recs[:] = [r for r in recs if not (rec[0] <= r[0] and r[1] <= rec[1] and rec[2] <= r[2] and r[3] <= rec[3])]
            else:
                recs[:] = [r for r in recs if not ((not r[5]) and r[4][0] == tok[0] and rec[0] <= r[0] and r[1] <= rec[1]
                                                   and rec[2] <= r[2] and r[3] <= rec[3])]
            recs.append(rec)

    def _filter(self, eng, toks):
        need = {}
        for (k, v) in toks:
            if k == "pe" and eng == "pe":
                continue
            if self.seen[eng].get(k, 0) >= v:
                continue
            if need.get(k, 0) < v:
                need[k] = v
        for k, v in need.items():
            self.seen[eng][k] = v
        return list(need.items())

    def _buftoks(self, reads, writes):
        toks = []
        for b in reads:
            if b.lw is not None:
                toks.append(b.lw)
        for b in writes:
            if b.lw is not None:
                toks.append(b.lw)
            toks.extend(b.rd)
        return toks

    def op(self, eng, fn, reads=(), writes=(), ar=None, aw=None):
        if eng == "pool":
            eng = "dve"
        assert ar is not None and aw is not None
        reads = [b for b in reads if isinstance(b, Buf)]
        writes = [b for b in writes if isinstance(b, Buf)]
        rt, new = self._rdeps(eng, ar, aw)
        waits = self._filter(eng, self._buftoks(reads, writes) + rt)
        self.cnt[eng] += 1
        tok = (eng, self.cnt[eng])
        self.ops[eng].append((waits, fn, (eng, 1)))
        self._commit(tok, reads, writes)
        self._rcommit(tok, new)
        return tok

    def dma(self, q, out, in_, reads=(), writes=()):
        sl, idx = self.dslots[q]
        slot = sl[idx % self.NDMA]
        self.dslots[q][1] = idx + 1
        key = slot[0]
        reads = [b for b in reads if isinstance(b, Buf)]
        writes = [b for b in writes if isinstance(b, Buf)]
        ar = [in_] if self._onchip(in_) else []
        aw = [out] if self._onchip(out) else []
        rt, new = self._rdeps(q, ar, aw)
        waits = self._filter(q, self._buftoks(reads, writes) + rt)
        if slot[1] > 0 and self.seen[q].get(key, 0) < slot[1]:
            waits.append((key, slot[1]))
            self.seen[q][key] = slot[1]
        slot[1] += 16
        tok = (key, slot[1])

        def fn(e, out=out, in_=in_):
            return e.dma_start(out=out, in_=in_)
        self.ops[q].append((waits, fn, (key, 16)))
        self._commit(tok, reads, writes)
        self._rcommit(tok, new)
        return tok

    def fence(self, src, dst):
        return
        toks = []
        for s in src:
            s = _b(s)
            if s.lw is not None:
                toks.append(s.lw)
            toks.extend(s.rd)
        for d in dst:
            _b(d).rd.extend(toks)

    def finish(self, final_bufs):
        waits = self._filter("sp", self._buftoks([_b(b) for b in final_bufs], []))
        self.ops["sp"].append((waits, None, None))
        nc = self.nc
        sems = self.sems
        ops = self.ops
        with nc.Block() as block:
            def mk(e):
                def body(eng):
                    for (waits, fn, inc) in ops[e]:
                        for (k, v) in waits:
                            eng.wait_ge(sems[k], v)
                        if fn is not None:
                            ins = fn(eng)
                            ins.then_inc(sems[inc[0]], inc[1])
                return body
            block.tensor(mk("pe"))
            block.vector(mk("dve"))
            block.scalar(mk("act"))
            block.gpsimd(mk("pool"))
            block.sync(mk("sp"))
        for cm in reversed(self._stack):
            cm.__exit__(None, None, None)
        self._stack = []
        return nc


class _Stop(Exception):
    pass


def build(T, dbg=False):
    import os
    STOP = os.environ.get("KSTOP")
    P = Prog()

    def chk(name):
        if STOP == name:
            raise _Stop()
    nc = P.nc
    NTI = T // TT
    NB = T // 128

    def din(name, shape, dt=F32):
        return nc.dram_tensor(name, list(shape), dt, kind="ExternalInput").ap()

    xT_in = din("xT", [D, T])
    c_in = din("c128", [128, 8])
    w_ada = din("w_ada", [NL, D, 6 * D])
    b_ada = din("b_ada", [NL, 128, 48])
    g_mix = din("g_mix", [NL, 128, 8])
    g_ffn = din("g_ffn", [NL, 128, 8])
    g_fin = din("g_fin", [128, 8])
    w_in = din("w_in", [NL, D, IN_COLS])
    mu_d = din("mu", [NL, 128, 14])
    pr_d = din("rwp", [NL, 128, 5, 4])
    wup_d = din("r_wup", [NL, 64, 512])
    aup_d = din("r_aup", [NL, 64, 512])
    gup_d = din("r_gup", [NL, 128, 512])
    gnw_d = din("gnw", [NL, 128, 512])
    gnb_d = din("gnb", [NL, 128, 512])
    lng_d = din("lng", [NL, 128, 512])
    lnb_d = din("lnb", [NL, 128, 512])
    wsT_d = din("wsT", [NL, 128, 8, 128])
    bs_d = din("bsT", [NL, 128, 4, 128])
    wbr_d = din("w_branch", [NL, 3, 512, D])
    wout_d = din("w_out", [NL, D, D])
    wfu_d = din("ffn_up", [NL, D, 2 * DFF])
    cw_d = din("convw", [NL, 128, NF, 3])
    cb_d = din("convb", [NL, 128, NF])
    wfd_d = din("ffn_down", [NL, DFF, D])
    cos_d = din("cosT", [128, T])
    sin_d = din("sinT", [128, T])
    cst_d = din("cst_f32", [128, 5, 128])
    cstb_d = din("cst_bf", [128, 8, 128], BF16)
    m12_d = din("m12", [128, 2, 256], BF16)
    m3_d = din("m3", [128, 4, 4, 64], BF16)
    mrw_d = din("mrw", [128, 4, 2, 128], BF16)
    mlo_d = din("mlo", [128, 4, 128], BF16)
    sel_d = din("sel", [128, 4, 8], BF16)
    yT_out = nc.dram_tensor("yT", [D, T], F32, kind="ExternalOutput").ap()
    x1T = nc.dram_tensor("x1T", [D, T], F32, kind="Internal").ap()
    Vd = [nc.dram_tensor("Vd%d" % l, [T, 512], BF16, kind="Internal").ap() for l in range(NL)]
    b_x1 = Buf("x1T")
    b_vd = [Buf("vd0"), Buf("vd1")]
    b_out = Buf("out")

    def MM(out, lhsT, rhs, start, stop, r, w):
        P.op("pe", lambda e: e.matmul(out, lhsT, rhs, start=start, stop=stop), reads=r, writes=w, ar=[lhsT, rhs], aw=[out])

    def TR(out, in_, ident, r, w):
        P.op("pe", lambda e: e.transpose(out, in_, ident), reads=r, writes=w, ar=[in_, ident], aw=[out])

    def ACT(out, in_, func, r, w, bias=0.0, scale=1.0):
        P.op("act", lambda e: e.activation(out=out, in_=in_, func=func, bias=bias, scale=scale), reads=r, writes=w,
             ar=[in_] + [x for x in (bias, scale) if hasattr(x, "offset")], aw=[out])

    def TTe(eng, out, in0, in1, op, r, w):
        P.op(eng, lambda e: e.tensor_tensor(out=out, in0=in0, in1=in1, op=op), reads=r, writes=w, ar=[in0, in1], aw=[out])

    def TS(eng, out, in0, s1, s2, op0, op1, r, w):
        if s2 is None:
            P.op(eng, lambda e: e.tensor_scalar(out=out, in0=in0, scalar1=s1, scalar2=None, op0=op0), reads=r, writes=w,
                 ar=[in0] + [x for x in (s1,) if hasattr(x, "offset")], aw=[out])
        else:
            P.op(eng, lambda e: e.tensor_scalar(out=out, in0=in0, scalar1=s1, scalar2=s2, op0=op0, op1=op1), reads=r, writes=w,
                 ar=[in0] + [x for x in (s1, s2) if hasattr(x, "offset")], aw=[out])

    def STT(eng, out, in0, sc, in1, op0, op1, r, w):
        P.op(eng, lambda e: e.scalar_tensor_tensor(out=out, in0=in0, scalar=sc, in1=in1, op0=op0, op1=op1), reads=r, writes=w,
             ar=[in0, in1] + [x for x in (sc,) if hasattr(x, "offset")], aw=[out])

    def CP(eng, out, in_, r, w):
        if eng == "act":
            P.op("act", lambda e: e.copy(out=out, in_=in_), reads=r, writes=w, ar=[in_], aw=[out])
        else:
            P.op(eng, lambda e: e.tensor_copy(out=out, in_=in_), reads=r, writes=w, ar=[in_], aw=[out])

    def MSET(eng, ap, val, w):
        P.op(eng, lambda e: e.memset(ap, val), writes=w, ar=[], aw=[ap])

    psf = [P.ps([128, 512], F32) for _ in range(6)]
    psb = [P.ps([128, 1024], BF16) for _ in range(2)]
    psi = [0, 0]

    def PSF():
        psi[0] += 1
        return psf[psi[0] % 6]

    def PSB():
        psi[1] += 1
        return psb[psi[1] % 2]

    cst = P.sb([128, 5, 128], F32)
    cstb = P.sb([128, 8, 128], BF16)
    m12 = P.sb([128, 2, 256], BF16)
    m3 = P.sb([128, 4, 4, 64], BF16)
    mrw = P.sb([128, 4, 2, 128], BF16)
    mlo = P.sb([128, 4, 128], BF16)
    sel = P.sb([128, 4, 8], BF16)
    for (t_, d_) in ((cst, cst_d), (cstb, cstb_d), (m12, m12_d), (m3, m3_d), (mrw, mrw_d), (mlo, mlo_d), (sel, sel_d)):
        P.dma("sp", t_[:], d_, writes=[t_])
    identf = cst[:, 0, :]
    onesf = cst[:, 1, :]
    blkones = cst[:, 2, :]
    identb = cstb[:, 0, :]
    permb = cstb[:, 1, :]
    onesb = cstb[:, 2, :]

    xt = P.sb([128, 8, TT], F32, "xt")
    scr = P.sb([128, 8, TT], F32, "scr")
    hT = P.sb([128, 8, TT], BF16, "hT")
    rstd = P.sb([128, TT], F32, "rstd")
    NWB = 2
    wb = [P.sb([128, 8, 512], BF16, "wb%d" % i) for i in range(NWB)]
    wbi = [0]
    QT = P.sb([128, 4, TT], BF16, "QT")
    KT = P.sb([128, 4, T], BF16, "KT")
    V1r = P.sb([128, 8, 512], BF16, "V1r")
    V2 = P.sb([128, 2, 4, 512], BF16, "V2")
    V3g = [P.sb([128, 2, 4, 512], BF16, "V3g0")] * 2
    acc = P.sb([128, 4, 2, TT], F32, "acc")
    big = P.sb([128, NF, TT], BF16, "big")
    YB = [P.sb([128, 4, TT], BF16, "YB%d" % i) for i in range(3)]
    cosb = P.sb([128, TT], F32, "cos")
    sinb = P.sb([128, TT], F32, "sin")
    zcar = P.sb([128, 14], F32, "zcar")
    H0 = P.sb([128, 4, 64], F32, "H0")
    H0b = P.sb([128, 4, 64], BF16, "H0b")
    ccar = P.sb([128, NF, 2], F32, "ccar")
    MSET("pool", KT[:], 0.0, [KT])

    modp = P.sb([128, NL, 48], F32, "mod")
    A1 = P.sb([128, NL, 8], F32, "A1")
    A2 = P.sb([128, NL, 8], F32, "A2")
    gfin = P.sb([128, 8], F32, "gfin")
    cact = P.sb([128, 8], F32, "cact")
    wtmp = scr
    bad = P.sb([128, NL, 48], F32, "bad")
    gm = P.sb([128, NL, 8], F32, "gm")
    gf = P.sb([128, NL, 8], F32, "gf")
    P.dma("sp", cact[:], c_in, writes=[cact])
    P.dma("sp", gfin[:], g_fin, writes=[gfin])
    for l in range(NL):
        P.dma("sp", bad[:, l, :], b_ada[l], writes=[bad])
        P.dma("sp", gm[:, l, :], g_mix[l], writes=[gm])
        P.dma("sp", gf[:, l, :], g_ffn[l], writes=[gf])
    ACT(cact[:], cact[:], AF.Silu, [cact], [cact])
    for l in range(NL):
        for pc in range(12):
            P.dma("sp", wtmp[:], w_ada[l][:, pc * 512:(pc + 1) * 512].rearrange("(k p) n -> p k n", p=128), writes=[wtmp])
            ps = PSF()
            for mc in range(4):
                for k in range(8):
                    MM(ps[:, mc:mc + 1], wtmp[:, k, mc * 128:(mc + 1) * 128], cact[:, k:k + 1], k == 0, k == 7, [wtmp, cact], [ps])
            TTe("dve", modp[:, l, pc * 4:(pc + 1) * 4], ps[:, 0:4], bad[:, l, pc * 4:(pc + 1) * 4], ALU.add, [ps, bad], [modp])
        STT("dve", A1[:, l, :], modp[:, l, 8:16], 1.0, gm[:, l, :], ALU.add, ALU.mult, [modp, gm], [A1])
        STT("dve", A2[:, l, :], modp[:, l, 32:40], 1.0, gf[:, l, :], ALU.add, ALU.mult, [modp, gf], [A2])

    mu = P.sb([128, 14], F32, "mu")
    omu = P.sb([128, 14], F32, "omu")
    rwp = P.sb([128, 5, 4], F32, "rwp")
    omka = P.sb([128, 4], F32, "omka")
    nw0 = P.sb([128, 4], F32, "nw0")
    wupb = P.sb([128, 512], BF16, "wupb")
    aupb = P.sb([128, 512], BF16, "aupb")
    gupb = P.sb([128, 512], BF16, "gupb")
    gnw = P.sb([128, 512], F32, "gnw")
    gnb = P.sb([128, 512], F32, "gnb")
    lng = P.sb([128, 512], F32, "lng")
    lnb = P.sb([128, 512], F32, "lnb")
    wsT = P.sb([128, 8, 128], BF16, "wsT")
    wsTf = View(scr[:, 0:2, :].rearrange("p a (g i) -> p (a g) i", g=4), scr)
    bsT = P.sb([128, 4, 128], F32, "bsT")
    cw = P.sb([128, NF, 3], F32, "cw")
    cb = P.sb([128, NF], F32, "cb")

    def load_w(src_ap, kc, ncols):
        wbi[0] += 1
        w = wb[wbi[0] % NWB]
        P.dma("pool", w[:, 0:kc, 0:ncols], src_ap.rearrange("(k p) n -> p k n", p=128), writes=[w])
        return w

    whi = [0]

    def load_wh(src_ap, kc, ncols):
        whi[0] += 1
        i_ = whi[0] % (2 * NWB)
        flat = wb[i_ // 2][:].rearrange("p a b -> p (a b)")[:, (i_ % 2) * 2048:(i_ % 2) * 2048 + kc * ncols]
        w = View(flat.rearrange("p (k n) -> p k n", k=kc), wb[i_ // 2])
        P.dma("pool", w[:], src_ap.rearrange("(k p) n -> p k n", p=128))
        return w

    def rmsnorm_mod(Acol, shcol, l):
        ACT(scr[:], xt[:], AF.Square, [xt], [scr])
        ps = PSF()
        for k in range(8):
            MM(ps[:], onesf, scr[:, k, :], k == 0, k == 7, [cst, scr], [ps])
        ACT(rstd[:], ps[:], AF.Sqrt, [ps], [rstd], bias=1e-6, scale=1.0 / D)
        P.op("dve", lambda e: e.reciprocal(out=rstd[:], in_=rstd[:]), ar=[rstd[:]], aw=[rstd[:]])
        for k in range(8):
            STT("dve", scr[:, k, :], xt[:, k, :], Acol[:, k:k + 1], rstd[:], ALU.mult, ALU.mult, [xt, rstd, A1, A2, gfin], [scr])
            if shcol is not None:
                TS("pool", hT[:, k, :], scr[:, k, :], shcol[:, k:k + 1], None, ALU.add, None, [scr, modp], [hT])

    GC = 0.7978845608028654

    def gelu_from(out, src, rd, wr, tmpa, tmpb):
        ACT(out, src, AF.Gelu_apprx_tanh, rd, wr)

    try:
        for l in range(NL):
            x_src = xT_in if l == 0 else x1T
            P.dma("sp", mu[:], mu_d[l], writes=[mu])
            TS("dve", omu[:], mu[:], -1.0, 1.0, ALU.mult, ALU.add, [mu], [omu])
            P.dma("sp", rwp[:], pr_d[l], writes=[rwp])
            TS("dve", omka[:], rwp[:, 3, :], -1.0, 1.0, ALU.mult, ALU.add, [rwp], [omka])
            TS("dve", nw0[:], rwp[:, 0, :], -1.0, None, ALU.mult, None, [rwp], [nw0])
            P.dma("pool", wupb[0:64, :], wup_d[l], writes=[wupb])
            P.dma("pool", aupb[64:128, :], aup_d[l], writes=[aupb])
            P.dma("pool", gupb[:], gup_d[l], writes=[gupb])
            for (t_, d_) in ((gnw, gnw_d), (gnb, gnb_d), (lng, lng_d), (lnb, lnb_d)):
                P.dma("sp", t_[:], d_[l], writes=[t_])
            P.dma("sp", wsTf[:], wsT_d[l], writes=[wsTf])
            for g in range(8):
                TTe("dve", wsT[:, g, :], wsTf[:, g, :], mrw[:, 0, 1, :], ALU.mult, [wsTf, mrw], [wsT])
            P.dma("sp", bsT[:], bs_d[l], writes=[bsT])
            P.dma("sp", cw[:], cw_d[l], writes=[cw])
            P.dma("sp", cb[:], cb_d[l], writes=[cb])
            MSET("pool", zcar[:], 0.0, [zcar])
            MSET("pool", H0[:], 0.0, [H0])
            MSET("pool", H0b[:], 0.0, [H0b])
            MSET("pool", ccar[:], 0.0, [ccar])
            MSET("pool", big[:, 0:4, :], 0.0, [big])
            for q in range(T // 512):
                P.dma("sp", Vd[l][q * 512:(q + 1) * 512, :].rearrange("(p a) c -> p (a c)", p=128), big[:, 0:4, :].rearrange("p a b -> p (a b)"), reads=[big], writes=[b_vd[l]])

            for it in range(NTI):
                t0 = it * TT
                P.dma("sp", xt[:], x_src[:, t0:t0 + TT].rearrange("(k p) t -> p k t", p=128), reads=[b_x1] if l else [], writes=[xt])
                P.dma("sp", cosb[:], cos_d[:, t0:t0 + TT], writes=[cosb])
                P.dma("sp", sinb[:], sin_d[:, t0:t0 + TT], writes=[sinb])
                chk('pro')
                rmsnorm_mod(A1[:, l, :], modp[:, l, 0:8], l)
                chk('norm')

                for pc in range(2):
                    w = load_w(w_in[l][:, pc * 512:(pc + 1) * 512], 8, 512)
                    for mc in range(4):
                        ps = PSF()
                        for k in range(8):
                            MM(ps[:], w[:, k, mc * 128:(mc + 1) * 128], hT[:, k, :], k == 0, k == 7, [w, hT], [ps])
                        chk('qk1')
                        qs = scr[:, mc, :]
                        qbf = big[:, 14 + mc, :]
                        CP("act", qbf, ps[:], [ps], [big])
                        ps2 = PSF()
                        MM(ps2[:], permb, qbf, True, True, [cstb, big], [ps2])
                        chk('qk2')
                        TTe("dve", qs, ps[:], cosb[:], ALU.mult, [ps, cosb, big, ps2], [scr])
                        TTe("dve", scr[:, 4 + mc, :], ps2[:], sinb[:], ALU.mult, [ps2, sinb], [scr])
                        chk('qk3')
                        dst = QT[:, mc, :] if pc == 0 else KT[:, mc, t0:t0 + TT]
                        TTe("pool", dst, qs, scr[:, 4 + mc, :], ALU.add, [scr], [QT if pc == 0 else KT])
                chk('qk')
                w = load_w(w_in[l][:, 1024:1536], 8, 512)
                for blk in range(4):
                    ps = PSF()
                    for k in range(8):
                        MM(ps[:], hT[:, k, blk * 128:(blk + 1) * 128], w[:, k, :], k == 0, k == 7, [w, hT], [ps])
                    gb = 4 * it + blk
                    CP("act", V1r[:, gb % 8, :], ps[:], [ps], [V1r])
                    P.dma("sp", Vd[l][t0 + blk * 128:t0 + (blk + 1) * 128, :], V1r[:, gb % 8, :], reads=[V1r], writes=[b_vd[l]])
                P.dma("sp", V2[:, it % 2, :, :], Vd[l][t0:t0 + TT, :].rearrange("(i r) c -> i r c", r=4), reads=[b_vd[l]], writes=[V2])

                chk('v')
                if "EP" not in P.__dict__:
                    P.EP = [(View(big[:, 16 + 2 * i_, :], Buf("E%d" % i_)), View(big[:, 17 + 2 * i_, :], Buf("P%d" % i_))) for i_ in range(3)]
                    P.epi = 0
                EPb = [x for pr in P.EP for x in pr]
                P.fence([big], EPb)
                pend = []

                def defer(fn_):
                    pend.append(fn_)
                    if len(pend) > 2:
                        pend.pop(0)()

                def flush():
                    while pend:
                        pend.pop(0)()
                for cfg in range(3):
                    n_sp = t0 // min(2048, T)
                    j3 = (t0 % min(2048, T)) // TT
                    if cfg < 2:
                        items = [(None, c, hh) for c in range(4) for hh in range(2)]
                    else:
                        items = [(rg_, c, hh) for rg_ in range(4) for c in range(4) for hh in range(2)]
                    for (rg_, c, hh) in items:
                        if True:
                            rows = slice(hh * 64, hh * 64 + 64)
                            if cfg < 2:
                                for half in range(2):
                                    psS = PSF()
                                    hp = []
                                    for u in range(2):
                                        qi = half * 2 + u
                                        if cfg == 0:
                                            gb = 4 * it + qi
                                            q_ap = QT[rows, c, qi * 128:(qi + 1) * 128]
                                            kc_ap = KT[rows, c, gb * 128:(gb + 1) * 128]
                                            kp_ap = KT[rows, c, (gb - 1) * 128:gb * 128] if gb > 0 else None
                                            vc_ap = V1r[:, gb % 8, c * 128:(c + 1) * 128]
                                            vp_ap = V1r[:, (gb - 1) % 8, c * 128:(c + 1) * 128]
                                            vb = V1r
                                        else:
                                            q_ap = QT[rows, c, qi:TT:4]
                                            kc_ap = KT[rows, c, t0 + qi:t0 + TT:4]
                                            kp_ap = KT[rows, c, t0 - TT + qi:t0:4] if it > 0 else None
                                            vc_ap = V2[:, it % 2, qi, c * 128:(c + 1) * 128]
                                            vp_ap = V2[:, (it - 1) % 2, qi, c * 128:(c + 1) * 128]
                                            vb = V2
                                        hp.append(kp_ap is not None)
                                        if kp_ap is not None:
                                            MM(psS[:, u * 256:u * 256 + 128], kp_ap, q_ap, True, True, [KT, QT], [psS])
                                        MM(psS[:, u * 256 + 128:u * 256 + 256], kc_ap, q_ap, True, True, [KT, QT], [psS])
                                        hp.append((vp_ap, vc_ap, vb))
                                    P.epi += 1
                                    Et, Pt = P.EP[P.epi % 3]
                                    for u in range(2):
                                        if hp[2 * u]:
                                            ACT(Et[:, u * 256:(u + 1) * 256], psS[:, u * 256:(u + 1) * 256], AF.Exp, [psS], [Et], scale=0.125)
                                        else:
                                            MSET("pool", Et[:, u * 256:u * 256 + 128], 0.0, [Et])
                                            ACT(Et[:, u * 256 + 128:u * 256 + 256], psS[:, u * 256 + 128:u * 256 + 256], AF.Exp, [psS], [Et], scale=0.125)
                                    Pm = Pt[:]
                                    TTe("dve", Pm, Et[:], m12[:].rearrange("p a b -> p (a b)"), ALU.mult, [Et, m12], [Pt])
                                    def Bp(hp=hp, Pm=Pm, Pt=Pt, rows=rows, c=c, cfg=cfg, half=half):
                                        psN = PSF()
                                        for u in range(2):
                                            vp_ap, vc_ap, vb = hp[2 * u + 1]
                                            has_prev = hp[2 * u]
                                            for z in range(2):
                                                o = psN[:, z * 256 + u * 128:z * 256 + u * 128 + 128]
                                                if has_prev:
                                                    MM(o, vp_ap if z == 0 else onesb, Pm[:, u * 256:u * 256 + 128], True, False, [vb, Pt, cstb], [psN])
                                                MM(o, vc_ap if z == 0 else onesb, Pm[:, u * 256 + 128:u * 256 + 256], not has_prev, True, [vb, Pt, cstb], [psN])
                                        src = psN[rows, :].rearrange("p (z u q) -> p z u q", z=2, u=2)
                                        if cfg == 0:
                                            dsta = acc[rows, c, :, half * 256:(half + 1) * 256].rearrange("p z (u q) -> p z u q", u=2)
                                            CP("act", dsta, src, [psN], [acc])
                                        else:
                                            dsta = acc[rows, c, :, :].rearrange("p z (q r) -> p z r q", r=4)[:, :, half * 2:half * 2 + 2, :]
                                            TTe("dve", dsta, src, dsta, ALU.add, [psN, acc], [acc])
                                    defer(Bp)
                            else:
                                SPAN = min(2048, T)
                                nk3 = SPAN // 16
                                kp_ = slice(0, nk3)
                                for rg in (rg_,):
                                    if c == 0 and hh == 0:
                                        flush()
                                        v3 = V3g[rg % 2]
                                        for sp_ in range(2):
                                            nn = n_sp - 1 + sp_
                                            if nn < 0:
                                                continue
                                            P.dma("sp", v3[kp_, sp_, :, :],
                                                  Vd[l][nn * SPAN:(nn + 1) * SPAN, :].rearrange("(i r) c -> i r c", r=16)[:, rg * 4:rg * 4 + 4, :],
                                                  reads=[b_vd[l]], writes=[v3])
                                    v3 = V3g[rg % 2]
                                    psS = PSF()
                                    hasp = n_sp > 0
                                    for rr in range(4):
                                        r = rg * 4 + rr
                                        q_ap = QT[rows, c, r:TT:16]
                                        kc_ap = KT[rows, c, n_sp * SPAN + r:(n_sp + 1) * SPAN:16]
                                        if hasp:
                                            kp_ap = KT[rows, c, (n_sp - 1) * SPAN + r:n_sp * SPAN:16]
                                            MM(psS[kp_, rr * 64:rr * 64 + 32], kp_ap, q_ap, True, True, [KT, QT], [psS])
                                        MM(psS[kp_, rr * 64 + 32:rr * 64 + 64], kc_ap, q_ap, True, True, [KT, QT], [psS])
                                    P.epi += 1
                                    Et, Pt = P.EP[P.epi % 3]
                                    if not hasp:
                                        MSET("pool", Et[kp_, 0:256], 0.0, [Et])
                                        for rr in range(4):
                                            ACT(Et[kp_, rr * 64 + 32:rr * 64 + 64], psS[kp_, rr * 64 + 32:rr * 64 + 64], AF.Exp, [psS], [Et], scale=0.125)
                                    else:
                                        ACT(Et[kp_, 0:256], psS[kp_, 0:256], AF.Exp, [psS], [Et], scale=0.125)
                                    Pm = Pt[kp_, 0:256]
                                    TTe("dve", Pm, Et[kp_, 0:256], m3[kp_, j3, :, :].rearrange("p a b -> p (a b)"), ALU.mult, [Et, m3], [Pt])
                                    def Bq(v3=v3, Pm=Pm, Pt=Pt, rows=rows, c=c, rg=rg, hasp=hasp, kp_=kp_):
                                        psN = PSF()
                                        for rr in range(4):
                                            for z in range(2):
                                                o = psN[:, z * 128 + rr * 32:z * 128 + rr * 32 + 32]
                                                if hasp:
                                                    MM(o, v3[kp_, 0, rr, c * 128:(c + 1) * 128] if z == 0 else onesb[kp_, :], Pm[:, rr * 64:rr * 64 + 32], True, False, [v3, Pt, cstb], [psN])
                                                MM(o, v3[kp_, 1, rr, c * 128:(c + 1) * 128] if z == 0 else onesb[kp_, :], Pm[:, rr * 64 + 32:rr * 64 + 64], not hasp, True, [v3, Pt, cstb], [psN])
                                        src = psN[rows, 0:256].rearrange("p (z r q) -> p z r q", z=2, r=4)
                                        dsta = acc[rows, c, :, :].rearrange("p z (q r) -> p z r q", r=16)[:, :, rg * 4:rg * 4 + 4, :]
                                        TTe("dve", dsta, src, dsta, ALU.add, [psN, acc], [acc])
                                    defer(Bq)
                flush()
                P.fence(EPb, [big])
                for c in range(4):
                    P.op("dve", lambda e, c=c: e.reciprocal(out=acc[:, c, 1, :], in_=acc[:, c, 1, :]), ar=[acc[:, c, 1, :]], aw=[acc[:, c, 1, :]])
                    TTe("dve", YB[0][:, c, :], acc[:, c, 0, :], acc[:, c, 1, :], ALU.mult, [acc], [YB[0]])
                chk('att')

                for pc in range(4):
                    ncol = 512 if pc < 3 else 256
                    w = load_w(w_in[l][:, 1536 + pc * 512:1536 + pc * 512 + ncol], 8, ncol)
                    for mc in range(ncol // 128):
                        ch = pc * 4 + mc
                        ps = PSF()
                        for k in range(8):
                            MM(ps[:], w[:, k, mc * 128:(mc + 1) * 128], hT[:, k, :], k == 0, k == 7, [w, hT], [ps])
                        zb = scr[:, ch % 4, :]
                        zp = scr[:, 4 + ch % 4, :]
                        CP("act", zb, ps[:], [ps], [scr])
                        CP("pool", zp[:, 1:TT], zb[:, 0:TT - 1], [scr], [scr])
                        CP("pool", zp[:, 0:1], zcar[:, ch:ch + 1], [zcar, scr], [scr])
                        CP("pool", zcar[:, ch:ch + 1], zb[:, TT - 1:TT], [scr], [zcar])
                        TS("dve", zb, zb, omu[:, ch:ch + 1], None, ALU.mult, None, [scr, omu], [scr])
                        STT("dve", big[:, ch, :], zp, mu[:, ch:ch + 1], zb, ALU.mult, ALU.add, [scr, mu], [big])
                chk('rwin'); rwkv_tile(P, locals()); chk('rw')

                gmlp_tile(P, locals()); chk('gmlp')

                wbrs = []
                for br in range(3):
                    for half in range(2):
                        wbt = load_wh(wbr_d[l, br][:, half * 512:(half + 1) * 512], 4, 512)
                        for mc in range(4):
                            m = half * 4 + mc
                            if mc % 2 == 0:
                                c0_ = 4352 + br * 1024 + half * 512 + (mc // 2) * 256
                                wg = load_wh(w_in[l][:, c0_:c0_ + 256], 8, 256)
                            psg = PSF()
                            for k in range(8):
                                MM(psg[:], wg[:, k, (mc % 2) * 128:(mc % 2 + 1) * 128], hT[:, k, :], k == 0, k == 7, [wg, hT], [psg])
                            psbr = PSF()
                            for k in range(4):
                                MM(psbr[:], wbt[:, k, mc * 128:(mc + 1) * 128], YB[br][:, k, :], k == 0, k == 3, [wbt, YB[br]], [psbr])
                            sg = acc[:, (br * 8 + m) % 4, 0, :]
                            ACT(sg, psg[:], AF.Sigmoid, [psg], [acc])
                            if br == 0:
                                TTe("dve", scr[:, m, :], sg, psbr[:], ALU.mult, [acc, psbr], [scr])
                            else:
                                TTe("dve", sg, sg, psbr[:], ALU.mult, [acc, psbr], [acc])
                                TTe("pool", scr[:, m, :], scr[:, m, :], sg, ALU.add, [acc, scr], [scr])
                chk('br')
                for m in range(8):
                    CP("act", hT[:, m, :], scr[:, m, :], [scr], [hT])
                for half in range(2):
                    w = load_w(wout_d[l][:, half * 512:(half + 1) * 512], 8, 512)
                    for mc in range(4):
                        m = half * 4 + mc
                        ps = PSF()
                        for k in range(8):
                            MM(ps[:], w[:, k, mc * 128:(mc + 1) * 128], hT[:, k, :], k == 0, k == 7, [w, hT], [ps])
                        STT("dve", xt[:, m, :], ps[:], modp[:, l, 16 + m:17 + m], xt[:, m, :], ALU.mult, ALU.add, [ps, modp, xt], [xt])

                chk('out')
                rmsnorm_mod(A2[:, l, :], modp[:, l, 24:32], l)
                for fg in range(11):
                    nf = 2
                    wa = load_wh(wfu_d[l][:, fg * 256:fg * 256 + 256], 8, 256)
                    wg = load_wh(wfu_d[l][:, DFF + fg * 256:DFF + fg * 256 + 256], 8, 256)
                    for mc in range(nf):
                        f = fg * 2 + mc
                        psa = PSF()
                        for k in range(8):
                            MM(psa[:], wa[:, k, mc * 128:(mc + 1) * 128], hT[:, k, :], k == 0, k == 7, [wa, hT], [psa])
                        psg = PSF()
                        for k in range(8):
                            MM(psg[:], wg[:, k, mc * 128:(mc + 1) * 128], hT[:, k, :], k == 0, k == 7, [wg, hT], [psg])
                        fb_ = 2 * (f % 2)
                        ab = acc[:, fb_, :, :].rearrange("p a b -> p (a b)")
                        CP("act", ab[:, 2:2 + TT], psa[:], [psa], [acc])
                        CP("pool", ab[:, 0:2], ccar[:, f, :], [ccar, acc], [acc])
                        CP("pool", ccar[:, f, :], ab[:, TT:TT + 2], [acc], [ccar])
                        c1 = acc[:, fb_ + 1, 0, :]
                        TS("dve", c1, ab[:, 2:2 + TT], cw[:, f, 2:3], cb[:, f:f + 1], ALU.mult, ALU.add, [acc, cw, cb], [acc])
                        STT("dve", c1, ab[:, 1:1 + TT], cw[:, f, 1:2], c1, ALU.mult, ALU.add, [acc, cw], [acc])
                        STT("dve", c1, ab[:, 0:TT], cw[:, f, 0:1], c1, ALU.mult, ALU.add, [acc, cw], [acc])
                        ge = acc[:, fb_ + 1, 1, :]
                        gelu_from(ge, c1, [acc], [acc], None, None)
                        TTe("dve", big[:, f, :], ge, psg[:], ALU.mult, [acc, psg], [big])
                for m in range(8):
                    wbi[0] += 1
                    wd = wb[wbi[0] % NWB]
                    wdv = wd[:].rearrange("p a b -> p (a b)")[:, 0:NF * 128].rearrange("p (f n) -> p f n", f=NF)
                    P.dma("pool", wdv, wfd_d[l][:, m * 128:(m + 1) * 128].rearrange("(f p) n -> p f n", p=128), writes=[wd])
                    ps = PSF()
                    for f in range(NF):
                        MM(ps[:], wdv[:, f, :], big[:, f, :], f == 0, f == NF - 1, [wd, big], [ps])
                    STT("dve", xt[:, m, :], ps[:], modp[:, l, 40 + m:41 + m], xt[:, m, :], ALU.mult, ALU.add, [ps, modp, xt], [xt])

                chk('ffn')
                if l < NL - 1:
                    P.dma("sp", x1T[:, t0:t0 + TT].rearrange("(k p) t -> p k t", p=128), xt[:], reads=[xt], writes=[b_x1])
                else:
                    rmsnorm_mod(gfin[:], None, l)
                    P.dma("sp", yT_out[:, t0:t0 + TT].rearrange("(k p) t -> p k t", p=128), scr[:], reads=[scr], writes=[b_out])
    except _Stop:
        src_t = xt
        ybi = {"att": 0, "rw": 1, "gmlp": 2}.get(STOP)
        if ybi is not None:
            MSET("pool", scr[:], 0.0, [scr])
            for c_ in range(4):
                CP("dve", scr[:, c_, :], YB[ybi][:, c_, :], [YB[ybi]], [scr])
            src_t = scr
        elif STOP == "br":
            src_t = scr
        elif STOP == "rw7":
            MSET("pool", scr[:], 0.0, [scr])
            CP("dve", scr[:, 0, :], acc[:, 0, 0, :], [acc], [scr])
            src_t = scr
        P.dma("sp", yT_out[:, 0:TT].rearrange("(k p) t -> p k t", p=128), src_t[:], reads=[src_t], writes=[b_out])
    return P.finish([b_out])


def gmlp_tile(P, L):
    g = L
    PSF, MM, ACT, TTe, TS, STT, CP = g["PSF"], g["MM"], g["ACT"], g["TTe"], g["TS"], g["STT"], g["CP"]
    hT, big, acc, scr, YB, wsT, bsT, lng, lnb = g["hT"], g["big"], g["acc"], g["scr"], g["YB"], g["wsT"], g["bsT"], g["lng"], g["lnb"]
    load_w, w_in, l, gelu_from = g["load_w"], g["w_in"], g["l"], g["gelu_from"]
    w = load_w(w_in[l][:, 3328:3840], 8, 512)
    for mc in range(4):
        ps = PSF()
        for k in range(8):
            MM(ps[:], w[:, k, mc * 128:(mc + 1) * 128], hT[:, k, :], k == 0, k == 7, [w, hT], [ps])
        gelu_from(scr[:, mc, :], ps[:], [ps], [scr], scr[:, 4, :], scr[:, 5, :])
    w = load_w(w_in[l][:, 3840:4352], 8, 512)
    stats = acc[:, 3, 1, 0:8]
    for blk in range(4):
        ps = PSF()
        for k in range(8):
            MM(ps[:], hT[:, k, blk * 128:(blk + 1) * 128], w[:, k, :], k == 0, k == 7, [w, hT], [ps])
        v = acc[:, 0, 0, :]
        gelu_from(v, ps[:], [ps], [acc], acc[:, 0, 1, :], acc[:, 1, 0, :])
        P.op("dve", lambda e: e.bn_stats(out=acc[:, 3, 1, 0:6], in_=v), ar=[v], aw=[acc[:, 3, 1, 0:6]])
        P.op("dve", lambda e: e.bn_aggr(out=acc[:, 3, 1, 6:8], in_=acc[:, 3, 1, 0:6]), ar=[acc[:, 3, 1, 0:6]], aw=[acc[:, 3, 1, 6:8]])
        ACT(acc[:, 3, 1, 7:8], acc[:, 3, 1, 7:8], AF.Sqrt, [acc], [acc], bias=1e-5)
        P.op("dve", lambda e: e.reciprocal(out=acc[:, 3, 1, 7:8], in_=acc[:, 3, 1, 7:8]), ar=[acc[:, 3, 1, 7:8]], aw=[acc[:, 3, 1, 7:8]])
        TS("dve", v, v, acc[:, 3, 1, 6:7], acc[:, 3, 1, 7:8], ALU.subtract, ALU.mult, [acc], [acc])
        TTe("dve", v, v, lng[:], ALU.mult, [acc, lng], [acc])
        vb = big[:, 18, :]
        TTe("dve", vb, v, lnb[:], ALU.add, [acc, lnb], [big])
        psA = PSF()
        psB = PSF()
        for gq in range(8):
            pp = psA if gq < 4 else psB
            c = gq // 2
            MM(pp[:, (gq % 4) * 128:(gq % 4) * 128 + 128], vb[:, c * 128:(c + 1) * 128], wsT[:, gq, :], True, True, [big, wsT], [pp])
        for gq in range(8):
            pp = psA if gq < 4 else psB
            c = gq // 2
            rows = slice((gq % 2) * 64, (gq % 2) * 64 + 64)
            tmp = acc[rows, 1, 1, 0:128]
            TTe("dve", tmp, pp[rows, (gq % 4) * 128:(gq % 4) * 128 + 128], bsT[rows, c, :], ALU.add, [pp, bsT], [acc])
            TTe("dve", YB[2][rows, c, blk * 128:(blk + 1) * 128], tmp, scr[rows, c, blk * 128:(blk + 1) * 128], ALU.mult, [acc, scr], [YB[2]])


def rwkv_tile(P, L):
    g = L
    PSF, PSB, MM, TR, ACT, TTe, TS, STT, CP, MSET = (g[k] for k in ("PSF", "PSB", "MM", "TR", "ACT", "TTe", "TS", "STT", "CP", "MSET"))
    big, scr, acc, YB, H0, H0b = g["big"], g["scr"], g["acc"], g["YB"], g["H0"], g["H0b"]
    rwp, omka, wupb, aupb, gupb, gnw, gnb = g["rwp"], g["omka"], g["wupb"], g["aupb"], g["gupb"], g["gnw"], g["gnb"]
    cst, cstb, mrw, mlo, sel = g["cst"], g["cstb"], g["mrw"], g["mlo"], g["sel"]
    chk = g["chk"]
    identb, identf, blkones = g["identb"], g["identf"], g["blkones"]
    rw = getattr(P, "_rw", None)
    if rw is None:
        sb = P.sb
        rw = dict(
            th=sb([128, 128], BF16), sgx=sb([128, 128], BF16),
            **{nm: View(scr[:, i_, :].rearrange("p (a b) -> p a b", a=4), scr) for i_, nm in enumerate(("lw", "cum", "cum2", "av", "kk", "km", "t1", "t2"))},
            AR=sb([128, 4, 2, 128], BF16), BK=sb([128, 4, 2, 128], BF16),
            wc=sb([128, 4], F32),
            tok=View(g["QT"][:, 0:3, :], g["QT"]),
            Xs=View(g["QT"][:, 3, :], g["QT"]),
            **{nm: View(g["V3g"][0][:, i_ // 2, (i_ % 2) * 2:(i_ % 2) * 2 + 2, :].rearrange("p a (h t) -> p (a h) t", h=4), g["V3g"][0])
               for i_, nm in enumerate(("N1", "L1", "N2", "L2"))},
            **{nm: View(YB[2][:, 2 * i_:2 * i_ + 2, :].rearrange("p a (h t) -> p (a h) t", h=4), YB[2]) for i_, nm in enumerate(("IL", "Mm"))},
            **{nm: View(big[:, 14 + 2 * i_:16 + 2 * i_, :].rearrange("p a (h t) -> p (a h) t", h=4), big)
               for i_, nm in enumerate(("Mm2", "Arb", "Aak", "Ark"))},
            Y=View(acc[:, 0, 0, :], acc), Y2=View(acc[:, 0, 1, :], acc), st=sb([128, 4, 8], F32), gt=View(acc[:, 1, 0, :], acc),
            Ht=sb([128, 4, 64], F32),
            Noff=View(g["rstd"][:].bitcast(BF16).rearrange("p (h t) -> p h t", h=8), g["rstd"]),
        )
        P._rw = rw
    rw = P._rw
    th, sgx, lw, cum, cum2, av, kk, km, t1, t2 = (rw[k] for k in ("th", "sgx", "lw", "cum", "cum2", "av", "kk", "km", "t1", "t2"))
    AR, BK, wc, tok = rw["AR"], rw["BK"], rw["wc"], rw["tok"]
    N1, L1, N2, L2, IL, Mm, Mm2, Arb, Aak, Ark = (rw[k] for k in ("N1", "L1", "N2", "L2", "IL", "Mm", "Mm2", "Arb", "Aak", "Ark"))
    V1r_ = g["V1r"]
    fs_ = (4, 5, 6) if g["it"] % 2 == 0 else (0, 1, 2)
    Us = View(V1r_[:, fs_[0], :], V1r_)
    yb = View(V1r_[:, fs_[1], :], V1r_)
    prod = View(V1r_[:, fs_[2], :].rearrange("p (a b) -> p a b", a=4), V1r_)
    Xs, Y, Y2, st, gt, Ht = (rw[k] for k in ("Xs", "Y", "Y2", "st", "gt", "Ht"))
    W0, A0, KK_, KA, RK = (rwp[:, i, :] for i in range(5))
    EH = 0.6065306597126334

    for blk in range(4):
        ts = slice(blk * 128, (blk + 1) * 128)
        zr = lambda c: big[:, 0 + c, ts]
        zk = lambda c: big[:, 4 + c, ts]
        zv = lambda c: big[:, 8 + c, ts]
        zwa = big[:, 12, ts]
        zg = big[:, 13, ts]
        ACT(th[0:64, :], big[0:64, 12, ts], AF.Tanh, [big], [th])
        ACT(sgx[:], zg, AF.Sigmoid, [big], [sgx])
        ps = PSF()
        psa = PSF()
        for c in range(4):
            MM(ps[:, c * 128:(c + 1) * 128], wupb[0:64, c * 128:(c + 1) * 128], th[0:64, :], True, True, [wupb, th], [ps])
            MM(psa[:, c * 128:(c + 1) * 128], aupb[64:128, c * 128:(c + 1) * 128], big[64:128, 12, ts], True, True, [aupb, big], [psa])
        for c in range(4):
            ACT(lw[:, c, :], ps[:, c * 128:(c + 1) * 128], AF.Exp, [ps, g["nw0"]], [lw], bias=g["nw0"][:, c:c + 1], scale=-1.0)
            ACT(av[:, c, :], psa[:, c * 128:(c + 1) * 128], AF.Sigmoid, [psa, rwp], [av], bias=A0[:, c:c + 1])
        TS("dve", lw[:], lw[:], 1.0, None, ALU.add, None, [lw], [lw])
        P.op("dve", lambda e: e.reciprocal(out=lw[:], in_=lw[:]), ar=[lw[:]], aw=[lw[:]])
        TS("dve", lw[:], lw[:], -EH, None, ALU.mult, None, [lw], [lw])
        psg = PSF()
        MM(psg[:], sgx[:], gupb[:], True, True, [sgx, gupb], [psg])
        CP("act", gt[:], psg[:], [psg], [gt])
        chk('rw1')
        for c in range(4):
            TS("dve", kk[:, c, :], zk(c), KK_[:, c:c + 1], None, ALU.mult, None, [big, rwp], [kk])
        ACT(t1[:], kk[:], AF.Square, [kk], [t1])
        ps = PSF()
        for c in range(4):
            MM(ps[:, c * 128:(c + 1) * 128], blkones, t1[:, c, :], True, True, [cst, t1], [ps])
        ACT(t1[:].rearrange("p a b -> p (a b)"), ps[:], AF.Sqrt, [ps], [t1])
        TS("dve", t1[:], t1[:], 1e-12, None, ALU.max, None, [t1], [t1])
        P.op("dve", lambda e: e.reciprocal(out=t1[:], in_=t1[:]), ar=[t1[:]], aw=[t1[:]])
        TTe("dve", kk[:], kk[:], t1[:], ALU.mult, [kk, t1], [kk])
        for c in range(4):
            TS("dve", t2[:, c, :], av[:, c, :], KA[:, c:c + 1], omka[:, c:c + 1], ALU.mult, ALU.add, [av, rwp, omka], [t2])
            TTe("dve", km[:, c, :], t2[:, c, :], zk(c), ALU.mult, [t2, big], [km])
        for c in range(4):
            STT("dve", prod[:, c, :], zr(c), RK[:, c:c + 1], km[:, c, :], ALU.mult, ALU.mult, [big, rwp, km], [prod])
        psr = PSF()
        for c in range(4):
            MM(psr[:, 0:8], prod[:, c, :], sel[:, c, :], c == 0, c == 3, [prod, sel], [psr])
        CP("act", st[:, 0, :], psr[:, 0:8], [psr], [st])
        chk('rw2')
        src, dst = lw, cum
        CP("pool", cum[:], lw[:], [lw], [cum])
        a_, b_ = cum, cum2
        for s in (1, 2, 4, 8, 16, 32, 64):
            CP("pool", b_[:, :, 0:s], a_[:, :, 0:s], [a_], [b_])
            TTe("dve", b_[:, :, s:128], a_[:, :, s:128], a_[:, :, 0:128 - s], ALU.add, [a_], [b_])
            a_, b_ = b_, a_
        cm = a_
        ot = b_
        ACT(t1[:], cm[:], AF.Exp, [cm], [t1])
        for c in range(4):
            TTe("dve", AR[:, c, 1, :], t1[:, c, :], zr(c), ALU.mult, [t1, big], [AR])
        ACT(wc[:], cm[:, :, 127], AF.Exp, [cm], [wc])
        TTe("dve", ot[:], cm[:], lw[:], ALU.subtract, [cm, lw], [ot])
        ACT(t1[:], ot[:], AF.Exp, [ot], [t1])
        STT("dve", AR[:, :, 0, :], kk[:], -1.0, t1[:], ALU.mult, ALU.mult, [kk, t1], [AR])
        ACT(t1[:], cm[:], AF.Exp, [cm], [t1], scale=-1.0)
        TTe("dve", t2[:], kk[:], av[:], ALU.mult, [kk, av], [t2])
        TTe("dve", BK[:, :, 0, :], t2[:], t1[:], ALU.mult, [t2, t1], [BK])
        TTe("dve", BK[:, :, 1, :], km[:], t1[:], ALU.mult, [km, t1], [BK])
        chk('rw3')
        pb = PSB()
        for c in range(4):
            TR(pb[:, c * 128:(c + 1) * 128], zv(c), identb, [big, cstb], [pb])
        CP("act", tok[:, 0, :], pb[:, 0:512], [pb], [tok])
        for q in range(2):
            pb = PSB()
            for c in range(4):
                TR(pb[:, c * 128:(c + 1) * 128], BK[:, c, q, :], identb, [BK, cstb], [pb])
            CP("act", tok[:, 1 + q, :], pb[:, 0:512], [pb], [tok])
        chk('rw4')
        for hg in range(2):
            p1 = PSF(); p2 = PSF(); p3 = PSF(); p4 = PSF()
            for hq in range(4):
                c, hh = hq, hg
                rows = slice(hh * 64, hh * 64 + 64)
                half = hq % 2
                pa = p1 if hq < 2 else p2
                MM(pa[:, half * 256:half * 256 + 256], BK[rows, c, 0, :], AR[rows, c, :, :].rearrange("p a b -> p (a b)"), True, True, [BK, AR], [pa])
                pk = p3 if hq < 2 else p4
                MM(pk[:, half * 256:half * 256 + 256], BK[rows, c, 1, :], AR[rows, c, :, :].rearrange("p a b -> p (a b)"), True, True, [BK, AR], [pk])
            chk('rw4a')
            for i2, (pa, pk) in enumerate(((p1, p3), (p2, p4))):
                h0 = hg * 4 + i2 * 2
                sa = pa[:].rearrange("p (h z t) -> p h z t", h=2, z=2)
                sk = pk[:].rearrange("p (h z t) -> p h z t", h=2, z=2)
                TTe("dve", N1[:, h0:h0 + 2, :], sa[:, :, 0, :], mrw[:, 0:2, 0, :], ALU.mult, [pa, mrw], [N1])
                TTe("dve", Arb[:, h0:h0 + 2, :], sa[:, :, 1, :], mrw[:, 0:2, 1, :], ALU.mult, [pa, mrw], [Arb])
                TTe("dve", Aak[:, h0:h0 + 2, :], sk[:, :, 0, :], mrw[:, 0:2, 0, :], ALU.mult, [pk, mrw], [Aak])
                TTe("dve", Ark[:, h0:h0 + 2, :], sk[:, :, 1, :], mrw[:, 0:2, 1, :], ALU.mult, [pk, mrw], [Ark])
            chk('rw4b')
            pl = PSF()
            for hq in range(4):
                c, hh = hq, hg
                rows = slice(hh * 64, hh * 64 + 64)
                MM(pl[:, hq * 128:(hq + 1) * 128], AR[rows, c, 0, :], BK[rows, c, 0, :], True, True, [AR, BK], [pl])
            TTe("dve", L1[:, hg * 4:hg * 4 + 4, :], pl[:].rearrange("p (h t) -> p h t", h=4), mlo[:], ALU.mult, [pl, mlo], [L1])
        chk('rw5')
        Noff = rw["Noff"]
        MSET("pool", Noff[:], 0.0, [Noff])
        CP("pool", Noff[0:64, :, 64:128], N1[0:64, :, 64:128], [N1], [Noff])
        MSET("pool", N1[0:64, :, 64:128], 0.0, [N1])
        MSET("pool", L1[64:128, :, 0:64], 0.0, [L1])
        for hg in range(2):
            hs = slice(hg * 4, hg * 4 + 4)
            TTe("pool", Mm[:, hs, :], N1[:, hs, :], cstb[:, 4:8, :], ALU.add, [N1, cstb], [Mm])
        Nk, Lk, Nn, Ln = N1, L1, N2, L2
        Mc, Mn = Mm, Mm2
        for lev in range(5):
            last = lev == 4
            for hg in range(2):
                hs = slice(hg * 4, hg * 4 + 4)
                pn = PSF()
                for hq in range(4):
                    h = hg * 4 + hq
                    MM(pn[:, hq * 128:(hq + 1) * 128], Lk[:, h, :], Nk[:, h, :], True, True, [Lk, Nk], [pn])
                if not last:
                    CP("act", Nn[:, hs, :], pn[:].rearrange("p (h t) -> p h t", h=4), [pn], [Nn])
                pL = PSF()
                for hq in range(4):
                    h = hg * 4 + hq
                    MM(pL[:, hq * 128:(hq + 1) * 128], Nk[:, h, :], Lk[:, h, :], True, True, [Lk, Nk], [pL])
                if not last:
                    CP("act", Ln[:, hs, :], pL[:].rearrange("p (h t) -> p h t", h=4), [pL], [Ln])
                TTe("dve", IL[:, hs, :], pL[:].rearrange("p (h t) -> p h t", h=4), cstb[:, 4:8, :], ALU.add, [pL, cstb], [IL])
            for hg in range(2):
                hs = slice(hg * 4, hg * 4 + 4)
                pm = PSF()
                for hq in range(4):
                    h = hg * 4 + hq
                    MM(pm[:, hq * 128:(hq + 1) * 128], IL[:, h, :], Mc[:, h, :], True, True, [IL, Mc], [pm])
                CP("act", Mn[:, hs, :], pm[:].rearrange("p (h t) -> p h t", h=4), [pm], [Mn])
            Nk, Nn = Nn, Nk
            Lk, Ln = Ln, Lk
            Mc, Mn = Mn, Mc
        Mf = Mc
        chk('rw6')
        px = PSF()
        for h in range(8):
            c, hh = h % 4, h // 4
            hn = 2 * c + hh
            rows = slice(hh * 64, hh * 64 + 64)
            o = px[:, hn * 64:(hn + 1) * 64]
            MM(o, AR[rows, c, 0, :], H0b[rows, c, :], True, False, [AR, H0b], [px])
            MM(o, Aak[:, h, :], tok[:, 0, hn * 64:(hn + 1) * 64], False, True, [Aak, tok], [px])
        CP("act", Xs[:], px[:], [px], [Xs])
        Noff = rw["Noff"]
        for rnd in range(2):
            pu = PSF()
            for h in range(8):
                c, hh = h % 4, h // 4
                hn = 2 * c + hh
                MM(pu[:, hn * 64:(hn + 1) * 64], Mf[:, h, :], Xs[:, hn * 64:(hn + 1) * 64], True, True, [Mf, Xs], [pu])
            CP("act", Us[:], pu[:], [pu], [Us])
            if rnd == 0:
                pw = PSF()
                for h in range(8):
                    c, hh = h % 4, h // 4
                    hn = 2 * c + hh
                    MM(pw[:, hn * 64:(hn + 1) * 64], Noff[:, h, :], Us[:, hn * 64:(hn + 1) * 64], True, True, [Noff, Us], [pw])
                TTe("dve", Xs[:], pw[:], Xs[:], ALU.add, [pw, Xs], [Xs])
        py = PSF()
        for h in range(8):
            c, hh = h % 4, h // 4
            hn = 2 * c + hh
            rows = slice(hh * 64, hh * 64 + 64)
            o = py[:, hn * 64:(hn + 1) * 64]
            MM(o, AR[rows, c, 1, :], H0b[rows, c, :], True, False, [AR, H0b], [py])
            MM(o, Arb[:, h, :], Us[:, hn * 64:(hn + 1) * 64], False, False, [Arb, Us], [py])
            MM(o, Ark[:, h, :], tok[:, 0, hn * 64:(hn + 1) * 64], False, True, [Ark, tok], [py])
        CP("act", Y[:], py[:], [py], [Y])
        ph = PSF()
        for c in range(4):
            o = ph[:, c * 128:(c + 1) * 128]
            MM(o, tok[:, 1, c * 128:(c + 1) * 128], Us[:, c * 128:(c + 1) * 128], True, False, [tok, Us], [ph])
            MM(o, tok[:, 2, c * 128:(c + 1) * 128], tok[:, 0, c * 128:(c + 1) * 128], False, True, [tok], [ph])
        for hh in range(2):
            rows = slice(hh * 64, hh * 64 + 64)
            src = ph[rows, :].rearrange("p (c x i) -> p c x i", c=4, x=2)[:, :, hh, :]
            TTe("dve", Ht[rows, :, :], src, H0[rows, :, :], ALU.add, [ph, H0], [Ht])
        for c in range(4):
            TS("dve", H0[:, c, :], Ht[:, c, :], wc[:, c:c + 1], None, ALU.mult, None, [Ht, wc], [H0])
        CP("act", H0b[:], H0[:], [H0], [H0b])
        chk('rw7')
        Y3 = Y[:].rearrange("p (h i) -> p h i", h=8)
        P.op("dve", lambda e: e.tensor_reduce(out=st[:, 1, :], in_=Y3, axis=AX.X, op=ALU.add), ar=[Y3], aw=[st[:, 1, :]])
        ACT(Y2[:], Y[:], AF.Square, [Y], [Y2])
        P.op("dve", lambda e: e.tensor_reduce(out=st[:, 2, :], in_=Y2[:].rearrange("p (h i) -> p h i", h=8), axis=AX.X, op=ALU.add), ar=[Y2[:]], aw=[st[:, 2, :]])
        TS("dve", st[:, 1, :], st[:, 1, :], 1.0 / 64, None, ALU.mult, None, [st], [st])
        TTe("dve", st[:, 3, :], st[:, 1, :], st[:, 1, :], ALU.mult, [st], [st])
        STT("dve", st[:, 2, :], st[:, 2, :], 1.0 / 64, st[:, 3, :], ALU.mult, ALU.subtract, [st], [st])
        ACT(st[:, 2, :], st[:, 2, :], AF.Sqrt, [st], [st], bias=64e-5)
        P.op("dve", lambda e: e.reciprocal(out=st[:, 2, :], in_=st[:, 2, :]), ar=[st[:, 2, :]], aw=[st[:, 2, :]])
        for h in range(8):
            TS("dve", Y2[:, h * 64:(h + 1) * 64], Y[:, h * 64:(h + 1) * 64], st[:, 1, h:h + 1], st[:, 2, h:h + 1], ALU.subtract, ALU.mult, [Y, st], [Y2])
        TTe("dve", Y2[:], Y2[:], gnw[:], ALU.mult, [Y2, gnw], [Y2])
        TTe("dve", Y2[:], Y2[:], gnb[:], ALU.add, [Y2, gnb], [Y2])
        for h in range(8):
            STT("dve", Y2[:, h * 64:(h + 1) * 64], tok[:, 0, h * 64:(h + 1) * 64], st[:, 0, h:h + 1], Y2[:, h * 64:(h + 1) * 64], ALU.mult, ALU.add, [tok, st, Y2], [Y2])
        TTe("dve", yb[:], Y2[:], gt[:], ALU.mult, [Y2, gt], [yb])
        pb = PSB()
        for c in range(4):
            TR(pb[:, c * 128:(c + 1) * 128], yb[:, c * 128:(c + 1) * 128], identb, [yb, cstb], [pb])
        CP("act", YB[1][:, :, ts], pb[:, 0:512].rearrange("p (c t) -> p c t", c=4), [pb], [YB[1]])


def _consts(T):
    bf = ml_dtypes.bfloat16
    cst = np.zeros((128, 5, 128), np.float32)
    cst[:, 0, :] = np.eye(128)
    cst[:, 1, :] = 1.0
    cst[0:64, 2, 0:64] = 1.0
    cst[64:128, 2, 64:128] = 1.0
    cstb = np.zeros((128, 8, 128), np.float32)
    cstb[:, 0, :] = np.eye(128)
    for m in range(128):
        cstb[(m // 64) * 64 + ((m % 64) + 32) % 64, 1, m] = 1.0
    cstb[:, 2, :] = 1.0
    for q_ in range(4):
        cstb[:, 4 + q_, :] = np.eye(128)
    ki = np.arange(128)[:, None]
    qi = np.arange(128)[None, :]
    m12 = np.zeros((128, 2, 256), np.float32)
    for u in range(2):
        m12[:, u, 0:128] = (ki >= qi)
        m12[:, u, 128:256] = (ki <= qi)
    m3 = np.zeros((128, 4, 4, 64), np.float32)
    for j in range(4):
        q = 32 * j + np.arange(32)[None, :]
        for rr in range(4):
            m3[:, j, rr, 0:32] = (ki >= q)
            m3[:, j, rr, 32:64] = (ki <= q)
    mrw = np.zeros((128, 4, 2, 128), np.float32)
    mrw[:, :, 0, :] = (ki < qi)[:, None, :]
    mrw[:, :, 1, :] = (ki <= qi)[:, None, :]
    mlo = np.zeros((128, 4, 128), np.float32)
    mlo[:, :, :] = (qi < ki)[:, None, :]
    sel = np.zeros((128, 4, 8), np.float32)
    for p in range(128):
        for c in range(4):
            sel[p, c, 2 * c + p // 64] = 1.0
    inv = (1.0 / (np.float32(10000.0) ** (np.arange(0, 64, 2, dtype=np.float32) / np.float32(64)))).astype(np.float32)
    ang = (np.arange(T, dtype=np.float32)[:, None] * inv[None, :]).astype(np.float32)
    cosv, sinv = np.cos(ang).astype(np.float32), np.sin(ang).astype(np.float32)
    cosT = np.zeros((128, T), np.float32)
    sinT = np.zeros((128, T), np.float32)
    for p in range(128):
        d = p % 64
        cosT[p] = cosv[:, d % 32]
        sinT[p] = sinv[:, d % 32] * (-1.0 if d < 32 else 1.0)
    return dict(cst_f32=cst, cst_bf=cstb.astype(bf), m12=m12.astype(bf), m3=m3.astype(bf), mrw=mrw.astype(bf),
                mlo=mlo.astype(bf), sel=sel.astype(bf), cosT=cosT, sinT=sinT)


def _col(v, n):
    return np.ascontiguousarray(np.asarray(v, np.float32).reshape(n, 128).T)


def _shared_inputs(inp, T):
    f = lambda a: np.ascontiguousarray(np.asarray(a, np.float32)[:NL]) if np.asarray(a).shape[0] == 2 and np.asarray(a).ndim >= 2 else np.ascontiguousarray(np.asarray(a, np.float32))
    d = dict(_consts(T))
    d["w_ada"] = f(inp["w_ada"])
    d["b_ada"] = np.stack([_col(inp["b_ada"][l], 48) for l in range(NL)])
    d["g_mix"] = np.stack([_col(inp["norm_mix"][l], 8) for l in range(NL)])
    d["g_ffn"] = np.stack([_col(inp["norm_ffn"][l], 8) for l in range(NL)])
    d["g_fin"] = _col(inp["norm_final"], 8)
    d["w_in"] = f(inp["w_in"])
    d["mu"] = np.stack([_col(inp["rwkv_mu"][l], 14) for l in range(NL)])
    d["rwp"] = np.stack([np.stack([_col(np.asarray(inp[k][l]).reshape(-1), 4) for k in
                                   ("rwkv_w0", "rwkv_a0", "rwkv_k_k", "rwkv_k_a", "rwkv_r_k")], axis=1) for l in range(NL)])
    d["r_wup"] = f(inp["rwkv_w_up"])
    d["r_aup"] = f(inp["rwkv_a_up"])
    d["r_gup"] = f(inp["rwkv_g_up"])
    bc = lambda a: np.ascontiguousarray(np.broadcast_to(f(a).reshape(NL, 1, 512), (NL, 128, 512)))
    d["gnw"] = bc(inp["rwkv_gn_w"])
    d["gnb"] = bc(inp["rwkv_gn_b"])
    d["lng"] = bc(inp["gmlp_ln_g"])
    d["lnb"] = bc(inp["gmlp_ln_b"])
    d["wsT"] = np.ascontiguousarray(np.transpose(f(inp["gmlp_w_s"]), (0, 3, 1, 2)))
    bsv = f(inp["gmlp_b_s"])
    bsT = np.zeros((NL, 128, 4, 128), np.float32)
    for g_ in range(8):
        bsT[:, (g_ % 2) * 64:(g_ % 2) * 64 + 64, g_ // 2, :] = bsv[:, g_, None, :]
    d["bsT"] = bsT
    d["w_branch"] = f(inp["w_branch"])
    d["w_out"] = f(inp["w_out"])
    d["ffn_up"] = f(inp["ffn_w_up"])
    d["convw"] = np.stack([np.ascontiguousarray(np.transpose(np.asarray(inp["ffn_conv_w"][l], np.float32).reshape(3, NF, 128), (2, 1, 0))) for l in range(NL)])
    d["convb"] = np.stack([_col(inp["ffn_conv_b"][l], NF) for l in range(NL)])
    d["ffn_down"] = f(inp["ffn_w_down"])
    return d


_NC_CACHE = {}


def run(inp, T=None):
    x = np.asarray(inp["x"], np.float32)
    B, S, _ = x.shape
    T = S
    if T not in _NC_CACHE:
        _NC_CACHE[T] = build(T)
    nc = _NC_CACHE[T]
    shared = _shared_inputs(inp, T)
    c = np.asarray(inp["c"], np.float32)
    in_maps = []
    for core in range(8):
        b = core % B
        m = dict(shared)
        m["xT"] = np.ascontiguousarray(x[b].T)
        m["c128"] = _col(c[b], 8)
        in_maps.append(m)
    res = run_bass_kernel_spmd(nc, in_maps, core_ids=list(range(8)))
    out = np.stack([np.ascontiguousarray(res.results[b]["yT"].T) for b in range(B)])
    return out.astype(np.float32)


def kernel(**inputs):
    return run(inputs)
```
